# Optimizing a Trainium2 kernel written in Bass

```python
import jax, jax.numpy as jnp
from jax import lax
import numpy as np

D_MODEL = 1024
BATCH = 8
SEQ = 4096
DEPTH = 2
DEC_BATCH = 16
DEC_SEQ = 32
PAST_LEN = 4096

CHUNK = 64
N_MEM = 256
MEM_HEADS = 4
MEM_HEAD_DIM = D_MODEL // MEM_HEADS
D_FF = 2816
LRU_WIDTH = D_MODEL // 2
LRU_BLOCKS = 8
LRU_BLOCK = LRU_WIDTH // LRU_BLOCKS
CONV_WIDTH = 4
LRU_C = 8.0
RET_HEADS = 4
RET_WIDTH = D_MODEL // 2
RET_HEAD_DIM = RET_WIDTH // RET_HEADS
ROPE_BASE = 10000.0
L0_IN = 2 * LRU_WIDTH + 4 * RET_WIDTH
RWKV_HEAD = 64
RWKV_HEADS = D_MODEL // RWKV_HEAD
LORA_W = 64
LORA_A = 64
LORA_G = 128
LN_EPS = 1e-5
RWKV_GN_EPS = 64e-5
ALPHA = (2 * DEPTH) ** 0.25
BETA = (8 * DEPTH) ** -0.25

kernel_name = "hybrid_streaming_encoder_step"


def layer_norm(x, g, b, eps=LN_EPS):
    xf = x.astype(jnp.float32)
    mu = xf.mean(-1, keepdims=True)
    var = jnp.square(xf - mu).mean(-1, keepdims=True)
    return ((xf - mu) * lax.rsqrt(var + eps) * g + b).astype(x.dtype)


def post_norm(x, sub, g, b):
    return layer_norm(ALPHA * x + sub, g, b)


def swiglu(x, w_up, w_down):
    gate, up = jnp.split(x @ w_up, 2, axis=-1)
    return (jax.nn.silu(gate) * up) @ w_down


def causal_conv(x, buf, w, b):
    T = x.shape[1]
    xpad = jnp.concatenate([buf.astype(x.dtype), x], axis=1)
    y = b + sum(xpad[:, j:j + T] * w[j] for j in range(CONV_WIDTH))
    return y, xpad[:, T:]


def rg_lru(x, h0, w_a, b_a, w_x, b_x, lam):
    B, T, _ = x.shape
    xb = x.reshape(B, T, LRU_BLOCKS, LRU_BLOCK)
    r = jax.nn.sigmoid(jnp.einsum('btgi,gij->btgj', xb, w_a).reshape(B, T, LRU_WIDTH) + b_a)
    i = jax.nn.sigmoid(jnp.einsum('btgi,gij->btgj', xb, w_x).reshape(B, T, LRU_WIDTH) + b_x)
    log_a = -LRU_C * jax.nn.softplus(-lam.astype(jnp.float32)) * r.astype(jnp.float32)
    a = jnp.exp(log_a)
    u = jnp.sqrt(-jnp.expm1(2.0 * log_a)) * (i * x).astype(jnp.float32)
    u = u.at[:, 0].add(a[:, 0] * h0.astype(jnp.float32))

    def combine(c1, c2):
        a1, b1 = c1
        a2, b2 = c2
        return a1 * a2, a2 * b1 + b2

    _, h = lax.associative_scan(combine, (a, u), axis=1)
    return h.astype(x.dtype), h[:, -1].astype(h0.dtype)


def rotary(x, pos):
    half = x.shape[-1] // 2
    inv_freq = ROPE_BASE ** (-jnp.arange(half, dtype=jnp.float32) / half)
    ang = pos.astype(jnp.float32)[:, None] * inv_freq[None, :]
    cos = jnp.cos(ang)[None, :, None, :]
    sin = jnp.sin(ang)[None, :, None, :]
    xf = x.astype(jnp.float32)
    x1, x2 = xf[..., :half], xf[..., half:]
    return jnp.concatenate([x1 * cos - x2 * sin, x1 * sin + x2 * cos], axis=-1)


def retention(q, k, v, s0, chunk):
    B, T, H, dk = q.shape
    dv = v.shape[-1]
    n = T // chunk
    log_g = jnp.log1p(-(2.0 ** (-5.0 - jnp.arange(H, dtype=jnp.float32))))
    qc = q.astype(jnp.float32).reshape(B, n, chunk, H, dk)
    kc = (k.astype(jnp.float32) * dk ** -0.5).reshape(B, n, chunk, H, dk)
    vc = v.astype(jnp.float32).reshape(B, n, chunk, H, dv)
    idx = jnp.arange(chunk, dtype=jnp.float32)
    diff = idx[:, None] - idx[None, :]
    dmask = jnp.where(diff >= 0, jnp.exp(log_g[:, None, None] * jnp.maximum(diff, 0.0)), 0.0)
    scores = jnp.einsum('bnihd,bnjhd->bnhij', qc, kc) * dmask
    inner = jnp.einsum('bnhij,bnjhv->bnihv', scores, vc)
    zeta = jnp.exp(log_g[:, None] * (chunk - 1.0 - idx)[None, :])
    kv_chunk = jnp.einsum('bnjhd,hj,bnjhv->bnhdv', kc, zeta, vc)
    chunk_decay = jnp.exp(log_g * chunk)[None, :, None, None]

    def step(S, kv):
        return chunk_decay * S + kv, S

    s_last, s_prev = lax.scan(step, s0.astype(jnp.float32), jnp.moveaxis(kv_chunk, 1, 0))
    s_prev = jnp.moveaxis(s_prev, 0, 1)
    xi = jnp.exp(log_g[None, :] * (idx[:, None] + 1.0))[:, :, None]
    cross = jnp.einsum('bnihd,bnhdv->bnihv', qc, s_prev) * xi
    return (inner + cross).reshape(B, T, H, dv), s_last


def mixer_ab(x, pos, conv_buf, h0, s0, w_in, conv_w, conv_b, lru_wa, lru_ba, lru_wx, lru_bx,
             lru_lambda, ret_gn_g, ret_gn_b, w_out):
    B, T, _ = x.shape
    xa, ga, q, k, v, g = jnp.split(x @ w_in, 6, axis=-1)
    xc, new_buf = causal_conv(xa, conv_buf, conv_w, conv_b)
    h, h_last = rg_lru(xc, h0, lru_wa, lru_ba, lru_wx, lru_bx, lru_lambda)
    ya = h * jax.nn.gelu(ga)
    q = rotary(q.reshape(B, T, RET_HEADS, RET_HEAD_DIM), pos)
    k = rotary(k.reshape(B, T, RET_HEADS, RET_HEAD_DIM), pos)
    v = v.reshape(B, T, RET_HEADS, RET_HEAD_DIM)
    o, s_last = retention(q, k, v, s0, min(CHUNK, T))
    o = layer_norm(o, ret_gn_g.reshape(RET_HEADS, RET_HEAD_DIM), ret_gn_b.reshape(RET_HEADS, RET_HEAD_DIM))
    yb = o.reshape(B, T, RET_WIDTH).astype(x.dtype) * jax.nn.silu(g)
    y = jnp.concatenate([ya, yb], axis=-1) @ w_out
    return y, (new_buf, h_last, s_last.astype(s0.dtype))


def mixer_c(x, shift_buf, s0, mu, w_rkv, w0, w1, w2, a0, a1, a2, g1, g2, k_k, k_a, r_k,
            gn_g, gn_b, w_out):
    B, T, D = x.shape
    f32 = jnp.float32
    x_prev = jnp.concatenate([shift_buf.astype(x.dtype), x[:, :-1]], axis=1)
    xm = x[None] + (x_prev - x)[None] * mu[:, None, None, :]
    rkv = jnp.einsum('pbtd,pde->pbte', xm[:3], w_rkv)
    r, k, v = rkv[0], rkv[1], rkv[2]
    w = -jax.nn.softplus(-(w0 + jnp.tanh(xm[3] @ w1) @ w2)) - 0.5
    decay = jnp.exp(-jnp.exp(w.astype(f32)))
    iclr = jax.nn.sigmoid(a0 + (xm[4] @ a1) @ a2)
    g = jax.nn.sigmoid(xm[5] @ g1) @ g2
    hs = lambda t: t.reshape(B, T, RWKV_HEADS, RWKV_HEAD).astype(f32)
    r, k, v, decay, iclr = hs(r), hs(k), hs(v), hs(decay), hs(iclr)
    kk = k * k_k.reshape(RWKV_HEADS, RWKV_HEAD)
    kk = kk / jnp.maximum(jnp.linalg.norm(kk, axis=-1, keepdims=True), 1e-12)
    k = k * (1.0 + (iclr - 1.0) * k_a.reshape(RWKV_HEADS, RWKV_HEAD))

    def step(S, inp):
        r_t, w_t, k_t, v_t, kk_t, b_t = inp
        sa = jnp.einsum('bhvk,bhk->bhv', S, -kk_t)
        S = S * w_t[:, :, None, :] + sa[..., None] * b_t[:, :, None, :] + v_t[..., None] * k_t[:, :, None, :]
        return S, jnp.einsum('bhvk,bhk->bhv', S, r_t)

    tm = lambda t: jnp.moveaxis(t, 1, 0)
    s_last, o = lax.scan(step, s0.astype(f32), (tm(r), tm(decay), tm(k), tm(v), tm(kk), tm(kk * iclr)))
    o = jnp.moveaxis(o, 0, 1)
    o = layer_norm(o, gn_g.reshape(RWKV_HEADS, RWKV_HEAD), gn_b.reshape(RWKV_HEADS, RWKV_HEAD), RWKV_GN_EPS)
    o = o + (r * k * r_k).sum(-1, keepdims=True) * v
    y = (o.reshape(B, T, D).astype(x.dtype) * g) @ w_out
    return y, (x[:, -1:].astype(shift_buf.dtype), s_last.astype(s0.dtype))


def cross_attn(x, mem_k, mem_v, w_q, w_o):
    B, T, D = x.shape
    q = (x @ w_q).reshape(B, T, MEM_HEADS, MEM_HEAD_DIM)
    s = jnp.einsum('bthd,bmhd->bhtm', q, mem_k).astype(jnp.float32) * MEM_HEAD_DIM ** -0.5
    p = jax.nn.softmax(s, axis=-1).astype(x.dtype)
    o = jnp.einsum('bhtm,bmhd->bthd', p, mem_v).reshape(B, T, D)
    return o @ w_o


def run_trunk(x, pos, mem_k, mem_v, states, ln_g, ln_b, ffn_up, ffn_down, xa_q, xa_o, mixer_params):
    new_states = []
    for layer in range(DEPTH):
        x = post_norm(x, 0.5 * swiglu(x, ffn_up[layer, 0], ffn_down[layer, 0]), ln_g[layer, 0], ln_b[layer, 0])
        if layer % 2 == 0:
            y, st = mixer_ab(x, pos, *states[layer], *mixer_params[layer])
        else:
            y, st = mixer_c(x, *states[layer], *mixer_params[layer])
        x = post_norm(x, y, ln_g[layer, 1], ln_b[layer, 1])
        x = post_norm(x, cross_attn(x, mem_k[layer], mem_v[layer], xa_q[layer], xa_o[layer]),
                      ln_g[layer, 2], ln_b[layer, 2])
        x = post_norm(x, 0.5 * swiglu(x, ffn_up[layer, 1], ffn_down[layer, 1]), ln_g[layer, 3], ln_b[layer, 3])
        new_states.append(st)
    return x, new_states


def setup_inputs(seed: int = 0) -> dict:
    key = jax.random.key(seed)
    ks = iter(jax.random.split(key, 64))
    f32 = jnp.float32

    def nrm(shape, scale):
        return scale * jax.random.normal(next(ks), shape, f32)

    def uni(shape, lo, hi):
        return jax.random.uniform(next(ks), shape, f32, minval=lo, maxval=hi)

    D = D_MODEL
    lam_p = uni((LRU_WIDTH,), 0.9, 0.999)
    return {
        'x_prompt': nrm((BATCH, SEQ, D), 1.0),
        'x_sample': nrm((DEC_BATCH, DEC_SEQ, D), 1.0),
        'mem_prompt': nrm((BATCH, N_MEM, D), 1.0),
        'state_conv0': nrm((DEC_BATCH, CONV_WIDTH - 1, LRU_WIDTH), 1.0),
        'state_lru0': nrm((DEC_BATCH, LRU_WIDTH), 0.5),
        'state_ret0': nrm((DEC_BATCH, RET_HEADS, RET_HEAD_DIM, RET_HEAD_DIM), 0.5),
        'state_shift1': nrm((DEC_BATCH, 1, D), 1.0),
        'state_wkv1': nrm((DEC_BATCH, RWKV_HEADS, RWKV_HEAD, RWKV_HEAD), 0.5),
        'cache_mem_k': nrm((DEPTH, DEC_BATCH, N_MEM, MEM_HEADS, MEM_HEAD_DIM), 1.0),
        'cache_mem_v': nrm((DEPTH, DEC_BATCH, N_MEM, MEM_HEADS, MEM_HEAD_DIM), 1.0),
        'ln_g': 1.0 + nrm((DEPTH, 4, D), 0.02),
        'ln_b': nrm((DEPTH, 4, D), 0.02),
        'ffn_up': nrm((DEPTH, 2, D, 2 * D_FF), D ** -0.5),
        'ffn_down': nrm((DEPTH, 2, D_FF, D), BETA * D_FF ** -0.5),
        'xa_q': nrm((DEPTH, D, D), D ** -0.5),
        'xa_k': nrm((DEPTH, D, D), D ** -0.5),
        'xa_v': nrm((DEPTH, D, D), D ** -0.5),
        'xa_o': nrm((DEPTH, D, D), BETA * D ** -0.5),
        'l0_w_in': nrm((D, L0_IN), D ** -0.5),
        'l0_conv_w': nrm((CONV_WIDTH, LRU_WIDTH), CONV_WIDTH ** -0.5),
        'l0_conv_b': nrm((LRU_WIDTH,), 0.01),
        'l0_lru_wa': nrm((LRU_BLOCKS, LRU_BLOCK, LRU_BLOCK), LRU_BLOCK ** -0.5),
        'l0_lru_ba': nrm((LRU_WIDTH,), 0.01),
        'l0_lru_wx': nrm((LRU_BLOCKS, LRU_BLOCK, LRU_BLOCK), LRU_BLOCK ** -0.5),
        'l0_lru_bx': nrm((LRU_WIDTH,), 0.01),
        'l0_lru_lambda': jnp.log(lam_p) - jnp.log1p(-lam_p),
        'l0_ret_gn_g': 1.0 + nrm((RET_WIDTH,), 0.02),
        'l0_ret_gn_b': nrm((RET_WIDTH,), 0.02),
        'l0_w_out': nrm((LRU_WIDTH + RET_WIDTH, D), BETA * (LRU_WIDTH + RET_WIDTH) ** -0.5),
        'l1_mu': uni((6, D), 0.0, 1.0),
        'l1_w_rkv': nrm((3, D, D), D ** -0.5),
        'l1_w0': uni((D,), -6.5, -1.5),
        'l1_w1': nrm((D, LORA_W), D ** -0.5),
        'l1_w2': nrm((LORA_W, D), 0.1 * LORA_W ** -0.5),
        'l1_a0': nrm((D,), 0.1),
        'l1_a1': nrm((D, LORA_A), D ** -0.5),
        'l1_a2': nrm((LORA_A, D), 0.1 * LORA_A ** -0.5),
        'l1_g1': nrm((D, LORA_G), D ** -0.5),
        'l1_g2': nrm((LORA_G, D), LORA_G ** -0.5),
        'l1_k_k': 0.85 + nrm((D,), 0.05),
        'l1_k_a': 1.0 + nrm((D,), 0.05),
        'l1_r_k': nrm((RWKV_HEADS, RWKV_HEAD), 0.1),
        'l1_gn_g': 1.0 + nrm((D,), 0.02),
        'l1_gn_b': nrm((D,), 0.02),
        'l1_w_out': nrm((D, D), BETA * D ** -0.5),
    }


def reference(x_prompt, x_sample, mem_prompt, state_conv0, state_lru0, state_ret0, state_shift1, state_wkv1,
              cache_mem_k, cache_mem_v, ln_g, ln_b, ffn_up, ffn_down, xa_q, xa_k, xa_v, xa_o,
              l0_w_in, l0_conv_w, l0_conv_b, l0_lru_wa, l0_lru_ba, l0_lru_wx, l0_lru_bx, l0_lru_lambda,
              l0_ret_gn_g, l0_ret_gn_b, l0_w_out, l1_mu, l1_w_rkv, l1_w0, l1_w1, l1_w2, l1_a0, l1_a1, l1_a2,
              l1_g1, l1_g2, l1_k_k, l1_k_a, l1_r_k, l1_gn_g, l1_gn_b, l1_w_out):
    params_ab = (l0_w_in, l0_conv_w, l0_conv_b, l0_lru_wa, l0_lru_ba, l0_lru_wx, l0_lru_bx, l0_lru_lambda,
                 l0_ret_gn_g, l0_ret_gn_b, l0_w_out)
    params_c = (l1_mu, l1_w_rkv, l1_w0, l1_w1, l1_w2, l1_a0, l1_a1, l1_a2, l1_g1, l1_g2, l1_k_k, l1_k_a,
                l1_r_k, l1_gn_g, l1_gn_b, l1_w_out)
    mixer_params = (params_ab, params_c)

    Bp, Tp, _ = x_prompt.shape
    dt = x_prompt.dtype
    pos_p = jnp.arange(Tp, dtype=jnp.int32)
    mem_k_p = jnp.einsum('bmd,lde->lbme', mem_prompt, xa_k).reshape(DEPTH, Bp, N_MEM, MEM_HEADS, MEM_HEAD_DIM)
    mem_v_p = jnp.einsum('bmd,lde->lbme', mem_prompt, xa_v).reshape(DEPTH, Bp, N_MEM, MEM_HEADS, MEM_HEAD_DIM)
    zero_states = ((jnp.zeros((Bp, CONV_WIDTH - 1, LRU_WIDTH), dt), jnp.zeros((Bp, LRU_WIDTH), dt),
                    jnp.zeros((Bp, RET_HEADS, RET_HEAD_DIM, RET_HEAD_DIM), dt)),
                   (jnp.zeros((Bp, 1, D_MODEL), dt), jnp.zeros((Bp, RWKV_HEADS, RWKV_HEAD, RWKV_HEAD), dt)))
    y_prompt, st_p = run_trunk(x_prompt, pos_p, mem_k_p, mem_v_p, zero_states, ln_g, ln_b, ffn_up, ffn_down,
                               xa_q, xa_o, mixer_params)

    Ts = x_sample.shape[1]
    pos_s = PAST_LEN + jnp.arange(Ts, dtype=jnp.int32)
    sample_states = ((state_conv0, state_lru0, state_ret0), (state_shift1, state_wkv1))
    y_sample, st_s = run_trunk(x_sample, pos_s, cache_mem_k, cache_mem_v, sample_states, ln_g, ln_b, ffn_up,
                               ffn_down, xa_q, xa_o, mixer_params)

    (p_conv0, p_lru0, p_ret0), (p_shift1, p_wkv1) = st_p
    (s_conv0, s_lru0, s_ret0), (s_shift1, s_wkv1) = st_s
    return (y_prompt, y_sample, mem_k_p, mem_v_p, p_conv0, p_lru0, p_ret0, p_shift1, p_wkv1,
            s_conv0, s_lru0, s_ret0, s_shift1, s_wkv1)
```

```python
import contextlib
import numpy as np
import concourse.bass as bass
import concourse.mybir as mybir
from concourse.bass_utils import run_bass_kernel_spmd

F32 = mybir.dt.float32
BF16 = mybir.dt.bfloat16
AF = mybir.ActivationFunctionType
ALU = mybir.AluOpType

D = 1024
KC = 8
NCORES = 8
TS = 32
DFF = 2816
ALPHA = 4.0 ** 0.25
LN_EPS = 1e-5
PAST = 4096
SAME_ENGINE_WAITS = True


class Eng:
    def __init__(self, name, eng, sem):
        self.name = name
        self.eng = eng
        self.sem = sem
        self.count = 0
        self.known = {}


class Buf:
    def __init__(self, t, name):
        self.t = t
        self.name = name
        self.w = {}
        self.r = {}

    def __getitem__(self, key):
        return self.t[key]


class KB:
    def __init__(self, nc, n_dsem=32):
        self.nc = nc
        self.stack = contextlib.ExitStack()
        mk = lambda n: self.stack.enter_context(nc.semaphore(n))
        self.PE = Eng("pe", nc.tensor, mk("s_pe"))
        self.DVE = Eng("dve", nc.vector, mk("s_dve"))
        self.ACT = Eng("act", nc.scalar, mk("s_act"))
        self.POOL = Eng("pool", nc.gpsimd, mk("s_pool"))
        self.SP = Eng("sp", nc.sync, mk("s_sp"))
        self.engs = [self.PE, self.DVE, self.ACT, self.POOL, self.SP]
        self.dpools = {q.name: [[mk(f"s_d{q.name}{i}"), 0] for i in range(n_dsem // 2)] for q in (self.SP, self.POOL)}
        self.dnexts = {q.name: 0 for q in (self.SP, self.POOL)}
        self.dsems = [s for p in self.dpools.values() for s in p]
        self.nid = 0
        self.pspool = []
        self.psnext = 0
        self.flip = 0

    def sb(self, name, shape, dtype, stack=None):
        self.nid += 1
        t = (stack or self.stack).enter_context(self.nc.sbuf_tensor(f"{name}_{self.nid}", list(shape), dtype))
        return Buf(t, name)

    def ps(self, name, shape, dtype):
        self.nid += 1
        t = self.stack.enter_context(self.nc.psum_tensor(f"{name}_{self.nid}", list(shape), dtype))
        return Buf(t, name)

    def dram(self, name, shape, dtype, kind):
        t = self.nc.dram_tensor(name, list(shape), dtype, kind=kind)
        return Buf(t.ap(), name)

    def psn(self):
        b = self.pspool[self.psnext]
        self.psnext = (self.psnext + 1) % len(self.pspool)
        return b

    def _wait(self, E, sem, val):
        if val <= 0:
            return
        key = id(sem)
        if E.known.get(key, 0) >= val:
            return
        E.eng.wait_ge(sem, val)
        E.known[key] = val

    def _deps(self, E, reads, writes, same_engine=True):
        for b in reads:
            for (sem, val) in list(b.w.values()):
                if (not same_engine) and sem is E.sem:
                    continue
                self._wait(E, sem, val)
        for b in writes:
            for (sem, val) in list(b.w.values()) + list(b.r.values()):
                if (not same_engine) and sem is E.sem:
                    continue
                self._wait(E, sem, val)

    def _mark(self, sem, val, reads, writes):
        for b in reads:
            b.r[id(sem)] = (sem, val)
        for b in writes:
            b.r = {}
            b.w[id(sem)] = (sem, val)

    def op(self, E, emit, reads=(), writes=(), same_engine=SAME_ENGINE_WAITS):
        self._deps(E, reads, writes, same_engine)
        ins = emit(E.eng)
        E.count += 1
        ins.then_inc(E.sem, 1)
        self._mark(E.sem, E.count, reads, writes)
        return ins

    def mm(self, out_ap, lhsT, rhs, start, stop, reads=(), writes=()):
        return self.op(self.PE, lambda e: e.matmul(out_ap, lhsT=lhsT, rhs=rhs, start=start, stop=stop),
                       reads=reads, writes=writes, same_engine=False)

    def tr(self, out_ap, in_ap, ident_ap, reads=(), writes=()):
        return self.op(self.PE, lambda e: e.transpose(out_ap, in_ap, ident_ap),
                       reads=reads, writes=writes, same_engine=False)

    def dma(self, Q, out_ap, in_ap, reads=(), writes=(), **kw):
        self._deps(Q, reads, writes)
        pool = self.dpools[Q.name]
        slot = pool[self.dnexts[Q.name]]
        self.dnexts[Q.name] = (self.dnexts[Q.name] + 1) % len(pool)
        sem, val = slot
        self._wait(Q, sem, val)
        ins = Q.eng.dma_start(out=out_ap, in_=in_ap, **kw)
        slot[1] = val + 16
        ins.then_inc(sem, 16)
        self._mark(sem, val + 16, reads, writes)
        return ins

    def barrier(self):
        for E in self.engs:
            for F in self.engs:
                if F is not E:
                    self._wait(E, F.sem, F.count)
            for (sem, val) in self.dsems:
                self._wait(E, sem, val)

    def rotate(self, limit=20000):
        if max(E.count for E in self.engs) < limit:
            return
        self.barrier()
        for E in self.engs:
            self.nid += 1
            E.sem = self.stack.enter_context(self.nc.semaphore(f"s_{E.name}_{self.nid}"))
            E.count = 0

    def finish(self):
        for (sem, val) in self.dsems:
            self._wait(self.SP, sem, val)
        for F in self.engs:
            if F is not self.SP:
                self._wait(self.SP, F.sem, F.count)

    def copy(self, out_ap, in_ap, reads, writes, eng=None):
        if eng is None:
            self.flip ^= 1
            eng = self.ACT if self.flip else self.DVE
        if eng is self.ACT:
            return self.op(self.ACT, lambda e: e.activation(out=out_ap, in_=in_ap, func=AF.Copy), reads=reads, writes=writes)
        return self.op(eng, lambda e: e.tensor_copy(out=out_ap, in_=in_ap), reads=reads, writes=writes)


class Ctx:
    pass


def declare_io(k, TP):
    NT = TP + 2 * TS
    g = Ctx()
    I = lambda n, s: k.dram(n, s, F32, "ExternalInput")
    O = lambda n, s: k.dram(n, s, F32, "ExternalOutput")
    g.x = I("x", [NT, D]); g.mem = I("mem", [256, D])
    g.st_conv = I("st_conv", [2, 3, 512]); g.st_lru = I("st_lru", [2, 512]); g.st_ret = I("st_ret", [2, 4, 128, 128])
    g.st_shift = I("st_shift", [2, D]); g.st_wkv = I("st_wkv", [2, 16, 64, 64])
    g.ck = I("ck", [2, 2, 256, D]); g.cv = I("cv", [2, 2, 256, D])
    g.ln_g = I("ln_g", [2, 4, D]); g.ln_b = I("ln_b", [2, 4, D])
    g.ffn_up = I("ffn_up", [2, 2, D, 2 * DFF]); g.ffn_down = I("ffn_down", [2, 2, DFF, D])
    g.xa_q = I("xa_q", [2, D, D]); g.xa_k = I("xa_k", [2, D, D]); g.xa_v = I("xa_v", [2, D, D]); g.xa_o = I("xa_o", [2, D, D])
    g.w_in = I("l0_w_in", [D, 3072]); g.w_rot = I("l0_w_rot", [D, 1024])
    g.conv_w = I("l0_conv_w", [4, 512]); g.conv_b = I("l0_conv_b", [512])
    g.lru_wa = I("l0_lru_wa", [8, 64, 64]); g.lru_ba = I("l0_lru_ba", [512])
    g.lru_wx = I("l0_lru_wx", [8, 64, 64]); g.lru_bx = I("l0_lru_bx", [512]); g.lru_lam = I("l0_lru_lambda", [512])
    g.ret_g = I("l0_ret_gn_g", [512]); g.ret_b = I("l0_ret_gn_b", [512]); g.w_out0 = I("l0_w_out", [D, D])
    g.mu = I("l1_mu", [6, D]); g.w_rkv = I("l1_w_rkv", [3, D, D]); g.w0 = I("l1_w0", [D]); g.w1 = I("l1_w1", [D, 64]); g.w2 = I("l1_w2", [64, D])
    g.a0 = I("l1_a0", [D]); g.a1 = I("l1_a1", [D, 64]); g.a2 = I("l1_a2", [64, D]); g.g1 = I("l1_g1", [D, 128]); g.g2 = I("l1_g2", [128, D])
    g.k_k = I("l1_k_k", [D]); g.k_a = I("l1_k_a", [D]); g.r_k = I("l1_r_k", [D]); g.gn_g = I("l1_gn_g", [D]); g.gn_b = I("l1_gn_b", [D])
    g.w_out1 = I("l1_w_out", [D, D])
    g.c_ident = I("c_ident", [128, 128]); g.c_cos = I("c_cos", [128, TP + TS]); g.c_sin = I("c_sin", [128, TP + TS])
    g.c_retM = I("c_retM", [2, 128, 4, 128]); g.c_retXI = I("c_retXI", [2, 128, 4, 128]); g.c_retZ = I("c_retZ", [2, 128, 4, 128])
    g.c_msk = I("c_msk", [2, 128, 3, 128]); g.c_onesbd = I("c_onesbd", [128, 128])
    g.y = O("y", [NT, D]); g.o_memk = O("o_memk", [2, 256, D]); g.o_memv = O("o_memv", [2, 256, D])
    g.o_conv = O("o_conv", [3, 3, 512]); g.o_lru = O("o_lru", [3, 512]); g.o_ret = O("o_ret", [3, 4, 128, 128])
    g.o_shift = O("o_shift", [3, D]); g.o_wkv = O("o_wkv", [3, 16, 64, 64])
    g.xs = [k.dram("scr0", [NT, D], F32, "Internal"), k.dram("scr1", [NT, D], F32, "Internal")]
    return g


def load_cols(k, st, name, src_ap, ncol, rows=128):
    b = k.sb(name, [rows, ncol], F32, st)
    k.dma(k.SP, b[:], src_ap.rearrange("(c p) -> p c", p=rows), writes=[b], allow_slow_non_contiguous=True)
    return b


def load_weight(k, g, dst, src2d, K, N, kc0=0):
    nk = max(1, K // 128)
    rows = min(K, 128)
    for kc in range(nk):
        for n0 in range(0, N, 704):
            n1 = min(N, n0 + 704)
            stg = g.stage[g.stage_i % len(g.stage)]
            g.stage_i += 1
            k.dma(k.SP, stg[0:rows, 0:n1 - n0], src2d[kc * 128: kc * 128 + rows, n0:n1], writes=[stg])
            k.copy(dst[0:rows, kc0 + kc, n0:n1], stg[0:rows, 0:n1 - n0], [stg], [dst])


def load_tok(k, g, src, row0, PT, NS, pool):
    b = pool[g.tok_i % len(pool)]
    g.tok_i += 1
    k.dma(k.SP, b[0:PT, 0:NS, :], src.t[row0: row0 + PT * NS, :].rearrange("(s p) d -> p s d", p=PT), writes=[b])
    return b


def to_featmajor(k, g, x32, PT, NS, xT, col0=0):
    for s in range(NS):
        ps = k.psn()
        for c in range(KC):
            k.tr(ps[:, c * PT:(c + 1) * PT], x32[0:PT, s, c * 128:(c + 1) * 128], g.ident[0:PT, 0:PT],
                 reads=[x32, g.ident], writes=[ps])
        k.copy(xT[:, 0:KC, col0 + s * PT: col0 + (s + 1) * PT],
               ps[:, 0:KC * PT].rearrange("p (c t) -> p c t", t=PT), [ps], [xT])


def ln_epilogue(k, g, ps, base_ap, base_buf, PT, dst, row0, alpha, do_ln=True):
    y = g.ybuf[g.y_i % 2]; o = g.obuf[g.y_i % 2]; sm = g.small[g.y_i % 2]
    g.y_i += 1
    DVE, ACT = k.DVE, k.ACT
    k.op(DVE, lambda e: e.scalar_tensor_tensor(out=y[0:PT, :], in0=base_ap, scalar=float(alpha), in1=ps[0:PT, :],
                                               op0=ALU.mult, op1=ALU.add), reads=[base_buf, ps], writes=[y])
    if not do_ln:
        k.dma(k.POOL, dst.t[row0:row0 + PT, :], y[0:PT, :], reads=[y], writes=[dst])
        return
    k.op(DVE, lambda e: e.bn_stats(out=sm[0:PT, 0:6], in_=y[0:PT, 0:512]), reads=[y], writes=[sm])
    k.op(DVE, lambda e: e.bn_stats(out=sm[0:PT, 6:12], in_=y[0:PT, 512:1024]), reads=[y], writes=[sm])
    k.op(DVE, lambda e: e.bn_aggr(out=sm[0:PT, 12:14], in_=sm[0:PT, 0:12]), reads=[sm], writes=[sm])
    k.op(ACT, lambda e: e.activation(out=sm[0:PT, 14:15], in_=sm[0:PT, 13:14], func=AF.Ln, bias=g.eps_ln[0:PT, 0:1]),
         reads=[sm, g.eps_ln], writes=[sm])
    k.op(ACT, lambda e: e.activation(out=sm[0:PT, 14:15], in_=sm[0:PT, 14:15], func=AF.Exp, scale=-0.5), reads=[sm], writes=[sm])
    k.op(DVE, lambda e: e.tensor_scalar(out=sm[0:PT, 15:16], in0=sm[0:PT, 12:13], scalar1=-1.0, scalar2=sm[0:PT, 14:15],
                                        op0=ALU.mult, op1=ALU.mult), reads=[sm], writes=[sm])
    k.op(ACT, lambda e: e.activation(out=o[0:PT, :], in_=y[0:PT, :], func=AF.Identity, scale=sm[0:PT, 14:15],
                                     bias=sm[0:PT, 15:16]), reads=[y, sm], writes=[o])
    k.op(DVE, lambda e: e.tensor_tensor(out=o[0:PT, :], in0=o[0:PT, :], in1=g.gtab[0:PT, :], op=ALU.mult),
         reads=[o, g.gtab], writes=[o])
    k.op(DVE, lambda e: e.tensor_tensor(out=o[0:PT, :], in0=o[0:PT, :], in1=g.btab[0:PT, :], op=ALU.add),
         reads=[o, g.btab], writes=[o])
    k.dma(k.POOL, dst.t[row0:row0 + PT, :], o[0:PT, :], reads=[o], writes=[dst])


def load_ln_tabs(k, g, l, j):
    k.dma(k.SP, g.gtab[:], g.ln_g.t[l, j].partition_broadcast(128), writes=[g.gtab])
    k.dma(k.SP, g.btab[:], g.ln_b.t[l, j].partition_broadcast(128), writes=[g.btab])


def segs(TP, tile):
    out = [(r, tile, 0) for r in range(0, TP, tile)]
    out += [(TP, TS, 1), (TP + TS, TS, 2)]
    return out


def stage_ffn(k, g, TP, l, which, src, dst):
    TT = 256
    with contextlib.ExitStack() as st:
        Wg = k.sb("Wup", [128, KC, 2 * DFF], BF16, st)
        Wd = k.sb("Wd", [128, 22, D], BF16, st)
        load_ln_tabs(k, g, l, 0 if which == 0 else 3)
        load_weight(k, g, Wg, g.ffn_up.t[l, which], D, 2 * DFF)
        load_weight(k, g, Wd, g.ffn_down.t[l, which], DFF, D)
        xTs = [k.sb("xT", [128, KC, TT], BF16, st) for _ in range(2)]
        hT = k.sb("hT", [128, 22, TT], BF16, st)
        sgs = [k.sb("sg", [128, TT], F32, st) for _ in range(2)]
        toks = [k.sb("tok2", [128, 2, D], F32, st) for _ in range(2)]
        for ti, (row0, ntok, seq) in enumerate(segs(TP, TT)):
            PT = min(128, ntok); NS = ntok // PT
            k.rotate()
            x32 = load_tok(k, g, src, row0, PT, NS, toks)
            xT = xTs[ti % 2]
            to_featmajor(k, g, x32, PT, NS, xT)
            for fc in range(22):
                ps = k.psn()
                for kc in range(KC):
                    k.mm(ps[:, 0:ntok], Wg[:, kc, fc * 128:(fc + 1) * 128], xT[:, kc, 0:ntok], kc == 0, kc == KC - 1,
                         reads=[Wg, xT], writes=[ps])
                for kc in range(KC):
                    k.mm(ps[:, 512:512 + ntok], Wg[:, kc, DFF + fc * 128: DFF + (fc + 1) * 128], xT[:, kc, 0:ntok],
                         kc == 0, kc == KC - 1, reads=[Wg, xT], writes=[ps])
                sg = sgs[fc % 2]
                k.op(k.ACT, lambda e: e.activation(out=sg[:, 0:ntok], in_=ps[:, 0:ntok], func=AF.Silu), reads=[ps], writes=[sg])
                k.op(k.DVE, lambda e: e.scalar_tensor_tensor(out=hT[:, fc, 0:ntok], in0=sg[:, 0:ntok], scalar=0.5,
                                                             in1=ps[:, 512:512 + ntok], op0=ALU.mult, op1=ALU.mult),
                     reads=[sg, ps], writes=[hT])
            for s in range(NS):
                ps = k.psn()
                for hf in range(2):
                    for fc in range(22):
                        k.mm(ps[0:PT, hf * 512:(hf + 1) * 512], hT[:, fc, s * PT:(s + 1) * PT], Wd[:, fc, hf * 512:(hf + 1) * 512],
                             fc == 0, fc == 21, reads=[hT, Wd], writes=[ps])
                ln_epilogue(k, g, ps, x32[0:PT, s, :], x32, PT, dst, row0 + s * PT, ALPHA)
        k.barrier()


def setup_globals(k, g):
    g.stage = [k.sb("stg", [128, 704], F32) for _ in range(2)]
    g.stage_i = 0
    g.tok_i = 0
    g.ybuf = [k.sb("yb", [128, D], F32) for _ in range(2)]
    g.obuf = [k.sb("ob", [128, D], F32) for _ in range(2)]
    g.small = [k.sb("sm", [128, 16], F32) for _ in range(2)]
    g.y_i = 0
    g.gtab = k.sb("gtab", [128, D], F32); g.btab = k.sb("btab", [128, D], F32)
    g.ident = k.sb("ident", [128, 128], F32)
    g.identb = k.sb("identb", [128, 128], BF16)
    g.eps_ln = k.sb("epsln", [128, 1], F32)
    k.dma(k.SP, g.ident[:], g.c_ident.t[:, :], writes=[g.ident])
    k.copy(g.identb[:], g.ident[:], [g.ident], [g.identb], eng=k.DVE)
    k.op(k.DVE, lambda e: e.memset(g.eps_ln[:], LN_EPS), writes=[g.eps_ln])
    g.one_c = k.sb("one_c", [128, 1], F32)
    k.op(k.DVE, lambda e: e.memset(g.one_c[:], 1.0), writes=[g.one_c])
    k.pspool = [k.ps("psp", [128, 1024], F32) for _ in range(4)]


def stage_xattn(k, g, TP, l, src, dst):
    DVE, ACT = k.DVE, k.ACT
    with contextlib.ExitStack() as st:
        Wq = k.sb("Wq", [128, KC, D], BF16, st); Wo = k.sb("Wo", [128, KC, D], BF16, st)
        Wk = k.sb("Wk", [128, KC, D], BF16, st); Wv = k.sb("Wv", [128, KC, D], BF16, st)
        load_ln_tabs(k, g, l, 2)
        load_weight(k, g, Wq, g.xa_q.t[l], D, D); load_weight(k, g, Wo, g.xa_o.t[l], D, D)
        load_weight(k, g, Wk, g.xa_k.t[l], D, D); load_weight(k, g, Wv, g.xa_v.t[l], D, D)
        ones = k.sb("ones", [128, 128], BF16, st)
        k.op(DVE, lambda e: e.memset(ones[:], 1.0), writes=[ones])
        m32 = k.sb("m32", [128, 2, D], F32, st)
        memT = k.sb("memT", [128, KC, 256], BF16, st)
        KTs = [k.sb("KT", [128, KC, 256], BF16, st) for _ in range(3)]
        Vts = [k.sb("Vt", [128, 2, D], BF16, st) for _ in range(3)]
        o32s = [k.sb("mo32", [128, D], F32, st) for _ in range(2)]
        k.dma(k.SP, m32[:], g.mem.t[:, :].rearrange("(s p) d -> p s d", p=128), writes=[m32])
        to_featmajor(k, g, m32, 128, 2, memT)
        for ec in range(KC):
            ps = k.psn()
            for kc in range(KC):
                k.mm(ps[:, 0:256], Wk[:, kc, ec * 128:(ec + 1) * 128], memT[:, kc, :], kc == 0, kc == KC - 1, reads=[Wk, memT], writes=[ps])
            k.copy(KTs[0][:, ec, :], ps[:, 0:256], [ps], [KTs[0]])
        oi = 0
        for (W, outd, isv) in ((Wk, g.o_memk, False), (Wv, g.o_memv, True)):
            for s in range(2):
                ps = k.psn()
                for hf in range(2):
                    for kc in range(KC):
                        k.mm(ps[:, hf * 512:(hf + 1) * 512], memT[:, kc, s * 128:(s + 1) * 128], W[:, kc, hf * 512:(hf + 1) * 512],
                             kc == 0, kc == KC - 1, reads=[W, memT], writes=[ps])
                o32 = o32s[oi % 2]; oi += 1
                k.copy(o32[:], ps[:, :], [ps], [o32])
                k.dma(k.POOL, outd.t[l, s * 128:(s + 1) * 128, :], o32[:], reads=[o32], writes=[outd])
                if isv:
                    k.copy(Vts[0][:, s, :], o32[:], [o32], [Vts[0]])
        for sq in range(2):
            k.dma(k.SP, m32[:], g.ck.t[l, sq].rearrange("(s p) d -> p s d", p=128), writes=[m32])
            to_featmajor(k, g, m32, 128, 2, KTs[1 + sq])
            k.dma(k.SP, m32[:], g.cv.t[l, sq].rearrange("(s p) d -> p s d", p=128), writes=[m32])
            k.copy(Vts[1 + sq][:], m32[:], [m32], [Vts[1 + sq]])
        xTs = [k.sb("xT", [128, KC, 256], BF16, st) for _ in range(2)]
        qT = k.sb("qT", [128, KC, 256], BF16, st)
        pTs = [k.sb("pT", [128, 2, 256], BF16, st) for _ in range(2)]
        rdens = [k.sb("rden", [128, 256], F32, st) for _ in range(2)]
        oT = k.sb("oT", [128, KC, 256], BF16, st)
        toks = [k.sb("tok4", [128, 2, D], F32, st) for _ in range(2)]
        for ti, (row0, ntok, seq) in enumerate(segs(TP, 256)):
            PT = min(128, ntok); NS = ntok // PT
            k.rotate()
            x32 = load_tok(k, g, src, row0, PT, NS, toks)
            xT = xTs[ti % 2]
            to_featmajor(k, g, x32, PT, NS, xT)
            KT = KTs[seq]; Vt = Vts[seq]
            for ec in range(KC):
                ps = k.psn()
                for kc in range(KC):
                    k.mm(ps[:, 0:ntok], Wq[:, kc, ec * 128:(ec + 1) * 128], xT[:, kc, 0:ntok], kc == 0, kc == KC - 1, reads=[Wq, xT], writes=[ps])
                k.op(ACT, lambda e: e.activation(out=qT[:, ec, 0:ntok], in_=ps[:, 0:ntok], func=AF.Copy, scale=0.0625), reads=[ps], writes=[qT])
            for h in range(4):
                pT = pTs[h % 2]; rden = rdens[h % 2]
                for mc in range(2):
                    ps = k.psn()
                    for dc in range(2):
                        k.mm(ps[:, 0:ntok], KT[:, 2 * h + dc, mc * 128:(mc + 1) * 128], qT[:, 2 * h + dc, 0:ntok], dc == 0, dc == 1,
                             reads=[KT, qT], writes=[ps])
                    k.op(ACT, lambda e: e.activation(out=pT[:, mc, 0:ntok], in_=ps[:, 0:ntok], func=AF.Exp), reads=[ps], writes=[pT])
                ps = k.psn()
                for mc in range(2):
                    k.mm(ps[:, 0:ntok], ones[:, :], pT[:, mc, 0:ntok], mc == 0, mc == 1, reads=[ones, pT], writes=[ps])
                k.op(ACT, lambda e: e.activation(out=rden[:, 0:ntok], in_=ps[:, 0:ntok], func=AF.Ln), reads=[ps], writes=[rden])
                k.op(ACT, lambda e: e.activation(out=rden[:, 0:ntok], in_=rden[:, 0:ntok], func=AF.Exp, scale=-1.0), reads=[rden], writes=[rden])
                for dc in range(2):
                    ps = k.psn()
                    for mc in range(2):
                        k.mm(ps[:, 0:ntok], Vt[:, mc, (2 * h + dc) * 128:(2 * h + dc + 1) * 128], pT[:, mc, 0:ntok], mc == 0, mc == 1,
                             reads=[Vt, pT], writes=[ps])
                    k.op(DVE, lambda e: e.tensor_tensor(out=oT[:, 2 * h + dc, 0:ntok], in0=ps[:, 0:ntok], in1=rden[:, 0:ntok], op=ALU.mult),
                         reads=[ps, rden], writes=[oT])
            for s in range(NS):
                ps = k.psn()
                for hf in range(2):
                    for ec in range(KC):
                        k.mm(ps[0:PT, hf * 512:(hf + 1) * 512], oT[:, ec, s * PT:(s + 1) * PT], Wo[:, ec, hf * 512:(hf + 1) * 512],
                             ec == 0, ec == KC - 1, reads=[oT, Wo], writes=[ps])
                ln_epilogue(k, g, ps, x32[0:PT, s, :], x32, PT, dst, row0 + s * PT, ALPHA)
        k.barrier()


def stage_mixer_ab(k, g, TP, src, dst):
    DVE, ACT = k.DVE, k.ACT
    TT = 256
    with contextlib.ExitStack() as st:
        Win = k.sb("Win", [128, KC, 3072], BF16, st); Wrot = k.sb("Wrot", [128, KC, 1024], BF16, st)
        Wout = k.sb("Wout", [128, KC, D], BF16, st)
        load_ln_tabs(k, g, 0, 1)
        load_weight(k, g, Win, g.w_in.t, D, 3072); load_weight(k, g, Wrot, g.w_rot.t, D, 1024)
        load_weight(k, g, Wout, g.w_out0.t, D, D)
        bd32 = k.sb("bd32", [128, 2, 4, 128], F32, st); Wbd = k.sb("Wbd", [128, 2, 4, 128], BF16, st)
        k.op(DVE, lambda e: e.memset(bd32[:], 0.0), writes=[bd32])
        for wi, wsrc in enumerate((g.lru_wa, g.lru_wx)):
            for hp in range(2):
                k.dma(k.SP, bd32[64 * hp:64 * hp + 64, wi, :, 64 * hp:64 * hp + 64],
                      wsrc.t.rearrange("(c hp) i j -> hp i c j", hp=2)[hp], writes=[bd32])
        k.copy(Wbd[:], bd32[:], [bd32], [Wbd], eng=DVE)
        cw = k.sb("cw", [128, 4, 4], F32, st)
        for j in range(4):
            k.dma(k.SP, cw[:, :, j], g.conv_w.t[j].rearrange("(c p) -> p c", p=128), writes=[cw], allow_slow_non_contiguous=True)
        cb = load_cols(k, st, "cb", g.conv_b.t, 4); ba = load_cols(k, st, "ba", g.lru_ba.t, 4); bx = load_cols(k, st, "bx", g.lru_bx.t, 4)
        lam = load_cols(k, st, "lam", g.lru_lam.t, 4); gng = load_cols(k, st, "gng", g.ret_g.t, 4); gnb = load_cols(k, st, "gnb", g.ret_b.t, 4)
        cl = k.sb("cl", [128, 4], F32, st); cl2 = k.sb("cl2", [128, 4], F32, st)
        k.op(ACT, lambda e: e.activation(out=cl[:], in_=lam[:], func=AF.Exp, scale=-1.0), reads=[lam], writes=[cl])
        k.op(ACT, lambda e: e.activation(out=cl[:], in_=cl[:], func=AF.Ln, bias=g.one_c[:, 0:1]), reads=[cl, g.one_c], writes=[cl])
        k.op(DVE, lambda e: e.tensor_scalar(out=cl2[:], in0=cl[:], scalar1=-16.0, scalar2=None, op0=ALU.mult), reads=[cl], writes=[cl2])
        k.op(DVE, lambda e: e.tensor_scalar(out=cl[:], in0=cl[:], scalar1=-8.0, scalar2=None, op0=ALU.mult), reads=[cl], writes=[cl])
        ones = k.sb("ones", [128, 128], BF16, st)
        k.op(DVE, lambda e: e.memset(ones[:], 1.0 / 128.0), writes=[ones])
        epsc = k.sb("epsc", [128, 1], F32, st)
        k.op(DVE, lambda e: e.memset(epsc[:], LN_EPS), writes=[epsc])
        cosT = k.sb("cosT", [128, TT], F32, st); sinT = k.sb("sinT", [128, TT], F32, st)
        retM = k.sb("retM", [128, 4, 128], F32, st); retXI = k.sb("retXI", [128, 4, 128], F32, st); retZ = k.sb("retZ", [128, 4, 128], F32, st)
        xaT = k.sb("xaT", [128, 4, 3 + TT], F32, st); hl = k.sb("hl", [128, 4], F32, st)
        S32 = k.sb("S32", [128, 4, 128], F32, st); Sb = k.sb("Sb", [128, 4, 128], BF16, st)
        xTs = [k.sb("xT", [128, KC, TT], BF16, st) for _ in range(2)]
        toks = [k.sb("tok2", [128, 2, D], F32, st) for _ in range(2)]
        gaT = k.sb("gaT", [128, 4, TT], F32, st)
        qr = k.sb("qr", [128, 4, TT], BF16, st); kz = k.sb("kz", [128, 4, TT], BF16, st)
        t1s = [k.sb("t1", [128, TT], F32, st) for _ in range(2)]; t2s = [k.sb("t2", [128, TT], F32, st) for _ in range(2)]
        t3s = [k.sb("t3", [128, TT], F32, st) for _ in range(2)]
        Ktok = k.sb("Ktok", [128, 2, 4, 128], BF16, st); Vtok = k.sb("Vtok", [128, 2, 512], BF16, st)
        PTb = k.sb("PTb", [128, 4, 128], BF16, st)
        oT = k.sb("oT", [128, 4, TT], F32, st); obf = k.sb("obf", [128, 4, TT], BF16, st); osq = k.sb("osq", [128, 4, TT], BF16, st)
        lru = [[k.sb(n, [128, TT], (BF16 if n == "xcb" else F32), st) for n in ("xc", "xcb", "rr", "ii", "aa", "hh")] for _ in range(2)]
        yT = k.sb("yT", [128, KC, TT], BF16, st)
        cur_seq = -1
        allsegs = segs(TP, TT)
        for ti, (row0, ntok, seq) in enumerate(allsegs):
            PT = min(128, ntok); NS = ntok // PT
            k.rotate()
            C = PT; nch = NS
            ci = 0 if seq == 0 else 1
            last = (ti + 1 == len(allsegs)) or (allsegs[ti + 1][2] != seq)
            if seq != cur_seq:
                cur_seq = seq
                k.op(DVE, lambda e: e.memset(PTb[:], 0.0), writes=[PTb])
                k.op(DVE, lambda e: e.memset(Ktok[:], 0.0), writes=[Ktok])
                k.op(DVE, lambda e: e.memset(Vtok[:], 0.0), writes=[Vtok])
                k.dma(k.SP, retM[:], g.c_retM.t[ci], writes=[retM]); k.dma(k.SP, retXI[:], g.c_retXI.t[ci], writes=[retXI])
                k.dma(k.SP, retZ[:], g.c_retZ.t[ci], writes=[retZ])
                if seq == 0:
                    k.op(DVE, lambda e: e.memset(xaT[:], 0.0), writes=[xaT])
                    k.op(DVE, lambda e: e.memset(hl[:], 0.0), writes=[hl])
                    k.op(DVE, lambda e: e.memset(S32[:], 0.0), writes=[S32])
                else:
                    for j in range(3):
                        k.dma(k.SP, xaT[:, :, j], g.st_conv.t[seq - 1, j].rearrange("(c p) -> p c", p=128), writes=[xaT], allow_slow_non_contiguous=True)
                    k.dma(k.SP, hl[:], g.st_lru.t[seq - 1].rearrange("(c p) -> p c", p=128), writes=[hl], allow_slow_non_contiguous=True)
                    k.dma(k.SP, S32[:], g.st_ret.t[seq - 1].rearrange("h d v -> d h v"), writes=[S32])
                k.copy(Sb[:], S32[:], [S32], [Sb], eng=ACT)
            pos0 = row0 if seq == 0 else TP
            k.dma(k.SP, cosT[:, 0:ntok], g.c_cos.t[:, pos0:pos0 + ntok], writes=[cosT])
            k.dma(k.SP, sinT[:, 0:ntok], g.c_sin.t[:, pos0:pos0 + ntok], writes=[sinT])
            x32 = load_tok(k, g, src, row0, PT, NS, toks)
            xT = xTs[ti % 2]
            to_featmajor(k, g, x32, PT, NS, xT)

            def proj(W, col0):
                ps = k.psn()
                for kc in range(KC):
                    k.mm(ps[:, 0:ntok], W[:, kc, col0:col0 + 128], xT[:, kc, 0:ntok], kc == 0, kc == KC - 1, reads=[W, xT], writes=[ps])
                return ps
            for c in range(4):
                ps = proj(Win, c * 128)
                k.copy(xaT[:, c, 3:3 + ntok], ps[:, 0:ntok], [ps], [xaT], eng=ACT)
                ps = proj(Win, 512 + c * 128)
                k.op(ACT, lambda e: e.activation(out=gaT[:, c, 0:ntok], in_=ps[:, 0:ntok], func=AF.Gelu_apprx_tanh), reads=[ps], writes=[gaT])
            for (dst_b, base, rbase, isk) in ((qr, 1024, 0, False), (kz, 1536, 512, True)):
                for h in range(4):
                    t1, t2, t3 = t1s[h % 2], t2s[h % 2], t3s[h % 2]
                    ps = proj(Win, base + h * 128)
                    ps2 = proj(Wrot, rbase + h * 128)
                    k.op(DVE, lambda e: e.tensor_tensor(out=t1[:, 0:ntok], in0=ps[:, 0:ntok], in1=cosT[:, 0:ntok], op=ALU.mult), reads=[ps, cosT], writes=[t1])
                    k.op(DVE, lambda e: e.tensor_tensor(out=t2[:, 0:ntok], in0=ps2[:, 0:ntok], in1=sinT[:, 0:ntok], op=ALU.mult), reads=[ps2, sinT], writes=[t2])
                    if not isk:
                        k.op(DVE, lambda e: e.tensor_tensor(out=qr[:, h, 0:ntok], in0=t1[:, 0:ntok], in1=t2[:, 0:ntok], op=ALU.add), reads=[t1, t2], writes=[qr])
                    else:
                        k.op(DVE, lambda e: e.tensor_tensor(out=t3[:, 0:ntok], in0=t1[:, 0:ntok], in1=t2[:, 0:ntok], op=ALU.add), reads=[t1, t2], writes=[t3])
                        k.op(DVE, lambda e: e.tensor_tensor(out=kz[:, h, 0:ntok].rearrange("p (n c) -> p n c", c=C),
                                                            in0=t3[:, 0:ntok].rearrange("p (n c) -> p n c", c=C),
                                                            in1=retZ[:, h, 0:C].unsqueeze(1).broadcast_to([128, nch, C]), op=ALU.mult),
                             reads=[t3, retZ], writes=[kz])
            for n in range(nch):
                cs = slice(n * C, (n + 1) * C)
                ps = k.psn()
                for kc in range(KC):
                    k.mm(ps[0:C, 0:512], xT[:, kc, cs], Win[:, kc, 2048:2560], kc == 0, kc == KC - 1, reads=[xT, Win], writes=[ps])
                k.copy(Vtok[0:C, n % 2, :], ps[0:C, 0:512], [ps], [Vtok])
                ps = k.psn()
                psb = ps.t[:, 0:256].bitcast(BF16)
                for h in range(4):
                    k.tr(psb[0:C, h * 128:(h + 1) * 128], kz[:, h, cs], g.identb[:, :], reads=[kz, g.identb], writes=[ps])
                k.copy(Ktok[0:C, n % 2, :, :], psb[0:C, 0:512].rearrange("p (h d) -> p h d", d=128), [ps], [Ktok])
                ps = k.psn()
                for h in range(4):
                    k.mm(ps[0:C, h * 128:h * 128 + C], kz[:, h, cs], qr[:, h, cs], True, True, reads=[kz, qr], writes=[ps])
                k.op(DVE, lambda e: e.tensor_tensor(out=PTb[0:C, :, 0:C], in0=ps[0:C, 0:512].rearrange("p (h c) -> p h c", c=128)[:, :, 0:C],
                                                    in1=retM[0:C, :, 0:C], op=ALU.mult), reads=[ps, retM], writes=[PTb])
                ps = k.psn()
                for h in range(4):
                    k.mm(ps[:, h * 128:h * 128 + C], Vtok[:, n % 2, h * 128:(h + 1) * 128], PTb[:, h, 0:C], True, False, reads=[Vtok, PTb], writes=[ps])
                    k.mm(ps[:, h * 128:h * 128 + C], Sb[:, h, :], qr[:, h, cs], False, True, reads=[Sb, qr], writes=[ps])
                k.op(DVE, lambda e: e.tensor_tensor(out=oT[:, :, cs], in0=ps[:, 0:512].rearrange("p (h c) -> p h c", c=128)[:, :, 0:C],
                                                    in1=retXI[:, :, 0:C], op=ALU.mult), reads=[ps, retXI], writes=[oT])
                ps = k.psn()
                for h in range(4):
                    k.mm(ps[:, h * 128:(h + 1) * 128], Ktok[:, n % 2, h, :], Vtok[:, n % 2, h * 128:(h + 1) * 128], True, True, reads=[Ktok, Vtok], writes=[ps])
                for h in range(4):
                    gam = float(np.exp(np.log1p(-(2.0 ** (-5.0 - h))) * C))
                    k.op(DVE, lambda e: e.scalar_tensor_tensor(out=S32[:, h, :], in0=S32[:, h, :], scalar=gam, in1=ps[:, h * 128:(h + 1) * 128],
                                                               op0=ALU.mult, op1=ALU.add), reads=[S32, ps], writes=[S32])
                k.copy(Sb[:], S32[:], [S32], [Sb], eng=ACT)
            k.op(ACT, lambda e: e.activation(out=obf[:, :, 0:ntok], in_=oT[:, :, 0:ntok], func=AF.Copy), reads=[oT], writes=[obf])
            k.op(ACT, lambda e: e.activation(out=osq[:, :, 0:ntok], in_=oT[:, :, 0:ntok], func=AF.Square), reads=[oT], writes=[osq])
            for h in range(4):
                t1, t2, t3 = t1s[h % 2], t2s[h % 2], t3s[h % 2]
                psm = k.psn()
                k.mm(psm[:, 0:ntok], ones[:, :], obf[:, h, 0:ntok], True, True, reads=[ones, obf], writes=[psm])
                k.mm(psm[:, 512:512 + ntok], ones[:, :], osq[:, h, 0:ntok], True, True, reads=[ones, osq], writes=[psm])
                k.op(ACT, lambda e: e.activation(out=t1[:, 0:ntok], in_=psm[:, 0:ntok], func=AF.Square), reads=[psm], writes=[t1])
                k.op(DVE, lambda e: e.tensor_tensor(out=t1[:, 0:ntok], in0=psm[:, 512:512 + ntok], in1=t1[:, 0:ntok], op=ALU.subtract), reads=[psm, t1], writes=[t1])
                k.op(ACT, lambda e: e.activation(out=t1[:, 0:ntok], in_=t1[:, 0:ntok], func=AF.Ln, bias=epsc[:, 0:1]), reads=[t1, epsc], writes=[t1])
                k.op(ACT, lambda e: e.activation(out=t1[:, 0:ntok], in_=t1[:, 0:ntok], func=AF.Exp, scale=-0.5), reads=[t1], writes=[t1])
                k.op(DVE, lambda e: e.tensor_tensor(out=t2[:, 0:ntok], in0=oT[:, h, 0:ntok], in1=psm[:, 0:ntok], op=ALU.subtract), reads=[oT, psm], writes=[t2])
                k.op(DVE, lambda e: e.tensor_tensor(out=t2[:, 0:ntok], in0=t2[:, 0:ntok], in1=t1[:, 0:ntok], op=ALU.mult), reads=[t1, t2], writes=[t2])
                k.op(DVE, lambda e: e.tensor_scalar(out=t2[:, 0:ntok], in0=t2[:, 0:ntok], scalar1=gng[:, h:h + 1], scalar2=gnb[:, h:h + 1],
                                                    op0=ALU.mult, op1=ALU.add), reads=[t2, gng, gnb], writes=[t2])
                ps = proj(Win, 2560 + h * 128)
                k.op(ACT, lambda e: e.activation(out=t3[:, 0:ntok], in_=ps[:, 0:ntok], func=AF.Silu), reads=[ps], writes=[t3])
                k.op(DVE, lambda e: e.tensor_tensor(out=yT[:, 4 + h, 0:ntok], in0=t2[:, 0:ntok], in1=t3[:, 0:ntok], op=ALU.mult), reads=[t2, t3], writes=[yT])
            for c in range(4):
                xc, xcb, rr, ii, aa, hh = lru[c % 2]
                k.op(DVE, lambda e: e.tensor_scalar(out=xc[:, 0:ntok], in0=xaT[:, c, 0:ntok], scalar1=cw[:, c, 0:1], scalar2=cb[:, c:c + 1],
                                                    op0=ALU.mult, op1=ALU.add), reads=[xaT, cw, cb], writes=[xc])
                for j in range(1, 4):
                    k.op(DVE, lambda e: e.scalar_tensor_tensor(out=xc[:, 0:ntok], in0=xaT[:, c, j:j + ntok], scalar=cw[:, c, j:j + 1], in1=xc[:, 0:ntok],
                                                               op0=ALU.mult, op1=ALU.add), reads=[xaT, cw, xc], writes=[xc])
                k.copy(xcb[:, 0:ntok], xc[:, 0:ntok], [xc], [xcb], eng=ACT)
                ps = k.psn()
                k.mm(ps[:, 0:ntok], Wbd[:, 0, c, :], xcb[:, 0:ntok], True, True, reads=[Wbd, xcb], writes=[ps])
                k.mm(ps[:, 512:512 + ntok], Wbd[:, 1, c, :], xcb[:, 0:ntok], True, True, reads=[Wbd, xcb], writes=[ps])
                k.op(ACT, lambda e: e.activation(out=rr[:, 0:ntok], in_=ps[:, 0:ntok], func=AF.Sigmoid, bias=ba[:, c:c + 1]), reads=[ps, ba], writes=[rr])
                k.op(ACT, lambda e: e.activation(out=ii[:, 0:ntok], in_=ps[:, 512:512 + ntok], func=AF.Sigmoid, bias=bx[:, c:c + 1]), reads=[ps, bx], writes=[ii])
                k.op(ACT, lambda e: e.activation(out=aa[:, 0:ntok], in_=rr[:, 0:ntok], func=AF.Exp, scale=cl[:, c:c + 1]), reads=[rr, cl], writes=[aa])
                k.op(ACT, lambda e: e.activation(out=rr[:, 0:ntok], in_=rr[:, 0:ntok], func=AF.Exp, scale=cl2[:, c:c + 1]), reads=[rr, cl2], writes=[rr])
                k.op(ACT, lambda e: e.activation(out=rr[:, 0:ntok], in_=rr[:, 0:ntok], func=AF.Ln, scale=-1.0, bias=g.one_c[:, 0:1]), reads=[rr, g.one_c], writes=[rr])
                k.op(ACT, lambda e: e.activation(out=rr[:, 0:ntok], in_=rr[:, 0:ntok], func=AF.Exp, scale=0.5), reads=[rr], writes=[rr])
                k.op(DVE, lambda e: e.tensor_tensor(out=ii[:, 0:ntok], in0=ii[:, 0:ntok], in1=xc[:, 0:ntok], op=ALU.mult), reads=[ii, xc], writes=[ii])
                k.op(DVE, lambda e: e.tensor_tensor(out=ii[:, 0:ntok], in0=ii[:, 0:ntok], in1=rr[:, 0:ntok], op=ALU.mult), reads=[ii, rr], writes=[ii])
                k.op(DVE, lambda e: e.tensor_tensor_scan(out=hh[:, 0:ntok], data0=aa[:, 0:ntok], data1=ii[:, 0:ntok], initial=hl[:, c:c + 1],
                                                         op0=ALU.mult, op1=ALU.add), reads=[aa, ii, hl], writes=[hh])
                k.op(DVE, lambda e: e.tensor_copy(out=hl[:, c:c + 1], in_=hh[:, ntok - 1:ntok]), reads=[hh], writes=[hl])
                k.op(DVE, lambda e: e.tensor_tensor(out=yT[:, c, 0:ntok], in0=hh[:, 0:ntok], in1=gaT[:, c, 0:ntok], op=ALU.mult), reads=[hh, gaT], writes=[yT])
            k.op(DVE, lambda e: e.tensor_copy(out=xaT[:, :, 0:3], in_=xaT[:, :, ntok:ntok + 3]), reads=[xaT], writes=[xaT])
            for s in range(NS):
                ps = k.psn()
                for hf in range(2):
                    for c in range(KC):
                        k.mm(ps[0:PT, hf * 512:(hf + 1) * 512], yT[:, c, s * PT:(s + 1) * PT], Wout[:, c, hf * 512:(hf + 1) * 512],
                             c == 0, c == KC - 1, reads=[yT, Wout], writes=[ps])
                ln_epilogue(k, g, ps, x32[0:PT, s, :], x32, PT, dst, row0 + s * PT, ALPHA)
            if last:
                for j in range(3):
                    k.dma(k.POOL, g.o_conv.t[seq, j].rearrange("(c p) -> p c", p=128), xaT[:, :, j], reads=[xaT], writes=[g.o_conv], allow_slow_non_contiguous=True)
                k.dma(k.POOL, g.o_lru.t[seq].rearrange("(c p) -> p c", p=128), hl[:], reads=[hl], writes=[g.o_lru], allow_slow_non_contiguous=True)
                k.dma(k.POOL, g.o_ret.t[seq].rearrange("h d v -> d h v"), S32[:], reads=[S32], writes=[g.o_ret])
        k.barrier()


def _interleave(a, b, ra=1, rb=1):
    alive_a, alive_b = a is not None, b is not None
    while alive_a or alive_b:
        for _ in range(ra):
            if alive_a:
                try:
                    next(a)
                except StopIteration:
                    alive_a = False
        for _ in range(rb):
            if alive_b:
                try:
                    next(b)
                except StopIteration:
                    alive_b = False


def stage_rwkv(k, g, TP, src, dst):
    DVE, ACT, POOL = k.DVE, k.ACT, k.DVE
    DK = float(np.exp(-0.5))
    NT1 = TP // 128 + 2
    opnd = k.dram("rw_opnd", [NT1, 7, 128, 1024], BF16, "Internal")
    wcd = k.dram("rw_wc", [NT1, 128, 2, KC], F32, "Internal")
    with contextlib.ExitStack() as st:
        Wr = k.sb("Wr", [128, KC, D], BF16, st); Wk = k.sb("Wk", [128, KC, D], BF16, st); Wv = k.sb("Wv", [128, KC, D], BF16, st)
        for i, W in enumerate((Wr, Wk, Wv)):
            load_weight(k, g, W, g.w_rkv.t[i], D, D)
        w1 = k.sb("w1", [128, KC, 64], BF16, st); a1 = k.sb("a1", [128, KC, 64], BF16, st); g1 = k.sb("g1", [128, KC, 128], BF16, st)
        w2 = k.sb("w2", [128, 1, D], BF16, st); a2 = k.sb("a2", [128, 1, D], BF16, st); g2 = k.sb("g2", [128, 1, D], BF16, st)
        load_weight(k, g, w1, g.w1.t, D, 64); load_weight(k, g, a1, g.a1.t, D, 64); load_weight(k, g, g1, g.g1.t, D, 128)
        load_weight(k, g, w2, g.w2.t, 64, D); load_weight(k, g, a2, g.a2.t, 64, D); load_weight(k, g, g2, g.g2.t, 128, D)
        mu = k.sb("mu", [128, 6, KC], F32, st)
        for p in range(6):
            k.dma(k.SP, mu[:, p, :], g.mu.t[p].rearrange("(c p) -> p c", p=128), writes=[mu], allow_slow_non_contiguous=True)
        w0c = load_cols(k, st, "w0c", g.w0.t, 8); a0c = load_cols(k, st, "a0c", g.a0.t, 8); kkc = load_cols(k, st, "kkc", g.k_k.t, 8)
        kac = load_cols(k, st, "kac", g.k_a.t, 8); rkc = load_cols(k, st, "rkc", g.r_k.t, 8)
        ob32 = k.sb("ob32", [128, 128], F32, st); onesbd = k.sb("onesbd", [128, 128], BF16, st)
        k.dma(k.SP, ob32[:], g.c_onesbd.t[:, :], writes=[ob32])
        k.copy(onesbd[:], ob32[:], [ob32], [onesbd], eng=DVE)
        onesf = k.sb("onesf", [128, 64], F32, st)
        k.op(DVE, lambda e: e.memset(onesf[:], 1.0), writes=[onesf])
        xprev = k.sb("xprev", [128, KC], F32, st)
        TT = 128
        toks = [k.sb("tok1", [128, 1, D], F32, st) for _ in range(2)]
        xT32 = k.sb("xT32", [128, KC, 1 + TT], F32, st); dd = k.sb("dd", [128, KC, TT], F32, st)
        xms = [k.sb("xm", [128, KC, TT], BF16, st) for _ in range(2)]
        F = lambda n: k.sb(n, [128, KC, TT], F32, st)
        iface = [[F(n + str(par)) for n in ("rT", "kT", "vT", "sg", "ic")] for par in range(2)]
        Lc, Ep, Em, Ea, kk, kf, tm, Lm = [F(n) for n in ("Lc", "Ep", "Em", "Ea", "kk", "kf", "tm", "Lm")]
        kkn = kk
        B = lambda n: k.sb(n, [128, KC, TT], BF16, st)
        kk2, rkb = B("kk2"), B("rkb")
        _o = [B(f"o{j}") for j in range(7)]
        _g1 = B("o5b")
        outs = [_o, _o[:5] + [_g1] + _o[6:]]
        th = k.sb("th", [128, TT], BF16, st)
        wcs = [k.sb("wcs", [128, 2, KC], F32, st) for _ in range(2)]
        p1segs = segs(TP, TT)

        def front(ti):
            row0, ntok, seq = p1segs[ti]
            PT = ntok
            k.rotate()
            rT, kT, vT, sg, ic = iface[ti % 2]
            At, Rt, Kt, Bt, Vb, Gb, Bon = outs[ti % 2]
            if ti == 0 or p1segs[ti - 1][2] != seq:
                if seq == 0:
                    k.op(DVE, lambda e: e.memset(xprev[:], 0.0), writes=[xprev])
                else:
                    k.dma(k.SP, xprev[:], g.st_shift.t[seq - 1].rearrange("(c p) -> p c", p=128), writes=[xprev], allow_slow_non_contiguous=True)
            x32 = load_tok(k, g, src, row0, PT, 1, toks)
            k.op(DVE, lambda e: e.tensor_copy(out=xT32[:, :, 0], in_=xprev[:, :]), reads=[xprev], writes=[xT32])
            to_featmajor(k, g, x32, PT, 1, xT32, col0=1)
            k.op(DVE, lambda e: e.tensor_copy(out=xprev[:, :], in_=xT32[:, :, ntok]), reads=[xT32], writes=[xprev])
            k.op(DVE, lambda e: e.tensor_tensor(out=dd[:, :, 0:ntok], in0=xT32[:, :, 0:ntok], in1=xT32[:, :, 1:1 + ntok], op=ALU.subtract),
                 reads=[xT32], writes=[dd])

            def mix(p):
                xm = xms[p % 2]
                for c in range(KC):
                    k.op(DVE, lambda e: e.scalar_tensor_tensor(out=xm[:, c, 0:ntok], in0=dd[:, c, 0:ntok], scalar=mu[:, p, c:c + 1],
                                                               in1=xT32[:, c, 1:1 + ntok], op0=ALU.mult, op1=ALU.add), reads=[dd, mu, xT32], writes=[xm])
                return xm

            def proj_full(W, xm, dstb):
                for ec in range(KC):
                    ps = k.psn()
                    for kc in range(KC):
                        k.mm(ps[:, 0:ntok], W[:, kc, ec * 128:(ec + 1) * 128], xm[:, kc, 0:ntok], kc == 0, kc == KC - 1, reads=[W, xm], writes=[ps])
                    k.copy(dstb[:, ec, 0:ntok], ps[:, 0:ntok], [ps], [dstb], eng=ACT)

            def lora(xm, wA, nA, wB, func1, emit2):
                ps = k.psn()
                for kc in range(KC):
                    k.mm(ps[0:nA, 0:ntok], wA[:, kc, :], xm[:, kc, 0:ntok], kc == 0, kc == KC - 1, reads=[wA, xm], writes=[ps])
                k.op(ACT, lambda e: e.activation(out=th[0:nA, 0:ntok], in_=ps[0:nA, 0:ntok], func=func1), reads=[ps], writes=[th])
                for c in range(KC):
                    ps2 = k.psn()
                    k.mm(ps2[:, 0:ntok], wB[0:nA, 0, c * 128:(c + 1) * 128], th[0:nA, 0:ntok], True, True, reads=[wB, th], writes=[ps2])
                    emit2(c, ps2)

            proj_full(Wr, mix(0), rT); proj_full(Wk, mix(1), kT); proj_full(Wv, mix(2), vT)
            lora(mix(3), w1, 64, w2, AF.Tanh, lambda c, ps2: k.op(ACT, lambda e: e.activation(
                out=sg[:, c, 0:ntok], in_=ps2[:, 0:ntok], func=AF.Sigmoid, bias=w0c[:, c:c + 1]), reads=[ps2, w0c], writes=[sg]))
            lora(mix(4), a1, 64, a2, AF.Copy, lambda c, ps2: k.op(ACT, lambda e: e.activation(
                out=ic[:, c, 0:ntok], in_=ps2[:, 0:ntok], func=AF.Sigmoid, bias=a0c[:, c:c + 1]), reads=[ps2, a0c], writes=[ic]))
            lora(mix(5), g1, 128, g2, AF.Sigmoid, lambda c, ps2: k.copy(Gb[:, c, 0:ntok], ps2[:, 0:ntok], [ps2], [Gb], eng=ACT))

        def back(ti):
            row0, ntok, seq = p1segs[ti]
            C = min(64, ntok); nch = ntok // C
            chunk0 = row0 // 64 if seq == 0 else TP // 64 + (seq - 1)
            rT, kT, vT, sg, ic = iface[ti % 2]
            At, Rt, Kt, Bt, Vb, Gb, Bon = outs[ti % 2]
            v3 = lambda ps: ps[:, :].rearrange("p (c t) -> p c t", t=128)[:, :, 0:ntok]
            for c in range(KC):
                for n in range(nch):
                    k.op(DVE, lambda e: e.tensor_tensor_scan(out=Lc[:, c, n * C:(n + 1) * C], data0=onesf[:, 0:C], data1=sg[:, c, n * C:(n + 1) * C],
                                                             initial=0.0, op0=ALU.mult, op1=ALU.add), reads=[onesf, sg], writes=[Lc])
            k.op(POOL, lambda e: e.tensor_tensor(out=Lm[:, :, 0:ntok], in0=Lc[:, :, 0:ntok], in1=sg[:, :, 0:ntok], op=ALU.subtract), reads=[Lc, sg], writes=[Lm])
            k.op(ACT, lambda e: e.activation(out=Ep[:, :, 0:ntok], in_=Lc[:, :, 0:ntok], func=AF.Exp, scale=-DK), reads=[Lc], writes=[Ep])
            k.op(ACT, lambda e: e.activation(out=Em[:, :, 0:ntok], in_=Lc[:, :, 0:ntok], func=AF.Exp, scale=DK), reads=[Lc], writes=[Em])
            k.op(ACT, lambda e: e.activation(out=Ea[:, :, 0:ntok], in_=Lm[:, :, 0:ntok], func=AF.Exp, scale=-DK), reads=[Lm], writes=[Ea])
            for c in range(KC):
                k.op(DVE, lambda e: e.tensor_scalar(out=kk[:, c, 0:ntok], in0=kT[:, c, 0:ntok], scalar1=kkc[:, c:c + 1], scalar2=None, op0=ALU.mult),
                     reads=[kT, kkc], writes=[kk])
            k.op(ACT, lambda e: e.activation(out=kk2[:, :, 0:ntok], in_=kk[:, :, 0:ntok], func=AF.Square), reads=[kk], writes=[kk2])
            ps = k.psn()
            if ntok == 128:
                for hf in range(2):
                    k.mm(ps[:, hf * 512:(hf + 1) * 512], onesbd[:, :], kk2[:, 4 * hf:4 * hf + 4, :].rearrange("p c t -> p (c t)"), True, True,
                         reads=[onesbd, kk2], writes=[ps])
            else:
                for c in range(KC):
                    k.mm(ps[:, c * 128:c * 128 + ntok], onesbd[:, :], kk2[:, c, 0:ntok], True, True, reads=[onesbd, kk2], writes=[ps])
            k.op(DVE, lambda e: e.tensor_scalar(out=tm[:, :, 0:ntok], in0=v3(ps), scalar1=1e-24, scalar2=None, op0=ALU.max), reads=[ps], writes=[tm])
            k.op(ACT, lambda e: e.activation(out=tm[:, :, 0:ntok], in_=tm[:, :, 0:ntok], func=AF.Ln), reads=[tm], writes=[tm])
            k.op(ACT, lambda e: e.activation(out=tm[:, :, 0:ntok], in_=tm[:, :, 0:ntok], func=AF.Exp, scale=-0.5), reads=[tm], writes=[tm])
            k.op(POOL, lambda e: e.tensor_tensor(out=kkn[:, :, 0:ntok], in0=kk[:, :, 0:ntok], in1=tm[:, :, 0:ntok], op=ALU.mult), reads=[kk, tm], writes=[kkn])
            for c in range(KC):
                k.op(DVE, lambda e: e.tensor_scalar(out=tm[:, c, 0:ntok], in0=ic[:, c, 0:ntok], scalar1=-1.0, scalar2=kac[:, c:c + 1],
                                                    op0=ALU.add, op1=ALU.mult), reads=[ic, kac], writes=[tm])
            k.op(DVE, lambda e: e.scalar_tensor_tensor(out=kf[:, :, 0:ntok], in0=tm[:, :, 0:ntok], scalar=1.0, in1=kT[:, :, 0:ntok],
                                                       op0=ALU.add, op1=ALU.mult), reads=[tm, kT], writes=[kf])
            for c in range(KC):
                k.op(DVE, lambda e: e.scalar_tensor_tensor(out=rkb[:, c, 0:ntok], in0=rT[:, c, 0:ntok], scalar=rkc[:, c:c + 1], in1=kf[:, c, 0:ntok],
                                                           op0=ALU.mult, op1=ALU.mult), reads=[rT, rkc, kf], writes=[rkb])
            ps = k.psn()
            if ntok == 128:
                for hf in range(2):
                    k.mm(ps[:, hf * 512:(hf + 1) * 512], onesbd[:, :], rkb[:, 4 * hf:4 * hf + 4, :].rearrange("p c t -> p (c t)"), True, True,
                         reads=[onesbd, rkb], writes=[ps])
            else:
                for c in range(KC):
                    k.mm(ps[:, c * 128:c * 128 + ntok], onesbd[:, :], rkb[:, c, 0:ntok], True, True, reads=[onesbd, rkb], writes=[ps])
            k.op(DVE, lambda e: e.tensor_tensor(out=Bon[:, :, 0:ntok], in0=v3(ps), in1=vT[:, :, 0:ntok], op=ALU.mult), reads=[ps, vT], writes=[Bon])
            k.op(DVE, lambda e: e.scalar_tensor_tensor(out=At[:, :, 0:ntok], in0=kkn[:, :, 0:ntok], scalar=-1.0, in1=Ea[:, :, 0:ntok],
                                                       op0=ALU.mult, op1=ALU.mult), reads=[kkn, Ea], writes=[At])
            k.op(POOL, lambda e: e.tensor_tensor(out=Rt[:, :, 0:ntok], in0=rT[:, :, 0:ntok], in1=Ep[:, :, 0:ntok], op=ALU.mult), reads=[rT, Ep], writes=[Rt])
            k.op(POOL, lambda e: e.tensor_tensor(out=Kt[:, :, 0:ntok], in0=kf[:, :, 0:ntok], in1=Em[:, :, 0:ntok], op=ALU.mult), reads=[kf, Em], writes=[Kt])
            k.op(POOL, lambda e: e.tensor_tensor(out=tm[:, :, 0:ntok], in0=kkn[:, :, 0:ntok], in1=ic[:, :, 0:ntok], op=ALU.mult), reads=[kkn, ic], writes=[tm])
            k.op(POOL, lambda e: e.tensor_tensor(out=Bt[:, :, 0:ntok], in0=tm[:, :, 0:ntok], in1=Em[:, :, 0:ntok], op=ALU.mult), reads=[tm, Em], writes=[Bt])
            k.copy(Vb[:, :, 0:ntok], vT[:, :, 0:ntok], [vT], [Vb], eng=ACT)
            for j, ob in enumerate(outs[ti % 2]):
                k.dma(k.POOL, opnd.t[ti, j].rearrange("p (c t) -> p c t", t=128)[:, :, 0:ntok], ob[:, :, 0:ntok], reads=[ob], writes=[opnd])
            wcb = wcs[ti % 2]
            for n in range(nch):
                k.op(DVE, lambda e: e.tensor_copy(out=wcb[:, n, :], in_=Ep[:, :, (n + 1) * C - 1]), reads=[Ep], writes=[wcb])
            k.dma(k.POOL, wcd.t[ti][:, 0:nch, :], wcb[:, 0:nch, :], reads=[wcb], writes=[wcd])

        front(0)
        for ti in range(len(p1segs)):
            if ti + 1 < len(p1segs):
                front(ti + 1)
            back(ti)
        k.barrier()
    P2E = DVE
    with contextlib.ExitStack() as st:
        Wout = k.sb("Wout", [128, KC, D], BF16, st)
        load_ln_tabs(k, g, 1, 1)
        load_weight(k, g, Wout, g.w_out1.t, D, D)
        gngc = load_cols(k, st, "gngc", g.gn_g.t, 8); gnbc = load_cols(k, st, "gnbc", g.gn_b.t, 8)
        ob32 = k.sb("ob32", [128, 128], F32, st); onesbd64 = k.sb("onesbd64", [128, 128], BF16, st)
        k.dma(k.SP, ob32[:], g.c_onesbd.t[:, :], writes=[ob32])
        k.op(ACT, lambda e: e.activation(out=onesbd64[:], in_=ob32[:], func=AF.Copy, scale=1.0 / 64.0), reads=[ob32], writes=[onesbd64])
        epsg = k.sb("epsg", [128, 1], F32, st)
        k.op(DVE, lambda e: e.memset(epsg[:], 64e-5), writes=[epsg])
        msk = k.sb("msk", [128, 3, 128], F32, st)
        Hx32 = k.sb("Hx32", [128, KC, 128], F32, st); Hb = k.sb("Hb", [128, KC, 128], BF16, st)
        Sx32 = k.sb("Sx32", [128, KC, 128], F32, st)
        X = lambda n: k.sb(n, [128, KC, 128], BF16, st)
        sets = []
        for par in range(2):
            s_ = Ctx()
            s_.Ear = k.sb("Ear", [128, KC, 2, 128], BF16, st)
            s_.Eb, s_.Ek, s_.Ev = X("Eb"), X("Ek"), X("Ev")
            s_.Gb = k.sb("Gb", [128, KC, 64], BF16, st); s_.Bon = k.sb("Bon", [128, KC, 64], BF16, st)
            s_.wc = k.sb("wc", [128, KC], F32, st); s_.x32 = k.sb("x32", [128, 1, D], F32, st)
            s_.VsT, s_.EbT, s_.EkT, s_.ArbT, s_.AakT, s_.ArkT, s_.PTb = [X(n) for n in ("VsT", "EbT", "EkT", "ArbT", "AakT", "ArkT", "PTb")]
            sets.append(s_)
        Mb = [X("Mb0"), X("Mb1")]; MTb = [X("MTb0"), X("MTb1")]
        Xb, Ub = X("Xb"), X("Ub")
        oT = k.sb("oT", [128, KC, 64], F32, st); tm = k.sb("tm", [128, KC, 64], F32, st)
        obf = k.sb("obf", [128, KC, 64], BF16, st); osq = k.sb("osq", [128, KC, 64], BF16, st); yT = k.sb("yT", [128, KC, 64], BF16, st)

        def zero_all():
            for s_ in sets:
                for zb in (s_.Ear, s_.Eb, s_.Ek, s_.Ev, s_.VsT, s_.EbT, s_.EkT, s_.ArbT, s_.AakT, s_.ArkT, s_.PTb):
                    k.op(DVE, lambda e: e.memset(zb[:], 0.0), writes=[zb])
            for zb in (Xb, Ub, Mb[0], Mb[1], MTb[0], MTb[1], obf, osq):
                k.op(DVE, lambda e: e.memset(zb[:], 0.0), writes=[zb])

        def fe2(ch, S, C, row0):
            tl, nn = ch
            R = 2 * C
            vR = lambda ps: ps[0:R, :].rearrange("p (c t) -> p c t", t=128)[:, :, 0:R]
            k.rotate()
            for hp in range(2):
                rw = slice(64 * hp, 64 * hp + 64); cl = slice(hp * C, (hp + 1) * C)
                src3 = lambda j: opnd.t[tl, j, 64 * hp:64 * hp + 64, :].rearrange("p (c t) -> p c t", t=128)[:, :, nn * 64:nn * 64 + C]
                k.dma(k.SP, S.Ear[rw, :, 0, cl], src3(0), reads=[opnd], writes=[S.Ear])
                k.dma(k.SP, S.Ear[rw, :, 1, cl], src3(1), reads=[opnd], writes=[S.Ear])
                k.dma(k.SP, S.Ek[rw, :, cl], src3(2), reads=[opnd], writes=[S.Ek])
                k.dma(k.SP, S.Eb[rw, :, cl], src3(3), reads=[opnd], writes=[S.Eb])
                k.dma(k.SP, S.Ev[rw, :, cl], src3(4), reads=[opnd], writes=[S.Ev])
            k.dma(k.SP, S.Gb[:, :, 0:C], opnd.t[tl, 5].rearrange("p (c t) -> p c t", t=128)[:, :, nn * 64:nn * 64 + C], reads=[opnd], writes=[S.Gb])
            k.dma(k.SP, S.Bon[:, :, 0:C], opnd.t[tl, 6].rearrange("p (c t) -> p c t", t=128)[:, :, nn * 64:nn * 64 + C], reads=[opnd], writes=[S.Bon])
            k.dma(k.SP, S.wc[:], wcd.t[tl][:, nn, :], reads=[wcd], writes=[S.wc])
            k.dma(k.SP, S.x32[0:C, 0, :], src.t[row0:row0 + C, :], reads=[src], writes=[S.x32])
            yield
            for (srcb, dstb) in ((S.Ev, S.VsT), (S.Eb, S.EbT), (S.Ek, S.EkT)):
                ps = k.psn()
                psb = ps.t[:, 0:512].bitcast(BF16)
                for c in range(KC):
                    k.tr(psb[0:R, c * 128:(c + 1) * 128], srcb[:, c, 0:R], g.identb[:, :], reads=[srcb, g.identb], writes=[ps])
                k.copy(dstb[0:R, :, :], psb[0:R, :].rearrange("p (c t) -> p c t", t=128), [ps], [dstb])
                yield
            mb = lambda j: msk[0:R, j, 0:R].unsqueeze(1).broadcast_to([R, KC, R])
            ea = lambda c: S.Ear[:, c, 0, 0:R]; er = lambda c: S.Ear[:, c, 1, 0:R]
            eb = lambda c: S.Eb[:, c, 0:R]; ek = lambda c: S.Ek[:, c, 0:R]
            for (lhs_sel, rhs_sel, dstb, mj) in ((ea, eb, Mb[0], 2), (eb, ea, MTb[0], 0), (eb, er, S.ArbT, 1), (ek, ea, S.AakT, 0), (ek, er, S.ArkT, 1)):
                ps = k.psn()
                for c in range(KC):
                    k.mm(ps[0:R, c * 128:c * 128 + R], lhs_sel(c), rhs_sel(c), True, True, reads=[S.Ear, S.Eb, S.Ek], writes=[ps])
                k.op(DVE, lambda e: e.tensor_tensor(out=dstb[0:R, :, 0:R], in0=vR(ps), in1=mb(mj), op=ALU.mult), reads=[ps, msk], writes=[dstb])
                yield
            k.op(P2E, lambda e: e.tensor_tensor(out=S.PTb[0:R, :, 0:R], in0=MTb[0][0:R, :, 0:R],
                                                 in1=g.identb[0:R, 0:R].unsqueeze(1).broadcast_to([R, KC, R]), op=ALU.add), reads=[MTb[0], g.identb], writes=[S.PTb])
            nlev = int(np.log2(C)) - 1
            cur = 0
            for j in range(1, nlev + 1):
                nx = 1 - cur
                ps = k.psn()
                for c in range(KC):
                    k.mm(ps[0:R, c * 128:c * 128 + R], MTb[cur][:, c, 0:R], Mb[cur][:, c, 0:R], True, True, reads=[MTb[cur], Mb[cur]], writes=[ps])
                k.copy(Mb[nx][0:R, :, 0:R], vR(ps), [ps], [Mb[nx]], eng=ACT)
                yield
                if j < nlev:
                    ps = k.psn()
                    for c in range(KC):
                        k.mm(ps[0:R, c * 128:c * 128 + R], Mb[cur][:, c, 0:R], MTb[cur][:, c, 0:R], True, True, reads=[MTb[cur], Mb[cur]], writes=[ps])
                    k.copy(MTb[nx][0:R, :, 0:R], vR(ps), [ps], [MTb[nx]], eng=ACT)
                    yield
                ps = k.psn()
                for c in range(KC):
                    k.mm(ps[0:R, c * 128:c * 128 + R], g.identb[:, 0:R], S.PTb[:, c, 0:R], True, False, reads=[g.identb, S.PTb], writes=[ps])
                    k.mm(ps[0:R, c * 128:c * 128 + R], Mb[nx][:, c, 0:R], S.PTb[:, c, 0:R], False, True, reads=[Mb[nx], S.PTb], writes=[ps])
                k.copy(S.PTb[0:R, :, 0:R], vR(ps), [ps], [S.PTb], eng=DVE)
                cur = nx
                yield

        def be2(ch, S, C, row0, seq, last):
            R = 2 * C; ntok = C
            ea = lambda c: S.Ear[:, c, 0, 0:R]; er = lambda c: S.Ear[:, c, 1, 0:R]
            v3 = lambda ps, w: ps[:, 0:512].rearrange("p (c t) -> p c t", t=64)[:, :, 0:w]
            v3b = lambda ps, w: ps[:, 512:1024].rearrange("p (c t) -> p c t", t=64)[:, :, 0:w]
            ps = k.psn()
            for c in range(KC):
                k.mm(ps[0:R, c * 128:(c + 1) * 128], ea(c), Hb[:, c, :], True, False, reads=[S.Ear, Hb], writes=[ps])
                k.mm(ps[0:R, c * 128:(c + 1) * 128], S.AakT[:, c, 0:R], S.VsT[:, c, :], False, True, reads=[S.AakT, S.VsT], writes=[ps])
            k.copy(Xb[0:R, :, :], ps[0:R, :].rearrange("p (c t) -> p c t", t=128), [ps], [Xb], eng=ACT)
            yield
            ps = k.psn()
            for c in range(KC):
                k.mm(ps[0:R, c * 128:(c + 1) * 128], S.PTb[:, c, 0:R], Xb[:, c, :], True, True, reads=[S.PTb, Xb], writes=[ps])
            k.copy(Ub[0:R, :, :], ps[0:R, :].rearrange("p (c t) -> p c t", t=128), [ps], [Ub], eng=ACT)
            yield
            ps = k.psn()
            for c in range(KC):
                k.mm(ps[:, c * 128:c * 128 + R], Hb[:, c, :], er(c), True, False, reads=[Hb, S.Ear], writes=[ps])
                k.mm(ps[:, c * 128:c * 128 + R], Ub[:, c, :], S.ArbT[:, c, 0:R], False, False, reads=[Ub, S.ArbT], writes=[ps])
                k.mm(ps[:, c * 128:c * 128 + R], S.VsT[:, c, :], S.ArkT[:, c, 0:R], False, True, reads=[S.VsT, S.ArkT], writes=[ps])
            psO = ps
            ps = k.psn()
            for c in range(KC):
                k.mm(ps[:, c * 128:(c + 1) * 128], S.EbT[:, c, :], Ub[:, c, :], True, False, reads=[S.EbT, Ub], writes=[ps])
                k.mm(ps[:, c * 128:(c + 1) * 128], S.EkT[:, c, :], S.VsT[:, c, :], False, True, reads=[S.EkT, S.VsT], writes=[ps])
            k.op(DVE, lambda e: e.tensor_tensor(out=Hx32[:], in0=ps[:, :].rearrange("p (c t) -> p c t", t=128), in1=Hx32[:], op=ALU.add), reads=[ps, Hx32], writes=[Hx32])
            k.op(P2E, lambda e: e.tensor_tensor(out=Hx32[:], in0=Hx32[:], in1=S.wc[:, :].unsqueeze(2).broadcast_to([128, KC, 128]), op=ALU.mult),
                 reads=[Hx32, S.wc], writes=[Hx32])
            k.copy(Hb[:], Hx32[:], [Hx32], [Hb], eng=ACT)
            yield
            for hp in range(2):
                rw = slice(64 * hp, 64 * hp + 64)
                k.copy(oT[rw, :, 0:ntok], psO[rw, :].rearrange("p (c t) -> p c t", t=128)[:, :, hp * C:(hp + 1) * C], [psO], [oT],
                       eng=(ACT if hp == 0 else DVE))
            yield
            k.op(ACT, lambda e: e.activation(out=obf[:, :, 0:ntok], in_=oT[:, :, 0:ntok], func=AF.Copy), reads=[oT], writes=[obf])
            k.op(ACT, lambda e: e.activation(out=osq[:, :, 0:ntok], in_=oT[:, :, 0:ntok], func=AF.Square), reads=[oT], writes=[osq])
            ps = k.psn()
            k.mm(ps[:, 0:512], onesbd64[:, :], obf[:, :, :].rearrange("p c t -> p (c t)"), True, True, reads=[onesbd64, obf], writes=[ps])
            k.mm(ps[:, 512:1024], onesbd64[:, :], osq[:, :, :].rearrange("p c t -> p (c t)"), True, True, reads=[onesbd64, osq], writes=[ps])
            yield
            k.op(ACT, lambda e: e.activation(out=tm[:, :, 0:ntok], in_=v3(ps, ntok), func=AF.Square), reads=[ps], writes=[tm])
            k.op(DVE, lambda e: e.tensor_tensor(out=tm[:, :, 0:ntok], in0=v3b(ps, ntok), in1=tm[:, :, 0:ntok], op=ALU.subtract), reads=[ps, tm], writes=[tm])
            k.op(ACT, lambda e: e.activation(out=tm[:, :, 0:ntok], in_=tm[:, :, 0:ntok], func=AF.Ln, bias=epsg[:, 0:1]), reads=[tm, epsg], writes=[tm])
            k.op(ACT, lambda e: e.activation(out=tm[:, :, 0:ntok], in_=tm[:, :, 0:ntok], func=AF.Exp, scale=-0.5), reads=[tm], writes=[tm])
            k.op(DVE, lambda e: e.tensor_tensor(out=oT[:, :, 0:ntok], in0=oT[:, :, 0:ntok], in1=v3(ps, ntok), op=ALU.subtract), reads=[oT, ps], writes=[oT])
            yield
            k.op(P2E, lambda e: e.tensor_tensor(out=oT[:, :, 0:ntok], in0=oT[:, :, 0:ntok], in1=tm[:, :, 0:ntok], op=ALU.mult), reads=[oT, tm], writes=[oT])
            k.op(P2E, lambda e: e.tensor_tensor(out=oT[:, :, 0:ntok], in0=oT[:, :, 0:ntok], in1=gngc[:, :].unsqueeze(2).broadcast_to([128, KC, ntok]), op=ALU.mult),
                 reads=[oT, gngc], writes=[oT])
            k.op(P2E, lambda e: e.tensor_tensor(out=oT[:, :, 0:ntok], in0=oT[:, :, 0:ntok], in1=gnbc[:, :].unsqueeze(2).broadcast_to([128, KC, ntok]), op=ALU.add),
                 reads=[oT, gnbc], writes=[oT])
            k.op(P2E, lambda e: e.tensor_tensor(out=oT[:, :, 0:ntok], in0=oT[:, :, 0:ntok], in1=S.Bon[:, :, 0:ntok], op=ALU.add), reads=[oT, S.Bon], writes=[oT])
            k.op(P2E, lambda e: e.tensor_tensor(out=yT[:, :, 0:ntok], in0=oT[:, :, 0:ntok], in1=S.Gb[:, :, 0:ntok], op=ALU.mult), reads=[oT, S.Gb], writes=[yT])
            yield
            ps = k.psn()
            for hf in range(2):
                for c in range(KC):
                    k.mm(ps[0:ntok, hf * 512:(hf + 1) * 512], yT[:, c, 0:ntok], Wout[:, c, hf * 512:(hf + 1) * 512], c == 0, c == KC - 1,
                         reads=[yT, Wout], writes=[ps])
            yield
            ln_epilogue(k, g, ps, S.x32[0:ntok, 0, :], S.x32, ntok, dst, row0, ALPHA)
            if last:
                rl = row0 + ntok - 1
                k.dma(k.POOL, g.o_shift.t[seq:seq + 1, :], src.t[rl:rl + 1, :], reads=[src], writes=[g.o_shift])
                ps = k.psn()
                for c in range(KC):
                    k.tr(ps[:, c * 128:(c + 1) * 128], Hx32[:, c, :], g.ident[:, :], reads=[Hx32, g.ident], writes=[ps])
                k.copy(Sx32[:], ps[:, :].rearrange("p (c t) -> p c t", t=128), [ps], [Sx32], eng=DVE)
                for hp in range(2):
                    k.dma(k.POOL, g.o_wkv.t[seq].rearrange("(c hp) v kk -> hp v c kk", hp=2)[hp],
                          Sx32[64 * hp:64 * hp + 64, :, 64 * hp:64 * hp + 64], reads=[Sx32], writes=[g.o_wkv])
            yield

        for seq in range(3):
            ci = 0 if seq == 0 else 1
            C = 64 if seq == 0 else 32
            chunks = [((n // 2, n % 2), n * 64) for n in range(TP // 64)] if seq == 0 else [((TP // 128 + seq - 1, 0), TP + (seq - 1) * TS)]
            zero_all()
            k.dma(k.SP, msk[:], g.c_msk.t[ci], writes=[msk])
            k.op(DVE, lambda e: e.memset(Hx32[:], 0.0), writes=[Hx32])
            if seq > 0:
                k.op(DVE, lambda e: e.memset(Sx32[:], 0.0), writes=[Sx32])
                for hp in range(2):
                    k.dma(k.SP, Sx32[64 * hp:64 * hp + 64, :, 64 * hp:64 * hp + 64],
                          g.st_wkv.t[seq - 1].rearrange("(c hp) v kk -> hp v c kk", hp=2)[hp], writes=[Sx32])
                ps = k.psn()
                for c in range(KC):
                    k.tr(ps[:, c * 128:(c + 1) * 128], Sx32[:, c, :], g.ident[:, :], reads=[Sx32, g.ident], writes=[ps])
                k.copy(Hx32[:], ps[:, :].rearrange("p (c t) -> p c t", t=128), [ps], [Hx32], eng=DVE)
            k.copy(Hb[:], Hx32[:], [Hx32], [Hb], eng=ACT)
            for _ in fe2(chunks[0][0], sets[0], C, chunks[0][1]):
                pass
            for i, (ch, row0) in enumerate(chunks):
                nxt = fe2(chunks[i + 1][0], sets[(i + 1) % 2], C, chunks[i + 1][1]) if i + 1 < len(chunks) else None
                _interleave(nxt, be2(ch, sets[i % 2], C, row0, seq, i == len(chunks) - 1), ra=1000, rb=1)
        k.barrier()


def build(TP, nstage=8):
    nc = bass.Bass("TRN2", target_bir_lowering=False)
    k = KB(nc)
    g = declare_io(k, TP)
    setup_globals(k, g)
    stages = [
        lambda s, d: stage_ffn(k, g, TP, 0, 0, s, d),
        lambda s, d: stage_mixer_ab(k, g, TP, s, d),
        lambda s, d: stage_xattn(k, g, TP, 0, s, d),
        lambda s, d: stage_ffn(k, g, TP, 0, 1, s, d),
        lambda s, d: stage_ffn(k, g, TP, 1, 0, s, d),
        lambda s, d: stage_rwkv(k, g, TP, s, d),
        lambda s, d: stage_xattn(k, g, TP, 1, s, d),
        lambda s, d: stage_ffn(k, g, TP, 1, 1, s, d),
    ][:nstage]
    src = g.x
    for i, stf in enumerate(stages):
        dst = g.y if i == len(stages) - 1 else g.xs[i % 2]
        stf(src, dst)
        src = dst
    k.finish()
    return nc


def host_consts(TP):
    c = {}
    c["c_ident"] = np.eye(128, dtype=np.float32)
    half = 64
    inv_freq = (10000.0 ** (-np.arange(half, dtype=np.float32) / np.float32(half))).astype(np.float32)
    pos = np.concatenate([np.arange(TP), PAST + np.arange(TS)]).astype(np.float32)
    ang = (pos[:, None] * inv_freq[None, :]).astype(np.float32)
    cos = np.cos(ang.astype(np.float64)).astype(np.float32).T
    sin = np.sin(ang.astype(np.float64)).astype(np.float32).T
    c["c_cos"] = np.ascontiguousarray(np.concatenate([cos, cos], 0))
    c["c_sin"] = np.ascontiguousarray(np.concatenate([-sin, sin], 0))
    M = np.zeros((2, 128, 4, 128), np.float32); XI = np.zeros((2, 128, 4, 128), np.float32); Z = np.zeros((2, 128, 4, 128), np.float32)
    for ci, C in enumerate((128, 32)):
        idx = np.arange(C, dtype=np.float64)
        for h in range(4):
            lg = np.log1p(-(2.0 ** (-5.0 - h)))
            m = np.where(idx[None, :] >= idx[:, None], np.exp(-lg * C), 0.0)
            M[ci, :C, h, :C] = m
            XI[ci, :, h, :C] = np.exp(lg * (idx + 1.0))[None, :]
            Z[ci, :, h, :C] = (np.exp(lg * (C - 1.0 - idx)) * 128 ** -0.5)[None, :]
    c["c_retM"] = M; c["c_retXI"] = XI; c["c_retZ"] = Z
    msk = np.zeros((2, 128, 3, 128), np.float32)
    for ci, C in enumerate((64, 32)):
        for hp in range(2):
            for s in range(C):
                msk[ci, hp * C + s, 0, hp * C + s + 1: hp * C + C] = 1.0
                msk[ci, hp * C + s, 1, hp * C + s: hp * C + C] = 1.0
        msk[ci, :, 2, :] = msk[ci, :, 0, :].T
    c["c_msk"] = msk
    ob = np.zeros((128, 128), np.float32); ob[:64, :64] = 1.0; ob[64:, 64:] = 1.0
    c["c_onesbd"] = ob
    return c


_W_NAMES = ["ln_g", "ln_b", "ffn_up", "ffn_down", "xa_q", "xa_k", "xa_v", "xa_o", "l0_w_in", "l0_conv_w", "l0_conv_b",
            "l0_lru_wa", "l0_lru_ba", "l0_lru_wx", "l0_lru_bx", "l0_lru_lambda", "l0_ret_gn_g", "l0_ret_gn_b", "l0_w_out",
            "l1_mu", "l1_w_rkv", "l1_w0", "l1_w1", "l1_w2", "l1_a0", "l1_a1", "l1_a2", "l1_g1", "l1_g2", "l1_k_k", "l1_k_a",
            "l1_gn_g", "l1_gn_b", "l1_w_out"]


def make_in_maps(inp, TP):
    f = lambda a: np.ascontiguousarray(np.asarray(a, dtype=np.float32))
    shared = {n: f(inp[n]) for n in _W_NAMES}
    shared["l1_r_k"] = f(inp["l1_r_k"]).reshape(-1)
    w_in = f(inp["l0_w_in"])
    rot = []
    for base in (1024, 1536):
        for h in range(4):
            b0 = base + h * 128
            rot.append(w_in[:, b0 + 64: b0 + 128]); rot.append(w_in[:, b0: b0 + 64])
    shared["l0_w_rot"] = np.ascontiguousarray(np.concatenate(rot, axis=1))
    shared.update(host_consts(TP))
    maps = []
    for b in range(NCORES):
        m = dict(shared)
        m["x"] = np.ascontiguousarray(np.concatenate([f(inp["x_prompt"][b]), f(inp["x_sample"][2 * b]), f(inp["x_sample"][2 * b + 1])], 0))
        m["mem"] = f(inp["mem_prompt"][b])
        sl = slice(2 * b, 2 * b + 2)
        m["st_conv"] = f(inp["state_conv0"][sl]); m["st_lru"] = f(inp["state_lru0"][sl]); m["st_ret"] = f(inp["state_ret0"][sl])
        m["st_shift"] = f(inp["state_shift1"][sl]).reshape(2, D); m["st_wkv"] = f(inp["state_wkv1"][sl])
        m["ck"] = f(inp["cache_mem_k"][:, sl]).reshape(2, 2, 256, D); m["cv"] = f(inp["cache_mem_v"][:, sl]).reshape(2, 2, 256, D)
        maps.append(m)
    return maps


_NC_CACHE = {}


def run(inp, TP, nstage=8, ncores=NCORES):
    key = (TP, nstage)
    if key not in _NC_CACHE:
        _NC_CACHE[key] = build(TP, nstage)
    nc = _NC_CACHE[key]
    res = run_bass_kernel_spmd(nc, make_in_maps(inp, TP)[:ncores], core_ids=list(range(ncores)))
    R = list(res.results)
    while len(R) < NCORES:
        R.append(R[0])
    st = lambda n: np.stack([r[n] for r in R], 0)
    y = st("y")
    y_p = y[:, :TP]
    y_s = y[:, TP:].reshape(NCORES * 2, TS, D)
    memk = st("o_memk").transpose(1, 0, 2, 3).reshape(2, NCORES, 256, 4, 256)
    memv = st("o_memv").transpose(1, 0, 2, 3).reshape(2, NCORES, 256, 4, 256)
    oc, ol, orr, osh, ow = st("o_conv"), st("o_lru"), st("o_ret"), st("o_shift"), st("o_wkv")
    pf = lambda a: np.ascontiguousarray(a[:, 0])
    sf = lambda a: np.ascontiguousarray(a[:, 1:3].reshape((NCORES * 2,) + a.shape[2:]))
    return (np.ascontiguousarray(y_p), np.ascontiguousarray(y_s), np.ascontiguousarray(memk), np.ascontiguousarray(memv),
            pf(oc), pf(ol), pf(orr), pf(osh)[:, None, :], pf(ow),
            sf(oc), sf(ol), sf(orr), sf(osh)[:, None, :], sf(ow))


def kernel(**inputs):
    TP = int(np.asarray(inputs["x_prompt"]).shape[1])
    return run(inputs, TP, 8)
```

```python
import contextlib
import numpy as np
import concourse.bass as bass
import concourse.mybir as mybir
from concourse.bass_utils import run_bass_kernel_spmd

F32 = mybir.dt.float32
BF16 = mybir.dt.bfloat16
AF = mybir.ActivationFunctionType
ALU = mybir.AluOpType

D = 1024
KC = 8
NCORES = 8
TS = 32
DFF = 2816
ALPHA = 4.0 ** 0.25
LN_EPS = 1e-5
PAST = 4096
SAME_ENGINE_WAITS = True


class Eng:
    def __init__(self, name, eng, sem):
        self.name = name
        self.eng = eng
        self.sem = sem
        self.count = 0
        self.known = {}


class Buf:
    def __init__(self, t, name):
        self.t = t
        self.name = name
        self.w = {}
        self.r = {}

    def __getitem__(self, key):
        return self.t[key]


class KB:
    def __init__(self, nc, n_dsem=32):
        self.nc = nc
        self.stack = contextlib.ExitStack()
        mk = lambda n: self.stack.enter_context(nc.semaphore(n))
        self.PE = Eng("pe", nc.tensor, mk("s_pe"))
        self.DVE = Eng("dve", nc.vector, mk("s_dve"))
        self.ACT = Eng("act", nc.scalar, mk("s_act"))
        self.POOL = Eng("pool", nc.gpsimd, mk("s_pool"))
        self.SP = Eng("sp", nc.sync, mk("s_sp"))
        self.engs = [self.PE, self.DVE, self.ACT, self.POOL, self.SP]
        self.dpools = {q.name: [[mk(f"s_d{q.name}{i}"), 0] for i in range(n_dsem // 2)] for q in (self.SP, self.POOL)}
        self.dnexts = {q.name: 0 for q in (self.SP, self.POOL)}
        self.dsems = [s for p in self.dpools.values() for s in p]
        self.nid = 0
        self.pspool = []
        self.psnext = 0
        self.flip = 0

    def sb(self, name, shape, dtype, stack=None):
        self.nid += 1
        t = (stack or self.stack).enter_context(self.nc.sbuf_tensor(f"{name}_{self.nid}", list(shape), dtype))
        return Buf(t, name)

    def ps(self, name, shape, dtype):
        self.nid += 1
        t = self.stack.enter_context(self.nc.psum_tensor(f"{name}_{self.nid}", list(shape), dtype))
        return Buf(t, name)

    def dram(self, name, shape, dtype, kind):
        t = self.nc.dram_tensor(name, list(shape), dtype, kind=kind)
        return Buf(t.ap(), name)

    def psn(self):
        b = self.pspool[self.psnext]
        self.psnext = (self.psnext + 1) % len(self.pspool)
        return b

    def _wait(self, E, sem, val):
        if val <= 0:
            return
        key = id(sem)
        if E.known.get(key, 0) >= val:
            return
        E.eng.wait_ge(sem, val)
        E.known[key] = val

    def _deps(self, E, reads, writes, same_engine=True):
        for b in reads:
            for (sem, val) in list(b.w.values()):
                if (not same_engine) and sem is E.sem:
                    continue
                self._wait(E, sem, val)
        for b in writes:
            for (sem, val) in list(b.w.values()) + list(b.r.values()):
                if (not same_engine) and sem is E.sem:
                    continue
                self._wait(E, sem, val)

    def _mark(self, sem, val, reads, writes):
        for b in reads:
            b.r[id(sem)] = (sem, val)
        for b in writes:
            b.r = {}
            b.w[id(sem)] = (sem, val)

    def op(self, E, emit, reads=(), writes=(), same_engine=SAME_ENGINE_WAITS):
        self._deps(E, reads, writes, same_engine)
        ins = emit(E.eng)
        E.count += 1
        ins.then_inc(E.sem, 1)
        self._mark(E.sem, E.count, reads, writes)
        return ins

    def mm(self, out_ap, lhsT, rhs, start, stop, reads=(), writes=()):
        return self.op(self.PE, lambda e: e.matmul(out_ap, lhsT=lhsT, rhs=rhs, start=start, stop=stop),
                       reads=reads, writes=writes, same_engine=False)

    def tr(self, out_ap, in_ap, ident_ap, reads=(), writes=()):
        return self.op(self.PE, lambda e: e.transpose(out_ap, in_ap, ident_ap),
                       reads=reads, writes=writes, same_engine=False)

    def dma(self, Q, out_ap, in_ap, reads=(), writes=(), **kw):
        self._deps(Q, reads, writes)
        pool = self.dpools[Q.name]
        slot = pool[self.dnexts[Q.name]]
        self.dnexts[Q.name] = (self.dnexts[Q.name] + 1) % len(pool)
        sem, val = slot
        self._wait(Q, sem, val)
        ins = Q.eng.dma_start(out=out_ap, in_=in_ap, **kw)
        slot[1] = val + 16
        ins.then_inc(sem, 16)
        self._mark(sem, val + 16, reads, writes)
        return ins

    def barrier(self):
        for E in self.engs:
            for F in self.engs:
                if F is not E:
                    self._wait(E, F.sem, F.count)
            for (sem, val) in self.dsems:
                self._wait(E, sem, val)

    def rotate(self, limit=20000):
        if max(E.count for E in self.engs) < limit:
            return
        self.barrier()
        for E in self.engs:
            self.nid += 1
            E.sem = self.stack.enter_context(self.nc.semaphore(f"s_{E.name}_{self.nid}"))
            E.count = 0

    def finish(self):
        for (sem, val) in self.dsems:
            self._wait(self.SP, sem, val)
        for F in self.engs:
            if F is not self.SP:
                self._wait(self.SP, F.sem, F.count)

    def copy(self, out_ap, in_ap, reads, writes, eng=None):
        if eng is None:
            self.flip ^= 1
            eng = self.ACT if self.flip else self.DVE
        if eng is self.ACT:
            return self.op(self.ACT, lambda e: e.activation(out=out_ap, in_=in_ap, func=AF.Copy), reads=reads, writes=writes)
        return self.op(eng, lambda e: e.tensor_copy(out=out_ap, in_=in_ap), reads=reads, writes=writes)


class Ctx:
    pass


def declare_io(k, TP):
    NT = TP + 2 * TS
    g = Ctx()
    I = lambda n, s: k.dram(n, s, F32, "ExternalInput")
    O = lambda n, s: k.dram(n, s, F32, "ExternalOutput")
    g.x = I("x", [NT, D]); g.mem = I("mem", [256, D])
    g.st_conv = I("st_conv", [2, 3, 512]); g.st_lru = I("st_lru", [2, 512]); g.st_ret = I("st_ret", [2, 4, 128, 128])
    g.st_shift = I("st_shift", [2, D]); g.st_wkv = I("st_wkv", [2, 16, 64, 64])
    g.ck = I("ck", [2, 2, 256, D]); g.cv = I("cv", [2, 2, 256, D])
    g.ln_g = I("ln_g", [2, 4, D]); g.ln_b = I("ln_b", [2, 4, D])
    g.ffn_up = I("ffn_up", [2, 2, D, 2 * DFF]); g.ffn_down = I("ffn_down", [2, 2, DFF, D])
    g.xa_q = I("xa_q", [2, D, D]); g.xa_k = I("xa_k", [2, D, D]); g.xa_v = I("xa_v", [2, D, D]); g.xa_o = I("xa_o", [2, D, D])
    g.w_in = I("l0_w_in", [D, 3072]); g.w_rot = I("l0_w_rot", [D, 1024])
    g.conv_w = I("l0_conv_w", [4, 512]); g.conv_b = I("l0_conv_b", [512])
    g.lru_wa = I("l0_lru_wa", [8, 64, 64]); g.lru_ba = I("l0_lru_ba", [512])
    g.lru_wx = I("l0_lru_wx", [8, 64, 64]); g.lru_bx = I("l0_lru_bx", [512]); g.lru_lam = I("l0_lru_lambda", [512])
    g.ret_g = I("l0_ret_gn_g", [512]); g.ret_b = I("l0_ret_gn_b", [512]); g.w_out0 = I("l0_w_out", [D, D])
    g.mu = I("l1_mu", [6, D]); g.w_rkv = I("l1_w_rkv", [3, D, D]); g.w0 = I("l1_w0", [D]); g.w1 = I("l1_w1", [D, 64]); g.w2 = I("l1_w2", [64, D])
    g.a0 = I("l1_a0", [D]); g.a1 = I("l1_a1", [D, 64]); g.a2 = I("l1_a2", [64, D]); g.g1 = I("l1_g1", [D, 128]); g.g2 = I("l1_g2", [128, D])
    g.k_k = I("l1_k_k", [D]); g.k_a = I("l1_k_a", [D]); g.r_k = I("l1_r_k", [D]); g.gn_g = I("l1_gn_g", [D]); g.gn_b = I("l1_gn_b", [D])
    g.w_out1 = I("l1_w_out", [D, D])
    g.c_ident = I("c_ident", [128, 128]); g.c_cos = I("c_cos", [128, TP + TS]); g.c_sin = I("c_sin", [128, TP + TS])
    g.c_retM = I("c_retM", [2, 128, 4, 128]); g.c_retXI = I("c_retXI", [2, 128, 4, 128]); g.c_retZ = I("c_retZ", [2, 128, 4, 128])
    g.c_msk = I("c_msk", [2, 128, 3, 128]); g.c_onesbd = I("c_onesbd", [128, 128])
    g.y = O("y", [NT, D]); g.o_memk = O("o_memk", [2, 256, D]); g.o_memv = O("o_memv", [2, 256, D])
    g.o_conv = O("o_conv", [3, 3, 512]); g.o_lru = O("o_lru", [3, 512]); g.o_ret = O("o_ret", [3, 4, 128, 128])
    g.o_shift = O("o_shift", [3, D]); g.o_wkv = O("o_wkv", [3, 16, 64, 64])
    g.xs = [k.dram("scr0", [NT, D], F32, "Internal"), k.dram("scr1", [NT, D], F32, "Internal")]
    return g


def load_cols(k, st, name, src_ap, ncol, rows=128):
    b = k.sb(name, [rows, ncol], F32, st)
    k.dma(k.SP, b[:], src_ap.rearrange("(c p) -> p c", p=rows), writes=[b], allow_slow_non_contiguous=True)
    return b


def load_weight(k, g, dst, src2d, K, N, kc0=0):
    nk = max(1, K // 128)
    rows = min(K, 128)
    for kc in range(nk):
        for n0 in range(0, N, 704):
            n1 = min(N, n0 + 704)
            stg = g.stage[g.stage_i % len(g.stage)]
            g.stage_i += 1
            k.dma(k.SP, stg[0:rows, 0:n1 - n0], src2d[kc * 128: kc * 128 + rows, n0:n1], writes=[stg])
            k.copy(dst[0:rows, kc0 + kc, n0:n1], stg[0:rows, 0:n1 - n0], [stg], [dst])


def load_tok(k, g, src, row0, PT, NS, pool):
    b = pool[g.tok_i % len(pool)]
    g.tok_i += 1
    k.dma(k.SP, b[0:PT, 0:NS, :], src.t[row0: row0 + PT * NS, :].rearrange("(s p) d -> p s d", p=PT), writes=[b])
    return b


def to_featmajor(k, g, x32, PT, NS, xT, col0=0):
    for s in range(NS):
        ps = k.psn()
        for c in range(KC):
            k.tr(ps[:, c * PT:(c + 1) * PT], x32[0:PT, s, c * 128:(c + 1) * 128], g.ident[0:PT, 0:PT],
                 reads=[x32, g.ident], writes=[ps])
        k.copy(xT[:, 0:KC, col0 + s * PT: col0 + (s + 1) * PT],
               ps[:, 0:KC * PT].rearrange("p (c t) -> p c t", t=PT), [ps], [xT])


def ln_epilogue(k, g, ps, base_ap, base_buf, PT, dst, row0, alpha, do_ln=True):
    y = g.ybuf[g.y_i % 2]; o = g.obuf[g.y_i % 2]; sm = g.small[g.y_i % 2]
    g.y_i += 1
    DVE, ACT = k.DVE, k.ACT
    k.op(DVE, lambda e: e.scalar_tensor_tensor(out=y[0:PT, :], in0=base_ap, scalar=float(alpha), in1=ps[0:PT, :],
                                               op0=ALU.mult, op1=ALU.add), reads=[base_buf, ps], writes=[y])
    if not do_ln:
        k.dma(k.POOL, dst.t[row0:row0 + PT, :], y[0:PT, :], reads=[y], writes=[dst])
        return
    k.op(DVE, lambda e: e.bn_stats(out=sm[0:PT, 0:6], in_=y[0:PT, 0:512]), reads=[y], writes=[sm])
    k.op(DVE, lambda e: e.bn_stats(out=sm[0:PT, 6:12], in_=y[0:PT, 512:1024]), reads=[y], writes=[sm])
    k.op(DVE, lambda e: e.bn_aggr(out=sm[0:PT, 12:14], in_=sm[0:PT, 0:12]), reads=[sm], writes=[sm])
    k.op(ACT, lambda e: e.activation(out=sm[0:PT, 14:15], in_=sm[0:PT, 13:14], func=AF.Ln, bias=g.eps_ln[0:PT, 0:1]),
         reads=[sm, g.eps_ln], writes=[sm])
    k.op(ACT, lambda e: e.activation(out=sm[0:PT, 14:15], in_=sm[0:PT, 14:15], func=AF.Exp, scale=-0.5), reads=[sm], writes=[sm])
    k.op(DVE, lambda e: e.tensor_scalar(out=sm[0:PT, 15:16], in0=sm[0:PT, 12:13], scalar1=-1.0, scalar2=sm[0:PT, 14:15],
                                        op0=ALU.mult, op1=ALU.mult), reads=[sm], writes=[sm])
    k.op(ACT, lambda e: e.activation(out=o[0:PT, :], in_=y[0:PT, :], func=AF.Identity, scale=sm[0:PT, 14:15],
                                     bias=sm[0:PT, 15:16]), reads=[y, sm], writes=[o])
    k.op(DVE, lambda e: e.tensor_tensor(out=o[0:PT, :], in0=o[0:PT, :], in1=g.gtab[0:PT, :], op=ALU.mult),
         reads=[o, g.gtab], writes=[o])
    k.op(DVE, lambda e: e.tensor_tensor(out=o[0:PT, :], in0=o[0:PT, :], in1=g.btab[0:PT, :], op=ALU.add),
         reads=[o, g.btab], writes=[o])
    k.dma(k.POOL, dst.t[row0:row0 + PT, :], o[0:PT, :], reads=[o], writes=[dst])


def load_ln_tabs(k, g, l, j):
    k.dma(k.SP, g.gtab[:], g.ln_g.t[l, j].partition_broadcast(128), writes=[g.gtab])
    k.dma(k.SP, g.btab[:], g.ln_b.t[l, j].partition_broadcast(128), writes=[g.btab])


def segs(TP, tile):
    out = [(r, tile, 0) for r in range(0, TP, tile)]
    out += [(TP, TS, 1), (TP + TS, TS, 2)]
    return out


def stage_ffn(k, g, TP, l, which, src, dst):
    TT = 256
    with contextlib.ExitStack() as st:
        Wg = k.sb("Wup", [128, KC, 2 * DFF], BF16, st)
        Wd = k.sb("Wd", [128, 22, D], BF16, st)
        load_ln_tabs(k, g, l, 0 if which == 0 else 3)
        load_weight(k, g, Wg, g.ffn_up.t[l, which], D, 2 * DFF)
        load_weight(k, g, Wd, g.ffn_down.t[l, which], DFF, D)
        xTs = [k.sb("xT", [128, KC, TT], BF16, st) for _ in range(2)]
        hT = k.sb("hT", [128, 22, TT], BF16, st)
        sgs = [k.sb("sg", [128, TT], F32, st) for _ in range(2)]
        toks = [k.sb("tok2", [128, 2, D], F32, st) for _ in range(2)]
        ffn_tiles = [(r, TT, 0) for r in range(0, TP, TT)] + [(TP, 2 * TS, 1)]
        for ti, (row0, ntok, seq) in enumerate(ffn_tiles):
            PT = min(128, ntok); NS = ntok // PT
            k.rotate()
            x32 = load_tok(k, g, src, row0, PT, NS, toks)
            xT = xTs[ti % 2]
            to_featmajor(k, g, x32, PT, NS, xT)
            for fc in range(22):
                ps = k.psn()
                for kc in range(KC):
                    k.mm(ps[:, 0:ntok], Wg[:, kc, fc * 128:(fc + 1) * 128], xT[:, kc, 0:ntok], kc == 0, kc == KC - 1,
                         reads=[Wg, xT], writes=[ps])
                for kc in range(KC):
                    k.mm(ps[:, 512:512 + ntok], Wg[:, kc, DFF + fc * 128: DFF + (fc + 1) * 128], xT[:, kc, 0:ntok],
                         kc == 0, kc == KC - 1, reads=[Wg, xT], writes=[ps])
                sg = sgs[fc % 2]
                k.op(k.ACT, lambda e: e.activation(out=sg[:, 0:ntok], in_=ps[:, 0:ntok], func=AF.Silu), reads=[ps], writes=[sg])
                k.op(k.DVE, lambda e: e.scalar_tensor_tensor(out=hT[:, fc, 0:ntok], in0=sg[:, 0:ntok], scalar=0.5,
                                                             in1=ps[:, 512:512 + ntok], op0=ALU.mult, op1=ALU.mult),
                     reads=[sg, ps], writes=[hT])
            for s in range(NS):
                ps = k.psn()
                for hf in range(2):
                    for fc in range(22):
                        k.mm(ps[0:PT, hf * 512:(hf + 1) * 512], hT[:, fc, s * PT:(s + 1) * PT], Wd[:, fc, hf * 512:(hf + 1) * 512],
                             fc == 0, fc == 21, reads=[hT, Wd], writes=[ps])
                ln_epilogue(k, g, ps, x32[0:PT, s, :], x32, PT, dst, row0 + s * PT, ALPHA)
        k.barrier()


def setup_globals(k, g):
    g.stage = [k.sb("stg", [128, 704], F32) for _ in range(2)]
    g.stage_i = 0
    g.tok_i = 0
    g.ybuf = [k.sb("yb", [128, D], F32) for _ in range(2)]
    g.obuf = [k.sb("ob", [128, D], F32) for _ in range(2)]
    g.small = [k.sb("sm", [128, 16], F32) for _ in range(2)]
    g.y_i = 0
    g.gtab = k.sb("gtab", [128, D], F32); g.btab = k.sb("btab", [128, D], F32)
    g.ident = k.sb("ident", [128, 128], F32)
    g.identb = k.sb("identb", [128, 128], BF16)
    g.eps_ln = k.sb("epsln", [128, 1], F32)
    k.dma(k.SP, g.ident[:], g.c_ident.t[:, :], writes=[g.ident])
    k.copy(g.identb[:], g.ident[:], [g.ident], [g.identb], eng=k.DVE)
    k.op(k.DVE, lambda e: e.memset(g.eps_ln[:], LN_EPS), writes=[g.eps_ln])
    g.one_c = k.sb("one_c", [128, 1], F32)
    k.op(k.DVE, lambda e: e.memset(g.one_c[:], 1.0), writes=[g.one_c])
    k.pspool = [k.ps("psp", [128, 1024], F32) for _ in range(4)]


def stage_xattn(k, g, TP, l, src, dst):
    DVE, ACT = k.DVE, k.ACT
    with contextlib.ExitStack() as st:
        Wq = k.sb("Wq", [128, KC, D], BF16, st); Wo = k.sb("Wo", [128, KC, D], BF16, st)
        Wk = k.sb("Wk", [128, KC, D], BF16, st); Wv = k.sb("Wv", [128, KC, D], BF16, st)
        load_ln_tabs(k, g, l, 2)
        load_weight(k, g, Wq, g.xa_q.t[l], D, D); load_weight(k, g, Wo, g.xa_o.t[l], D, D)
        load_weight(k, g, Wk, g.xa_k.t[l], D, D); load_weight(k, g, Wv, g.xa_v.t[l], D, D)
        ones = k.sb("ones", [128, 128], BF16, st)
        k.op(DVE, lambda e: e.memset(ones[:], 1.0), writes=[ones])
        m32 = k.sb("m32", [128, 2, D], F32, st)
        memT = k.sb("memT", [128, KC, 256], BF16, st)
        KTs = [k.sb("KT", [128, KC, 256], BF16, st) for _ in range(3)]
        Vts = [k.sb("Vt", [128, 2, D], BF16, st) for _ in range(3)]
        o32s = [k.sb("mo32", [128, D], F32, st) for _ in range(2)]
        k.dma(k.SP, m32[:], g.mem.t[:, :].rearrange("(s p) d -> p s d", p=128), writes=[m32])
        to_featmajor(k, g, m32, 128, 2, memT)
        for ec in range(KC):
            ps = k.psn()
            for kc in range(KC):
                k.mm(ps[:, 0:256], Wk[:, kc, ec * 128:(ec + 1) * 128], memT[:, kc, :], kc == 0, kc == KC - 1, reads=[Wk, memT], writes=[ps])
            k.copy(KTs[0][:, ec, :], ps[:, 0:256], [ps], [KTs[0]])
        oi = 0
        for (W, outd, isv) in ((Wk, g.o_memk, False), (Wv, g.o_memv, True)):
            for s in range(2):
                ps = k.psn()
                for hf in range(2):
                    for kc in range(KC):
                        k.mm(ps[:, hf * 512:(hf + 1) * 512], memT[:, kc, s * 128:(s + 1) * 128], W[:, kc, hf * 512:(hf + 1) * 512],
                             kc == 0, kc == KC - 1, reads=[W, memT], writes=[ps])
                o32 = o32s[oi % 2]; oi += 1
                k.copy(o32[:], ps[:, :], [ps], [o32])
                k.dma(k.POOL, outd.t[l, s * 128:(s + 1) * 128, :], o32[:], reads=[o32], writes=[outd])
                if isv:
                    k.copy(Vts[0][:, s, :], o32[:], [o32], [Vts[0]])
        for sq in range(2):
            k.dma(k.SP, m32[:], g.ck.t[l, sq].rearrange("(s p) d -> p s d", p=128), writes=[m32])
            to_featmajor(k, g, m32, 128, 2, KTs[1 + sq])
            k.dma(k.SP, m32[:], g.cv.t[l, sq].rearrange("(s p) d -> p s d", p=128), writes=[m32])
            k.copy(Vts[1 + sq][:], m32[:], [m32], [Vts[1 + sq]])
        xTs = [k.sb("xT", [128, KC, 256], BF16, st) for _ in range(2)]
        qT = k.sb("qT", [128, KC, 256], BF16, st)
        pTs = [k.sb("pT", [128, 2, 256], BF16, st) for _ in range(2)]
        rdens = [k.sb("rden", [128, 256], F32, st) for _ in range(2)]
        oT = k.sb("oT", [128, KC, 256], BF16, st)
        toks = [k.sb("tok4", [128, 2, D], F32, st) for _ in range(2)]
        for ti, (row0, ntok, seq) in enumerate(segs(TP, 256)):
            PT = min(128, ntok); NS = ntok // PT
            k.rotate()
            x32 = load_tok(k, g, src, row0, PT, NS, toks)
            xT = xTs[ti % 2]
            to_featmajor(k, g, x32, PT, NS, xT)
            KT = KTs[seq]; Vt = Vts[seq]
            for ec in range(KC):
                ps = k.psn()
                for kc in range(KC):
                    k.mm(ps[:, 0:ntok], Wq[:, kc, ec * 128:(ec + 1) * 128], xT[:, kc, 0:ntok], kc == 0, kc == KC - 1, reads=[Wq, xT], writes=[ps])
                k.op(ACT, lambda e: e.activation(out=qT[:, ec, 0:ntok], in_=ps[:, 0:ntok], func=AF.Copy, scale=0.0625), reads=[ps], writes=[qT])
            for h in range(4):
                pT = pTs[h % 2]; rden = rdens[h % 2]
                for mc in range(2):
                    ps = k.psn()
                    for dc in range(2):
                        k.mm(ps[:, 0:ntok], KT[:, 2 * h + dc, mc * 128:(mc + 1) * 128], qT[:, 2 * h + dc, 0:ntok], dc == 0, dc == 1,
                             reads=[KT, qT], writes=[ps])
                    k.op(ACT, lambda e: e.activation(out=pT[:, mc, 0:ntok], in_=ps[:, 0:ntok], func=AF.Exp), reads=[ps], writes=[pT])
                ps = k.psn()
                for mc in range(2):
                    k.mm(ps[:, 0:ntok], ones[:, :], pT[:, mc, 0:ntok], mc == 0, mc == 1, reads=[ones, pT], writes=[ps])
                k.op(ACT, lambda e: e.activation(out=rden[:, 0:ntok], in_=ps[:, 0:ntok], func=AF.Ln), reads=[ps], writes=[rden])
                k.op(ACT, lambda e: e.activation(out=rden[:, 0:ntok], in_=rden[:, 0:ntok], func=AF.Exp, scale=-1.0), reads=[rden], writes=[rden])
                for dc in range(2):
                    ps = k.psn()
                    for mc in range(2):
                        k.mm(ps[:, 0:ntok], Vt[:, mc, (2 * h + dc) * 128:(2 * h + dc + 1) * 128], pT[:, mc, 0:ntok], mc == 0, mc == 1,
                             reads=[Vt, pT], writes=[ps])
                    k.op(DVE, lambda e: e.tensor_tensor(out=oT[:, 2 * h + dc, 0:ntok], in0=ps[:, 0:ntok], in1=rden[:, 0:ntok], op=ALU.mult),
                         reads=[ps, rden], writes=[oT])
            for s in range(NS):
                ps = k.psn()
                for hf in range(2):
                    for ec in range(KC):
                        k.mm(ps[0:PT, hf * 512:(hf + 1) * 512], oT[:, ec, s * PT:(s + 1) * PT], Wo[:, ec, hf * 512:(hf + 1) * 512],
                             ec == 0, ec == KC - 1, reads=[oT, Wo], writes=[ps])
                ln_epilogue(k, g, ps, x32[0:PT, s, :], x32, PT, dst, row0 + s * PT, ALPHA)
        k.barrier()


def stage_mixer_ab(k, g, TP, src, dst):
    DVE, ACT = k.DVE, k.ACT
    TT = 256
    with contextlib.ExitStack() as st:
        Win = k.sb("Win", [128, KC, 3072], BF16, st); Wrot = k.sb("Wrot", [128, KC, 1024], BF16, st)
        Wout = k.sb("Wout", [128, KC, D], BF16, st)
        load_ln_tabs(k, g, 0, 1)
        load_weight(k, g, Win, g.w_in.t, D, 3072); load_weight(k, g, Wrot, g.w_rot.t, D, 1024)
        load_weight(k, g, Wout, g.w_out0.t, D, D)
        bd32 = k.sb("bd32", [128, 2, 4, 128], F32, st); Wbd = k.sb("Wbd", [128, 2, 4, 128], BF16, st)
        k.op(DVE, lambda e: e.memset(bd32[:], 0.0), writes=[bd32])
        for wi, wsrc in enumerate((g.lru_wa, g.lru_wx)):
            for hp in range(2):
                k.dma(k.SP, bd32[64 * hp:64 * hp + 64, wi, :, 64 * hp:64 * hp + 64],
                      wsrc.t.rearrange("(c hp) i j -> hp i c j", hp=2)[hp], writes=[bd32])
        k.copy(Wbd[:], bd32[:], [bd32], [Wbd], eng=DVE)
        cw = k.sb("cw", [128, 4, 4], F32, st)
        for j in range(4):
            k.dma(k.SP, cw[:, :, j], g.conv_w.t[j].rearrange("(c p) -> p c", p=128), writes=[cw], allow_slow_non_contiguous=True)
        cb = load_cols(k, st, "cb", g.conv_b.t, 4); ba = load_cols(k, st, "ba", g.lru_ba.t, 4); bx = load_cols(k, st, "bx", g.lru_bx.t, 4)
        lam = load_cols(k, st, "lam", g.lru_lam.t, 4); gng = load_cols(k, st, "gng", g.ret_g.t, 4); gnb = load_cols(k, st, "gnb", g.ret_b.t, 4)
        cl = k.sb("cl", [128, 4], F32, st); cl2 = k.sb("cl2", [128, 4], F32, st)
        k.op(ACT, lambda e: e.activation(out=cl[:], in_=lam[:], func=AF.Exp, scale=-1.0), reads=[lam], writes=[cl])
        k.op(ACT, lambda e: e.activation(out=cl[:], in_=cl[:], func=AF.Ln, bias=g.one_c[:, 0:1]), reads=[cl, g.one_c], writes=[cl])
        k.op(DVE, lambda e: e.tensor_scalar(out=cl2[:], in0=cl[:], scalar1=-16.0, scalar2=None, op0=ALU.mult), reads=[cl], writes=[cl2])
        k.op(DVE, lambda e: e.tensor_scalar(out=cl[:], in0=cl[:], scalar1=-8.0, scalar2=None, op0=ALU.mult), reads=[cl], writes=[cl])
        ones = k.sb("ones", [128, 128], BF16, st)
        k.op(DVE, lambda e: e.memset(ones[:], 1.0 / 128.0), writes=[ones])
        epsc = k.sb("epsc", [128, 1], F32, st)
        k.op(DVE, lambda e: e.memset(epsc[:], LN_EPS), writes=[epsc])
        cosT = k.sb("cosT", [128, TT], F32, st); sinT = k.sb("sinT", [128, TT], F32, st)
        retM = k.sb("retM", [128, 4, 128], F32, st); retXI = k.sb("retXI", [128, 4, 128], F32, st); retZ = k.sb("retZ", [128, 4, 128], F32, st)
        xaT = k.sb("xaT", [128, 4, 3 + TT], F32, st); hl = k.sb("hl", [128, 4], F32, st)
        S32 = k.sb("S32", [128, 4, 128], F32, st); Sb = k.sb("Sb", [128, 4, 128], BF16, st)
        xTs = [k.sb("xT", [128, KC, TT], BF16, st) for _ in range(2)]
        toks = [k.sb("tok2", [128, 2, D], F32, st) for _ in range(2)]
        gaT = k.sb("gaT", [128, 4, TT], F32, st)
        qr = k.sb("qr", [128, 4, TT], BF16, st); kz = k.sb("kz", [128, 4, TT], BF16, st)
        t1s = [k.sb("t1", [128, TT], F32, st) for _ in range(2)]; t2s = [k.sb("t2", [128, TT], F32, st) for _ in range(2)]
        t3s = [k.sb("t3", [128, TT], F32, st) for _ in range(2)]
        Ktok = k.sb("Ktok", [128, 2, 4, 128], BF16, st); Vtok = k.sb("Vtok", [128, 2, 512], BF16, st)
        PTb = k.sb("PTb", [128, 4, 128], BF16, st)
        oT = k.sb("oT", [128, 4, TT], F32, st); obf = k.sb("obf", [128, 4, TT], BF16, st); osq = k.sb("osq", [128, 4, TT], BF16, st)
        lru = [[k.sb(n, [128, TT], (BF16 if n == "xcb" else F32), st) for n in ("xc", "xcb", "rr", "ii", "aa", "hh")] for _ in range(2)]
        yT = k.sb("yT", [128, KC, TT], BF16, st)
        cur_seq = -1
        allsegs = segs(TP, TT)
        for ti, (row0, ntok, seq) in enumerate(allsegs):
            PT = min(128, ntok); NS = ntok // PT
            k.rotate()
            C = PT; nch = NS
            ci = 0 if seq == 0 else 1
            last = (ti + 1 == len(allsegs)) or (allsegs[ti + 1][2] != seq)
            if seq != cur_seq:
                cur_seq = seq
                k.op(DVE, lambda e: e.memset(PTb[:], 0.0), writes=[PTb])
                k.op(DVE, lambda e: e.memset(Ktok[:], 0.0), writes=[Ktok])
                k.op(DVE, lambda e: e.memset(Vtok[:], 0.0), writes=[Vtok])
                k.dma(k.SP, retM[:], g.c_retM.t[ci], writes=[retM]); k.dma(k.SP, retXI[:], g.c_retXI.t[ci], writes=[retXI])
                k.dma(k.SP, retZ[:], g.c_retZ.t[ci], writes=[retZ])
                if seq == 0:
                    k.op(DVE, lambda e: e.memset(xaT[:], 0.0), writes=[xaT])
                    k.op(DVE, lambda e: e.memset(hl[:], 0.0), writes=[hl])
                    k.op(DVE, lambda e: e.memset(S32[:], 0.0), writes=[S32])
                else:
                    for j in range(3):
                        k.dma(k.SP, xaT[:, :, j], g.st_conv.t[seq - 1, j].rearrange("(c p) -> p c", p=128), writes=[xaT], allow_slow_non_contiguous=True)
                    k.dma(k.SP, hl[:], g.st_lru.t[seq - 1].rearrange("(c p) -> p c", p=128), writes=[hl], allow_slow_non_contiguous=True)
                    k.dma(k.SP, S32[:], g.st_ret.t[seq - 1].rearrange("h d v -> d h v"), writes=[S32])
                k.copy(Sb[:], S32[:], [S32], [Sb], eng=ACT)
            pos0 = row0 if seq == 0 else TP
            k.dma(k.SP, cosT[:, 0:ntok], g.c_cos.t[:, pos0:pos0 + ntok], writes=[cosT])
            k.dma(k.SP, sinT[:, 0:ntok], g.c_sin.t[:, pos0:pos0 + ntok], writes=[sinT])
            x32 = load_tok(k, g, src, row0, PT, NS, toks)
            xT = xTs[ti % 2]
            to_featmajor(k, g, x32, PT, NS, xT)

            def proj(W, col0):
                ps = k.psn()
                for kc in range(KC):
                    k.mm(ps[:, 0:ntok], W[:, kc, col0:col0 + 128], xT[:, kc, 0:ntok], kc == 0, kc == KC - 1, reads=[W, xT], writes=[ps])
                return ps
            for c in range(4):
                ps = proj(Win, c * 128)
                k.copy(xaT[:, c, 3:3 + ntok], ps[:, 0:ntok], [ps], [xaT], eng=ACT)
                ps = proj(Win, 512 + c * 128)
                k.op(ACT, lambda e: e.activation(out=gaT[:, c, 0:ntok], in_=ps[:, 0:ntok], func=AF.Gelu_apprx_tanh), reads=[ps], writes=[gaT])
            for (dst_b, base, rbase, isk) in ((qr, 1024, 0, False), (kz, 1536, 512, True)):
                for h in range(4):
                    t1, t2, t3 = t1s[h % 2], t2s[h % 2], t3s[h % 2]
                    ps = proj(Win, base + h * 128)
                    ps2 = proj(Wrot, rbase + h * 128)
                    k.op(DVE, lambda e: e.tensor_tensor(out=t1[:, 0:ntok], in0=ps[:, 0:ntok], in1=cosT[:, 0:ntok], op=ALU.mult), reads=[ps, cosT], writes=[t1])
                    k.op(DVE, lambda e: e.tensor_tensor(out=t2[:, 0:ntok], in0=ps2[:, 0:ntok], in1=sinT[:, 0:ntok], op=ALU.mult), reads=[ps2, sinT], writes=[t2])
                    if not isk:
                        k.op(DVE, lambda e: e.tensor_tensor(out=qr[:, h, 0:ntok], in0=t1[:, 0:ntok], in1=t2[:, 0:ntok], op=ALU.add), reads=[t1, t2], writes=[qr])
                    else:
                        k.op(DVE, lambda e: e.tensor_tensor(out=t3[:, 0:ntok], in0=t1[:, 0:ntok], in1=t2[:, 0:ntok], op=ALU.add), reads=[t1, t2], writes=[t3])
                        k.op(DVE, lambda e: e.tensor_tensor(out=kz[:, h, 0:ntok].rearrange("p (n c) -> p n c", c=C),
                                                            in0=t3[:, 0:ntok].rearrange("p (n c) -> p n c", c=C),
                                                            in1=retZ[:, h, 0:C].unsqueeze(1).broadcast_to([128, nch, C]), op=ALU.mult),
                             reads=[t3, retZ], writes=[kz])
            for n in range(nch):
                cs = slice(n * C, (n + 1) * C)
                ps = k.psn()
                for kc in range(KC):
                    k.mm(ps[0:C, 0:512], xT[:, kc, cs], Win[:, kc, 2048:2560], kc == 0, kc == KC - 1, reads=[xT, Win], writes=[ps])
                k.copy(Vtok[0:C, n % 2, :], ps[0:C, 0:512], [ps], [Vtok])
                ps = k.psn()
                psb = ps.t[:, 0:256].bitcast(BF16)
                for h in range(4):
                    k.tr(psb[0:C, h * 128:(h + 1) * 128], kz[:, h, cs], g.identb[:, :], reads=[kz, g.identb], writes=[ps])
                k.copy(Ktok[0:C, n % 2, :, :], psb[0:C, 0:512].rearrange("p (h d) -> p h d", d=128), [ps], [Ktok])
                ps = k.psn()
                for h in range(4):
                    k.mm(ps[0:C, h * 128:h * 128 + C], kz[:, h, cs], qr[:, h, cs], True, True, reads=[kz, qr], writes=[ps])
                k.op(DVE, lambda e: e.tensor_tensor(out=PTb[0:C, :, 0:C], in0=ps[0:C, 0:512].rearrange("p (h c) -> p h c", c=128)[:, :, 0:C],
                                                    in1=retM[0:C, :, 0:C], op=ALU.mult), reads=[ps, retM], writes=[PTb])
                ps = k.psn()
                for h in range(4):
                    k.mm(ps[:, h * 128:h * 128 + C], Vtok[:, n % 2, h * 128:(h + 1) * 128], PTb[:, h, 0:C], True, False, reads=[Vtok, PTb], writes=[ps])
                    k.mm(ps[:, h * 128:h * 128 + C], Sb[:, h, :], qr[:, h, cs], False, True, reads=[Sb, qr], writes=[ps])
                k.op(DVE, lambda e: e.tensor_tensor(out=oT[:, :, cs], in0=ps[:, 0:512].rearrange("p (h c) -> p h c", c=128)[:, :, 0:C],
                                                    in1=retXI[:, :, 0:C], op=ALU.mult), reads=[ps, retXI], writes=[oT])
                ps = k.psn()
                for h in range(4):
                    k.mm(ps[:, h * 128:(h + 1) * 128], Ktok[:, n % 2, h, :], Vtok[:, n % 2, h * 128:(h + 1) * 128], True, True, reads=[Ktok, Vtok], writes=[ps])
                for h in range(4):
                    gam = float(np.exp(np.log1p(-(2.0 ** (-5.0 - h))) * C))
                    k.op(DVE, lambda e: e.scalar_tensor_tensor(out=S32[:, h, :], in0=S32[:, h, :], scalar=gam, in1=ps[:, h * 128:(h + 1) * 128],
                                                               op0=ALU.mult, op1=ALU.add), reads=[S32, ps], writes=[S32])
                k.copy(Sb[:], S32[:], [S32], [Sb], eng=ACT)
            k.op(ACT, lambda e: e.activation(out=obf[:, :, 0:ntok], in_=oT[:, :, 0:ntok], func=AF.Copy), reads=[oT], writes=[obf])
            k.op(ACT, lambda e: e.activation(out=osq[:, :, 0:ntok], in_=oT[:, :, 0:ntok], func=AF.Square), reads=[oT], writes=[osq])
            for h in range(4):
                t1, t2, t3 = t1s[h % 2], t2s[h % 2], t3s[h % 2]
                psm = k.psn()
                k.mm(psm[:, 0:ntok], ones[:, :], obf[:, h, 0:ntok], True, True, reads=[ones, obf], writes=[psm])
                k.mm(psm[:, 512:512 + ntok], ones[:, :], osq[:, h, 0:ntok], True, True, reads=[ones, osq], writes=[psm])
                k.op(ACT, lambda e: e.activation(out=t1[:, 0:ntok], in_=psm[:, 0:ntok], func=AF.Square), reads=[psm], writes=[t1])
                k.op(DVE, lambda e: e.tensor_tensor(out=t1[:, 0:ntok], in0=psm[:, 512:512 + ntok], in1=t1[:, 0:ntok], op=ALU.subtract), reads=[psm, t1], writes=[t1])
                k.op(ACT, lambda e: e.activation(out=t1[:, 0:ntok], in_=t1[:, 0:ntok], func=AF.Ln, bias=epsc[:, 0:1]), reads=[t1, epsc], writes=[t1])
                k.op(ACT, lambda e: e.activation(out=t1[:, 0:ntok], in_=t1[:, 0:ntok], func=AF.Exp, scale=-0.5), reads=[t1], writes=[t1])
                k.op(DVE, lambda e: e.tensor_tensor(out=t2[:, 0:ntok], in0=oT[:, h, 0:ntok], in1=psm[:, 0:ntok], op=ALU.subtract), reads=[oT, psm], writes=[t2])
                k.op(DVE, lambda e: e.tensor_tensor(out=t2[:, 0:ntok], in0=t2[:, 0:ntok], in1=t1[:, 0:ntok], op=ALU.mult), reads=[t1, t2], writes=[t2])
                k.op(DVE, lambda e: e.tensor_scalar(out=t2[:, 0:ntok], in0=t2[:, 0:ntok], scalar1=gng[:, h:h + 1], scalar2=gnb[:, h:h + 1],
                                                    op0=ALU.mult, op1=ALU.add), reads=[t2, gng, gnb], writes=[t2])
                ps = proj(Win, 2560 + h * 128)
                k.op(ACT, lambda e: e.activation(out=t3[:, 0:ntok], in_=ps[:, 0:ntok], func=AF.Silu), reads=[ps], writes=[t3])
                k.op(DVE, lambda e: e.tensor_tensor(out=yT[:, 4 + h, 0:ntok], in0=t2[:, 0:ntok], in1=t3[:, 0:ntok], op=ALU.mult), reads=[t2, t3], writes=[yT])
            for c in range(4):
                xc, xcb, rr, ii, aa, hh = lru[c % 2]
                k.op(DVE, lambda e: e.tensor_scalar(out=xc[:, 0:ntok], in0=xaT[:, c, 0:ntok], scalar1=cw[:, c, 0:1], scalar2=cb[:, c:c + 1],
                                                    op0=ALU.mult, op1=ALU.add), reads=[xaT, cw, cb], writes=[xc])
                for j in range(1, 4):
                    k.op(DVE, lambda e: e.scalar_tensor_tensor(out=xc[:, 0:ntok], in0=xaT[:, c, j:j + ntok], scalar=cw[:, c, j:j + 1], in1=xc[:, 0:ntok],
                                                               op0=ALU.mult, op1=ALU.add), reads=[xaT, cw, xc], writes=[xc])
                k.copy(xcb[:, 0:ntok], xc[:, 0:ntok], [xc], [xcb], eng=ACT)
                ps = k.psn()
                k.mm(ps[:, 0:ntok], Wbd[:, 0, c, :], xcb[:, 0:ntok], True, True, reads=[Wbd, xcb], writes=[ps])
                k.mm(ps[:, 512:512 + ntok], Wbd[:, 1, c, :], xcb[:, 0:ntok], True, True, reads=[Wbd, xcb], writes=[ps])
                k.op(ACT, lambda e: e.activation(out=rr[:, 0:ntok], in_=ps[:, 0:ntok], func=AF.Sigmoid, bias=ba[:, c:c + 1]), reads=[ps, ba], writes=[rr])
                k.op(ACT, lambda e: e.activation(out=ii[:, 0:ntok], in_=ps[:, 512:512 + ntok], func=AF.Sigmoid, bias=bx[:, c:c + 1]), reads=[ps, bx], writes=[ii])
                k.op(ACT, lambda e: e.activation(out=aa[:, 0:ntok], in_=rr[:, 0:ntok], func=AF.Exp, scale=cl[:, c:c + 1]), reads=[rr, cl], writes=[aa])
                k.op(ACT, lambda e: e.activation(out=rr[:, 0:ntok], in_=rr[:, 0:ntok], func=AF.Exp, scale=cl2[:, c:c + 1]), reads=[rr, cl2], writes=[rr])
                k.op(ACT, lambda e: e.activation(out=rr[:, 0:ntok], in_=rr[:, 0:ntok], func=AF.Ln, scale=-1.0, bias=g.one_c[:, 0:1]), reads=[rr, g.one_c], writes=[rr])
                k.op(ACT, lambda e: e.activation(out=rr[:, 0:ntok], in_=rr[:, 0:ntok], func=AF.Exp, scale=0.5), reads=[rr], writes=[rr])
                k.op(DVE, lambda e: e.tensor_tensor(out=ii[:, 0:ntok], in0=ii[:, 0:ntok], in1=xc[:, 0:ntok], op=ALU.mult), reads=[ii, xc], writes=[ii])
                k.op(DVE, lambda e: e.tensor_tensor(out=ii[:, 0:ntok], in0=ii[:, 0:ntok], in1=rr[:, 0:ntok], op=ALU.mult), reads=[ii, rr], writes=[ii])
                k.op(DVE, lambda e: e.tensor_tensor_scan(out=hh[:, 0:ntok], data0=aa[:, 0:ntok], data1=ii[:, 0:ntok], initial=hl[:, c:c + 1],
                                                         op0=ALU.mult, op1=ALU.add), reads=[aa, ii, hl], writes=[hh])
                k.op(DVE, lambda e: e.tensor_copy(out=hl[:, c:c + 1], in_=hh[:, ntok - 1:ntok]), reads=[hh], writes=[hl])
                k.op(DVE, lambda e: e.tensor_tensor(out=yT[:, c, 0:ntok], in0=hh[:, 0:ntok], in1=gaT[:, c, 0:ntok], op=ALU.mult), reads=[hh, gaT], writes=[yT])
            k.op(DVE, lambda e: e.tensor_copy(out=xaT[:, :, 0:3], in_=xaT[:, :, ntok:ntok + 3]), reads=[xaT], writes=[xaT])
            for s in range(NS):
                ps = k.psn()
                for hf in range(2):
                    for c in range(KC):
                        k.mm(ps[0:PT, hf * 512:(hf + 1) * 512], yT[:, c, s * PT:(s + 1) * PT], Wout[:, c, hf * 512:(hf + 1) * 512],
                             c == 0, c == KC - 1, reads=[yT, Wout], writes=[ps])
                ln_epilogue(k, g, ps, x32[0:PT, s, :], x32, PT, dst, row0 + s * PT, ALPHA)
            if last:
                for j in range(3):
                    k.dma(k.POOL, g.o_conv.t[seq, j].rearrange("(c p) -> p c", p=128), xaT[:, :, j], reads=[xaT], writes=[g.o_conv], allow_slow_non_contiguous=True)
                k.dma(k.POOL, g.o_lru.t[seq].rearrange("(c p) -> p c", p=128), hl[:], reads=[hl], writes=[g.o_lru], allow_slow_non_contiguous=True)
                k.dma(k.POOL, g.o_ret.t[seq].rearrange("h d v -> d h v"), S32[:], reads=[S32], writes=[g.o_ret])
        k.barrier()


def _interleave(a, b, ra=1, rb=1):
    alive_a, alive_b = a is not None, b is not None
    while alive_a or alive_b:
        for _ in range(ra):
            if alive_a:
                try:
                    next(a)
                except StopIteration:
                    alive_a = False
        for _ in range(rb):
            if alive_b:
                try:
                    next(b)
                except StopIteration:
                    alive_b = False


def stage_rwkv(k, g, TP, src, dst):
    DVE, ACT, POOL = k.DVE, k.ACT, k.DVE
    DK = float(np.exp(-0.5))
    NT1 = TP // 128 + 2
    opnd = k.dram("rw_opnd", [NT1, 7, 128, 1024], BF16, "Internal")
    wcd = k.dram("rw_wc", [NT1, 128, 2, KC], F32, "Internal")
    with contextlib.ExitStack() as st:
        Wr = k.sb("Wr", [128, KC, D], BF16, st); Wk = k.sb("Wk", [128, KC, D], BF16, st); Wv = k.sb("Wv", [128, KC, D], BF16, st)
        for i, W in enumerate((Wr, Wk, Wv)):
            load_weight(k, g, W, g.w_rkv.t[i], D, D)
        w1 = k.sb("w1", [128, KC, 64], BF16, st); a1 = k.sb("a1", [128, KC, 64], BF16, st); g1 = k.sb("g1", [128, KC, 128], BF16, st)
        w2 = k.sb("w2", [128, 1, D], BF16, st); a2 = k.sb("a2", [128, 1, D], BF16, st); g2 = k.sb("g2", [128, 1, D], BF16, st)
        load_weight(k, g, w1, g.w1.t, D, 64); load_weight(k, g, a1, g.a1.t, D, 64); load_weight(k, g, g1, g.g1.t, D, 128)
        load_weight(k, g, w2, g.w2.t, 64, D); load_weight(k, g, a2, g.a2.t, 64, D); load_weight(k, g, g2, g.g2.t, 128, D)
        mu = k.sb("mu", [128, 6, KC], F32, st)
        for p in range(6):
            k.dma(k.SP, mu[:, p, :], g.mu.t[p].rearrange("(c p) -> p c", p=128), writes=[mu], allow_slow_non_contiguous=True)
        w0c = load_cols(k, st, "w0c", g.w0.t, 8); a0c = load_cols(k, st, "a0c", g.a0.t, 8); kkc = load_cols(k, st, "kkc", g.k_k.t, 8)
        kac = load_cols(k, st, "kac", g.k_a.t, 8); rkc = load_cols(k, st, "rkc", g.r_k.t, 8)
        ob32 = k.sb("ob32", [128, 128], F32, st); onesbd = k.sb("onesbd", [128, 128], BF16, st)
        k.dma(k.SP, ob32[:], g.c_onesbd.t[:, :], writes=[ob32])
        k.copy(onesbd[:], ob32[:], [ob32], [onesbd], eng=DVE)
        onesf = k.sb("onesf", [128, 64], F32, st)
        k.op(DVE, lambda e: e.memset(onesf[:], 1.0), writes=[onesf])
        xprev = k.sb("xprev", [128, KC], F32, st)
        TT = 128
        toks = [k.sb("tok1", [128, 1, D], F32, st) for _ in range(2)]
        xT32 = k.sb("xT32", [128, KC, 1 + TT], F32, st); dd = k.sb("dd", [128, KC, TT], F32, st)
        xms = [k.sb("xm", [128, KC, TT], BF16, st) for _ in range(2)]
        F = lambda n: k.sb(n, [128, KC, TT], F32, st)
        iface = [[F(n + str(par)) for n in ("rT", "kT", "vT", "sg", "ic")] for par in range(2)]
        Lc, Ep, Em, Ea, kk, kf, tm, Lm = [F(n) for n in ("Lc", "Ep", "Em", "Ea", "kk", "kf", "tm", "Lm")]
        kkn = kk
        B = lambda n: k.sb(n, [128, KC, TT], BF16, st)
        kk2, rkb = B("kk2"), B("rkb")
        _o = [B(f"o{j}") for j in range(7)]
        _g1 = B("o5b")
        outs = [_o, _o[:5] + [_g1] + _o[6:]]
        th = k.sb("th", [128, TT], BF16, st)
        wcs = [k.sb("wcs", [128, 2, KC], F32, st) for _ in range(2)]
        p1segs = segs(TP, TT)

        def front(ti):
            row0, ntok, seq = p1segs[ti]
            PT = ntok
            k.rotate()
            rT, kT, vT, sg, ic = iface[ti % 2]
            At, Rt, Kt, Bt, Vb, Gb, Bon = outs[ti % 2]
            if ti == 0 or p1segs[ti - 1][2] != seq:
                if seq == 0:
                    k.op(DVE, lambda e: e.memset(xprev[:], 0.0), writes=[xprev])
                else:
                    k.dma(k.SP, xprev[:], g.st_shift.t[seq - 1].rearrange("(c p) -> p c", p=128), writes=[xprev], allow_slow_non_contiguous=True)
            x32 = load_tok(k, g, src, row0, PT, 1, toks)
            k.op(DVE, lambda e: e.tensor_copy(out=xT32[:, :, 0], in_=xprev[:, :]), reads=[xprev], writes=[xT32])
            to_featmajor(k, g, x32, PT, 1, xT32, col0=1)
            k.op(DVE, lambda e: e.tensor_copy(out=xprev[:, :], in_=xT32[:, :, ntok]), reads=[xT32], writes=[xprev])
            k.op(DVE, lambda e: e.tensor_tensor(out=dd[:, :, 0:ntok], in0=xT32[:, :, 0:ntok], in1=xT32[:, :, 1:1 + ntok], op=ALU.subtract),
                 reads=[xT32], writes=[dd])

            def mix(p):
                xm = xms[p % 2]
                for c in range(KC):
                    k.op(DVE, lambda e: e.scalar_tensor_tensor(out=xm[:, c, 0:ntok], in0=dd[:, c, 0:ntok], scalar=mu[:, p, c:c + 1],
                                                               in1=xT32[:, c, 1:1 + ntok], op0=ALU.mult, op1=ALU.add), reads=[dd, mu, xT32], writes=[xm])
                return xm

            def proj_full(W, xm, dstb):
                for ec in range(KC):
                    ps = k.psn()
                    for kc in range(KC):
                        k.mm(ps[:, 0:ntok], W[:, kc, ec * 128:(ec + 1) * 128], xm[:, kc, 0:ntok], kc == 0, kc == KC - 1, reads=[W, xm], writes=[ps])
                    k.copy(dstb[:, ec, 0:ntok], ps[:, 0:ntok], [ps], [dstb], eng=ACT)

            def lora(xm, wA, nA, wB, func1, emit2):
                ps = k.psn()
                for kc in range(KC):
                    k.mm(ps[0:nA, 0:ntok], wA[:, kc, :], xm[:, kc, 0:ntok], kc == 0, kc == KC - 1, reads=[wA, xm], writes=[ps])
                k.op(ACT, lambda e: e.activation(out=th[0:nA, 0:ntok], in_=ps[0:nA, 0:ntok], func=func1), reads=[ps], writes=[th])
                for c in range(KC):
                    ps2 = k.psn()
                    k.mm(ps2[:, 0:ntok], wB[0:nA, 0, c * 128:(c + 1) * 128], th[0:nA, 0:ntok], True, True, reads=[wB, th], writes=[ps2])
                    emit2(c, ps2)

            proj_full(Wr, mix(0), rT); proj_full(Wk, mix(1), kT); proj_full(Wv, mix(2), vT)
            lora(mix(3), w1, 64, w2, AF.Tanh, lambda c, ps2: k.op(ACT, lambda e: e.activation(
                out=sg[:, c, 0:ntok], in_=ps2[:, 0:ntok], func=AF.Sigmoid, bias=w0c[:, c:c + 1]), reads=[ps2, w0c], writes=[sg]))
            lora(mix(4), a1, 64, a2, AF.Copy, lambda c, ps2: k.op(ACT, lambda e: e.activation(
                out=ic[:, c, 0:ntok], in_=ps2[:, 0:ntok], func=AF.Sigmoid, bias=a0c[:, c:c + 1]), reads=[ps2, a0c], writes=[ic]))
            lora(mix(5), g1, 128, g2, AF.Sigmoid, lambda c, ps2: k.copy(Gb[:, c, 0:ntok], ps2[:, 0:ntok], [ps2], [Gb], eng=ACT))

        def back(ti):
            row0, ntok, seq = p1segs[ti]
            C = min(64, ntok); nch = ntok // C
            chunk0 = row0 // 64 if seq == 0 else TP // 64 + (seq - 1)
            rT, kT, vT, sg, ic = iface[ti % 2]
            At, Rt, Kt, Bt, Vb, Gb, Bon = outs[ti % 2]
            v3 = lambda ps: ps[:, :].rearrange("p (c t) -> p c t", t=128)[:, :, 0:ntok]
            for c in range(KC):
                for n in range(nch):
                    k.op(DVE, lambda e: e.tensor_tensor_scan(out=Lc[:, c, n * C:(n + 1) * C], data0=onesf[:, 0:C], data1=sg[:, c, n * C:(n + 1) * C],
                                                             initial=0.0, op0=ALU.mult, op1=ALU.add), reads=[onesf, sg], writes=[Lc])
            k.op(POOL, lambda e: e.tensor_tensor(out=Lm[:, :, 0:ntok], in0=Lc[:, :, 0:ntok], in1=sg[:, :, 0:ntok], op=ALU.subtract), reads=[Lc, sg], writes=[Lm])
            k.op(ACT, lambda e: e.activation(out=Ep[:, :, 0:ntok], in_=Lc[:, :, 0:ntok], func=AF.Exp, scale=-DK), reads=[Lc], writes=[Ep])
            k.op(ACT, lambda e: e.activation(out=Em[:, :, 0:ntok], in_=Lc[:, :, 0:ntok], func=AF.Exp, scale=DK), reads=[Lc], writes=[Em])
            k.op(ACT, lambda e: e.activation(out=Ea[:, :, 0:ntok], in_=Lm[:, :, 0:ntok], func=AF.Exp, scale=-DK), reads=[Lm], writes=[Ea])
            for c in range(KC):
                k.op(DVE, lambda e: e.tensor_scalar(out=kk[:, c, 0:ntok], in0=kT[:, c, 0:ntok], scalar1=kkc[:, c:c + 1], scalar2=None, op0=ALU.mult),
                     reads=[kT, kkc], writes=[kk])
            k.op(ACT, lambda e: e.activation(out=kk2[:, :, 0:ntok], in_=kk[:, :, 0:ntok], func=AF.Square), reads=[kk], writes=[kk2])
            ps = k.psn()
            if ntok == 128:
                for hf in range(2):
                    k.mm(ps[:, hf * 512:(hf + 1) * 512], onesbd[:, :], kk2[:, 4 * hf:4 * hf + 4, :].rearrange("p c t -> p (c t)"), True, True,
                         reads=[onesbd, kk2], writes=[ps])
            else:
                for c in range(KC):
                    k.mm(ps[:, c * 128:c * 128 + ntok], onesbd[:, :], kk2[:, c, 0:ntok], True, True, reads=[onesbd, kk2], writes=[ps])
            k.op(DVE, lambda e: e.tensor_scalar(out=tm[:, :, 0:ntok], in0=v3(ps), scalar1=1e-24, scalar2=None, op0=ALU.max), reads=[ps], writes=[tm])
            k.op(ACT, lambda e: e.activation(out=tm[:, :, 0:ntok], in_=tm[:, :, 0:ntok], func=AF.Ln), reads=[tm], writes=[tm])
            k.op(ACT, lambda e: e.activation(out=tm[:, :, 0:ntok], in_=tm[:, :, 0:ntok], func=AF.Exp, scale=-0.5), reads=[tm], writes=[tm])
            k.op(POOL, lambda e: e.tensor_tensor(out=kkn[:, :, 0:ntok], in0=kk[:, :, 0:ntok], in1=tm[:, :, 0:ntok], op=ALU.mult), reads=[kk, tm], writes=[kkn])
            for c in range(KC):
                k.op(DVE, lambda e: e.tensor_scalar(out=tm[:, c, 0:ntok], in0=ic[:, c, 0:ntok], scalar1=-1.0, scalar2=kac[:, c:c + 1],
                                                    op0=ALU.add, op1=ALU.mult), reads=[ic, kac], writes=[tm])
            k.op(DVE, lambda e: e.scalar_tensor_tensor(out=kf[:, :, 0:ntok], in0=tm[:, :, 0:ntok], scalar=1.0, in1=kT[:, :, 0:ntok],
                                                       op0=ALU.add, op1=ALU.mult), reads=[tm, kT], writes=[kf])
            for c in range(KC):
                k.op(DVE, lambda e: e.scalar_tensor_tensor(out=rkb[:, c, 0:ntok], in0=rT[:, c, 0:ntok], scalar=rkc[:, c:c + 1], in1=kf[:, c, 0:ntok],
                                                           op0=ALU.mult, op1=ALU.mult), reads=[rT, rkc, kf], writes=[rkb])
            ps = k.psn()
            if ntok == 128:
                for hf in range(2):
                    k.mm(ps[:, hf * 512:(hf + 1) * 512], onesbd[:, :], rkb[:, 4 * hf:4 * hf + 4, :].rearrange("p c t -> p (c t)"), True, True,
                         reads=[onesbd, rkb], writes=[ps])
            else:
                for c in range(KC):
                    k.mm(ps[:, c * 128:c * 128 + ntok], onesbd[:, :], rkb[:, c, 0:ntok], True, True, reads=[onesbd, rkb], writes=[ps])
            k.op(DVE, lambda e: e.tensor_tensor(out=Bon[:, :, 0:ntok], in0=v3(ps), in1=vT[:, :, 0:ntok], op=ALU.mult), reads=[ps, vT], writes=[Bon])
            k.op(DVE, lambda e: e.scalar_tensor_tensor(out=At[:, :, 0:ntok], in0=kkn[:, :, 0:ntok], scalar=-1.0, in1=Ea[:, :, 0:ntok],
                                                       op0=ALU.mult, op1=ALU.mult), reads=[kkn, Ea], writes=[At])
            k.op(POOL, lambda e: e.tensor_tensor(out=Rt[:, :, 0:ntok], in0=rT[:, :, 0:ntok], in1=Ep[:, :, 0:ntok], op=ALU.mult), reads=[rT, Ep], writes=[Rt])
            k.op(POOL, lambda e: e.tensor_tensor(out=Kt[:, :, 0:ntok], in0=kf[:, :, 0:ntok], in1=Em[:, :, 0:ntok], op=ALU.mult), reads=[kf, Em], writes=[Kt])
            k.op(POOL, lambda e: e.tensor_tensor(out=tm[:, :, 0:ntok], in0=kkn[:, :, 0:ntok], in1=ic[:, :, 0:ntok], op=ALU.mult), reads=[kkn, ic], writes=[tm])
            k.op(POOL, lambda e: e.tensor_tensor(out=Bt[:, :, 0:ntok], in0=tm[:, :, 0:ntok], in1=Em[:, :, 0:ntok], op=ALU.mult), reads=[tm, Em], writes=[Bt])
            k.copy(Vb[:, :, 0:ntok], vT[:, :, 0:ntok], [vT], [Vb], eng=ACT)
            for j, ob in enumerate(outs[ti % 2]):
                k.dma(k.POOL, opnd.t[ti, j].rearrange("p (c t) -> p c t", t=128)[:, :, 0:ntok], ob[:, :, 0:ntok], reads=[ob], writes=[opnd])
            wcb = wcs[ti % 2]
            for n in range(nch):
                k.op(DVE, lambda e: e.tensor_copy(out=wcb[:, n, :], in_=Ep[:, :, (n + 1) * C - 1]), reads=[Ep], writes=[wcb])
            k.dma(k.POOL, wcd.t[ti][:, 0:nch, :], wcb[:, 0:nch, :], reads=[wcb], writes=[wcd])

        front(0)
        for ti in range(len(p1segs)):
            if ti + 1 < len(p1segs):
                front(ti + 1)
            back(ti)
        k.barrier()
    P2E = DVE
    with contextlib.ExitStack() as st:
        Wout = k.sb("Wout", [128, KC, D], BF16, st)
        load_ln_tabs(k, g, 1, 1)
        load_weight(k, g, Wout, g.w_out1.t, D, D)
        gngc = load_cols(k, st, "gngc", g.gn_g.t, 8); gnbc = load_cols(k, st, "gnbc", g.gn_b.t, 8)
        ob32 = k.sb("ob32", [128, 128], F32, st); onesbd64 = k.sb("onesbd64", [128, 128], BF16, st)
        k.dma(k.SP, ob32[:], g.c_onesbd.t[:, :], writes=[ob32])
        k.op(ACT, lambda e: e.activation(out=onesbd64[:], in_=ob32[:], func=AF.Copy, scale=1.0 / 64.0), reads=[ob32], writes=[onesbd64])
        epsg = k.sb("epsg", [128, 1], F32, st)
        k.op(DVE, lambda e: e.memset(epsg[:], 64e-5), writes=[epsg])
        msk = k.sb("msk", [128, 3, 128], F32, st)
        Hx32 = k.sb("Hx32", [128, KC, 128], F32, st); Hb = k.sb("Hb", [128, KC, 128], BF16, st)
        Sx32 = k.sb("Sx32", [128, KC, 128], F32, st)
        X = lambda n: k.sb(n, [128, KC, 128], BF16, st)
        sets = []
        for par in range(2):
            s_ = Ctx()
            s_.Ear = k.sb("Ear", [128, KC, 2, 128], BF16, st)
            s_.Eb, s_.Ek, s_.Ev = X("Eb"), X("Ek"), X("Ev")
            s_.Gb = k.sb("Gb", [128, KC, 64], BF16, st); s_.Bon = k.sb("Bon", [128, KC, 64], BF16, st)
            s_.wc = k.sb("wc", [128, KC], F32, st); s_.x32 = k.sb("x32", [128, 1, D], F32, st)
            s_.VsT, s_.EbT, s_.EkT, s_.ArbT, s_.AakT, s_.ArkT, s_.PTb = [X(n) for n in ("VsT", "EbT", "EkT", "ArbT", "AakT", "ArkT", "PTb")]
            sets.append(s_)
        Mb = [X("Mb0"), X("Mb1")]; MTb = [X("MTb0"), X("MTb1")]
        Xb, Ub = X("Xb"), X("Ub")
        oT = k.sb("oT", [128, KC, 64], F32, st); tm = k.sb("tm", [128, KC, 64], F32, st)
        obf = k.sb("obf", [128, KC, 64], BF16, st); osq = k.sb("osq", [128, KC, 64], BF16, st); yT = k.sb("yT", [128, KC, 64], BF16, st)

        def zero_all():
            for s_ in sets:
                for zb in (s_.Ear, s_.Eb, s_.Ek, s_.Ev, s_.VsT, s_.EbT, s_.EkT, s_.ArbT, s_.AakT, s_.ArkT, s_.PTb):
                    k.op(DVE, lambda e: e.memset(zb[:], 0.0), writes=[zb])
            for zb in (Xb, Ub, Mb[0], Mb[1], MTb[0], MTb[1], obf, osq):
                k.op(DVE, lambda e: e.memset(zb[:], 0.0), writes=[zb])

        def fe2(ch, S, C, row0):
            tl, nn = ch
            R = 2 * C
            vR = lambda ps: ps[0:R, :].rearrange("p (c t) -> p c t", t=128)[:, :, 0:R]
            k.rotate()
            for hp in range(2):
                rw = slice(64 * hp, 64 * hp + 64); cl = slice(hp * C, (hp + 1) * C)
                src3 = lambda j: opnd.t[tl, j, 64 * hp:64 * hp + 64, :].rearrange("p (c t) -> p c t", t=128)[:, :, nn * 64:nn * 64 + C]
                k.dma(k.SP, S.Ear[rw, :, 0, cl], src3(0), reads=[opnd], writes=[S.Ear])
                k.dma(k.SP, S.Ear[rw, :, 1, cl], src3(1), reads=[opnd], writes=[S.Ear])
                k.dma(k.SP, S.Ek[rw, :, cl], src3(2), reads=[opnd], writes=[S.Ek])
                k.dma(k.SP, S.Eb[rw, :, cl], src3(3), reads=[opnd], writes=[S.Eb])
                k.dma(k.SP, S.Ev[rw, :, cl], src3(4), reads=[opnd], writes=[S.Ev])
            k.dma(k.SP, S.Gb[:, :, 0:C], opnd.t[tl, 5].rearrange("p (c t) -> p c t", t=128)[:, :, nn * 64:nn * 64 + C], reads=[opnd], writes=[S.Gb])
            k.dma(k.SP, S.Bon[:, :, 0:C], opnd.t[tl, 6].rearrange("p (c t) -> p c t", t=128)[:, :, nn * 64:nn * 64 + C], reads=[opnd], writes=[S.Bon])
            k.dma(k.SP, S.wc[:], wcd.t[tl][:, nn, :], reads=[wcd], writes=[S.wc])
            k.dma(k.SP, S.x32[0:C, 0, :], src.t[row0:row0 + C, :], reads=[src], writes=[S.x32])
            yield
            for (srcb, dstb) in ((S.Ev, S.VsT), (S.Eb, S.EbT), (S.Ek, S.EkT)):
                ps = k.psn()
                psb = ps.t[:, 0:512].bitcast(BF16)
                for c in range(KC):
                    k.tr(psb[0:R, c * 128:(c + 1) * 128], srcb[:, c, 0:R], g.identb[:, :], reads=[srcb, g.identb], writes=[ps])
                k.copy(dstb[0:R, :, :], psb[0:R, :].rearrange("p (c t) -> p c t", t=128), [ps], [dstb])
                yield
            mb = lambda j: msk[0:R, j, 0:R].unsqueeze(1).broadcast_to([R, KC, R])
            ea = lambda c: S.Ear[:, c, 0, 0:R]; er = lambda c: S.Ear[:, c, 1, 0:R]
            eb = lambda c: S.Eb[:, c, 0:R]; ek = lambda c: S.Ek[:, c, 0:R]
            for (lhs_sel, rhs_sel, dstb, mj) in ((ea, eb, Mb[0], 2), (eb, ea, MTb[0], 0), (eb, er, S.ArbT, 1), (ek, ea, S.AakT, 0), (ek, er, S.ArkT, 1)):
                ps = k.psn()
                for c in range(KC):
                    k.mm(ps[0:R, c * 128:c * 128 + R], lhs_sel(c), rhs_sel(c), True, True, reads=[S.Ear, S.Eb, S.Ek], writes=[ps])
                k.op(DVE, lambda e: e.tensor_tensor(out=dstb[0:R, :, 0:R], in0=vR(ps), in1=mb(mj), op=ALU.mult), reads=[ps, msk], writes=[dstb])
                yield
            k.op(P2E, lambda e: e.tensor_tensor(out=S.PTb[0:R, :, 0:R], in0=MTb[0][0:R, :, 0:R],
                                                 in1=g.identb[0:R, 0:R].unsqueeze(1).broadcast_to([R, KC, R]), op=ALU.add), reads=[MTb[0], g.identb], writes=[S.PTb])
            nlev = int(np.log2(C)) - 1
            cur = 0
            for j in range(1, nlev + 1):
                nx = 1 - cur
                ps = k.psn()
                for c in range(KC):
                    k.mm(ps[0:R, c * 128:c * 128 + R], MTb[cur][:, c, 0:R], Mb[cur][:, c, 0:R], True, True, reads=[MTb[cur], Mb[cur]], writes=[ps])
                k.copy(Mb[nx][0:R, :, 0:R], vR(ps), [ps], [Mb[nx]], eng=ACT)
                yield
                if j < nlev:
                    ps = k.psn()
                    for c in range(KC):
                        k.mm(ps[0:R, c * 128:c * 128 + R], Mb[cur][:, c, 0:R], MTb[cur][:, c, 0:R], True, True, reads=[MTb[cur], Mb[cur]], writes=[ps])
                    k.copy(MTb[nx][0:R, :, 0:R], vR(ps), [ps], [MTb[nx]], eng=ACT)
                    yield
                ps = k.psn()
                for c in range(KC):
                    k.mm(ps[0:R, c * 128:c * 128 + R], g.identb[:, 0:R], S.PTb[:, c, 0:R], True, False, reads=[g.identb, S.PTb], writes=[ps])
                    k.mm(ps[0:R, c * 128:c * 128 + R], Mb[nx][:, c, 0:R], S.PTb[:, c, 0:R], False, True, reads=[Mb[nx], S.PTb], writes=[ps])
                k.copy(S.PTb[0:R, :, 0:R], vR(ps), [ps], [S.PTb], eng=DVE)
                cur = nx
                yield

        def be2(ch, S, C, row0, seq, last):
            R = 2 * C; ntok = C
            ea = lambda c: S.Ear[:, c, 0, 0:R]; er = lambda c: S.Ear[:, c, 1, 0:R]
            v3 = lambda ps, w: ps[:, 0:512].rearrange("p (c t) -> p c t", t=64)[:, :, 0:w]
            v3b = lambda ps, w: ps[:, 512:1024].rearrange("p (c t) -> p c t", t=64)[:, :, 0:w]
            ps = k.psn()
            for c in range(KC):
                k.mm(ps[0:R, c * 128:(c + 1) * 128], ea(c), Hb[:, c, :], True, False, reads=[S.Ear, Hb], writes=[ps])
                k.mm(ps[0:R, c * 128:(c + 1) * 128], S.AakT[:, c, 0:R], S.VsT[:, c, :], False, True, reads=[S.AakT, S.VsT], writes=[ps])
            k.copy(Xb[0:R, :, :], ps[0:R, :].rearrange("p (c t) -> p c t", t=128), [ps], [Xb], eng=ACT)
            yield
            ps = k.psn()
            for c in range(KC):
                k.mm(ps[0:R, c * 128:(c + 1) * 128], S.PTb[:, c, 0:R], Xb[:, c, :], True, True, reads=[S.PTb, Xb], writes=[ps])
            k.copy(Ub[0:R, :, :], ps[0:R, :].rearrange("p (c t) -> p c t", t=128), [ps], [Ub], eng=ACT)
            yield
            ps = k.psn()
            for c in range(KC):
                k.mm(ps[:, c * 128:c * 128 + R], Hb[:, c, :], er(c), True, False, reads=[Hb, S.Ear], writes=[ps])
                k.mm(ps[:, c * 128:c * 128 + R], Ub[:, c, :], S.ArbT[:, c, 0:R], False, False, reads=[Ub, S.ArbT], writes=[ps])
                k.mm(ps[:, c * 128:c * 128 + R], S.VsT[:, c, :], S.ArkT[:, c, 0:R], False, True, reads=[S.VsT, S.ArkT], writes=[ps])
            psO = ps
            ps = k.psn()
            for c in range(KC):
                k.mm(ps[:, c * 128:(c + 1) * 128], S.EbT[:, c, :], Ub[:, c, :], True, False, reads=[S.EbT, Ub], writes=[ps])
                k.mm(ps[:, c * 128:(c + 1) * 128], S.EkT[:, c, :], S.VsT[:, c, :], False, True, reads=[S.EkT, S.VsT], writes=[ps])
            k.op(DVE, lambda e: e.tensor_tensor(out=Hx32[:], in0=ps[:, :].rearrange("p (c t) -> p c t", t=128), in1=Hx32[:], op=ALU.add), reads=[ps, Hx32], writes=[Hx32])
            k.op(P2E, lambda e: e.tensor_tensor(out=Hx32[:], in0=Hx32[:], in1=S.wc[:, :].unsqueeze(2).broadcast_to([128, KC, 128]), op=ALU.mult),
                 reads=[Hx32, S.wc], writes=[Hx32])
            k.copy(Hb[:], Hx32[:], [Hx32], [Hb], eng=ACT)
            yield
            for hp in range(2):
                rw = slice(64 * hp, 64 * hp + 64)
                k.copy(oT[rw, :, 0:ntok], psO[rw, :].rearrange("p (c t) -> p c t", t=128)[:, :, hp * C:(hp + 1) * C], [psO], [oT],
                       eng=(ACT if hp == 0 else DVE))
            yield
            k.op(ACT, lambda e: e.activation(out=obf[:, :, 0:ntok], in_=oT[:, :, 0:ntok], func=AF.Copy), reads=[oT], writes=[obf])
            k.op(ACT, lambda e: e.activation(out=osq[:, :, 0:ntok], in_=oT[:, :, 0:ntok], func=AF.Square), reads=[oT], writes=[osq])
            ps = k.psn()
            k.mm(ps[:, 0:512], onesbd64[:, :], obf[:, :, :].rearrange("p c t -> p (c t)"), True, True, reads=[onesbd64, obf], writes=[ps])
            k.mm(ps[:, 512:1024], onesbd64[:, :], osq[:, :, :].rearrange("p c t -> p (c t)"), True, True, reads=[onesbd64, osq], writes=[ps])
            yield
            k.op(ACT, lambda e: e.activation(out=tm[:, :, 0:ntok], in_=v3(ps, ntok), func=AF.Square), reads=[ps], writes=[tm])
            k.op(DVE, lambda e: e.tensor_tensor(out=tm[:, :, 0:ntok], in0=v3b(ps, ntok), in1=tm[:, :, 0:ntok], op=ALU.subtract), reads=[ps, tm], writes=[tm])
            k.op(ACT, lambda e: e.activation(out=tm[:, :, 0:ntok], in_=tm[:, :, 0:ntok], func=AF.Ln, bias=epsg[:, 0:1]), reads=[tm, epsg], writes=[tm])
            k.op(ACT, lambda e: e.activation(out=tm[:, :, 0:ntok], in_=tm[:, :, 0:ntok], func=AF.Exp, scale=-0.5), reads=[tm], writes=[tm])
            k.op(DVE, lambda e: e.tensor_tensor(out=oT[:, :, 0:ntok], in0=oT[:, :, 0:ntok], in1=v3(ps, ntok), op=ALU.subtract), reads=[oT, ps], writes=[oT])
            yield
            k.op(P2E, lambda e: e.tensor_tensor(out=oT[:, :, 0:ntok], in0=oT[:, :, 0:ntok], in1=tm[:, :, 0:ntok], op=ALU.mult), reads=[oT, tm], writes=[oT])
            k.op(P2E, lambda e: e.tensor_tensor(out=oT[:, :, 0:ntok], in0=oT[:, :, 0:ntok], in1=gngc[:, :].unsqueeze(2).broadcast_to([128, KC, ntok]), op=ALU.mult),
                 reads=[oT, gngc], writes=[oT])
            k.op(P2E, lambda e: e.tensor_tensor(out=oT[:, :, 0:ntok], in0=oT[:, :, 0:ntok], in1=gnbc[:, :].unsqueeze(2).broadcast_to([128, KC, ntok]), op=ALU.add),
                 reads=[oT, gnbc], writes=[oT])
            k.op(P2E, lambda e: e.tensor_tensor(out=oT[:, :, 0:ntok], in0=oT[:, :, 0:ntok], in1=S.Bon[:, :, 0:ntok], op=ALU.add), reads=[oT, S.Bon], writes=[oT])
            k.op(P2E, lambda e: e.tensor_tensor(out=yT[:, :, 0:ntok], in0=oT[:, :, 0:ntok], in1=S.Gb[:, :, 0:ntok], op=ALU.mult), reads=[oT, S.Gb], writes=[yT])
            yield
            ps = k.psn()
            for hf in range(2):
                for c in range(KC):
                    k.mm(ps[0:ntok, hf * 512:(hf + 1) * 512], yT[:, c, 0:ntok], Wout[:, c, hf * 512:(hf + 1) * 512], c == 0, c == KC - 1,
                         reads=[yT, Wout], writes=[ps])
            yield
            ln_epilogue(k, g, ps, S.x32[0:ntok, 0, :], S.x32, ntok, dst, row0, ALPHA)
            if last:
                rl = row0 + ntok - 1
                k.dma(k.POOL, g.o_shift.t[seq:seq + 1, :], src.t[rl:rl + 1, :], reads=[src], writes=[g.o_shift])
                ps = k.psn()
                for c in range(KC):
                    k.tr(ps[:, c * 128:(c + 1) * 128], Hx32[:, c, :], g.ident[:, :], reads=[Hx32, g.ident], writes=[ps])
                k.copy(Sx32[:], ps[:, :].rearrange("p (c t) -> p c t", t=128), [ps], [Sx32], eng=DVE)
                for hp in range(2):
                    k.dma(k.POOL, g.o_wkv.t[seq].rearrange("(c hp) v kk -> hp v c kk", hp=2)[hp],
                          Sx32[64 * hp:64 * hp + 64, :, 64 * hp:64 * hp + 64], reads=[Sx32], writes=[g.o_wkv])
            yield

        for seq in range(3):
            ci = 0 if seq == 0 else 1
            C = 64 if seq == 0 else 32
            chunks = [((n // 2, n % 2), n * 64) for n in range(TP // 64)] if seq == 0 else [((TP // 128 + seq - 1, 0), TP + (seq - 1) * TS)]
            zero_all()
            k.dma(k.SP, msk[:], g.c_msk.t[ci], writes=[msk])
            k.op(DVE, lambda e: e.memset(Hx32[:], 0.0), writes=[Hx32])
            if seq > 0:
                k.op(DVE, lambda e: e.memset(Sx32[:], 0.0), writes=[Sx32])
                for hp in range(2):
                    k.dma(k.SP, Sx32[64 * hp:64 * hp + 64, :, 64 * hp:64 * hp + 64],
                          g.st_wkv.t[seq - 1].rearrange("(c hp) v kk -> hp v c kk", hp=2)[hp], writes=[Sx32])
                ps = k.psn()
                for c in range(KC):
                    k.tr(ps[:, c * 128:(c + 1) * 128], Sx32[:, c, :], g.ident[:, :], reads=[Sx32, g.ident], writes=[ps])
                k.copy(Hx32[:], ps[:, :].rearrange("p (c t) -> p c t", t=128), [ps], [Hx32], eng=DVE)
            k.copy(Hb[:], Hx32[:], [Hx32], [Hb], eng=ACT)
            for _ in fe2(chunks[0][0], sets[0], C, chunks[0][1]):
                pass
            for i, (ch, row0) in enumerate(chunks):
                nxt = fe2(chunks[i + 1][0], sets[(i + 1) % 2], C, chunks[i + 1][1]) if i + 1 < len(chunks) else None
                _interleave(nxt, be2(ch, sets[i % 2], C, row0, seq, i == len(chunks) - 1), ra=1000, rb=1)
        k.barrier()


def build(TP, nstage=8):
    nc = bass.Bass("TRN2", target_bir_lowering=False)
    k = KB(nc)
    g = declare_io(k, TP)
    setup_globals(k, g)
    stages = [
        lambda s, d: stage_ffn(k, g, TP, 0, 0, s, d),
        lambda s, d: stage_mixer_ab(k, g, TP, s, d),
        lambda s, d: stage_xattn(k, g, TP, 0, s, d),
        lambda s, d: stage_ffn(k, g, TP, 0, 1, s, d),
        lambda s, d: stage_ffn(k, g, TP, 1, 0, s, d),
        lambda s, d: stage_rwkv(k, g, TP, s, d),
        lambda s, d: stage_xattn(k, g, TP, 1, s, d),
        lambda s, d: stage_ffn(k, g, TP, 1, 1, s, d),
    ][:nstage]
    src = g.x
    for i, stf in enumerate(stages):
        dst = g.y if i == len(stages) - 1 else g.xs[i % 2]
        stf(src, dst)
        src = dst
    k.finish()
    return nc


def host_consts(TP):
    c = {}
    c["c_ident"] = np.eye(128, dtype=np.float32)
    half = 64
    inv_freq = (10000.0 ** (-np.arange(half, dtype=np.float32) / np.float32(half))).astype(np.float32)
    pos = np.concatenate([np.arange(TP), PAST + np.arange(TS)]).astype(np.float32)
    ang = (pos[:, None] * inv_freq[None, :]).astype(np.float32)
    cos = np.cos(ang.astype(np.float64)).astype(np.float32).T
    sin = np.sin(ang.astype(np.float64)).astype(np.float32).T
    c["c_cos"] = np.ascontiguousarray(np.concatenate([cos, cos], 0))
    c["c_sin"] = np.ascontiguousarray(np.concatenate([-sin, sin], 0))
    M = np.zeros((2, 128, 4, 128), np.float32); XI = np.zeros((2, 128, 4, 128), np.float32); Z = np.zeros((2, 128, 4, 128), np.float32)
    for ci, C in enumerate((128, 32)):
        idx = np.arange(C, dtype=np.float64)
        for h in range(4):
            lg = np.log1p(-(2.0 ** (-5.0 - h)))
            m = np.where(idx[None, :] >= idx[:, None], np.exp(-lg * C), 0.0)
            M[ci, :C, h, :C] = m
            XI[ci, :, h, :C] = np.exp(lg * (idx + 1.0))[None, :]
            Z[ci, :, h, :C] = (np.exp(lg * (C - 1.0 - idx)) * 128 ** -0.5)[None, :]
    c["c_retM"] = M; c["c_retXI"] = XI; c["c_retZ"] = Z
    msk = np.zeros((2, 128, 3, 128), np.float32)
    for ci, C in enumerate((64, 32)):
        for hp in range(2):
            for s in range(C):
                msk[ci, hp * C + s, 0, hp * C + s + 1: hp * C + C] = 1.0
                msk[ci, hp * C + s, 1, hp * C + s: hp * C + C] = 1.0
        msk[ci, :, 2, :] = msk[ci, :, 0, :].T
    c["c_msk"] = msk
    ob = np.zeros((128, 128), np.float32); ob[:64, :64] = 1.0; ob[64:, 64:] = 1.0
    c["c_onesbd"] = ob
    return c


_W_NAMES = ["ln_g", "ln_b", "ffn_up", "ffn_down", "xa_q", "xa_k", "xa_v", "xa_o", "l0_w_in", "l0_conv_w", "l0_conv_b",
            "l0_lru_wa", "l0_lru_ba", "l0_lru_wx", "l0_lru_bx", "l0_lru_lambda", "l0_ret_gn_g", "l0_ret_gn_b", "l0_w_out",
            "l1_mu", "l1_w_rkv", "l1_w0", "l1_w1", "l1_w2", "l1_a0", "l1_a1", "l1_a2", "l1_g1", "l1_g2", "l1_k_k", "l1_k_a",
            "l1_gn_g", "l1_gn_b", "l1_w_out"]


def make_in_maps(inp, TP):
    f = lambda a: np.ascontiguousarray(np.asarray(a, dtype=np.float32))
    shared = {n: f(inp[n]) for n in _W_NAMES}
    shared["l1_r_k"] = f(inp["l1_r_k"]).reshape(-1)
    w_in = f(inp["l0_w_in"])
    rot = []
    for base in (1024, 1536):
        for h in range(4):
            b0 = base + h * 128
            rot.append(w_in[:, b0 + 64: b0 + 128]); rot.append(w_in[:, b0: b0 + 64])
    shared["l0_w_rot"] = np.ascontiguousarray(np.concatenate(rot, axis=1))
    shared.update(host_consts(TP))
    maps = []
    for b in range(NCORES):
        m = dict(shared)
        m["x"] = np.ascontiguousarray(np.concatenate([f(inp["x_prompt"][b]), f(inp["x_sample"][2 * b]), f(inp["x_sample"][2 * b + 1])], 0))
        m["mem"] = f(inp["mem_prompt"][b])
        sl = slice(2 * b, 2 * b + 2)
        m["st_conv"] = f(inp["state_conv0"][sl]); m["st_lru"] = f(inp["state_lru0"][sl]); m["st_ret"] = f(inp["state_ret0"][sl])
        m["st_shift"] = f(inp["state_shift1"][sl]).reshape(2, D); m["st_wkv"] = f(inp["state_wkv1"][sl])
        m["ck"] = f(inp["cache_mem_k"][:, sl]).reshape(2, 2, 256, D); m["cv"] = f(inp["cache_mem_v"][:, sl]).reshape(2, 2, 256, D)
        maps.append(m)
    return maps


_NC_CACHE = {}


def run(inp, TP, nstage=8, ncores=NCORES):
    key = (TP, nstage)
    if key not in _NC_CACHE:
        _NC_CACHE[key] = build(TP, nstage)
    nc = _NC_CACHE[key]
    res = run_bass_kernel_spmd(nc, make_in_maps(inp, TP)[:ncores], core_ids=list(range(ncores)))
    R = list(res.results)
    while len(R) < NCORES:
        R.append(R[0])
    st = lambda n: np.stack([r[n] for r in R], 0)
    y = st("y")
    y_p = y[:, :TP]
    y_s = y[:, TP:].reshape(NCORES * 2, TS, D)
    memk = st("o_memk").transpose(1, 0, 2, 3).reshape(2, NCORES, 256, 4, 256)
    memv = st("o_memv").transpose(1, 0, 2, 3).reshape(2, NCORES, 256, 4, 256)
    oc, ol, orr, osh, ow = st("o_conv"), st("o_lru"), st("o_ret"), st("o_shift"), st("o_wkv")
    pf = lambda a: np.ascontiguousarray(a[:, 0])
    sf = lambda a: np.ascontiguousarray(a[:, 1:3].reshape((NCORES * 2,) + a.shape[2:]))
    return (np.ascontiguousarray(y_p), np.ascontiguousarray(y_s), np.ascontiguousarray(memk), np.ascontiguousarray(memv),
            pf(oc), pf(ol), pf(orr), pf(osh)[:, None, :], pf(ow),
            sf(oc), sf(ol), sf(orr), sf(osh)[:, None, :], sf(ow))


def kernel(**inputs):
    TP = int(np.asarray(inputs["x_prompt"]).shape[1])
    return run(inputs, TP, 8)
```

```python
import contextlib
import numpy as np
import concourse.bass as bass
import concourse.mybir as mybir
from concourse.bass_utils import run_bass_kernel_spmd

F32 = mybir.dt.float32
BF16 = mybir.dt.bfloat16
AF = mybir.ActivationFunctionType
ALU = mybir.AluOpType

D = 1024
KC = 8
NCORES = 8
TS = 32
DFF = 2816
ALPHA = 4.0 ** 0.25
LN_EPS = 1e-5
PAST = 4096
SAME_ENGINE_WAITS = True


class Eng:
    def __init__(self, name, eng, sem):
        self.name = name
        self.eng = eng
        self.sem = sem
        self.count = 0
        self.known = {}


class Buf:
    def __init__(self, t, name):
        self.t = t
        self.name = name
        self.w = {}
        self.r = {}

    def __getitem__(self, key):
        return self.t[key]


class KB:
    def __init__(self, nc, n_dsem=32):
        self.nc = nc
        self.stack = contextlib.ExitStack()
        mk = lambda n: self.stack.enter_context(nc.semaphore(n))
        self.PE = Eng("pe", nc.tensor, mk("s_pe"))
        self.DVE = Eng("dve", nc.vector, mk("s_dve"))
        self.ACT = Eng("act", nc.scalar, mk("s_act"))
        self.POOL = Eng("pool", nc.gpsimd, mk("s_pool"))
        self.SP = Eng("sp", nc.sync, mk("s_sp"))
        self.engs = [self.PE, self.DVE, self.ACT, self.POOL, self.SP]
        self.dpools = {q.name: [[mk(f"s_d{q.name}{i}"), 0] for i in range(n_dsem // 2)] for q in (self.SP, self.POOL)}
        self.dnexts = {q.name: 0 for q in (self.SP, self.POOL)}
        self.dsems = [s for p in self.dpools.values() for s in p]
        self.nid = 0
        self.pspool = []
        self.psnext = 0
        self.flip = 0

    def sb(self, name, shape, dtype, stack=None):
        self.nid += 1
        t = (stack or self.stack).enter_context(self.nc.sbuf_tensor(f"{name}_{self.nid}", list(shape), dtype))
        return Buf(t, name)

    def ps(self, name, shape, dtype):
        self.nid += 1
        t = self.stack.enter_context(self.nc.psum_tensor(f"{name}_{self.nid}", list(shape), dtype))
        return Buf(t, name)

    def dram(self, name, shape, dtype, kind):
        t = self.nc.dram_tensor(name, list(shape), dtype, kind=kind)
        return Buf(t.ap(), name)

    def psn(self):
        b = self.pspool[self.psnext]
        self.psnext = (self.psnext + 1) % len(self.pspool)
        return b

    def _wait(self, E, sem, val):
        if val <= 0:
            return
        key = id(sem)
        if E.known.get(key, 0) >= val:
            return
        E.eng.wait_ge(sem, val)
        E.known[key] = val

    def _deps(self, E, reads, writes, same_engine=True):
        for b in reads:
            for (sem, val) in list(b.w.values()):
                if (not same_engine) and sem is E.sem:
                    continue
                self._wait(E, sem, val)
        for b in writes:
            for (sem, val) in list(b.w.values()) + list(b.r.values()):
                if (not same_engine) and sem is E.sem:
                    continue
                self._wait(E, sem, val)

    def _mark(self, sem, val, reads, writes):
        for b in reads:
            b.r[id(sem)] = (sem, val)
        for b in writes:
            b.r = {}
            b.w[id(sem)] = (sem, val)

    def op(self, E, emit, reads=(), writes=(), same_engine=SAME_ENGINE_WAITS):
        self._deps(E, reads, writes, same_engine)
        ins = emit(E.eng)
        E.count += 1
        ins.then_inc(E.sem, 1)
        self._mark(E.sem, E.count, reads, writes)
        return ins

    def mm(self, out_ap, lhsT, rhs, start, stop, reads=(), writes=()):
        return self.op(self.PE, lambda e: e.matmul(out_ap, lhsT=lhsT, rhs=rhs, start=start, stop=stop),
                       reads=reads, writes=writes, same_engine=False)

    def tr(self, out_ap, in_ap, ident_ap, reads=(), writes=()):
        return self.op(self.PE, lambda e: e.transpose(out_ap, in_ap, ident_ap),
                       reads=reads, writes=writes, same_engine=False)

    def dma(self, Q, out_ap, in_ap, reads=(), writes=(), **kw):
        self._deps(Q, reads, writes)
        pool = self.dpools[Q.name]
        slot = pool[self.dnexts[Q.name]]
        self.dnexts[Q.name] = (self.dnexts[Q.name] + 1) % len(pool)
        sem, val = slot
        self._wait(Q, sem, val)
        ins = Q.eng.dma_start(out=out_ap, in_=in_ap, **kw)
        slot[1] = val + 16
        ins.then_inc(sem, 16)
        self._mark(sem, val + 16, reads, writes)
        return ins

    def barrier(self):
        for E in self.engs:
            for F in self.engs:
                if F is not E:
                    self._wait(E, F.sem, F.count)
            for (sem, val) in self.dsems:
                self._wait(E, sem, val)

    def rotate(self, limit=20000):
        if max(E.count for E in self.engs) < limit:
            return
        self.barrier()
        for E in self.engs:
            self.nid += 1
            E.sem = self.stack.enter_context(self.nc.semaphore(f"s_{E.name}_{self.nid}"))
            E.count = 0

    def finish(self):
        for (sem, val) in self.dsems:
            self._wait(self.SP, sem, val)
        for F in self.engs:
            if F is not self.SP:
                self._wait(self.SP, F.sem, F.count)

    def copy(self, out_ap, in_ap, reads, writes, eng=None):
        if eng is None:
            self.flip ^= 1
            eng = self.ACT if self.flip else self.DVE
        if eng is self.ACT:
            return self.op(self.ACT, lambda e: e.activation(out=out_ap, in_=in_ap, func=AF.Copy), reads=reads, writes=writes)
        return self.op(eng, lambda e: e.tensor_copy(out=out_ap, in_=in_ap), reads=reads, writes=writes)


class Ctx:
    pass


def declare_io(k, TP):
    NT = TP + 2 * TS
    g = Ctx()
    I = lambda n, s: k.dram(n, s, F32, "ExternalInput")
    O = lambda n, s: k.dram(n, s, F32, "ExternalOutput")
    g.x = I("x", [NT, D]); g.mem = I("mem", [256, D])
    g.st_conv = I("st_conv", [2, 3, 512]); g.st_lru = I("st_lru", [2, 512]); g.st_ret = I("st_ret", [2, 4, 128, 128])
    g.st_shift = I("st_shift", [2, D]); g.st_wkv = I("st_wkv", [2, 16, 64, 64])
    g.ck = I("ck", [2, 2, 256, D]); g.cv = I("cv", [2, 2, 256, D])
    g.ln_g = I("ln_g", [2, 4, D]); g.ln_b = I("ln_b", [2, 4, D])
    g.ffn_up = I("ffn_up", [2, 2, D, 2 * DFF]); g.ffn_down = I("ffn_down", [2, 2, DFF, D])
    g.xa_q = I("xa_q", [2, D, D]); g.xa_k = I("xa_k", [2, D, D]); g.xa_v = I("xa_v", [2, D, D]); g.xa_o = I("xa_o", [2, D, D])
    g.w_in = I("l0_w_in", [D, 3072]); g.w_rot = I("l0_w_rot", [D, 1024])
    g.conv_w = I("l0_conv_w", [4, 512]); g.conv_b = I("l0_conv_b", [512])
    g.lru_wa = I("l0_lru_wa", [8, 64, 64]); g.lru_ba = I("l0_lru_ba", [512])
    g.lru_wx = I("l0_lru_wx", [8, 64, 64]); g.lru_bx = I("l0_lru_bx", [512]); g.lru_lam = I("l0_lru_lambda", [512])
    g.ret_g = I("l0_ret_gn_g", [512]); g.ret_b = I("l0_ret_gn_b", [512]); g.w_out0 = I("l0_w_out", [D, D])
    g.mu = I("l1_mu", [6, D]); g.w_rkv = I("l1_w_rkv", [3, D, D]); g.w0 = I("l1_w0", [D]); g.w1 = I("l1_w1", [D, 64]); g.w2 = I("l1_w2", [64, D])
    g.a0 = I("l1_a0", [D]); g.a1 = I("l1_a1", [D, 64]); g.a2 = I("l1_a2", [64, D]); g.g1 = I("l1_g1", [D, 128]); g.g2 = I("l1_g2", [128, D])
    g.k_k = I("l1_k_k", [D]); g.k_a = I("l1_k_a", [D]); g.r_k = I("l1_r_k", [D]); g.gn_g = I("l1_gn_g", [D]); g.gn_b = I("l1_gn_b", [D])
    g.w_out1 = I("l1_w_out", [D, D])
    g.c_ident = I("c_ident", [128, 128]); g.c_cos = I("c_cos", [128, TP + TS]); g.c_sin = I("c_sin", [128, TP + TS])
    g.c_retM = I("c_retM", [2, 128, 4, 128]); g.c_retXI = I("c_retXI", [2, 128, 4, 128]); g.c_retZ = I("c_retZ", [2, 128, 4, 128])
    g.c_msk = I("c_msk", [2, 128, 3, 128]); g.c_onesbd = I("c_onesbd", [128, 128])
    g.y = O("y", [NT, D]); g.o_memk = O("o_memk", [2, 256, D]); g.o_memv = O("o_memv", [2, 256, D])
    g.o_conv = O("o_conv", [3, 3, 512]); g.o_lru = O("o_lru", [3, 512]); g.o_ret = O("o_ret", [3, 4, 128, 128])
    g.o_shift = O("o_shift", [3, D]); g.o_wkv = O("o_wkv", [3, 16, 64, 64])
    g.xs = [k.dram("scr0", [NT, D], F32, "Internal"), k.dram("scr1", [NT, D], F32, "Internal")]
    return g


def load_cols(k, st, name, src_ap, ncol, rows=128):
    b = k.sb(name, [rows, ncol], F32, st)
    k.dma(k.SP, b[:], src_ap.rearrange("(c p) -> p c", p=rows), writes=[b], allow_slow_non_contiguous=True)
    return b


def load_weight(k, g, dst, src2d, K, N, kc0=0):
    nk = max(1, K // 128)
    rows = min(K, 128)
    for kc in range(nk):
        for n0 in range(0, N, 704):
            n1 = min(N, n0 + 704)
            stg = g.stage[g.stage_i % len(g.stage)]
            g.stage_i += 1
            k.dma(k.SP, stg[0:rows, 0:n1 - n0], src2d[kc * 128: kc * 128 + rows, n0:n1], writes=[stg])
            k.copy(dst[0:rows, kc0 + kc, n0:n1], stg[0:rows, 0:n1 - n0], [stg], [dst])


def load_tok(k, g, src, row0, PT, NS, pool):
    b = pool[g.tok_i % len(pool)]
    g.tok_i += 1
    k.dma(k.SP, b[0:PT, 0:NS, :], src.t[row0: row0 + PT * NS, :].rearrange("(s p) d -> p s d", p=PT), writes=[b])
    return b


def to_featmajor(k, g, x32, PT, NS, xT, col0=0):
    for s in range(NS):
        ps = k.psn()
        for c in range(KC):
            k.tr(ps[:, c * PT:(c + 1) * PT], x32[0:PT, s, c * 128:(c + 1) * 128], g.ident[0:PT, 0:PT],
                 reads=[x32, g.ident], writes=[ps])
        k.copy(xT[:, 0:KC, col0 + s * PT: col0 + (s + 1) * PT],
               ps[:, 0:KC * PT].rearrange("p (c t) -> p c t", t=PT), [ps], [xT])


def ln_epilogue(k, g, ps, base_ap, base_buf, PT, dst, row0, alpha, do_ln=True):
    y = g.ybuf[g.y_i % 2]; o = g.obuf[g.y_i % 2]; sm = g.small[g.y_i % 2]
    g.y_i += 1
    DVE, ACT = k.DVE, k.ACT
    k.op(DVE, lambda e: e.scalar_tensor_tensor(out=y[0:PT, :], in0=base_ap, scalar=float(alpha), in1=ps[0:PT, :],
                                               op0=ALU.mult, op1=ALU.add), reads=[base_buf, ps], writes=[y])
    if not do_ln:
        k.dma(k.POOL, dst.t[row0:row0 + PT, :], y[0:PT, :], reads=[y], writes=[dst])
        return
    k.op(DVE, lambda e: e.bn_stats(out=sm[0:PT, 0:6], in_=y[0:PT, 0:512]), reads=[y], writes=[sm])
    k.op(DVE, lambda e: e.bn_stats(out=sm[0:PT, 6:12], in_=y[0:PT, 512:1024]), reads=[y], writes=[sm])
    k.op(DVE, lambda e: e.bn_aggr(out=sm[0:PT, 12:14], in_=sm[0:PT, 0:12]), reads=[sm], writes=[sm])
    k.op(ACT, lambda e: e.activation(out=sm[0:PT, 14:15], in_=sm[0:PT, 13:14], func=AF.Ln, bias=g.eps_ln[0:PT, 0:1]),
         reads=[sm, g.eps_ln], writes=[sm])
    k.op(ACT, lambda e: e.activation(out=sm[0:PT, 14:15], in_=sm[0:PT, 14:15], func=AF.Exp, scale=-0.5), reads=[sm], writes=[sm])
    k.op(DVE, lambda e: e.tensor_scalar(out=sm[0:PT, 15:16], in0=sm[0:PT, 12:13], scalar1=-1.0, scalar2=sm[0:PT, 14:15],
                                        op0=ALU.mult, op1=ALU.mult), reads=[sm], writes=[sm])
    k.op(ACT, lambda e: e.activation(out=o[0:PT, :], in_=y[0:PT, :], func=AF.Identity, scale=sm[0:PT, 14:15],
                                     bias=sm[0:PT, 15:16]), reads=[y, sm], writes=[o])
    k.op(DVE, lambda e: e.tensor_tensor(out=o[0:PT, :], in0=o[0:PT, :], in1=g.gtab[0:PT, :], op=ALU.mult),
         reads=[o, g.gtab], writes=[o])
    k.op(DVE, lambda e: e.tensor_tensor(out=o[0:PT, :], in0=o[0:PT, :], in1=g.btab[0:PT, :], op=ALU.add),
         reads=[o, g.btab], writes=[o])
    k.dma(k.POOL, dst.t[row0:row0 + PT, :], o[0:PT, :], reads=[o], writes=[dst])


def load_ln_tabs(k, g, l, j):
    k.dma(k.SP, g.gtab[:], g.ln_g.t[l, j].partition_broadcast(128), writes=[g.gtab])
    k.dma(k.SP, g.btab[:], g.ln_b.t[l, j].partition_broadcast(128), writes=[g.btab])


def segs(TP, tile):
    out = [(r, tile, 0) for r in range(0, TP, tile)]
    out += [(TP, TS, 1), (TP + TS, TS, 2)]
    return out


def stage_ffn(k, g, TP, l, which, src, dst):
    TT = 256
    with contextlib.ExitStack() as st:
        Wg = k.sb("Wup", [128, KC, 2 * DFF], BF16, st)
        Wd = k.sb("Wd", [128, 22, D], BF16, st)
        load_ln_tabs(k, g, l, 0 if which == 0 else 3)
        load_weight(k, g, Wg, g.ffn_up.t[l, which], D, 2 * DFF)
        load_weight(k, g, Wd, g.ffn_down.t[l, which], DFF, D)
        xTs = [k.sb("xT", [128, KC, TT], BF16, st) for _ in range(2)]
        hT = k.sb("hT", [128, 22, TT], BF16, st)
        sgs = [k.sb("sg", [128, TT], F32, st) for _ in range(2)]
        toks = [k.sb("tok2", [128, 2, D], F32, st) for _ in range(2)]
        ffn_tiles = [(r, TT, 0) for r in range(0, TP, TT)] + [(TP, 2 * TS, 1)]
        for ti, (row0, ntok, seq) in enumerate(ffn_tiles):
            PT = min(128, ntok); NS = ntok // PT
            k.rotate()
            x32 = load_tok(k, g, src, row0, PT, NS, toks)
            xT = xTs[ti % 2]
            to_featmajor(k, g, x32, PT, NS, xT)
            for fc in range(22):
                ps = k.psn()
                for kc in range(KC):
                    k.mm(ps[:, 0:ntok], Wg[:, kc, fc * 128:(fc + 1) * 128], xT[:, kc, 0:ntok], kc == 0, kc == KC - 1,
                         reads=[Wg, xT], writes=[ps])
                for kc in range(KC):
                    k.mm(ps[:, 512:512 + ntok], Wg[:, kc, DFF + fc * 128: DFF + (fc + 1) * 128], xT[:, kc, 0:ntok],
                         kc == 0, kc == KC - 1, reads=[Wg, xT], writes=[ps])
                sg = sgs[fc % 2]
                k.op(k.ACT, lambda e: e.activation(out=sg[:, 0:ntok], in_=ps[:, 0:ntok], func=AF.Silu), reads=[ps], writes=[sg])
                k.op(k.DVE, lambda e: e.scalar_tensor_tensor(out=hT[:, fc, 0:ntok], in0=sg[:, 0:ntok], scalar=0.5,
                                                             in1=ps[:, 512:512 + ntok], op0=ALU.mult, op1=ALU.mult),
                     reads=[sg, ps], writes=[hT])
            for s in range(NS):
                ps = k.psn()
                for hf in range(2):
                    for fc in range(22):
                        k.mm(ps[0:PT, hf * 512:(hf + 1) * 512], hT[:, fc, s * PT:(s + 1) * PT], Wd[:, fc, hf * 512:(hf + 1) * 512],
                             fc == 0, fc == 21, reads=[hT, Wd], writes=[ps])
                ln_epilogue(k, g, ps, x32[0:PT, s, :], x32, PT, dst, row0 + s * PT, ALPHA)
        k.barrier()


def setup_globals(k, g):
    g.stage = [k.sb("stg", [128, 704], F32) for _ in range(2)]
    g.stage_i = 0
    g.tok_i = 0
    g.ybuf = [k.sb("yb", [128, D], F32) for _ in range(2)]
    g.obuf = [k.sb("ob", [128, D], F32) for _ in range(2)]
    g.small = [k.sb("sm", [128, 16], F32) for _ in range(2)]
    g.y_i = 0
    g.gtab = k.sb("gtab", [128, D], F32); g.btab = k.sb("btab", [128, D], F32)
    g.ident = k.sb("ident", [128, 128], F32)
    g.identb = k.sb("identb", [128, 128], BF16)
    g.eps_ln = k.sb("epsln", [128, 1], F32)
    k.dma(k.SP, g.ident[:], g.c_ident.t[:, :], writes=[g.ident])
    k.copy(g.identb[:], g.ident[:], [g.ident], [g.identb], eng=k.DVE)
    k.op(k.DVE, lambda e: e.memset(g.eps_ln[:], LN_EPS), writes=[g.eps_ln])
    g.one_c = k.sb("one_c", [128, 1], F32)
    k.op(k.DVE, lambda e: e.memset(g.one_c[:], 1.0), writes=[g.one_c])
    k.pspool = [k.ps("psp", [128, 1024], F32) for _ in range(4)]


def stage_xattn(k, g, TP, l, src, dst):
    DVE, ACT = k.DVE, k.ACT
    with contextlib.ExitStack() as st:
        Wq = k.sb("Wq", [128, KC, D], BF16, st); Wo = k.sb("Wo", [128, KC, D], BF16, st)
        Wk = k.sb("Wk", [128, KC, D], BF16, st); Wv = k.sb("Wv", [128, KC, D], BF16, st)
        load_ln_tabs(k, g, l, 2)
        load_weight(k, g, Wq, g.xa_q.t[l], D, D); load_weight(k, g, Wo, g.xa_o.t[l], D, D)
        load_weight(k, g, Wk, g.xa_k.t[l], D, D); load_weight(k, g, Wv, g.xa_v.t[l], D, D)
        ones = k.sb("ones", [128, 128], BF16, st)
        k.op(DVE, lambda e: e.memset(ones[:], 1.0), writes=[ones])
        m32 = k.sb("m32", [128, 2, D], F32, st)
        memT = k.sb("memT", [128, KC, 256], BF16, st)
        KTs = [k.sb("KT", [128, KC, 256], BF16, st) for _ in range(3)]
        Vts = [k.sb("Vt", [128, 2, D], BF16, st) for _ in range(3)]
        o32s = [k.sb("mo32", [128, D], F32, st) for _ in range(2)]
        k.dma(k.SP, m32[:], g.mem.t[:, :].rearrange("(s p) d -> p s d", p=128), writes=[m32])
        to_featmajor(k, g, m32, 128, 2, memT)
        for ec in range(KC):
            ps = k.psn()
            for kc in range(KC):
                k.mm(ps[:, 0:256], Wk[:, kc, ec * 128:(ec + 1) * 128], memT[:, kc, :], kc == 0, kc == KC - 1, reads=[Wk, memT], writes=[ps])
            k.copy(KTs[0][:, ec, :], ps[:, 0:256], [ps], [KTs[0]])
        oi = 0
        for (W, outd, isv) in ((Wk, g.o_memk, False), (Wv, g.o_memv, True)):
            for s in range(2):
                ps = k.psn()
                for hf in range(2):
                    for kc in range(KC):
                        k.mm(ps[:, hf * 512:(hf + 1) * 512], memT[:, kc, s * 128:(s + 1) * 128], W[:, kc, hf * 512:(hf + 1) * 512],
                             kc == 0, kc == KC - 1, reads=[W, memT], writes=[ps])
                o32 = o32s[oi % 2]; oi += 1
                k.copy(o32[:], ps[:, :], [ps], [o32])
                k.dma(k.POOL, outd.t[l, s * 128:(s + 1) * 128, :], o32[:], reads=[o32], writes=[outd])
                if isv:
                    k.copy(Vts[0][:, s, :], o32[:], [o32], [Vts[0]])
        for sq in range(2):
            k.dma(k.SP, m32[:], g.ck.t[l, sq].rearrange("(s p) d -> p s d", p=128), writes=[m32])
            to_featmajor(k, g, m32, 128, 2, KTs[1 + sq])
            k.dma(k.SP, m32[:], g.cv.t[l, sq].rearrange("(s p) d -> p s d", p=128), writes=[m32])
            k.copy(Vts[1 + sq][:], m32[:], [m32], [Vts[1 + sq]])
        XT = 512
        xTs = [k.sb("xT", [128, KC, XT], BF16, st) for _ in range(2)]
        qT = k.sb("qT", [128, KC, XT], BF16, st)
        pTs = [k.sb("pT", [128, 2, XT], BF16, st) for _ in range(2)]
        rdens = [k.sb("rden", [128, XT], F32, st) for _ in range(2)]
        oT = k.sb("oT", [128, KC, XT], BF16, st)
        toks = [k.sb("tok4", [128, 4, D], F32, st)]
        for ti, (row0, ntok, seq) in enumerate(segs(TP, XT)):
            PT = min(128, ntok); NS = ntok // PT
            k.rotate()
            x32 = load_tok(k, g, src, row0, PT, NS, toks)
            xT = xTs[ti % 2]
            to_featmajor(k, g, x32, PT, NS, xT)
            KT = KTs[seq]; Vt = Vts[seq]
            for ec in range(KC):
                ps = k.psn()
                for kc in range(KC):
                    k.mm(ps[:, 0:ntok], Wq[:, kc, ec * 128:(ec + 1) * 128], xT[:, kc, 0:ntok], kc == 0, kc == KC - 1, reads=[Wq, xT], writes=[ps])
                k.op(ACT, lambda e: e.activation(out=qT[:, ec, 0:ntok], in_=ps[:, 0:ntok], func=AF.Copy, scale=0.0625), reads=[ps], writes=[qT])
            for h in range(4):
                pT = pTs[h % 2]; rden = rdens[h % 2]
                for mc in range(2):
                    ps = k.psn()
                    for dc in range(2):
                        k.mm(ps[:, 0:ntok], KT[:, 2 * h + dc, mc * 128:(mc + 1) * 128], qT[:, 2 * h + dc, 0:ntok], dc == 0, dc == 1,
                             reads=[KT, qT], writes=[ps])
                    k.op(ACT, lambda e: e.activation(out=pT[:, mc, 0:ntok], in_=ps[:, 0:ntok], func=AF.Exp), reads=[ps], writes=[pT])
                ps = k.psn()
                for mc in range(2):
                    k.mm(ps[:, 0:ntok], ones[:, :], pT[:, mc, 0:ntok], mc == 0, mc == 1, reads=[ones, pT], writes=[ps])
                k.op(ACT, lambda e: e.activation(out=rden[:, 0:ntok], in_=ps[:, 0:ntok], func=AF.Ln), reads=[ps], writes=[rden])
                k.op(ACT, lambda e: e.activation(out=rden[:, 0:ntok], in_=rden[:, 0:ntok], func=AF.Exp, scale=-1.0), reads=[rden], writes=[rden])
                for dc in range(2):
                    ps = k.psn()
                    for mc in range(2):
                        k.mm(ps[:, 0:ntok], Vt[:, mc, (2 * h + dc) * 128:(2 * h + dc + 1) * 128], pT[:, mc, 0:ntok], mc == 0, mc == 1,
                             reads=[Vt, pT], writes=[ps])
                    k.op(DVE, lambda e: e.tensor_tensor(out=oT[:, 2 * h + dc, 0:ntok], in0=ps[:, 0:ntok], in1=rden[:, 0:ntok], op=ALU.mult),
                         reads=[ps, rden], writes=[oT])
            for s in range(NS):
                ps = k.psn()
                for hf in range(2):
                    for ec in range(KC):
                        k.mm(ps[0:PT, hf * 512:(hf + 1) * 512], oT[:, ec, s * PT:(s + 1) * PT], Wo[:, ec, hf * 512:(hf + 1) * 512],
                             ec == 0, ec == KC - 1, reads=[oT, Wo], writes=[ps])
                ln_epilogue(k, g, ps, x32[0:PT, s, :], x32, PT, dst, row0 + s * PT, ALPHA)
        k.barrier()


def stage_mixer_ab(k, g, TP, src, dst):
    DVE, ACT = k.DVE, k.ACT
    TT = 256
    with contextlib.ExitStack() as st:
        Win = k.sb("Win", [128, KC, 3072], BF16, st); Wrot = k.sb("Wrot", [128, KC, 1024], BF16, st)
        Wout = k.sb("Wout", [128, KC, D], BF16, st)
        load_ln_tabs(k, g, 0, 1)
        load_weight(k, g, Win, g.w_in.t, D, 3072); load_weight(k, g, Wrot, g.w_rot.t, D, 1024)
        load_weight(k, g, Wout, g.w_out0.t, D, D)
        bd32 = k.sb("bd32", [128, 2, 4, 128], F32, st); Wbd = k.sb("Wbd", [128, 2, 4, 128], BF16, st)
        k.op(DVE, lambda e: e.memset(bd32[:], 0.0), writes=[bd32])
        for wi, wsrc in enumerate((g.lru_wa, g.lru_wx)):
            for hp in range(2):
                k.dma(k.SP, bd32[64 * hp:64 * hp + 64, wi, :, 64 * hp:64 * hp + 64],
                      wsrc.t.rearrange("(c hp) i j -> hp i c j", hp=2)[hp], writes=[bd32])
        k.copy(Wbd[:], bd32[:], [bd32], [Wbd], eng=DVE)
        cw = k.sb("cw", [128, 4, 4], F32, st)
        for j in range(4):
            k.dma(k.SP, cw[:, :, j], g.conv_w.t[j].rearrange("(c p) -> p c", p=128), writes=[cw], allow_slow_non_contiguous=True)
        cb = load_cols(k, st, "cb", g.conv_b.t, 4); ba = load_cols(k, st, "ba", g.lru_ba.t, 4); bx = load_cols(k, st, "bx", g.lru_bx.t, 4)
        lam = load_cols(k, st, "lam", g.lru_lam.t, 4); gng = load_cols(k, st, "gng", g.ret_g.t, 4); gnb = load_cols(k, st, "gnb", g.ret_b.t, 4)
        cl = k.sb("cl", [128, 4], F32, st); cl2 = k.sb("cl2", [128, 4], F32, st)
        k.op(ACT, lambda e: e.activation(out=cl[:], in_=lam[:], func=AF.Exp, scale=-1.0), reads=[lam], writes=[cl])
        k.op(ACT, lambda e: e.activation(out=cl[:], in_=cl[:], func=AF.Ln, bias=g.one_c[:, 0:1]), reads=[cl, g.one_c], writes=[cl])
        k.op(DVE, lambda e: e.tensor_scalar(out=cl2[:], in0=cl[:], scalar1=-16.0, scalar2=None, op0=ALU.mult), reads=[cl], writes=[cl2])
        k.op(DVE, lambda e: e.tensor_scalar(out=cl[:], in0=cl[:], scalar1=-8.0, scalar2=None, op0=ALU.mult), reads=[cl], writes=[cl])
        ones = k.sb("ones", [128, 128], BF16, st)
        k.op(DVE, lambda e: e.memset(ones[:], 1.0 / 128.0), writes=[ones])
        epsc = k.sb("epsc", [128, 1], F32, st)
        k.op(DVE, lambda e: e.memset(epsc[:], LN_EPS), writes=[epsc])
        cosT = k.sb("cosT", [128, TT], F32, st); sinT = k.sb("sinT", [128, TT], F32, st)
        retM = k.sb("retM", [128, 4, 128], F32, st); retXI = k.sb("retXI", [128, 4, 128], F32, st); retZ = k.sb("retZ", [128, 4, 128], F32, st)
        xaT = k.sb("xaT", [128, 4, 3 + TT], F32, st); hl = k.sb("hl", [128, 4], F32, st)
        S32 = k.sb("S32", [128, 4, 128], F32, st); Sb = k.sb("Sb", [128, 4, 128], BF16, st)
        xTs = [k.sb("xT", [128, KC, TT], BF16, st) for _ in range(2)]
        toks = [k.sb("tok2", [128, 2, D], F32, st) for _ in range(2)]
        gaT = k.sb("gaT", [128, 4, TT], F32, st)
        qr = k.sb("qr", [128, 4, TT], BF16, st); kz = k.sb("kz", [128, 4, TT], BF16, st)
        t1s = [k.sb("t1", [128, TT], F32, st) for _ in range(2)]; t2s = [k.sb("t2", [128, TT], F32, st) for _ in range(2)]
        t3s = [k.sb("t3", [128, TT], F32, st) for _ in range(2)]
        Ktok = k.sb("Ktok", [128, 2, 4, 128], BF16, st); Vtok = k.sb("Vtok", [128, 2, 512], BF16, st)
        PTb = k.sb("PTb", [128, 4, 128], BF16, st)
        oT = k.sb("oT", [128, 4, TT], F32, st); obf = k.sb("obf", [128, 4, TT], BF16, st); osq = k.sb("osq", [128, 4, TT], BF16, st)
        lru = [[k.sb(n, [128, TT], (BF16 if n == "xcb" else F32), st) for n in ("xc", "xcb", "rr", "ii", "aa", "hh")] for _ in range(2)]
        yT = k.sb("yT", [128, KC, TT], BF16, st)
        cur_seq = -1
        allsegs = segs(TP, TT)
        for ti, (row0, ntok, seq) in enumerate(allsegs):
            PT = min(128, ntok); NS = ntok // PT
            k.rotate()
            C = PT; nch = NS
            ci = 0 if seq == 0 else 1
            last = (ti + 1 == len(allsegs)) or (allsegs[ti + 1][2] != seq)
            if seq != cur_seq:
                cur_seq = seq
                k.op(DVE, lambda e: e.memset(PTb[:], 0.0), writes=[PTb])
                k.op(DVE, lambda e: e.memset(Ktok[:], 0.0), writes=[Ktok])
                k.op(DVE, lambda e: e.memset(Vtok[:], 0.0), writes=[Vtok])
                k.dma(k.SP, retM[:], g.c_retM.t[ci], writes=[retM]); k.dma(k.SP, retXI[:], g.c_retXI.t[ci], writes=[retXI])
                k.dma(k.SP, retZ[:], g.c_retZ.t[ci], writes=[retZ])
                if seq == 0:
                    k.op(DVE, lambda e: e.memset(xaT[:], 0.0), writes=[xaT])
                    k.op(DVE, lambda e: e.memset(hl[:], 0.0), writes=[hl])
                    k.op(DVE, lambda e: e.memset(S32[:], 0.0), writes=[S32])
                else:
                    for j in range(3):
                        k.dma(k.SP, xaT[:, :, j], g.st_conv.t[seq - 1, j].rearrange("(c p) -> p c", p=128), writes=[xaT], allow_slow_non_contiguous=True)
                    k.dma(k.SP, hl[:], g.st_lru.t[seq - 1].rearrange("(c p) -> p c", p=128), writes=[hl], allow_slow_non_contiguous=True)
                    k.dma(k.SP, S32[:], g.st_ret.t[seq - 1].rearrange("h d v -> d h v"), writes=[S32])
                k.copy(Sb[:], S32[:], [S32], [Sb], eng=ACT)
            pos0 = row0 if seq == 0 else TP
            k.dma(k.SP, cosT[:, 0:ntok], g.c_cos.t[:, pos0:pos0 + ntok], writes=[cosT])
            k.dma(k.SP, sinT[:, 0:ntok], g.c_sin.t[:, pos0:pos0 + ntok], writes=[sinT])
            x32 = load_tok(k, g, src, row0, PT, NS, toks)
            xT = xTs[ti % 2]
            to_featmajor(k, g, x32, PT, NS, xT)

            def proj(W, col0):
                ps = k.psn()
                for kc in range(KC):
                    k.mm(ps[:, 0:ntok], W[:, kc, col0:col0 + 128], xT[:, kc, 0:ntok], kc == 0, kc == KC - 1, reads=[W, xT], writes=[ps])
                return ps
            for c in range(4):
                ps = proj(Win, c * 128)
                k.copy(xaT[:, c, 3:3 + ntok], ps[:, 0:ntok], [ps], [xaT], eng=ACT)
                ps = proj(Win, 512 + c * 128)
                k.op(ACT, lambda e: e.activation(out=gaT[:, c, 0:ntok], in_=ps[:, 0:ntok], func=AF.Gelu_apprx_tanh), reads=[ps], writes=[gaT])
            for (dst_b, base, rbase, isk) in ((qr, 1024, 0, False), (kz, 1536, 512, True)):
                for h in range(4):
                    t1, t2, t3 = t1s[h % 2], t2s[h % 2], t3s[h % 2]
                    ps = proj(Win, base + h * 128)
                    ps2 = proj(Wrot, rbase + h * 128)
                    k.op(DVE, lambda e: e.tensor_tensor(out=t1[:, 0:ntok], in0=ps[:, 0:ntok], in1=cosT[:, 0:ntok], op=ALU.mult), reads=[ps, cosT], writes=[t1])
                    k.op(DVE, lambda e: e.tensor_tensor(out=t2[:, 0:ntok], in0=ps2[:, 0:ntok], in1=sinT[:, 0:ntok], op=ALU.mult), reads=[ps2, sinT], writes=[t2])
                    if not isk:
                        k.op(DVE, lambda e: e.tensor_tensor(out=qr[:, h, 0:ntok], in0=t1[:, 0:ntok], in1=t2[:, 0:ntok], op=ALU.add), reads=[t1, t2], writes=[qr])
                    else:
                        k.op(DVE, lambda e: e.tensor_tensor(out=t3[:, 0:ntok], in0=t1[:, 0:ntok], in1=t2[:, 0:ntok], op=ALU.add), reads=[t1, t2], writes=[t3])
                        k.op(DVE, lambda e: e.tensor_tensor(out=kz[:, h, 0:ntok].rearrange("p (n c) -> p n c", c=C),
                                                            in0=t3[:, 0:ntok].rearrange("p (n c) -> p n c", c=C),
                                                            in1=retZ[:, h, 0:C].unsqueeze(1).broadcast_to([128, nch, C]), op=ALU.mult),
                             reads=[t3, retZ], writes=[kz])
            for n in range(nch):
                cs = slice(n * C, (n + 1) * C)
                ps = k.psn()
                for kc in range(KC):
                    k.mm(ps[0:C, 0:512], xT[:, kc, cs], Win[:, kc, 2048:2560], kc == 0, kc == KC - 1, reads=[xT, Win], writes=[ps])
                k.copy(Vtok[0:C, n % 2, :], ps[0:C, 0:512], [ps], [Vtok])
                ps = k.psn()
                psb = ps.t[:, 0:256].bitcast(BF16)
                for h in range(4):
                    k.tr(psb[0:C, h * 128:(h + 1) * 128], kz[:, h, cs], g.identb[:, :], reads=[kz, g.identb], writes=[ps])
                k.copy(Ktok[0:C, n % 2, :, :], psb[0:C, 0:512].rearrange("p (h d) -> p h d", d=128), [ps], [Ktok])
                ps = k.psn()
                for h in range(4):
                    k.mm(ps[0:C, h * 128:h * 128 + C], kz[:, h, cs], qr[:, h, cs], True, True, reads=[kz, qr], writes=[ps])
                k.op(DVE, lambda e: e.tensor_tensor(out=PTb[0:C, :, 0:C], in0=ps[0:C, 0:512].rearrange("p (h c) -> p h c", c=128)[:, :, 0:C],
                                                    in1=retM[0:C, :, 0:C], op=ALU.mult), reads=[ps, retM], writes=[PTb])
                ps = k.psn()
                for h in range(4):
                    k.mm(ps[:, h * 128:h * 128 + C], Vtok[:, n % 2, h * 128:(h + 1) * 128], PTb[:, h, 0:C], True, False, reads=[Vtok, PTb], writes=[ps])
                    k.mm(ps[:, h * 128:h * 128 + C], Sb[:, h, :], qr[:, h, cs], False, True, reads=[Sb, qr], writes=[ps])
                k.op(DVE, lambda e: e.tensor_tensor(out=oT[:, :, cs], in0=ps[:, 0:512].rearrange("p (h c) -> p h c", c=128)[:, :, 0:C],
                                                    in1=retXI[:, :, 0:C], op=ALU.mult), reads=[ps, retXI], writes=[oT])
                ps = k.psn()
                for h in range(4):
                    k.mm(ps[:, h * 128:(h + 1) * 128], Ktok[:, n % 2, h, :], Vtok[:, n % 2, h * 128:(h + 1) * 128], True, True, reads=[Ktok, Vtok], writes=[ps])
                for h in range(4):
                    gam = float(np.exp(np.log1p(-(2.0 ** (-5.0 - h))) * C))
                    k.op(DVE, lambda e: e.scalar_tensor_tensor(out=S32[:, h, :], in0=S32[:, h, :], scalar=gam, in1=ps[:, h * 128:(h + 1) * 128],
                                                               op0=ALU.mult, op1=ALU.add), reads=[S32, ps], writes=[S32])
                k.copy(Sb[:], S32[:], [S32], [Sb], eng=ACT)
            k.op(ACT, lambda e: e.activation(out=obf[:, :, 0:ntok], in_=oT[:, :, 0:ntok], func=AF.Copy), reads=[oT], writes=[obf])
            k.op(ACT, lambda e: e.activation(out=osq[:, :, 0:ntok], in_=oT[:, :, 0:ntok], func=AF.Square), reads=[oT], writes=[osq])
            for h in range(4):
                t1, t2, t3 = t1s[h % 2], t2s[h % 2], t3s[h % 2]
                psm = k.psn()
                k.mm(psm[:, 0:ntok], ones[:, :], obf[:, h, 0:ntok], True, True, reads=[ones, obf], writes=[psm])
                k.mm(psm[:, 512:512 + ntok], ones[:, :], osq[:, h, 0:ntok], True, True, reads=[ones, osq], writes=[psm])
                k.op(ACT, lambda e: e.activation(out=t1[:, 0:ntok], in_=psm[:, 0:ntok], func=AF.Square), reads=[psm], writes=[t1])
                k.op(DVE, lambda e: e.tensor_tensor(out=t1[:, 0:ntok], in0=psm[:, 512:512 + ntok], in1=t1[:, 0:ntok], op=ALU.subtract), reads=[psm, t1], writes=[t1])
                k.op(ACT, lambda e: e.activation(out=t1[:, 0:ntok], in_=t1[:, 0:ntok], func=AF.Ln, bias=epsc[:, 0:1]), reads=[t1, epsc], writes=[t1])
                k.op(ACT, lambda e: e.activation(out=t1[:, 0:ntok], in_=t1[:, 0:ntok], func=AF.Exp, scale=-0.5), reads=[t1], writes=[t1])
                k.op(DVE, lambda e: e.tensor_tensor(out=t2[:, 0:ntok], in0=oT[:, h, 0:ntok], in1=psm[:, 0:ntok], op=ALU.subtract), reads=[oT, psm], writes=[t2])
                k.op(DVE, lambda e: e.tensor_tensor(out=t2[:, 0:ntok], in0=t2[:, 0:ntok], in1=t1[:, 0:ntok], op=ALU.mult), reads=[t1, t2], writes=[t2])
                k.op(DVE, lambda e: e.tensor_scalar(out=t2[:, 0:ntok], in0=t2[:, 0:ntok], scalar1=gng[:, h:h + 1], scalar2=gnb[:, h:h + 1],
                                                    op0=ALU.mult, op1=ALU.add), reads=[t2, gng, gnb], writes=[t2])
                ps = proj(Win, 2560 + h * 128)
                k.op(ACT, lambda e: e.activation(out=t3[:, 0:ntok], in_=ps[:, 0:ntok], func=AF.Silu), reads=[ps], writes=[t3])
                k.op(DVE, lambda e: e.tensor_tensor(out=yT[:, 4 + h, 0:ntok], in0=t2[:, 0:ntok], in1=t3[:, 0:ntok], op=ALU.mult), reads=[t2, t3], writes=[yT])
            for c in range(4):
                xc, xcb, rr, ii, aa, hh = lru[c % 2]
                k.op(DVE, lambda e: e.tensor_scalar(out=xc[:, 0:ntok], in0=xaT[:, c, 0:ntok], scalar1=cw[:, c, 0:1], scalar2=cb[:, c:c + 1],
                                                    op0=ALU.mult, op1=ALU.add), reads=[xaT, cw, cb], writes=[xc])
                for j in range(1, 4):
                    k.op(DVE, lambda e: e.scalar_tensor_tensor(out=xc[:, 0:ntok], in0=xaT[:, c, j:j + ntok], scalar=cw[:, c, j:j + 1], in1=xc[:, 0:ntok],
                                                               op0=ALU.mult, op1=ALU.add), reads=[xaT, cw, xc], writes=[xc])
                k.copy(xcb[:, 0:ntok], xc[:, 0:ntok], [xc], [xcb], eng=ACT)
                ps = k.psn()
                k.mm(ps[:, 0:ntok], Wbd[:, 0, c, :], xcb[:, 0:ntok], True, True, reads=[Wbd, xcb], writes=[ps])
                k.mm(ps[:, 512:512 + ntok], Wbd[:, 1, c, :], xcb[:, 0:ntok], True, True, reads=[Wbd, xcb], writes=[ps])
                k.op(ACT, lambda e: e.activation(out=rr[:, 0:ntok], in_=ps[:, 0:ntok], func=AF.Sigmoid, bias=ba[:, c:c + 1]), reads=[ps, ba], writes=[rr])
                k.op(ACT, lambda e: e.activation(out=ii[:, 0:ntok], in_=ps[:, 512:512 + ntok], func=AF.Sigmoid, bias=bx[:, c:c + 1]), reads=[ps, bx], writes=[ii])
                k.op(ACT, lambda e: e.activation(out=aa[:, 0:ntok], in_=rr[:, 0:ntok], func=AF.Exp, scale=cl[:, c:c + 1]), reads=[rr, cl], writes=[aa])
                k.op(ACT, lambda e: e.activation(out=rr[:, 0:ntok], in_=rr[:, 0:ntok], func=AF.Exp, scale=cl2[:, c:c + 1]), reads=[rr, cl2], writes=[rr])
                k.op(ACT, lambda e: e.activation(out=rr[:, 0:ntok], in_=rr[:, 0:ntok], func=AF.Ln, scale=-1.0, bias=g.one_c[:, 0:1]), reads=[rr, g.one_c], writes=[rr])
                k.op(ACT, lambda e: e.activation(out=rr[:, 0:ntok], in_=rr[:, 0:ntok], func=AF.Exp, scale=0.5), reads=[rr], writes=[rr])
                k.op(DVE, lambda e: e.tensor_tensor(out=ii[:, 0:ntok], in0=ii[:, 0:ntok], in1=xc[:, 0:ntok], op=ALU.mult), reads=[ii, xc], writes=[ii])
                k.op(DVE, lambda e: e.tensor_tensor(out=ii[:, 0:ntok], in0=ii[:, 0:ntok], in1=rr[:, 0:ntok], op=ALU.mult), reads=[ii, rr], writes=[ii])
                k.op(DVE, lambda e: e.tensor_tensor_scan(out=hh[:, 0:ntok], data0=aa[:, 0:ntok], data1=ii[:, 0:ntok], initial=hl[:, c:c + 1],
                                                         op0=ALU.mult, op1=ALU.add), reads=[aa, ii, hl], writes=[hh])
                k.op(DVE, lambda e: e.tensor_copy(out=hl[:, c:c + 1], in_=hh[:, ntok - 1:ntok]), reads=[hh], writes=[hl])
                k.op(DVE, lambda e: e.tensor_tensor(out=yT[:, c, 0:ntok], in0=hh[:, 0:ntok], in1=gaT[:, c, 0:ntok], op=ALU.mult), reads=[hh, gaT], writes=[yT])
            k.op(DVE, lambda e: e.tensor_copy(out=xaT[:, :, 0:3], in_=xaT[:, :, ntok:ntok + 3]), reads=[xaT], writes=[xaT])
            for s in range(NS):
                ps = k.psn()
                for hf in range(2):
                    for c in range(KC):
                        k.mm(ps[0:PT, hf * 512:(hf + 1) * 512], yT[:, c, s * PT:(s + 1) * PT], Wout[:, c, hf * 512:(hf + 1) * 512],
                             c == 0, c == KC - 1, reads=[yT, Wout], writes=[ps])
                ln_epilogue(k, g, ps, x32[0:PT, s, :], x32, PT, dst, row0 + s * PT, ALPHA)
            if last:
                for j in range(3):
                    k.dma(k.POOL, g.o_conv.t[seq, j].rearrange("(c p) -> p c", p=128), xaT[:, :, j], reads=[xaT], writes=[g.o_conv], allow_slow_non_contiguous=True)
                k.dma(k.POOL, g.o_lru.t[seq].rearrange("(c p) -> p c", p=128), hl[:], reads=[hl], writes=[g.o_lru], allow_slow_non_contiguous=True)
                k.dma(k.POOL, g.o_ret.t[seq].rearrange("h d v -> d h v"), S32[:], reads=[S32], writes=[g.o_ret])
        k.barrier()


def _interleave(a, b, ra=1, rb=1):
    alive_a, alive_b = a is not None, b is not None
    while alive_a or alive_b:
        for _ in range(ra):
            if alive_a:
                try:
                    next(a)
                except StopIteration:
                    alive_a = False
        for _ in range(rb):
            if alive_b:
                try:
                    next(b)
                except StopIteration:
                    alive_b = False


def stage_rwkv(k, g, TP, src, dst):
    DVE, ACT, POOL = k.DVE, k.ACT, k.DVE
    DK = float(np.exp(-0.5))
    NT1 = TP // 128 + 2
    opnd = k.dram("rw_opnd", [NT1, 7, 128, 1024], BF16, "Internal")
    wcd = k.dram("rw_wc", [NT1, 128, 2, KC], F32, "Internal")
    with contextlib.ExitStack() as st:
        Wr = k.sb("Wr", [128, KC, D], BF16, st); Wk = k.sb("Wk", [128, KC, D], BF16, st); Wv = k.sb("Wv", [128, KC, D], BF16, st)
        for i, W in enumerate((Wr, Wk, Wv)):
            load_weight(k, g, W, g.w_rkv.t[i], D, D)
        w1 = k.sb("w1", [128, KC, 64], BF16, st); a1 = k.sb("a1", [128, KC, 64], BF16, st); g1 = k.sb("g1", [128, KC, 128], BF16, st)
        w2 = k.sb("w2", [128, 1, D], BF16, st); a2 = k.sb("a2", [128, 1, D], BF16, st); g2 = k.sb("g2", [128, 1, D], BF16, st)
        load_weight(k, g, w1, g.w1.t, D, 64); load_weight(k, g, a1, g.a1.t, D, 64); load_weight(k, g, g1, g.g1.t, D, 128)
        load_weight(k, g, w2, g.w2.t, 64, D); load_weight(k, g, a2, g.a2.t, 64, D); load_weight(k, g, g2, g.g2.t, 128, D)
        mu = k.sb("mu", [128, 6, KC], F32, st)
        for p in range(6):
            k.dma(k.SP, mu[:, p, :], g.mu.t[p].rearrange("(c p) -> p c", p=128), writes=[mu], allow_slow_non_contiguous=True)
        w0c = load_cols(k, st, "w0c", g.w0.t, 8); a0c = load_cols(k, st, "a0c", g.a0.t, 8); kkc = load_cols(k, st, "kkc", g.k_k.t, 8)
        kac = load_cols(k, st, "kac", g.k_a.t, 8); rkc = load_cols(k, st, "rkc", g.r_k.t, 8)
        ob32 = k.sb("ob32", [128, 128], F32, st); onesbd = k.sb("onesbd", [128, 128], BF16, st)
        k.dma(k.SP, ob32[:], g.c_onesbd.t[:, :], writes=[ob32])
        k.copy(onesbd[:], ob32[:], [ob32], [onesbd], eng=DVE)
        onesf = k.sb("onesf", [128, 64], F32, st)
        k.op(DVE, lambda e: e.memset(onesf[:], 1.0), writes=[onesf])
        xprev = k.sb("xprev", [128, KC], F32, st)
        TT = 128
        toks = [k.sb("tok1", [128, 1, D], F32, st) for _ in range(2)]
        xT32 = k.sb("xT32", [128, KC, 1 + TT], F32, st); dd = k.sb("dd", [128, KC, TT], F32, st)
        xms = [k.sb("xm", [128, KC, TT], BF16, st) for _ in range(2)]
        F = lambda n: k.sb(n, [128, KC, TT], F32, st)
        iface = [[F(n + str(par)) for n in ("rT", "kT", "vT", "sg", "ic")] for par in range(2)]
        Lc, Ep, Em, Ea, kk, kf, tm, Lm = [F(n) for n in ("Lc", "Ep", "Em", "Ea", "kk", "kf", "tm", "Lm")]
        kkn = kk
        B = lambda n: k.sb(n, [128, KC, TT], BF16, st)
        kk2, rkb = B("kk2"), B("rkb")
        _o = [B(f"o{j}") for j in range(7)]
        _g1 = B("o5b")
        outs = [_o, _o[:5] + [_g1] + _o[6:]]
        th = k.sb("th", [128, TT], BF16, st)
        wcs = [k.sb("wcs", [128, 2, KC], F32, st) for _ in range(2)]
        p1segs = segs(TP, TT)

        def front(ti):
            row0, ntok, seq = p1segs[ti]
            PT = ntok
            k.rotate()
            rT, kT, vT, sg, ic = iface[ti % 2]
            At, Rt, Kt, Bt, Vb, Gb, Bon = outs[ti % 2]
            if ti == 0 or p1segs[ti - 1][2] != seq:
                if seq == 0:
                    k.op(DVE, lambda e: e.memset(xprev[:], 0.0), writes=[xprev])
                else:
                    k.dma(k.SP, xprev[:], g.st_shift.t[seq - 1].rearrange("(c p) -> p c", p=128), writes=[xprev], allow_slow_non_contiguous=True)
            x32 = load_tok(k, g, src, row0, PT, 1, toks)
            k.op(DVE, lambda e: e.tensor_copy(out=xT32[:, :, 0], in_=xprev[:, :]), reads=[xprev], writes=[xT32])
            to_featmajor(k, g, x32, PT, 1, xT32, col0=1)
            k.op(DVE, lambda e: e.tensor_copy(out=xprev[:, :], in_=xT32[:, :, ntok]), reads=[xT32], writes=[xprev])
            k.op(DVE, lambda e: e.tensor_tensor(out=dd[:, :, 0:ntok], in0=xT32[:, :, 0:ntok], in1=xT32[:, :, 1:1 + ntok], op=ALU.subtract),
                 reads=[xT32], writes=[dd])

            def mix(p):
                xm = xms[p % 2]
                for c in range(KC):
                    k.op(DVE, lambda e: e.scalar_tensor_tensor(out=xm[:, c, 0:ntok], in0=dd[:, c, 0:ntok], scalar=mu[:, p, c:c + 1],
                                                               in1=xT32[:, c, 1:1 + ntok], op0=ALU.mult, op1=ALU.add), reads=[dd, mu, xT32], writes=[xm])
                return xm

            def proj_full(W, xm, dstb):
                for ec in range(KC):
                    ps = k.psn()
                    for kc in range(KC):
                        k.mm(ps[:, 0:ntok], W[:, kc, ec * 128:(ec + 1) * 128], xm[:, kc, 0:ntok], kc == 0, kc == KC - 1, reads=[W, xm], writes=[ps])
                    k.copy(dstb[:, ec, 0:ntok], ps[:, 0:ntok], [ps], [dstb], eng=ACT)

            def lora(xm, wA, nA, wB, func1, emit2):
                ps = k.psn()
                for kc in range(KC):
                    k.mm(ps[0:nA, 0:ntok], wA[:, kc, :], xm[:, kc, 0:ntok], kc == 0, kc == KC - 1, reads=[wA, xm], writes=[ps])
                k.op(ACT, lambda e: e.activation(out=th[0:nA, 0:ntok], in_=ps[0:nA, 0:ntok], func=func1), reads=[ps], writes=[th])
                for c in range(KC):
                    ps2 = k.psn()
                    k.mm(ps2[:, 0:ntok], wB[0:nA, 0, c * 128:(c + 1) * 128], th[0:nA, 0:ntok], True, True, reads=[wB, th], writes=[ps2])
                    emit2(c, ps2)

            proj_full(Wr, mix(0), rT); proj_full(Wk, mix(1), kT); proj_full(Wv, mix(2), vT)
            lora(mix(3), w1, 64, w2, AF.Tanh, lambda c, ps2: k.op(ACT, lambda e: e.activation(
                out=sg[:, c, 0:ntok], in_=ps2[:, 0:ntok], func=AF.Sigmoid, bias=w0c[:, c:c + 1]), reads=[ps2, w0c], writes=[sg]))
            lora(mix(4), a1, 64, a2, AF.Copy, lambda c, ps2: k.op(ACT, lambda e: e.activation(
                out=ic[:, c, 0:ntok], in_=ps2[:, 0:ntok], func=AF.Sigmoid, bias=a0c[:, c:c + 1]), reads=[ps2, a0c], writes=[ic]))
            lora(mix(5), g1, 128, g2, AF.Sigmoid, lambda c, ps2: k.copy(Gb[:, c, 0:ntok], ps2[:, 0:ntok], [ps2], [Gb], eng=ACT))

        def back(ti):
            row0, ntok, seq = p1segs[ti]
            C = min(64, ntok); nch = ntok // C
            chunk0 = row0 // 64 if seq == 0 else TP // 64 + (seq - 1)
            rT, kT, vT, sg, ic = iface[ti % 2]
            At, Rt, Kt, Bt, Vb, Gb, Bon = outs[ti % 2]
            v3 = lambda ps: ps[:, :].rearrange("p (c t) -> p c t", t=128)[:, :, 0:ntok]
            for c in range(KC):
                for n in range(nch):
                    k.op(DVE, lambda e: e.tensor_tensor_scan(out=Lc[:, c, n * C:(n + 1) * C], data0=onesf[:, 0:C], data1=sg[:, c, n * C:(n + 1) * C],
                                                             initial=0.0, op0=ALU.mult, op1=ALU.add), reads=[onesf, sg], writes=[Lc])
            k.op(POOL, lambda e: e.tensor_tensor(out=Lm[:, :, 0:ntok], in0=Lc[:, :, 0:ntok], in1=sg[:, :, 0:ntok], op=ALU.subtract), reads=[Lc, sg], writes=[Lm])
            k.op(ACT, lambda e: e.activation(out=Ep[:, :, 0:ntok], in_=Lc[:, :, 0:ntok], func=AF.Exp, scale=-DK), reads=[Lc], writes=[Ep])
            k.op(ACT, lambda e: e.activation(out=Em[:, :, 0:ntok], in_=Lc[:, :, 0:ntok], func=AF.Exp, scale=DK), reads=[Lc], writes=[Em])
            k.op(ACT, lambda e: e.activation(out=Ea[:, :, 0:ntok], in_=Lm[:, :, 0:ntok], func=AF.Exp, scale=-DK), reads=[Lm], writes=[Ea])
            for c in range(KC):
                k.op(DVE, lambda e: e.tensor_scalar(out=kk[:, c, 0:ntok], in0=kT[:, c, 0:ntok], scalar1=kkc[:, c:c + 1], scalar2=None, op0=ALU.mult),
                     reads=[kT, kkc], writes=[kk])
            k.op(ACT, lambda e: e.activation(out=kk2[:, :, 0:ntok], in_=kk[:, :, 0:ntok], func=AF.Square), reads=[kk], writes=[kk2])
            ps = k.psn()
            if ntok == 128:
                for hf in range(2):
                    k.mm(ps[:, hf * 512:(hf + 1) * 512], onesbd[:, :], kk2[:, 4 * hf:4 * hf + 4, :].rearrange("p c t -> p (c t)"), True, True,
                         reads=[onesbd, kk2], writes=[ps])
            else:
                for c in range(KC):
                    k.mm(ps[:, c * 128:c * 128 + ntok], onesbd[:, :], kk2[:, c, 0:ntok], True, True, reads=[onesbd, kk2], writes=[ps])
            k.op(DVE, lambda e: e.tensor_scalar(out=tm[:, :, 0:ntok], in0=v3(ps), scalar1=1e-24, scalar2=None, op0=ALU.max), reads=[ps], writes=[tm])
            k.op(ACT, lambda e: e.activation(out=tm[:, :, 0:ntok], in_=tm[:, :, 0:ntok], func=AF.Ln), reads=[tm], writes=[tm])
            k.op(ACT, lambda e: e.activation(out=tm[:, :, 0:ntok], in_=tm[:, :, 0:ntok], func=AF.Exp, scale=-0.5), reads=[tm], writes=[tm])
            k.op(POOL, lambda e: e.tensor_tensor(out=kkn[:, :, 0:ntok], in0=kk[:, :, 0:ntok], in1=tm[:, :, 0:ntok], op=ALU.mult), reads=[kk, tm], writes=[kkn])
            for c in range(KC):
                k.op(DVE, lambda e: e.tensor_scalar(out=tm[:, c, 0:ntok], in0=ic[:, c, 0:ntok], scalar1=-1.0, scalar2=kac[:, c:c + 1],
                                                    op0=ALU.add, op1=ALU.mult), reads=[ic, kac], writes=[tm])
            k.op(DVE, lambda e: e.scalar_tensor_tensor(out=kf[:, :, 0:ntok], in0=tm[:, :, 0:ntok], scalar=1.0, in1=kT[:, :, 0:ntok],
                                                       op0=ALU.add, op1=ALU.mult), reads=[tm, kT], writes=[kf])
            for c in range(KC):
                k.op(DVE, lambda e: e.scalar_tensor_tensor(out=rkb[:, c, 0:ntok], in0=rT[:, c, 0:ntok], scalar=rkc[:, c:c + 1], in1=kf[:, c, 0:ntok],
                                                           op0=ALU.mult, op1=ALU.mult), reads=[rT, rkc, kf], writes=[rkb])
            ps = k.psn()
            if ntok == 128:
                for hf in range(2):
                    k.mm(ps[:, hf * 512:(hf + 1) * 512], onesbd[:, :], rkb[:, 4 * hf:4 * hf + 4, :].rearrange("p c t -> p (c t)"), True, True,
                         reads=[onesbd, rkb], writes=[ps])
            else:
                for c in range(KC):
                    k.mm(ps[:, c * 128:c * 128 + ntok], onesbd[:, :], rkb[:, c, 0:ntok], True, True, reads=[onesbd, rkb], writes=[ps])
            k.op(DVE, lambda e: e.tensor_tensor(out=Bon[:, :, 0:ntok], in0=v3(ps), in1=vT[:, :, 0:ntok], op=ALU.mult), reads=[ps, vT], writes=[Bon])
            k.op(DVE, lambda e: e.scalar_tensor_tensor(out=At[:, :, 0:ntok], in0=kkn[:, :, 0:ntok], scalar=-1.0, in1=Ea[:, :, 0:ntok],
                                                       op0=ALU.mult, op1=ALU.mult), reads=[kkn, Ea], writes=[At])
            k.op(POOL, lambda e: e.tensor_tensor(out=Rt[:, :, 0:ntok], in0=rT[:, :, 0:ntok], in1=Ep[:, :, 0:ntok], op=ALU.mult), reads=[rT, Ep], writes=[Rt])
            k.op(POOL, lambda e: e.tensor_tensor(out=Kt[:, :, 0:ntok], in0=kf[:, :, 0:ntok], in1=Em[:, :, 0:ntok], op=ALU.mult), reads=[kf, Em], writes=[Kt])
            k.op(POOL, lambda e: e.tensor_tensor(out=tm[:, :, 0:ntok], in0=kkn[:, :, 0:ntok], in1=ic[:, :, 0:ntok], op=ALU.mult), reads=[kkn, ic], writes=[tm])
            k.op(POOL, lambda e: e.tensor_tensor(out=Bt[:, :, 0:ntok], in0=tm[:, :, 0:ntok], in1=Em[:, :, 0:ntok], op=ALU.mult), reads=[tm, Em], writes=[Bt])
            k.copy(Vb[:, :, 0:ntok], vT[:, :, 0:ntok], [vT], [Vb], eng=ACT)
            for j, ob in enumerate(outs[ti % 2]):
                k.dma(k.POOL, opnd.t[ti, j].rearrange("p (c t) -> p c t", t=128)[:, :, 0:ntok], ob[:, :, 0:ntok], reads=[ob], writes=[opnd])
            wcb = wcs[ti % 2]
            for n in range(nch):
                k.op(DVE, lambda e: e.tensor_copy(out=wcb[:, n, :], in_=Ep[:, :, (n + 1) * C - 1]), reads=[Ep], writes=[wcb])
            k.dma(k.POOL, wcd.t[ti][:, 0:nch, :], wcb[:, 0:nch, :], reads=[wcb], writes=[wcd])

        front(0)
        for ti in range(len(p1segs)):
            if ti + 1 < len(p1segs):
                front(ti + 1)
            back(ti)
        k.barrier()
    P2E = DVE
    with contextlib.ExitStack() as st:
        Wout = k.sb("Wout", [128, KC, D], BF16, st)
        load_ln_tabs(k, g, 1, 1)
        load_weight(k, g, Wout, g.w_out1.t, D, D)
        gngc = load_cols(k, st, "gngc", g.gn_g.t, 8); gnbc = load_cols(k, st, "gnbc", g.gn_b.t, 8)
        ob32 = k.sb("ob32", [128, 128], F32, st); onesbd64 = k.sb("onesbd64", [128, 128], BF16, st)
        k.dma(k.SP, ob32[:], g.c_onesbd.t[:, :], writes=[ob32])
        k.op(ACT, lambda e: e.activation(out=onesbd64[:], in_=ob32[:], func=AF.Copy, scale=1.0 / 64.0), reads=[ob32], writes=[onesbd64])
        epsg = k.sb("epsg", [128, 1], F32, st)
        k.op(DVE, lambda e: e.memset(epsg[:], 64e-5), writes=[epsg])
        msk = k.sb("msk", [128, 3, 128], F32, st)
        Hx32 = k.sb("Hx32", [128, KC, 128], F32, st); Hb = k.sb("Hb", [128, KC, 128], BF16, st)
        Sx32 = k.sb("Sx32", [128, KC, 128], F32, st)
        X = lambda n: k.sb(n, [128, KC, 128], BF16, st)
        sets = []
        for par in range(2):
            s_ = Ctx()
            s_.Ear = k.sb("Ear", [128, KC, 2, 128], BF16, st)
            s_.Eb, s_.Ek, s_.Ev = X("Eb"), X("Ek"), X("Ev")
            s_.Gb = k.sb("Gb", [128, KC, 64], BF16, st); s_.Bon = k.sb("Bon", [128, KC, 64], BF16, st)
            s_.wc = k.sb("wc", [128, KC], F32, st); s_.x32 = k.sb("x32", [128, 1, D], F32, st)
            s_.VsT, s_.EbT, s_.EkT, s_.ArbT, s_.AakT, s_.ArkT, s_.PTb = [X(n) for n in ("VsT", "EbT", "EkT", "ArbT", "AakT", "ArkT", "PTb")]
            sets.append(s_)
        Mb = [X("Mb0"), X("Mb1")]; MTb = [X("MTb0"), X("MTb1")]
        Xb, Ub = X("Xb"), X("Ub")
        oT = k.sb("oT", [128, KC, 64], F32, st); tm = k.sb("tm", [128, KC, 64], F32, st)
        obf = k.sb("obf", [128, KC, 64], BF16, st); osq = k.sb("osq", [128, KC, 64], BF16, st); yT = k.sb("yT", [128, KC, 64], BF16, st)

        def zero_all():
            for s_ in sets:
                for zb in (s_.Ear, s_.Eb, s_.Ek, s_.Ev, s_.VsT, s_.EbT, s_.EkT, s_.ArbT, s_.AakT, s_.ArkT, s_.PTb):
                    k.op(DVE, lambda e: e.memset(zb[:], 0.0), writes=[zb])
            for zb in (Xb, Ub, Mb[0], Mb[1], MTb[0], MTb[1], obf, osq):
                k.op(DVE, lambda e: e.memset(zb[:], 0.0), writes=[zb])

        def fe2(ch, S, C, row0):
            tl, nn = ch
            R = 2 * C
            vR = lambda ps: ps[0:R, :].rearrange("p (c t) -> p c t", t=128)[:, :, 0:R]
            k.rotate()
            for hp in range(2):
                rw = slice(64 * hp, 64 * hp + 64); cl = slice(hp * C, (hp + 1) * C)
                src3 = lambda j: opnd.t[tl, j, 64 * hp:64 * hp + 64, :].rearrange("p (c t) -> p c t", t=128)[:, :, nn * 64:nn * 64 + C]
                k.dma(k.SP, S.Ear[rw, :, 0, cl], src3(0), reads=[opnd], writes=[S.Ear])
                k.dma(k.SP, S.Ear[rw, :, 1, cl], src3(1), reads=[opnd], writes=[S.Ear])
                k.dma(k.SP, S.Ek[rw, :, cl], src3(2), reads=[opnd], writes=[S.Ek])
                k.dma(k.SP, S.Eb[rw, :, cl], src3(3), reads=[opnd], writes=[S.Eb])
                k.dma(k.SP, S.Ev[rw, :, cl], src3(4), reads=[opnd], writes=[S.Ev])
            k.dma(k.SP, S.Gb[:, :, 0:C], opnd.t[tl, 5].rearrange("p (c t) -> p c t", t=128)[:, :, nn * 64:nn * 64 + C], reads=[opnd], writes=[S.Gb])
            k.dma(k.SP, S.Bon[:, :, 0:C], opnd.t[tl, 6].rearrange("p (c t) -> p c t", t=128)[:, :, nn * 64:nn * 64 + C], reads=[opnd], writes=[S.Bon])
            k.dma(k.SP, S.wc[:], wcd.t[tl][:, nn, :], reads=[wcd], writes=[S.wc])
            k.dma(k.SP, S.x32[0:C, 0, :], src.t[row0:row0 + C, :], reads=[src], writes=[S.x32])
            yield
            for (srcb, dstb) in ((S.Ev, S.VsT), (S.Eb, S.EbT), (S.Ek, S.EkT)):
                ps = k.psn()
                psb = ps.t[:, 0:512].bitcast(BF16)
                for c in range(KC):
                    k.tr(psb[0:R, c * 128:(c + 1) * 128], srcb[:, c, 0:R], g.identb[:, :], reads=[srcb, g.identb], writes=[ps])
                k.copy(dstb[0:R, :, :], psb[0:R, :].rearrange("p (c t) -> p c t", t=128), [ps], [dstb])
                yield
            mb = lambda j: msk[0:R, j, 0:R].unsqueeze(1).broadcast_to([R, KC, R])
            ea = lambda c: S.Ear[:, c, 0, 0:R]; er = lambda c: S.Ear[:, c, 1, 0:R]
            eb = lambda c: S.Eb[:, c, 0:R]; ek = lambda c: S.Ek[:, c, 0:R]
            for (lhs_sel, rhs_sel, dstb, mj) in ((ea, eb, Mb[0], 2), (eb, ea, MTb[0], 0), (eb, er, S.ArbT, 1), (ek, ea, S.AakT, 0), (ek, er, S.ArkT, 1)):
                ps = k.psn()
                for c in range(KC):
                    k.mm(ps[0:R, c * 128:c * 128 + R], lhs_sel(c), rhs_sel(c), True, True, reads=[S.Ear, S.Eb, S.Ek], writes=[ps])
                k.op(DVE, lambda e: e.tensor_tensor(out=dstb[0:R, :, 0:R], in0=vR(ps), in1=mb(mj), op=ALU.mult), reads=[ps, msk], writes=[dstb])
                yield
            k.op(P2E, lambda e: e.tensor_tensor(out=S.PTb[0:R, :, 0:R], in0=MTb[0][0:R, :, 0:R],
                                                 in1=g.identb[0:R, 0:R].unsqueeze(1).broadcast_to([R, KC, R]), op=ALU.add), reads=[MTb[0], g.identb], writes=[S.PTb])
            nlev = int(np.log2(C)) - 1
            cur = 0
            for j in range(1, nlev + 1):
                nx = 1 - cur
                ps = k.psn()
                for c in range(KC):
                    k.mm(ps[0:R, c * 128:c * 128 + R], MTb[cur][:, c, 0:R], Mb[cur][:, c, 0:R], True, True, reads=[MTb[cur], Mb[cur]], writes=[ps])
                k.copy(Mb[nx][0:R, :, 0:R], vR(ps), [ps], [Mb[nx]], eng=ACT)
                yield
                if j < nlev:
                    ps = k.psn()
                    for c in range(KC):
                        k.mm(ps[0:R, c * 128:c * 128 + R], Mb[cur][:, c, 0:R], MTb[cur][:, c, 0:R], True, True, reads=[MTb[cur], Mb[cur]], writes=[ps])
                    k.copy(MTb[nx][0:R, :, 0:R], vR(ps), [ps], [MTb[nx]], eng=ACT)
                    yield
                ps = k.psn()
                for c in range(KC):
                    k.mm(ps[0:R, c * 128:c * 128 + R], g.identb[:, 0:R], S.PTb[:, c, 0:R], True, False, reads=[g.identb, S.PTb], writes=[ps])
                    k.mm(ps[0:R, c * 128:c * 128 + R], Mb[nx][:, c, 0:R], S.PTb[:, c, 0:R], False, True, reads=[Mb[nx], S.PTb], writes=[ps])
                k.copy(S.PTb[0:R, :, 0:R], vR(ps), [ps], [S.PTb], eng=DVE)
                cur = nx
                yield

        def be2(ch, S, C, row0, seq, last):
            R = 2 * C; ntok = C
            ea = lambda c: S.Ear[:, c, 0, 0:R]; er = lambda c: S.Ear[:, c, 1, 0:R]
            v3 = lambda ps, w: ps[:, 0:512].rearrange("p (c t) -> p c t", t=64)[:, :, 0:w]
            v3b = lambda ps, w: ps[:, 512:1024].rearrange("p (c t) -> p c t", t=64)[:, :, 0:w]
            ps = k.psn()
            for c in range(KC):
                k.mm(ps[0:R, c * 128:(c + 1) * 128], ea(c), Hb[:, c, :], True, False, reads=[S.Ear, Hb], writes=[ps])
                k.mm(ps[0:R, c * 128:(c + 1) * 128], S.AakT[:, c, 0:R], S.VsT[:, c, :], False, True, reads=[S.AakT, S.VsT], writes=[ps])
            k.copy(Xb[0:R, :, :], ps[0:R, :].rearrange("p (c t) -> p c t", t=128), [ps], [Xb], eng=ACT)
            yield
            ps = k.psn()
            for c in range(KC):
                k.mm(ps[0:R, c * 128:(c + 1) * 128], S.PTb[:, c, 0:R], Xb[:, c, :], True, True, reads=[S.PTb, Xb], writes=[ps])
            k.copy(Ub[0:R, :, :], ps[0:R, :].rearrange("p (c t) -> p c t", t=128), [ps], [Ub], eng=ACT)
            yield
            ps = k.psn()
            for c in range(KC):
                k.mm(ps[:, c * 128:c * 128 + R], Hb[:, c, :], er(c), True, False, reads=[Hb, S.Ear], writes=[ps])
                k.mm(ps[:, c * 128:c * 128 + R], Ub[:, c, :], S.ArbT[:, c, 0:R], False, False, reads=[Ub, S.ArbT], writes=[ps])
                k.mm(ps[:, c * 128:c * 128 + R], S.VsT[:, c, :], S.ArkT[:, c, 0:R], False, True, reads=[S.VsT, S.ArkT], writes=[ps])
            psO = ps
            ps = k.psn()
            for c in range(KC):
                k.mm(ps[:, c * 128:(c + 1) * 128], S.EbT[:, c, :], Ub[:, c, :], True, False, reads=[S.EbT, Ub], writes=[ps])
                k.mm(ps[:, c * 128:(c + 1) * 128], S.EkT[:, c, :], S.VsT[:, c, :], False, True, reads=[S.EkT, S.VsT], writes=[ps])
            k.op(DVE, lambda e: e.tensor_tensor(out=Hx32[:], in0=ps[:, :].rearrange("p (c t) -> p c t", t=128), in1=Hx32[:], op=ALU.add), reads=[ps, Hx32], writes=[Hx32])
            k.op(P2E, lambda e: e.tensor_tensor(out=Hx32[:], in0=Hx32[:], in1=S.wc[:, :].unsqueeze(2).broadcast_to([128, KC, 128]), op=ALU.mult),
                 reads=[Hx32, S.wc], writes=[Hx32])
            k.copy(Hb[:], Hx32[:], [Hx32], [Hb], eng=ACT)
            yield
            for hp in range(2):
                rw = slice(64 * hp, 64 * hp + 64)
                k.copy(oT[rw, :, 0:ntok], psO[rw, :].rearrange("p (c t) -> p c t", t=128)[:, :, hp * C:(hp + 1) * C], [psO], [oT],
                       eng=(ACT if hp == 0 else DVE))
            yield
            k.op(ACT, lambda e: e.activation(out=obf[:, :, 0:ntok], in_=oT[:, :, 0:ntok], func=AF.Copy), reads=[oT], writes=[obf])
            k.op(ACT, lambda e: e.activation(out=osq[:, :, 0:ntok], in_=oT[:, :, 0:ntok], func=AF.Square), reads=[oT], writes=[osq])
            ps = k.psn()
            k.mm(ps[:, 0:512], onesbd64[:, :], obf[:, :, :].rearrange("p c t -> p (c t)"), True, True, reads=[onesbd64, obf], writes=[ps])
            k.mm(ps[:, 512:1024], onesbd64[:, :], osq[:, :, :].rearrange("p c t -> p (c t)"), True, True, reads=[onesbd64, osq], writes=[ps])
            yield
            k.op(ACT, lambda e: e.activation(out=tm[:, :, 0:ntok], in_=v3(ps, ntok), func=AF.Square), reads=[ps], writes=[tm])
            k.op(DVE, lambda e: e.tensor_tensor(out=tm[:, :, 0:ntok], in0=v3b(ps, ntok), in1=tm[:, :, 0:ntok], op=ALU.subtract), reads=[ps, tm], writes=[tm])
            k.op(ACT, lambda e: e.activation(out=tm[:, :, 0:ntok], in_=tm[:, :, 0:ntok], func=AF.Ln, bias=epsg[:, 0:1]), reads=[tm, epsg], writes=[tm])
            k.op(ACT, lambda e: e.activation(out=tm[:, :, 0:ntok], in_=tm[:, :, 0:ntok], func=AF.Exp, scale=-0.5), reads=[tm], writes=[tm])
            k.op(DVE, lambda e: e.tensor_tensor(out=oT[:, :, 0:ntok], in0=oT[:, :, 0:ntok], in1=v3(ps, ntok), op=ALU.subtract), reads=[oT, ps], writes=[oT])
            yield
            k.op(P2E, lambda e: e.tensor_tensor(out=oT[:, :, 0:ntok], in0=oT[:, :, 0:ntok], in1=tm[:, :, 0:ntok], op=ALU.mult), reads=[oT, tm], writes=[oT])
            k.op(P2E, lambda e: e.tensor_tensor(out=oT[:, :, 0:ntok], in0=oT[:, :, 0:ntok], in1=gngc[:, :].unsqueeze(2).broadcast_to([128, KC, ntok]), op=ALU.mult),
                 reads=[oT, gngc], writes=[oT])
            k.op(P2E, lambda e: e.tensor_tensor(out=oT[:, :, 0:ntok], in0=oT[:, :, 0:ntok], in1=gnbc[:, :].unsqueeze(2).broadcast_to([128, KC, ntok]), op=ALU.add),
                 reads=[oT, gnbc], writes=[oT])
            k.op(P2E, lambda e: e.tensor_tensor(out=oT[:, :, 0:ntok], in0=oT[:, :, 0:ntok], in1=S.Bon[:, :, 0:ntok], op=ALU.add), reads=[oT, S.Bon], writes=[oT])
            k.op(P2E, lambda e: e.tensor_tensor(out=yT[:, :, 0:ntok], in0=oT[:, :, 0:ntok], in1=S.Gb[:, :, 0:ntok], op=ALU.mult), reads=[oT, S.Gb], writes=[yT])
            yield
            ps = k.psn()
            for hf in range(2):
                for c in range(KC):
                    k.mm(ps[0:ntok, hf * 512:(hf + 1) * 512], yT[:, c, 0:ntok], Wout[:, c, hf * 512:(hf + 1) * 512], c == 0, c == KC - 1,
                         reads=[yT, Wout], writes=[ps])
            yield
            ln_epilogue(k, g, ps, S.x32[0:ntok, 0, :], S.x32, ntok, dst, row0, ALPHA)
            if last:
                rl = row0 + ntok - 1
                k.dma(k.POOL, g.o_shift.t[seq:seq + 1, :], src.t[rl:rl + 1, :], reads=[src], writes=[g.o_shift])
                ps = k.psn()
                for c in range(KC):
                    k.tr(ps[:, c * 128:(c + 1) * 128], Hx32[:, c, :], g.ident[:, :], reads=[Hx32, g.ident], writes=[ps])
                k.copy(Sx32[:], ps[:, :].rearrange("p (c t) -> p c t", t=128), [ps], [Sx32], eng=DVE)
                for hp in range(2):
                    k.dma(k.POOL, g.o_wkv.t[seq].rearrange("(c hp) v kk -> hp v c kk", hp=2)[hp],
                          Sx32[64 * hp:64 * hp + 64, :, 64 * hp:64 * hp + 64], reads=[Sx32], writes=[g.o_wkv])
            yield

        for seq in range(3):
            ci = 0 if seq == 0 else 1
            C = 64 if seq == 0 else 32
            chunks = [((n // 2, n % 2), n * 64) for n in range(TP // 64)] if seq == 0 else [((TP // 128 + seq - 1, 0), TP + (seq - 1) * TS)]
            zero_all()
            k.dma(k.SP, msk[:], g.c_msk.t[ci], writes=[msk])
            k.op(DVE, lambda e: e.memset(Hx32[:], 0.0), writes=[Hx32])
            if seq > 0:
                k.op(DVE, lambda e: e.memset(Sx32[:], 0.0), writes=[Sx32])
                for hp in range(2):
                    k.dma(k.SP, Sx32[64 * hp:64 * hp + 64, :, 64 * hp:64 * hp + 64],
                          g.st_wkv.t[seq - 1].rearrange("(c hp) v kk -> hp v c kk", hp=2)[hp], writes=[Sx32])
                ps = k.psn()
                for c in range(KC):
                    k.tr(ps[:, c * 128:(c + 1) * 128], Sx32[:, c, :], g.ident[:, :], reads=[Sx32, g.ident], writes=[ps])
                k.copy(Hx32[:], ps[:, :].rearrange("p (c t) -> p c t", t=128), [ps], [Hx32], eng=DVE)
            k.copy(Hb[:], Hx32[:], [Hx32], [Hb], eng=ACT)
            for _ in fe2(chunks[0][0], sets[0], C, chunks[0][1]):
                pass
            for i, (ch, row0) in enumerate(chunks):
                nxt = fe2(chunks[i + 1][0], sets[(i + 1) % 2], C, chunks[i + 1][1]) if i + 1 < len(chunks) else None
                _interleave(nxt, be2(ch, sets[i % 2], C, row0, seq, i == len(chunks) - 1), ra=1000, rb=1)
        k.barrier()


def build(TP, nstage=8):
    nc = bass.Bass("TRN2", target_bir_lowering=False)
    k = KB(nc)
    g = declare_io(k, TP)
    setup_globals(k, g)
    stages = [
        lambda s, d: stage_ffn(k, g, TP, 0, 0, s, d),
        lambda s, d: stage_mixer_ab(k, g, TP, s, d),
        lambda s, d: stage_xattn(k, g, TP, 0, s, d),
        lambda s, d: stage_ffn(k, g, TP, 0, 1, s, d),
        lambda s, d: stage_ffn(k, g, TP, 1, 0, s, d),
        lambda s, d: stage_rwkv(k, g, TP, s, d),
        lambda s, d: stage_xattn(k, g, TP, 1, s, d),
        lambda s, d: stage_ffn(k, g, TP, 1, 1, s, d),
    ][:nstage]
    src = g.x
    for i, stf in enumerate(stages):
        dst = g.y if i == len(stages) - 1 else g.xs[i % 2]
        stf(src, dst)
        src = dst
    k.finish()
    return nc


def host_consts(TP):
    c = {}
    c["c_ident"] = np.eye(128, dtype=np.float32)
    half = 64
    inv_freq = (10000.0 ** (-np.arange(half, dtype=np.float32) / np.float32(half))).astype(np.float32)
    pos = np.concatenate([np.arange(TP), PAST + np.arange(TS)]).astype(np.float32)
    ang = (pos[:, None] * inv_freq[None, :]).astype(np.float32)
    cos = np.cos(ang.astype(np.float64)).astype(np.float32).T
    sin = np.sin(ang.astype(np.float64)).astype(np.float32).T
    c["c_cos"] = np.ascontiguousarray(np.concatenate([cos, cos], 0))
    c["c_sin"] = np.ascontiguousarray(np.concatenate([-sin, sin], 0))
    M = np.zeros((2, 128, 4, 128), np.float32); XI = np.zeros((2, 128, 4, 128), np.float32); Z = np.zeros((2, 128, 4, 128), np.float32)
    for ci, C in enumerate((128, 32)):
        idx = np.arange(C, dtype=np.float64)
        for h in range(4):
            lg = np.log1p(-(2.0 ** (-5.0 - h)))
            m = np.where(idx[None, :] >= idx[:, None], np.exp(-lg * C), 0.0)
            M[ci, :C, h, :C] = m
            XI[ci, :, h, :C] = np.exp(lg * (idx + 1.0))[None, :]
            Z[ci, :, h, :C] = (np.exp(lg * (C - 1.0 - idx)) * 128 ** -0.5)[None, :]
    c["c_retM"] = M; c["c_retXI"] = XI; c["c_retZ"] = Z
    msk = np.zeros((2, 128, 3, 128), np.float32)
    for ci, C in enumerate((64, 32)):
        for hp in range(2):
            for s in range(C):
                msk[ci, hp * C + s, 0, hp * C + s + 1: hp * C + C] = 1.0
                msk[ci, hp * C + s, 1, hp * C + s: hp * C + C] = 1.0
        msk[ci, :, 2, :] = msk[ci, :, 0, :].T
    c["c_msk"] = msk
    ob = np.zeros((128, 128), np.float32); ob[:64, :64] = 1.0; ob[64:, 64:] = 1.0
    c["c_onesbd"] = ob
    return c


_W_NAMES = ["ln_g", "ln_b", "ffn_up", "ffn_down", "xa_q", "xa_k", "xa_v", "xa_o", "l0_w_in", "l0_conv_w", "l0_conv_b",
            "l0_lru_wa", "l0_lru_ba", "l0_lru_wx", "l0_lru_bx", "l0_lru_lambda", "l0_ret_gn_g", "l0_ret_gn_b", "l0_w_out",
            "l1_mu", "l1_w_rkv", "l1_w0", "l1_w1", "l1_w2", "l1_a0", "l1_a1", "l1_a2", "l1_g1", "l1_g2", "l1_k_k", "l1_k_a",
            "l1_gn_g", "l1_gn_b", "l1_w_out"]


def make_in_maps(inp, TP):
    f = lambda a: np.ascontiguousarray(np.asarray(a, dtype=np.float32))
    shared = {n: f(inp[n]) for n in _W_NAMES}
    shared["l1_r_k"] = f(inp["l1_r_k"]).reshape(-1)
    w_in = f(inp["l0_w_in"])
    rot = []
    for base in (1024, 1536):
        for h in range(4):
            b0 = base + h * 128
            rot.append(w_in[:, b0 + 64: b0 + 128]); rot.append(w_in[:, b0: b0 + 64])
    shared["l0_w_rot"] = np.ascontiguousarray(np.concatenate(rot, axis=1))
    shared.update(host_consts(TP))
    maps = []
    for b in range(NCORES):
        m = dict(shared)
        m["x"] = np.ascontiguousarray(np.concatenate([f(inp["x_prompt"][b]), f(inp["x_sample"][2 * b]), f(inp["x_sample"][2 * b + 1])], 0))
        m["mem"] = f(inp["mem_prompt"][b])
        sl = slice(2 * b, 2 * b + 2)
        m["st_conv"] = f(inp["state_conv0"][sl]); m["st_lru"] = f(inp["state_lru0"][sl]); m["st_ret"] = f(inp["state_ret0"][sl])
        m["st_shift"] = f(inp["state_shift1"][sl]).reshape(2, D); m["st_wkv"] = f(inp["state_wkv1"][sl])
        m["ck"] = f(inp["cache_mem_k"][:, sl]).reshape(2, 2, 256, D); m["cv"] = f(inp["cache_mem_v"][:, sl]).reshape(2, 2, 256, D)
        maps.append(m)
    return maps


_NC_CACHE = {}


def run(inp, TP, nstage=8, ncores=NCORES):
    key = (TP, nstage)
    if key not in _NC_CACHE:
        _NC_CACHE[key] = build(TP, nstage)
    nc = _NC_CACHE[key]
    res = run_bass_kernel_spmd(nc, make_in_maps(inp, TP)[:ncores], core_ids=list(range(ncores)))
    R = list(res.results)
    while len(R) < NCORES:
        R.append(R[0])
    st = lambda n: np.stack([r[n] for r in R], 0)
    y = st("y")
    y_p = y[:, :TP]
    y_s = y[:, TP:].reshape(NCORES * 2, TS, D)
    memk = st("o_memk").transpose(1, 0, 2, 3).reshape(2, NCORES, 256, 4, 256)
    memv = st("o_memv").transpose(1, 0, 2, 3).reshape(2, NCORES, 256, 4, 256)
    oc, ol, orr, osh, ow = st("o_conv"), st("o_lru"), st("o_ret"), st("o_shift"), st("o_wkv")
    pf = lambda a: np.ascontiguousarray(a[:, 0])
    sf = lambda a: np.ascontiguousarray(a[:, 1:3].reshape((NCORES * 2,) + a.shape[2:]))
    return (np.ascontiguousarray(y_p), np.ascontiguousarray(y_s), np.ascontiguousarray(memk), np.ascontiguousarray(memv),
            pf(oc), pf(ol), pf(orr), pf(osh)[:, None, :], pf(ow),
            sf(oc), sf(ol), sf(orr), sf(osh)[:, None, :], sf(ow))


def kernel(**inputs):
    TP = int(np.asarray(inputs["x_prompt"]).shape[1])
    return run(inputs, TP, 8)
```

```python
import contextlib
import numpy as np
import concourse.bass as bass
import concourse.mybir as mybir
from concourse.bass_utils import run_bass_kernel_spmd

F32 = mybir.dt.float32
BF16 = mybir.dt.bfloat16
AF = mybir.ActivationFunctionType
ALU = mybir.AluOpType

D = 1024
KC = 8
NCORES = 8
TS = 32
DFF = 2816
ALPHA = 4.0 ** 0.25
LN_EPS = 1e-5
PAST = 4096
SAME_ENGINE_WAITS = True


class Eng:
    def __init__(self, name, eng, sem):
        self.name = name
        self.eng = eng
        self.sem = sem
        self.count = 0
        self.known = {}


class Buf:
    def __init__(self, t, name):
        self.t = t
        self.name = name
        self.w = {}
        self.r = {}

    def __getitem__(self, key):
        return self.t[key]


class KB:
    def __init__(self, nc, n_dsem=32):
        self.nc = nc
        self.stack = contextlib.ExitStack()
        mk = lambda n: self.stack.enter_context(nc.semaphore(n))
        self.PE = Eng("pe", nc.tensor, mk("s_pe"))
        self.DVE = Eng("dve", nc.vector, mk("s_dve"))
        self.ACT = Eng("act", nc.scalar, mk("s_act"))
        self.POOL = Eng("pool", nc.gpsimd, mk("s_pool"))
        self.SP = Eng("sp", nc.sync, mk("s_sp"))
        self.engs = [self.PE, self.DVE, self.ACT, self.POOL, self.SP]
        self.dpools = {q.name: [[mk(f"s_d{q.name}{i}"), 0] for i in range(n_dsem // 2)] for q in (self.SP, self.POOL)}
        self.dnexts = {q.name: 0 for q in (self.SP, self.POOL)}
        self.dsems = [s for p in self.dpools.values() for s in p]
        self.nid = 0
        self.pspool = []
        self.psnext = 0
        self.flip = 0

    def sb(self, name, shape, dtype, stack=None):
        self.nid += 1
        t = (stack or self.stack).enter_context(self.nc.sbuf_tensor(f"{name}_{self.nid}", list(shape), dtype))
        return Buf(t, name)

    def ps(self, name, shape, dtype):
        self.nid += 1
        t = self.stack.enter_context(self.nc.psum_tensor(f"{name}_{self.nid}", list(shape), dtype))
        return Buf(t, name)

    def dram(self, name, shape, dtype, kind):
        t = self.nc.dram_tensor(name, list(shape), dtype, kind=kind)
        return Buf(t.ap(), name)

    def psn(self):
        b = self.pspool[self.psnext]
        self.psnext = (self.psnext + 1) % len(self.pspool)
        return b

    def _wait(self, E, sem, val):
        if val <= 0:
            return
        key = id(sem)
        if E.known.get(key, 0) >= val:
            return
        E.eng.wait_ge(sem, val)
        E.known[key] = val

    def _deps(self, E, reads, writes, same_engine=True):
        for b in reads:
            for (sem, val) in list(b.w.values()):
                if (not same_engine) and sem is E.sem:
                    continue
                self._wait(E, sem, val)
        for b in writes:
            for (sem, val) in list(b.w.values()) + list(b.r.values()):
                if (not same_engine) and sem is E.sem:
                    continue
                self._wait(E, sem, val)

    def _mark(self, sem, val, reads, writes):
        for b in reads:
            b.r[id(sem)] = (sem, val)
        for b in writes:
            b.r = {}
            b.w[id(sem)] = (sem, val)

    def op(self, E, emit, reads=(), writes=(), same_engine=SAME_ENGINE_WAITS):
        self._deps(E, reads, writes, same_engine)
        ins = emit(E.eng)
        E.count += 1
        ins.then_inc(E.sem, 1)
        self._mark(E.sem, E.count, reads, writes)
        return ins

    def mm(self, out_ap, lhsT, rhs, start, stop, reads=(), writes=()):
        return self.op(self.PE, lambda e: e.matmul(out_ap, lhsT=lhsT, rhs=rhs, start=start, stop=stop),
                       reads=reads, writes=writes, same_engine=False)

    def tr(self, out_ap, in_ap, ident_ap, reads=(), writes=()):
        return self.op(self.PE, lambda e: e.transpose(out_ap, in_ap, ident_ap),
                       reads=reads, writes=writes, same_engine=False)

    def dma(self, Q, out_ap, in_ap, reads=(), writes=(), **kw):
        self._deps(Q, reads, writes)
        pool = self.dpools[Q.name]
        slot = pool[self.dnexts[Q.name]]
        self.dnexts[Q.name] = (self.dnexts[Q.name] + 1) % len(pool)
        sem, val = slot
        self._wait(Q, sem, val)
        ins = Q.eng.dma_start(out=out_ap, in_=in_ap, **kw)
        slot[1] = val + 16
        ins.then_inc(sem, 16)
        self._mark(sem, val + 16, reads, writes)
        return ins

    def barrier(self):
        for E in self.engs:
            for F in self.engs:
                if F is not E:
                    self._wait(E, F.sem, F.count)
            for (sem, val) in self.dsems:
                self._wait(E, sem, val)

    def rotate(self, limit=20000):
        if max(E.count for E in self.engs) < limit:
            return
        self.barrier()
        for E in self.engs:
            self.nid += 1
            E.sem = self.stack.enter_context(self.nc.semaphore(f"s_{E.name}_{self.nid}"))
            E.count = 0

    def finish(self):
        for (sem, val) in self.dsems:
            self._wait(self.SP, sem, val)
        for F in self.engs:
            if F is not self.SP:
                self._wait(self.SP, F.sem, F.count)

    def copy(self, out_ap, in_ap, reads, writes, eng=None):
        if eng is None:
            self.flip ^= 1
            eng = self.ACT if self.flip else self.DVE
        if eng is self.ACT:
            return self.op(self.ACT, lambda e: e.activation(out=out_ap, in_=in_ap, func=AF.Copy), reads=reads, writes=writes)
        return self.op(eng, lambda e: e.tensor_copy(out=out_ap, in_=in_ap), reads=reads, writes=writes)


class Ctx:
    pass


def declare_io(k, TP):
    NT = TP + 2 * TS
    g = Ctx()
    I = lambda n, s: k.dram(n, s, F32, "ExternalInput")
    O = lambda n, s: k.dram(n, s, F32, "ExternalOutput")
    g.x = I("x", [NT, D]); g.mem = I("mem", [256, D])
    g.st_conv = I("st_conv", [2, 3, 512]); g.st_lru = I("st_lru", [2, 512]); g.st_ret = I("st_ret", [2, 4, 128, 128])
    g.st_shift = I("st_shift", [2, D]); g.st_wkv = I("st_wkv", [2, 16, 64, 64])
    g.ck = I("ck", [2, 2, 256, D]); g.cv = I("cv", [2, 2, 256, D])
    g.ln_g = I("ln_g", [2, 4, D]); g.ln_b = I("ln_b", [2, 4, D])
    g.ffn_up = I("ffn_up", [2, 2, D, 2 * DFF]); g.ffn_down = I("ffn_down", [2, 2, DFF, D])
    g.xa_q = I("xa_q", [2, D, D]); g.xa_k = I("xa_k", [2, D, D]); g.xa_v = I("xa_v", [2, D, D]); g.xa_o = I("xa_o", [2, D, D])
    g.w_in = I("l0_w_in", [D, 3072]); g.w_rot = I("l0_w_rot", [D, 1024])
    g.conv_w = I("l0_conv_w", [4, 512]); g.conv_b = I("l0_conv_b", [512])
    g.lru_wa = I("l0_lru_wa", [8, 64, 64]); g.lru_ba = I("l0_lru_ba", [512])
    g.lru_wx = I("l0_lru_wx", [8, 64, 64]); g.lru_bx = I("l0_lru_bx", [512]); g.lru_lam = I("l0_lru_lambda", [512])
    g.ret_g = I("l0_ret_gn_g", [512]); g.ret_b = I("l0_ret_gn_b", [512]); g.w_out0 = I("l0_w_out", [D, D])
    g.mu = I("l1_mu", [6, D]); g.w_rkv = I("l1_w_rkv", [3, D, D]); g.w0 = I("l1_w0", [D]); g.w1 = I("l1_w1", [D, 64]); g.w2 = I("l1_w2", [64, D])
    g.a0 = I("l1_a0", [D]); g.a1 = I("l1_a1", [D, 64]); g.a2 = I("l1_a2", [64, D]); g.g1 = I("l1_g1", [D, 128]); g.g2 = I("l1_g2", [128, D])
    g.k_k = I("l1_k_k", [D]); g.k_a = I("l1_k_a", [D]); g.r_k = I("l1_r_k", [D]); g.gn_g = I("l1_gn_g", [D]); g.gn_b = I("l1_gn_b", [D])
    g.w_out1 = I("l1_w_out", [D, D])
    g.c_ident = I("c_ident", [128, 128]); g.c_cos = I("c_cos", [128, TP + TS]); g.c_sin = I("c_sin", [128, TP + TS])
    g.c_retM = I("c_retM", [2, 128, 4, 128]); g.c_retXI = I("c_retXI", [2, 128, 4, 128]); g.c_retZ = I("c_retZ", [2, 128, 4, 128])
    g.c_msk = I("c_msk", [2, 128, 3, 128]); g.c_onesbd = I("c_onesbd", [128, 128])
    g.y = O("y", [NT, D]); g.o_memk = O("o_memk", [2, 256, D]); g.o_memv = O("o_memv", [2, 256, D])
    g.o_conv = O("o_conv", [3, 3, 512]); g.o_lru = O("o_lru", [3, 512]); g.o_ret = O("o_ret", [3, 4, 128, 128])
    g.o_shift = O("o_shift", [3, D]); g.o_wkv = O("o_wkv", [3, 16, 64, 64])
    g.xs = [k.dram("scr0", [NT, D], F32, "Internal"), k.dram("scr1", [NT, D], F32, "Internal")]
    return g


def load_cols(k, st, name, src_ap, ncol, rows=128):
    b = k.sb(name, [rows, ncol], F32, st)
    k.dma(k.SP, b[:], src_ap.rearrange("(c p) -> p c", p=rows), writes=[b], allow_slow_non_contiguous=True)
    return b


def load_weight(k, g, dst, src2d, K, N, kc0=0):
    nk = max(1, K // 128)
    rows = min(K, 128)
    for kc in range(nk):
        for n0 in range(0, N, 704):
            n1 = min(N, n0 + 704)
            stg = g.stage[g.stage_i % len(g.stage)]
            g.stage_i += 1
            k.dma(k.SP, stg[0:rows, 0:n1 - n0], src2d[kc * 128: kc * 128 + rows, n0:n1], writes=[stg])
            k.copy(dst[0:rows, kc0 + kc, n0:n1], stg[0:rows, 0:n1 - n0], [stg], [dst])


def load_tok(k, g, src, row0, PT, NS, pool):
    b = pool[g.tok_i % len(pool)]
    g.tok_i += 1
    k.dma(k.SP, b[0:PT, 0:NS, :], src.t[row0: row0 + PT * NS, :].rearrange("(s p) d -> p s d", p=PT), writes=[b])
    return b


def to_featmajor(k, g, x32, PT, NS, xT, col0=0):
    for s in range(NS):
        ps = k.psn()
        for c in range(KC):
            k.tr(ps[:, c * PT:(c + 1) * PT], x32[0:PT, s, c * 128:(c + 1) * 128], g.ident[0:PT, 0:PT],
                 reads=[x32, g.ident], writes=[ps])
        k.copy(xT[:, 0:KC, col0 + s * PT: col0 + (s + 1) * PT],
               ps[:, 0:KC * PT].rearrange("p (c t) -> p c t", t=PT), [ps], [xT])


def ln_epilogue(k, g, ps, base_ap, base_buf, PT, dst, row0, alpha, do_ln=True):
    y = g.ybuf[g.y_i % 2]; o = g.obuf[g.y_i % 2]; sm = g.small[g.y_i % 2]
    g.y_i += 1
    DVE, ACT = k.DVE, k.ACT
    k.op(DVE, lambda e: e.scalar_tensor_tensor(out=y[0:PT, :], in0=base_ap, scalar=float(alpha), in1=ps[0:PT, :],
                                               op0=ALU.mult, op1=ALU.add), reads=[base_buf, ps], writes=[y])
    if not do_ln:
        k.dma(k.POOL, dst.t[row0:row0 + PT, :], y[0:PT, :], reads=[y], writes=[dst])
        return
    k.op(DVE, lambda e: e.bn_stats(out=sm[0:PT, 0:6], in_=y[0:PT, 0:512]), reads=[y], writes=[sm])
    k.op(DVE, lambda e: e.bn_stats(out=sm[0:PT, 6:12], in_=y[0:PT, 512:1024]), reads=[y], writes=[sm])
    k.op(DVE, lambda e: e.bn_aggr(out=sm[0:PT, 12:14], in_=sm[0:PT, 0:12]), reads=[sm], writes=[sm])
    k.op(ACT, lambda e: e.activation(out=sm[0:PT, 14:15], in_=sm[0:PT, 13:14], func=AF.Ln, bias=g.eps_ln[0:PT, 0:1]),
         reads=[sm, g.eps_ln], writes=[sm])
    k.op(ACT, lambda e: e.activation(out=sm[0:PT, 14:15], in_=sm[0:PT, 14:15], func=AF.Exp, scale=-0.5), reads=[sm], writes=[sm])
    k.op(DVE, lambda e: e.tensor_scalar(out=sm[0:PT, 15:16], in0=sm[0:PT, 12:13], scalar1=-1.0, scalar2=sm[0:PT, 14:15],
                                        op0=ALU.mult, op1=ALU.mult), reads=[sm], writes=[sm])
    k.op(ACT, lambda e: e.activation(out=o[0:PT, :], in_=y[0:PT, :], func=AF.Identity, scale=sm[0:PT, 14:15],
                                     bias=sm[0:PT, 15:16]), reads=[y, sm], writes=[o])
    k.op(DVE, lambda e: e.tensor_tensor(out=o[0:PT, :], in0=o[0:PT, :], in1=g.gtab[0:PT, :], op=ALU.mult),
         reads=[o, g.gtab], writes=[o])
    k.op(DVE, lambda e: e.tensor_tensor(out=o[0:PT, :], in0=o[0:PT, :], in1=g.btab[0:PT, :], op=ALU.add),
         reads=[o, g.btab], writes=[o])
    k.dma(k.POOL, dst.t[row0:row0 + PT, :], o[0:PT, :], reads=[o], writes=[dst])


def load_ln_tabs(k, g, l, j):
    k.dma(k.SP, g.gtab[:], g.ln_g.t[l, j].partition_broadcast(128), writes=[g.gtab])
    k.dma(k.SP, g.btab[:], g.ln_b.t[l, j].partition_broadcast(128), writes=[g.btab])


def segs(TP, tile):
    out = [(r, tile, 0) for r in range(0, TP, tile)]
    out += [(TP, TS, 1), (TP + TS, TS, 2)]
    return out


def stage_ffn(k, g, TP, l, which, src, dst):
    TT = 256
    with contextlib.ExitStack() as st:
        Wg = k.sb("Wup", [128, KC, 2 * DFF], BF16, st)
        Wd = k.sb("Wd", [128, 22, D], BF16, st)
        load_ln_tabs(k, g, l, 0 if which == 0 else 3)
        load_weight(k, g, Wg, g.ffn_up.t[l, which], D, 2 * DFF)
        load_weight(k, g, Wd, g.ffn_down.t[l, which], DFF, D)
        xTs = [k.sb("xT", [128, KC, TT], BF16, st) for _ in range(2)]
        hT = k.sb("hT", [128, 22, TT], BF16, st)
        sgs = [k.sb("sg", [128, TT], F32, st) for _ in range(2)]
        toks = [k.sb("tok2", [128, 2, D], F32, st) for _ in range(2)]
        ffn_tiles = [(r, TT, 0) for r in range(0, TP, TT)] + [(TP, 2 * TS, 1)]
        for ti, (row0, ntok, seq) in enumerate(ffn_tiles):
            PT = min(128, ntok); NS = ntok // PT
            k.rotate()
            x32 = load_tok(k, g, src, row0, PT, NS, toks)
            xT = xTs[ti % 2]
            to_featmajor(k, g, x32, PT, NS, xT)
            for fc in range(22):
                ps = k.psn()
                for kc in range(KC):
                    k.mm(ps[:, 0:ntok], Wg[:, kc, fc * 128:(fc + 1) * 128], xT[:, kc, 0:ntok], kc == 0, kc == KC - 1,
                         reads=[Wg, xT], writes=[ps])
                for kc in range(KC):
                    k.mm(ps[:, 512:512 + ntok], Wg[:, kc, DFF + fc * 128: DFF + (fc + 1) * 128], xT[:, kc, 0:ntok],
                         kc == 0, kc == KC - 1, reads=[Wg, xT], writes=[ps])
                sg = sgs[fc % 2]
                k.op(k.ACT, lambda e: e.activation(out=sg[:, 0:ntok], in_=ps[:, 0:ntok], func=AF.Silu), reads=[ps], writes=[sg])
                k.op(k.DVE, lambda e: e.scalar_tensor_tensor(out=hT[:, fc, 0:ntok], in0=sg[:, 0:ntok], scalar=0.5,
                                                             in1=ps[:, 512:512 + ntok], op0=ALU.mult, op1=ALU.mult),
                     reads=[sg, ps], writes=[hT])
            for s in range(NS):
                ps = k.psn()
                for hf in range(2):
                    for fc in range(22):
                        k.mm(ps[0:PT, hf * 512:(hf + 1) * 512], hT[:, fc, s * PT:(s + 1) * PT], Wd[:, fc, hf * 512:(hf + 1) * 512],
                             fc == 0, fc == 21, reads=[hT, Wd], writes=[ps])
                ln_epilogue(k, g, ps, x32[0:PT, s, :], x32, PT, dst, row0 + s * PT, ALPHA)
        k.barrier()


def setup_globals(k, g):
    g.stage = [k.sb("stg", [128, 704], F32) for _ in range(2)]
    g.stage_i = 0
    g.tok_i = 0
    g.ybuf = [k.sb("yb", [128, D], F32) for _ in range(2)]
    g.obuf = [k.sb("ob", [128, D], F32) for _ in range(2)]
    g.small = [k.sb("sm", [128, 16], F32) for _ in range(2)]
    g.y_i = 0
    g.gtab = k.sb("gtab", [128, D], F32); g.btab = k.sb("btab", [128, D], F32)
    g.ident = k.sb("ident", [128, 128], F32)
    g.identb = k.sb("identb", [128, 128], BF16)
    g.eps_ln = k.sb("epsln", [128, 1], F32)
    k.dma(k.SP, g.ident[:], g.c_ident.t[:, :], writes=[g.ident])
    k.copy(g.identb[:], g.ident[:], [g.ident], [g.identb], eng=k.DVE)
    k.op(k.DVE, lambda e: e.memset(g.eps_ln[:], LN_EPS), writes=[g.eps_ln])
    g.one_c = k.sb("one_c", [128, 1], F32)
    k.op(k.DVE, lambda e: e.memset(g.one_c[:], 1.0), writes=[g.one_c])
    k.pspool = [k.ps("psp", [128, 1024], F32) for _ in range(4)]


def stage_xattn(k, g, TP, l, src, dst):
    DVE, ACT = k.DVE, k.ACT
    with contextlib.ExitStack() as st:
        Wq = k.sb("Wq", [128, KC, D], BF16, st); Wo = k.sb("Wo", [128, KC, D], BF16, st)
        Wk = k.sb("Wk", [128, KC, D], BF16, st); Wv = k.sb("Wv", [128, KC, D], BF16, st)
        load_ln_tabs(k, g, l, 2)
        load_weight(k, g, Wq, g.xa_q.t[l], D, D); load_weight(k, g, Wo, g.xa_o.t[l], D, D)
        load_weight(k, g, Wk, g.xa_k.t[l], D, D); load_weight(k, g, Wv, g.xa_v.t[l], D, D)
        ones = k.sb("ones", [128, 128], BF16, st)
        k.op(DVE, lambda e: e.memset(ones[:], 1.0), writes=[ones])
        m32 = k.sb("m32", [128, 2, D], F32, st)
        memT = k.sb("memT", [128, KC, 256], BF16, st)
        KTs = [k.sb("KT", [128, KC, 256], BF16, st) for _ in range(3)]
        Vts = [k.sb("Vt", [128, 2, D], BF16, st) for _ in range(3)]
        o32s = [k.sb("mo32", [128, D], F32, st) for _ in range(2)]
        k.dma(k.SP, m32[:], g.mem.t[:, :].rearrange("(s p) d -> p s d", p=128), writes=[m32])
        to_featmajor(k, g, m32, 128, 2, memT)
        for ec in range(KC):
            ps = k.psn()
            for kc in range(KC):
                k.mm(ps[:, 0:256], Wk[:, kc, ec * 128:(ec + 1) * 128], memT[:, kc, :], kc == 0, kc == KC - 1, reads=[Wk, memT], writes=[ps])
            k.copy(KTs[0][:, ec, :], ps[:, 0:256], [ps], [KTs[0]])
        oi = 0
        for (W, outd, isv) in ((Wk, g.o_memk, False), (Wv, g.o_memv, True)):
            for s in range(2):
                ps = k.psn()
                for hf in range(2):
                    for kc in range(KC):
                        k.mm(ps[:, hf * 512:(hf + 1) * 512], memT[:, kc, s * 128:(s + 1) * 128], W[:, kc, hf * 512:(hf + 1) * 512],
                             kc == 0, kc == KC - 1, reads=[W, memT], writes=[ps])
                o32 = o32s[oi % 2]; oi += 1
                k.copy(o32[:], ps[:, :], [ps], [o32])
                k.dma(k.POOL, outd.t[l, s * 128:(s + 1) * 128, :], o32[:], reads=[o32], writes=[outd])
                if isv:
                    k.copy(Vts[0][:, s, :], o32[:], [o32], [Vts[0]])
        for sq in range(2):
            k.dma(k.SP, m32[:], g.ck.t[l, sq].rearrange("(s p) d -> p s d", p=128), writes=[m32])
            to_featmajor(k, g, m32, 128, 2, KTs[1 + sq])
            k.dma(k.SP, m32[:], g.cv.t[l, sq].rearrange("(s p) d -> p s d", p=128), writes=[m32])
            k.copy(Vts[1 + sq][:], m32[:], [m32], [Vts[1 + sq]])
        XT = 512
        xTs = [k.sb("xT", [128, KC, XT], BF16, st) for _ in range(2)]
        qT = k.sb("qT", [128, KC, XT], BF16, st)
        pTs = [k.sb("pT", [128, 2, XT], BF16, st) for _ in range(2)]
        rdens = [k.sb("rden", [128, XT], F32, st) for _ in range(2)]
        oT = k.sb("oT", [128, KC, XT], BF16, st)
        toks = [k.sb("tok4", [128, 4, D], F32, st)]
        for ti, (row0, ntok, seq) in enumerate(segs(TP, XT)):
            PT = min(128, ntok); NS = ntok // PT
            k.rotate()
            x32 = load_tok(k, g, src, row0, PT, NS, toks)
            xT = xTs[ti % 2]
            to_featmajor(k, g, x32, PT, NS, xT)
            KT = KTs[seq]; Vt = Vts[seq]
            for ec in range(KC):
                ps = k.psn()
                for kc in range(KC):
                    k.mm(ps[:, 0:ntok], Wq[:, kc, ec * 128:(ec + 1) * 128], xT[:, kc, 0:ntok], kc == 0, kc == KC - 1, reads=[Wq, xT], writes=[ps])
                k.op(ACT, lambda e: e.activation(out=qT[:, ec, 0:ntok], in_=ps[:, 0:ntok], func=AF.Copy, scale=0.0625), reads=[ps], writes=[qT])
            for h in range(4):
                pT = pTs[h % 2]; rden = rdens[h % 2]
                for mc in range(2):
                    ps = k.psn()
                    for dc in range(2):
                        k.mm(ps[:, 0:ntok], KT[:, 2 * h + dc, mc * 128:(mc + 1) * 128], qT[:, 2 * h + dc, 0:ntok], dc == 0, dc == 1,
                             reads=[KT, qT], writes=[ps])
                    k.op(ACT, lambda e: e.activation(out=pT[:, mc, 0:ntok], in_=ps[:, 0:ntok], func=AF.Exp), reads=[ps], writes=[pT])
                ps = k.psn()
                for mc in range(2):
                    k.mm(ps[:, 0:ntok], ones[:, :], pT[:, mc, 0:ntok], mc == 0, mc == 1, reads=[ones, pT], writes=[ps])
                k.op(ACT, lambda e: e.activation(out=rden[:, 0:ntok], in_=ps[:, 0:ntok], func=AF.Ln), reads=[ps], writes=[rden])
                k.op(ACT, lambda e: e.activation(out=rden[:, 0:ntok], in_=rden[:, 0:ntok], func=AF.Exp, scale=-1.0), reads=[rden], writes=[rden])
                for dc in range(2):
                    ps = k.psn()
                    for mc in range(2):
                        k.mm(ps[:, 0:ntok], Vt[:, mc, (2 * h + dc) * 128:(2 * h + dc + 1) * 128], pT[:, mc, 0:ntok], mc == 0, mc == 1,
                             reads=[Vt, pT], writes=[ps])
                    k.op(DVE, lambda e: e.tensor_tensor(out=oT[:, 2 * h + dc, 0:ntok], in0=ps[:, 0:ntok], in1=rden[:, 0:ntok], op=ALU.mult),
                         reads=[ps, rden], writes=[oT])
            for s in range(NS):
                ps = k.psn()
                for hf in range(2):
                    for ec in range(KC):
                        k.mm(ps[0:PT, hf * 512:(hf + 1) * 512], oT[:, ec, s * PT:(s + 1) * PT], Wo[:, ec, hf * 512:(hf + 1) * 512],
                             ec == 0, ec == KC - 1, reads=[oT, Wo], writes=[ps])
                ln_epilogue(k, g, ps, x32[0:PT, s, :], x32, PT, dst, row0 + s * PT, ALPHA)
        k.barrier()


def stage_mixer_ab(k, g, TP, src, dst):
    DVE, ACT = k.DVE, k.ACT
    TT = 256
    with contextlib.ExitStack() as st:
        Win = k.sb("Win", [128, KC, 3072], BF16, st); Wrot = k.sb("Wrot", [128, KC, 1024], BF16, st)
        Wout = k.sb("Wout", [128, KC, D], BF16, st)
        load_ln_tabs(k, g, 0, 1)
        load_weight(k, g, Win, g.w_in.t, D, 3072); load_weight(k, g, Wrot, g.w_rot.t, D, 1024)
        load_weight(k, g, Wout, g.w_out0.t, D, D)
        bd32 = k.sb("bd32", [128, 2, 4, 128], F32, st); Wbd = k.sb("Wbd", [128, 2, 4, 128], BF16, st)
        k.op(DVE, lambda e: e.memset(bd32[:], 0.0), writes=[bd32])
        for wi, wsrc in enumerate((g.lru_wa, g.lru_wx)):
            for hp in range(2):
                k.dma(k.SP, bd32[64 * hp:64 * hp + 64, wi, :, 64 * hp:64 * hp + 64],
                      wsrc.t.rearrange("(c hp) i j -> hp i c j", hp=2)[hp], writes=[bd32])
        k.copy(Wbd[:], bd32[:], [bd32], [Wbd], eng=DVE)
        cw = k.sb("cw", [128, 4, 4], F32, st)
        for j in range(4):
            k.dma(k.SP, cw[:, :, j], g.conv_w.t[j].rearrange("(c p) -> p c", p=128), writes=[cw], allow_slow_non_contiguous=True)
        cb = load_cols(k, st, "cb", g.conv_b.t, 4); ba = load_cols(k, st, "ba", g.lru_ba.t, 4); bx = load_cols(k, st, "bx", g.lru_bx.t, 4)
        lam = load_cols(k, st, "lam", g.lru_lam.t, 4); gng = load_cols(k, st, "gng", g.ret_g.t, 4); gnb = load_cols(k, st, "gnb", g.ret_b.t, 4)
        cl = k.sb("cl", [128, 4], F32, st); cl2 = k.sb("cl2", [128, 4], F32, st)
        k.op(ACT, lambda e: e.activation(out=cl[:], in_=lam[:], func=AF.Exp, scale=-1.0), reads=[lam], writes=[cl])
        k.op(ACT, lambda e: e.activation(out=cl[:], in_=cl[:], func=AF.Ln, bias=g.one_c[:, 0:1]), reads=[cl, g.one_c], writes=[cl])
        k.op(DVE, lambda e: e.tensor_scalar(out=cl2[:], in0=cl[:], scalar1=-16.0, scalar2=None, op0=ALU.mult), reads=[cl], writes=[cl2])
        k.op(DVE, lambda e: e.tensor_scalar(out=cl[:], in0=cl[:], scalar1=-8.0, scalar2=None, op0=ALU.mult), reads=[cl], writes=[cl])
        ones = k.sb("ones", [128, 128], BF16, st)
        k.op(DVE, lambda e: e.memset(ones[:], 1.0 / 128.0), writes=[ones])
        epsc = k.sb("epsc", [128, 1], F32, st)
        k.op(DVE, lambda e: e.memset(epsc[:], LN_EPS), writes=[epsc])
        cosT = k.sb("cosT", [128, TT], F32, st); sinT = k.sb("sinT", [128, TT], F32, st)
        retM = k.sb("retM", [128, 4, 128], F32, st); retXI = k.sb("retXI", [128, 4, 128], F32, st); retZ = k.sb("retZ", [128, 4, 128], F32, st)
        xaT = k.sb("xaT", [128, 4, 3 + TT], F32, st); hl = k.sb("hl", [128, 4], F32, st)
        S32 = k.sb("S32", [128, 4, 128], F32, st); Sb = k.sb("Sb", [128, 4, 128], BF16, st)
        xTs = [k.sb("xT", [128, KC, TT], BF16, st) for _ in range(2)]
        toks = [k.sb("tok2", [128, 2, D], F32, st) for _ in range(2)]
        gaT = k.sb("gaT", [128, 4, TT], F32, st)
        qr = k.sb("qr", [128, 4, TT], BF16, st); kz = k.sb("kz", [128, 4, TT], BF16, st)
        t1s = [k.sb("t1", [128, TT], F32, st) for _ in range(2)]; t2s = [k.sb("t2", [128, TT], F32, st) for _ in range(2)]
        t3s = [k.sb("t3", [128, TT], F32, st) for _ in range(2)]
        Ktok = k.sb("Ktok", [128, 2, 4, 128], BF16, st); Vtok = k.sb("Vtok", [128, 2, 512], BF16, st)
        PTb = k.sb("PTb", [128, 4, 128], BF16, st)
        oT = k.sb("oT", [128, 4, TT], F32, st); obf = k.sb("obf", [128, 4, TT], BF16, st); osq = k.sb("osq", [128, 4, TT], BF16, st)
        lru = [[k.sb(n, [128, TT], (BF16 if n == "xcb" else F32), st) for n in ("xc", "xcb", "rr", "ii", "aa", "hh")] for _ in range(2)]
        yT = k.sb("yT", [128, KC, TT], BF16, st)
        T1 = k.sb("T1", [128, 4, TT], F32, st); T3 = k.sb("T3", [128, 4, TT], F32, st)
        cur_seq = -1
        allsegs = segs(TP, TT)
        for ti, (row0, ntok, seq) in enumerate(allsegs):
            PT = min(128, ntok); NS = ntok // PT
            k.rotate()
            C = PT; nch = NS
            ci = 0 if seq == 0 else 1
            last = (ti + 1 == len(allsegs)) or (allsegs[ti + 1][2] != seq)
            if seq != cur_seq:
                cur_seq = seq
                k.op(DVE, lambda e: e.memset(PTb[:], 0.0), writes=[PTb])
                k.op(DVE, lambda e: e.memset(Ktok[:], 0.0), writes=[Ktok])
                k.op(DVE, lambda e: e.memset(Vtok[:], 0.0), writes=[Vtok])
                k.dma(k.SP, retM[:], g.c_retM.t[ci], writes=[retM]); k.dma(k.SP, retXI[:], g.c_retXI.t[ci], writes=[retXI])
                k.dma(k.SP, retZ[:], g.c_retZ.t[ci], writes=[retZ])
                if seq == 0:
                    k.op(DVE, lambda e: e.memset(xaT[:], 0.0), writes=[xaT])
                    k.op(DVE, lambda e: e.memset(hl[:], 0.0), writes=[hl])
                    k.op(DVE, lambda e: e.memset(S32[:], 0.0), writes=[S32])
                else:
                    for j in range(3):
                        k.dma(k.SP, xaT[:, :, j], g.st_conv.t[seq - 1, j].rearrange("(c p) -> p c", p=128), writes=[xaT], allow_slow_non_contiguous=True)
                    k.dma(k.SP, hl[:], g.st_lru.t[seq - 1].rearrange("(c p) -> p c", p=128), writes=[hl], allow_slow_non_contiguous=True)
                    k.dma(k.SP, S32[:], g.st_ret.t[seq - 1].rearrange("h d v -> d h v"), writes=[S32])
                k.copy(Sb[:], S32[:], [S32], [Sb], eng=ACT)
            pos0 = row0 if seq == 0 else TP
            k.dma(k.SP, cosT[:, 0:ntok], g.c_cos.t[:, pos0:pos0 + ntok], writes=[cosT])
            k.dma(k.SP, sinT[:, 0:ntok], g.c_sin.t[:, pos0:pos0 + ntok], writes=[sinT])
            x32 = load_tok(k, g, src, row0, PT, NS, toks)
            xT = xTs[ti % 2]
            to_featmajor(k, g, x32, PT, NS, xT)

            def proj(W, col0):
                ps = k.psn()
                for kc in range(KC):
                    k.mm(ps[:, 0:ntok], W[:, kc, col0:col0 + 128], xT[:, kc, 0:ntok], kc == 0, kc == KC - 1, reads=[W, xT], writes=[ps])
                return ps
            for c in range(4):
                ps = proj(Win, c * 128)
                k.copy(xaT[:, c, 3:3 + ntok], ps[:, 0:ntok], [ps], [xaT], eng=ACT)
                ps = proj(Win, 512 + c * 128)
                k.op(ACT, lambda e: e.activation(out=gaT[:, c, 0:ntok], in_=ps[:, 0:ntok], func=AF.Gelu_apprx_tanh), reads=[ps], writes=[gaT])
            for (dst_b, base, rbase, isk) in ((qr, 1024, 0, False), (kz, 1536, 512, True)):
                for h in range(4):
                    t1, t2, t3 = t1s[h % 2], t2s[h % 2], t3s[h % 2]
                    ps = proj(Win, base + h * 128)
                    ps2 = proj(Wrot, rbase + h * 128)
                    k.op(DVE, lambda e: e.tensor_tensor(out=t1[:, 0:ntok], in0=ps[:, 0:ntok], in1=cosT[:, 0:ntok], op=ALU.mult), reads=[ps, cosT], writes=[t1])
                    k.op(DVE, lambda e: e.tensor_tensor(out=t2[:, 0:ntok], in0=ps2[:, 0:ntok], in1=sinT[:, 0:ntok], op=ALU.mult), reads=[ps2, sinT], writes=[t2])
                    if not isk:
                        k.op(DVE, lambda e: e.tensor_tensor(out=qr[:, h, 0:ntok], in0=t1[:, 0:ntok], in1=t2[:, 0:ntok], op=ALU.add), reads=[t1, t2], writes=[qr])
                    else:
                        k.op(DVE, lambda e: e.tensor_tensor(out=t3[:, 0:ntok], in0=t1[:, 0:ntok], in1=t2[:, 0:ntok], op=ALU.add), reads=[t1, t2], writes=[t3])
                        k.op(DVE, lambda e: e.tensor_tensor(out=kz[:, h, 0:ntok].rearrange("p (n c) -> p n c", c=C),
                                                            in0=t3[:, 0:ntok].rearrange("p (n c) -> p n c", c=C),
                                                            in1=retZ[:, h, 0:C].unsqueeze(1).broadcast_to([128, nch, C]), op=ALU.mult),
                             reads=[t3, retZ], writes=[kz])
            for n in range(nch):
                cs = slice(n * C, (n + 1) * C)
                ps = k.psn()
                for kc in range(KC):
                    k.mm(ps[0:C, 0:512], xT[:, kc, cs], Win[:, kc, 2048:2560], kc == 0, kc == KC - 1, reads=[xT, Win], writes=[ps])
                k.copy(Vtok[0:C, n % 2, :], ps[0:C, 0:512], [ps], [Vtok])
                ps = k.psn()
                psb = ps.t[:, 0:256].bitcast(BF16)
                for h in range(4):
                    k.tr(psb[0:C, h * 128:(h + 1) * 128], kz[:, h, cs], g.identb[:, :], reads=[kz, g.identb], writes=[ps])
                k.copy(Ktok[0:C, n % 2, :, :], psb[0:C, 0:512].rearrange("p (h d) -> p h d", d=128), [ps], [Ktok])
                ps = k.psn()
                for h in range(4):
                    k.mm(ps[0:C, h * 128:h * 128 + C], kz[:, h, cs], qr[:, h, cs], True, True, reads=[kz, qr], writes=[ps])
                k.op(DVE, lambda e: e.tensor_tensor(out=PTb[0:C, :, 0:C], in0=ps[0:C, 0:512].rearrange("p (h c) -> p h c", c=128)[:, :, 0:C],
                                                    in1=retM[0:C, :, 0:C], op=ALU.mult), reads=[ps, retM], writes=[PTb])
                ps = k.psn()
                for h in range(4):
                    k.mm(ps[:, h * 128:h * 128 + C], Vtok[:, n % 2, h * 128:(h + 1) * 128], PTb[:, h, 0:C], True, False, reads=[Vtok, PTb], writes=[ps])
                    k.mm(ps[:, h * 128:h * 128 + C], Sb[:, h, :], qr[:, h, cs], False, True, reads=[Sb, qr], writes=[ps])
                k.op(DVE, lambda e: e.tensor_tensor(out=oT[:, :, cs], in0=ps[:, 0:512].rearrange("p (h c) -> p h c", c=128)[:, :, 0:C],
                                                    in1=retXI[:, :, 0:C], op=ALU.mult), reads=[ps, retXI], writes=[oT])
                ps = k.psn()
                for h in range(4):
                    k.mm(ps[:, h * 128:(h + 1) * 128], Ktok[:, n % 2, h, :], Vtok[:, n % 2, h * 128:(h + 1) * 128], True, True, reads=[Ktok, Vtok], writes=[ps])
                for h in range(4):
                    gam = float(np.exp(np.log1p(-(2.0 ** (-5.0 - h))) * C))
                    k.op(DVE, lambda e: e.scalar_tensor_tensor(out=S32[:, h, :], in0=S32[:, h, :], scalar=gam, in1=ps[:, h * 128:(h + 1) * 128],
                                                               op0=ALU.mult, op1=ALU.add), reads=[S32, ps], writes=[S32])
                k.copy(Sb[:], S32[:], [S32], [Sb], eng=ACT)
            k.op(ACT, lambda e: e.activation(out=obf[:, :, 0:ntok], in_=oT[:, :, 0:ntok], func=AF.Copy), reads=[oT], writes=[obf])
            k.op(ACT, lambda e: e.activation(out=osq[:, :, 0:ntok], in_=oT[:, :, 0:ntok], func=AF.Square), reads=[oT], writes=[osq])
            vH = lambda ps_: ps_[:, :].rearrange("p (h t) -> p h t", t=TT)[:, :, 0:ntok]
            psm = k.psn(); psq = k.psn()
            for h in range(4):
                k.mm(psm[:, h * TT:h * TT + ntok], ones[:, :], obf[:, h, 0:ntok], True, True, reads=[ones, obf], writes=[psm])
            for h in range(4):
                k.mm(psq[:, h * TT:h * TT + ntok], ones[:, :], osq[:, h, 0:ntok], True, True, reads=[ones, osq], writes=[psq])
            k.op(ACT, lambda e: e.activation(out=T1[:, :, 0:ntok], in_=vH(psm), func=AF.Square), reads=[psm], writes=[T1])
            k.op(DVE, lambda e: e.tensor_tensor(out=T1[:, :, 0:ntok], in0=vH(psq), in1=T1[:, :, 0:ntok], op=ALU.subtract), reads=[psq, T1], writes=[T1])
            k.op(ACT, lambda e: e.activation(out=T1[:, :, 0:ntok], in_=T1[:, :, 0:ntok], func=AF.Ln, bias=epsc[:, 0:1]), reads=[T1, epsc], writes=[T1])
            k.op(ACT, lambda e: e.activation(out=T1[:, :, 0:ntok], in_=T1[:, :, 0:ntok], func=AF.Exp, scale=-0.5), reads=[T1], writes=[T1])
            k.op(DVE, lambda e: e.tensor_tensor(out=oT[:, :, 0:ntok], in0=oT[:, :, 0:ntok], in1=vH(psm), op=ALU.subtract), reads=[oT, psm], writes=[oT])
            k.op(DVE, lambda e: e.tensor_tensor(out=oT[:, :, 0:ntok], in0=oT[:, :, 0:ntok], in1=T1[:, :, 0:ntok], op=ALU.mult), reads=[oT, T1], writes=[oT])
            for h in range(4):
                k.op(DVE, lambda e: e.tensor_scalar(out=oT[:, h, 0:ntok], in0=oT[:, h, 0:ntok], scalar1=gng[:, h:h + 1], scalar2=gnb[:, h:h + 1],
                                                    op0=ALU.mult, op1=ALU.add), reads=[oT, gng, gnb], writes=[oT])
            for h in range(4):
                ps = proj(Win, 2560 + h * 128)
                k.op(ACT, lambda e: e.activation(out=T3[:, h, 0:ntok], in_=ps[:, 0:ntok], func=AF.Silu), reads=[ps], writes=[T3])
            k.op(DVE, lambda e: e.tensor_tensor(out=yT[:, 4:8, 0:ntok], in0=oT[:, :, 0:ntok], in1=T3[:, :, 0:ntok], op=ALU.mult), reads=[oT, T3], writes=[yT])
            for c in range(4):
                xc, xcb, rr, ii, aa, hh = lru[c % 2]
                k.op(DVE, lambda e: e.tensor_scalar(out=xc[:, 0:ntok], in0=xaT[:, c, 0:ntok], scalar1=cw[:, c, 0:1], scalar2=cb[:, c:c + 1],
                                                    op0=ALU.mult, op1=ALU.add), reads=[xaT, cw, cb], writes=[xc])
                for j in range(1, 4):
                    k.op(DVE, lambda e: e.scalar_tensor_tensor(out=xc[:, 0:ntok], in0=xaT[:, c, j:j + ntok], scalar=cw[:, c, j:j + 1], in1=xc[:, 0:ntok],
                                                               op0=ALU.mult, op1=ALU.add), reads=[xaT, cw, xc], writes=[xc])
                k.copy(xcb[:, 0:ntok], xc[:, 0:ntok], [xc], [xcb], eng=ACT)
                ps = k.psn()
                k.mm(ps[:, 0:ntok], Wbd[:, 0, c, :], xcb[:, 0:ntok], True, True, reads=[Wbd, xcb], writes=[ps])
                k.mm(ps[:, 512:512 + ntok], Wbd[:, 1, c, :], xcb[:, 0:ntok], True, True, reads=[Wbd, xcb], writes=[ps])
                k.op(ACT, lambda e: e.activation(out=rr[:, 0:ntok], in_=ps[:, 0:ntok], func=AF.Sigmoid, bias=ba[:, c:c + 1]), reads=[ps, ba], writes=[rr])
                k.op(ACT, lambda e: e.activation(out=ii[:, 0:ntok], in_=ps[:, 512:512 + ntok], func=AF.Sigmoid, bias=bx[:, c:c + 1]), reads=[ps, bx], writes=[ii])
                k.op(ACT, lambda e: e.activation(out=aa[:, 0:ntok], in_=rr[:, 0:ntok], func=AF.Exp, scale=cl[:, c:c + 1]), reads=[rr, cl], writes=[aa])
                k.op(ACT, lambda e: e.activation(out=rr[:, 0:ntok], in_=rr[:, 0:ntok], func=AF.Exp, scale=cl2[:, c:c + 1]), reads=[rr, cl2], writes=[rr])
                k.op(ACT, lambda e: e.activation(out=rr[:, 0:ntok], in_=rr[:, 0:ntok], func=AF.Ln, scale=-1.0, bias=g.one_c[:, 0:1]), reads=[rr, g.one_c], writes=[rr])
                k.op(ACT, lambda e: e.activation(out=rr[:, 0:ntok], in_=rr[:, 0:ntok], func=AF.Exp, scale=0.5), reads=[rr], writes=[rr])
                k.op(DVE, lambda e: e.tensor_tensor(out=ii[:, 0:ntok], in0=ii[:, 0:ntok], in1=xc[:, 0:ntok], op=ALU.mult), reads=[ii, xc], writes=[ii])
                k.op(DVE, lambda e: e.tensor_tensor(out=ii[:, 0:ntok], in0=ii[:, 0:ntok], in1=rr[:, 0:ntok], op=ALU.mult), reads=[ii, rr], writes=[ii])
                k.op(DVE, lambda e: e.tensor_tensor_scan(out=hh[:, 0:ntok], data0=aa[:, 0:ntok], data1=ii[:, 0:ntok], initial=hl[:, c:c + 1],
                                                         op0=ALU.mult, op1=ALU.add), reads=[aa, ii, hl], writes=[hh])
                k.op(DVE, lambda e: e.tensor_copy(out=hl[:, c:c + 1], in_=hh[:, ntok - 1:ntok]), reads=[hh], writes=[hl])
                k.op(DVE, lambda e: e.tensor_tensor(out=yT[:, c, 0:ntok], in0=hh[:, 0:ntok], in1=gaT[:, c, 0:ntok], op=ALU.mult), reads=[hh, gaT], writes=[yT])
            k.op(DVE, lambda e: e.tensor_copy(out=xaT[:, :, 0:3], in_=xaT[:, :, ntok:ntok + 3]), reads=[xaT], writes=[xaT])
            for s in range(NS):
                ps = k.psn()
                for hf in range(2):
                    for c in range(KC):
                        k.mm(ps[0:PT, hf * 512:(hf + 1) * 512], yT[:, c, s * PT:(s + 1) * PT], Wout[:, c, hf * 512:(hf + 1) * 512],
                             c == 0, c == KC - 1, reads=[yT, Wout], writes=[ps])
                ln_epilogue(k, g, ps, x32[0:PT, s, :], x32, PT, dst, row0 + s * PT, ALPHA)
            if last:
                for j in range(3):
                    k.dma(k.POOL, g.o_conv.t[seq, j].rearrange("(c p) -> p c", p=128), xaT[:, :, j], reads=[xaT], writes=[g.o_conv], allow_slow_non_contiguous=True)
                k.dma(k.POOL, g.o_lru.t[seq].rearrange("(c p) -> p c", p=128), hl[:], reads=[hl], writes=[g.o_lru], allow_slow_non_contiguous=True)
                k.dma(k.POOL, g.o_ret.t[seq].rearrange("h d v -> d h v"), S32[:], reads=[S32], writes=[g.o_ret])
        k.barrier()


def _interleave(a, b, ra=1, rb=1):
    alive_a, alive_b = a is not None, b is not None
    while alive_a or alive_b:
        for _ in range(ra):
            if alive_a:
                try:
                    next(a)
                except StopIteration:
                    alive_a = False
        for _ in range(rb):
            if alive_b:
                try:
                    next(b)
                except StopIteration:
                    alive_b = False


def stage_rwkv(k, g, TP, src, dst):
    DVE, ACT, POOL = k.DVE, k.ACT, k.DVE
    DK = float(np.exp(-0.5))
    NT1 = TP // 128 + 2
    opnd = k.dram("rw_opnd", [NT1, 7, 128, 1024], BF16, "Internal")
    wcd = k.dram("rw_wc", [NT1, 128, 2, KC], F32, "Internal")
    with contextlib.ExitStack() as st:
        Wr = k.sb("Wr", [128, KC, D], BF16, st); Wk = k.sb("Wk", [128, KC, D], BF16, st); Wv = k.sb("Wv", [128, KC, D], BF16, st)
        for i, W in enumerate((Wr, Wk, Wv)):
            load_weight(k, g, W, g.w_rkv.t[i], D, D)
        w1 = k.sb("w1", [128, KC, 64], BF16, st); a1 = k.sb("a1", [128, KC, 64], BF16, st); g1 = k.sb("g1", [128, KC, 128], BF16, st)
        w2 = k.sb("w2", [128, 1, D], BF16, st); a2 = k.sb("a2", [128, 1, D], BF16, st); g2 = k.sb("g2", [128, 1, D], BF16, st)
        load_weight(k, g, w1, g.w1.t, D, 64); load_weight(k, g, a1, g.a1.t, D, 64); load_weight(k, g, g1, g.g1.t, D, 128)
        load_weight(k, g, w2, g.w2.t, 64, D); load_weight(k, g, a2, g.a2.t, 64, D); load_weight(k, g, g2, g.g2.t, 128, D)
        mu = k.sb("mu", [128, 6, KC], F32, st)
        for p in range(6):
            k.dma(k.SP, mu[:, p, :], g.mu.t[p].rearrange("(c p) -> p c", p=128), writes=[mu], allow_slow_non_contiguous=True)
        w0c = load_cols(k, st, "w0c", g.w0.t, 8); a0c = load_cols(k, st, "a0c", g.a0.t, 8); kkc = load_cols(k, st, "kkc", g.k_k.t, 8)
        kac = load_cols(k, st, "kac", g.k_a.t, 8); rkc = load_cols(k, st, "rkc", g.r_k.t, 8)
        ob32 = k.sb("ob32", [128, 128], F32, st); onesbd = k.sb("onesbd", [128, 128], BF16, st)
        k.dma(k.SP, ob32[:], g.c_onesbd.t[:, :], writes=[ob32])
        k.copy(onesbd[:], ob32[:], [ob32], [onesbd], eng=DVE)
        onesf = k.sb("onesf", [128, 64], F32, st)
        k.op(DVE, lambda e: e.memset(onesf[:], 1.0), writes=[onesf])
        xprev = k.sb("xprev", [128, KC], F32, st)
        TT = 128
        toks = [k.sb("tok1", [128, 1, D], F32, st) for _ in range(2)]
        xT32 = k.sb("xT32", [128, KC, 1 + TT], F32, st); dd = k.sb("dd", [128, KC, TT], F32, st)
        xms = [k.sb("xm", [128, KC, TT], BF16, st) for _ in range(2)]
        F = lambda n: k.sb(n, [128, KC, TT], F32, st)
        iface = [[F(n + str(par)) for n in ("rT", "kT", "vT", "sg", "ic")] for par in range(2)]
        Lc, Ep, Em, Ea, kk, kf, tm, Lm = [F(n) for n in ("Lc", "Ep", "Em", "Ea", "kk", "kf", "tm", "Lm")]
        kkn = kk
        B = lambda n: k.sb(n, [128, KC, TT], BF16, st)
        kk2, rkb = B("kk2"), B("rkb")
        _o = [B(f"o{j}") for j in range(7)]
        _g1 = B("o5b")
        outs = [_o, _o[:5] + [_g1] + _o[6:]]
        th = k.sb("th", [128, TT], BF16, st)
        wcs = [k.sb("wcs", [128, 2, KC], F32, st) for _ in range(2)]
        p1segs = segs(TP, TT)

        def front(ti):
            row0, ntok, seq = p1segs[ti]
            PT = ntok
            k.rotate()
            rT, kT, vT, sg, ic = iface[ti % 2]
            At, Rt, Kt, Bt, Vb, Gb, Bon = outs[ti % 2]
            if ti == 0 or p1segs[ti - 1][2] != seq:
                if seq == 0:
                    k.op(DVE, lambda e: e.memset(xprev[:], 0.0), writes=[xprev])
                else:
                    k.dma(k.SP, xprev[:], g.st_shift.t[seq - 1].rearrange("(c p) -> p c", p=128), writes=[xprev], allow_slow_non_contiguous=True)
            x32 = load_tok(k, g, src, row0, PT, 1, toks)
            k.op(DVE, lambda e: e.tensor_copy(out=xT32[:, :, 0], in_=xprev[:, :]), reads=[xprev], writes=[xT32])
            to_featmajor(k, g, x32, PT, 1, xT32, col0=1)
            k.op(DVE, lambda e: e.tensor_copy(out=xprev[:, :], in_=xT32[:, :, ntok]), reads=[xT32], writes=[xprev])
            k.op(DVE, lambda e: e.tensor_tensor(out=dd[:, :, 0:ntok], in0=xT32[:, :, 0:ntok], in1=xT32[:, :, 1:1 + ntok], op=ALU.subtract),
                 reads=[xT32], writes=[dd])

            def mix(p):
                xm = xms[p % 2]
                for c in range(KC):
                    k.op(DVE, lambda e: e.scalar_tensor_tensor(out=xm[:, c, 0:ntok], in0=dd[:, c, 0:ntok], scalar=mu[:, p, c:c + 1],
                                                               in1=xT32[:, c, 1:1 + ntok], op0=ALU.mult, op1=ALU.add), reads=[dd, mu, xT32], writes=[xm])
                return xm

            def proj_full(W, xm, dstb):
                for ec in range(KC):
                    ps = k.psn()
                    for kc in range(KC):
                        k.mm(ps[:, 0:ntok], W[:, kc, ec * 128:(ec + 1) * 128], xm[:, kc, 0:ntok], kc == 0, kc == KC - 1, reads=[W, xm], writes=[ps])
                    k.copy(dstb[:, ec, 0:ntok], ps[:, 0:ntok], [ps], [dstb], eng=ACT)

            def lora(xm, wA, nA, wB, func1, emit2):
                ps = k.psn()
                for kc in range(KC):
                    k.mm(ps[0:nA, 0:ntok], wA[:, kc, :], xm[:, kc, 0:ntok], kc == 0, kc == KC - 1, reads=[wA, xm], writes=[ps])
                k.op(ACT, lambda e: e.activation(out=th[0:nA, 0:ntok], in_=ps[0:nA, 0:ntok], func=func1), reads=[ps], writes=[th])
                for c in range(KC):
                    ps2 = k.psn()
                    k.mm(ps2[:, 0:ntok], wB[0:nA, 0, c * 128:(c + 1) * 128], th[0:nA, 0:ntok], True, True, reads=[wB, th], writes=[ps2])
                    emit2(c, ps2)

            proj_full(Wr, mix(0), rT); proj_full(Wk, mix(1), kT); proj_full(Wv, mix(2), vT)
            lora(mix(3), w1, 64, w2, AF.Tanh, lambda c, ps2: k.op(ACT, lambda e: e.activation(
                out=sg[:, c, 0:ntok], in_=ps2[:, 0:ntok], func=AF.Sigmoid, bias=w0c[:, c:c + 1]), reads=[ps2, w0c], writes=[sg]))
            lora(mix(4), a1, 64, a2, AF.Copy, lambda c, ps2: k.op(ACT, lambda e: e.activation(
                out=ic[:, c, 0:ntok], in_=ps2[:, 0:ntok], func=AF.Sigmoid, bias=a0c[:, c:c + 1]), reads=[ps2, a0c], writes=[ic]))
            lora(mix(5), g1, 128, g2, AF.Sigmoid, lambda c, ps2: k.copy(Gb[:, c, 0:ntok], ps2[:, 0:ntok], [ps2], [Gb], eng=ACT))

        def back(ti):
            row0, ntok, seq = p1segs[ti]
            C = min(64, ntok); nch = ntok // C
            chunk0 = row0 // 64 if seq == 0 else TP // 64 + (seq - 1)
            rT, kT, vT, sg, ic = iface[ti % 2]
            At, Rt, Kt, Bt, Vb, Gb, Bon = outs[ti % 2]
            v3 = lambda ps: ps[:, :].rearrange("p (c t) -> p c t", t=128)[:, :, 0:ntok]
            for c in range(KC):
                for n in range(nch):
                    k.op(DVE, lambda e: e.tensor_tensor_scan(out=Lc[:, c, n * C:(n + 1) * C], data0=onesf[:, 0:C], data1=sg[:, c, n * C:(n + 1) * C],
                                                             initial=0.0, op0=ALU.mult, op1=ALU.add), reads=[onesf, sg], writes=[Lc])
            k.op(POOL, lambda e: e.tensor_tensor(out=Lm[:, :, 0:ntok], in0=Lc[:, :, 0:ntok], in1=sg[:, :, 0:ntok], op=ALU.subtract), reads=[Lc, sg], writes=[Lm])
            k.op(ACT, lambda e: e.activation(out=Ep[:, :, 0:ntok], in_=Lc[:, :, 0:ntok], func=AF.Exp, scale=-DK), reads=[Lc], writes=[Ep])
            k.op(ACT, lambda e: e.activation(out=Em[:, :, 0:ntok], in_=Lc[:, :, 0:ntok], func=AF.Exp, scale=DK), reads=[Lc], writes=[Em])
            k.op(ACT, lambda e: e.activation(out=Ea[:, :, 0:ntok], in_=Lm[:, :, 0:ntok], func=AF.Exp, scale=-DK), reads=[Lm], writes=[Ea])
            for c in range(KC):
                k.op(DVE, lambda e: e.tensor_scalar(out=kk[:, c, 0:ntok], in0=kT[:, c, 0:ntok], scalar1=kkc[:, c:c + 1], scalar2=None, op0=ALU.mult),
                     reads=[kT, kkc], writes=[kk])
            k.op(ACT, lambda e: e.activation(out=kk2[:, :, 0:ntok], in_=kk[:, :, 0:ntok], func=AF.Square), reads=[kk], writes=[kk2])
            ps = k.psn()
            if ntok == 128:
                for hf in range(2):
                    k.mm(ps[:, hf * 512:(hf + 1) * 512], onesbd[:, :], kk2[:, 4 * hf:4 * hf + 4, :].rearrange("p c t -> p (c t)"), True, True,
                         reads=[onesbd, kk2], writes=[ps])
            else:
                for c in range(KC):
                    k.mm(ps[:, c * 128:c * 128 + ntok], onesbd[:, :], kk2[:, c, 0:ntok], True, True, reads=[onesbd, kk2], writes=[ps])
            k.op(DVE, lambda e: e.tensor_scalar(out=tm[:, :, 0:ntok], in0=v3(ps), scalar1=1e-24, scalar2=None, op0=ALU.max), reads=[ps], writes=[tm])
            k.op(ACT, lambda e: e.activation(out=tm[:, :, 0:ntok], in_=tm[:, :, 0:ntok], func=AF.Ln), reads=[tm], writes=[tm])
            k.op(ACT, lambda e: e.activation(out=tm[:, :, 0:ntok], in_=tm[:, :, 0:ntok], func=AF.Exp, scale=-0.5), reads=[tm], writes=[tm])
            k.op(POOL, lambda e: e.tensor_tensor(out=kkn[:, :, 0:ntok], in0=kk[:, :, 0:ntok], in1=tm[:, :, 0:ntok], op=ALU.mult), reads=[kk, tm], writes=[kkn])
            for c in range(KC):
                k.op(DVE, lambda e: e.tensor_scalar(out=tm[:, c, 0:ntok], in0=ic[:, c, 0:ntok], scalar1=-1.0, scalar2=kac[:, c:c + 1],
                                                    op0=ALU.add, op1=ALU.mult), reads=[ic, kac], writes=[tm])
            k.op(DVE, lambda e: e.scalar_tensor_tensor(out=kf[:, :, 0:ntok], in0=tm[:, :, 0:ntok], scalar=1.0, in1=kT[:, :, 0:ntok],
                                                       op0=ALU.add, op1=ALU.mult), reads=[tm, kT], writes=[kf])
            for c in range(KC):
                k.op(DVE, lambda e: e.scalar_tensor_tensor(out=rkb[:, c, 0:ntok], in0=rT[:, c, 0:ntok], scalar=rkc[:, c:c + 1], in1=kf[:, c, 0:ntok],
                                                           op0=ALU.mult, op1=ALU.mult), reads=[rT, rkc, kf], writes=[rkb])
            ps = k.psn()
            if ntok == 128:
                for hf in range(2):
                    k.mm(ps[:, hf * 512:(hf + 1) * 512], onesbd[:, :], rkb[:, 4 * hf:4 * hf + 4, :].rearrange("p c t -> p (c t)"), True, True,
                         reads=[onesbd, rkb], writes=[ps])
            else:
                for c in range(KC):
                    k.mm(ps[:, c * 128:c * 128 + ntok], onesbd[:, :], rkb[:, c, 0:ntok], True, True, reads=[onesbd, rkb], writes=[ps])
            k.op(DVE, lambda e: e.tensor_tensor(out=Bon[:, :, 0:ntok], in0=v3(ps), in1=vT[:, :, 0:ntok], op=ALU.mult), reads=[ps, vT], writes=[Bon])
            k.op(DVE, lambda e: e.scalar_tensor_tensor(out=At[:, :, 0:ntok], in0=kkn[:, :, 0:ntok], scalar=-1.0, in1=Ea[:, :, 0:ntok],
                                                       op0=ALU.mult, op1=ALU.mult), reads=[kkn, Ea], writes=[At])
            k.op(POOL, lambda e: e.tensor_tensor(out=Rt[:, :, 0:ntok], in0=rT[:, :, 0:ntok], in1=Ep[:, :, 0:ntok], op=ALU.mult), reads=[rT, Ep], writes=[Rt])
            k.op(POOL, lambda e: e.tensor_tensor(out=Kt[:, :, 0:ntok], in0=kf[:, :, 0:ntok], in1=Em[:, :, 0:ntok], op=ALU.mult), reads=[kf, Em], writes=[Kt])
            k.op(POOL, lambda e: e.tensor_tensor(out=tm[:, :, 0:ntok], in0=kkn[:, :, 0:ntok], in1=ic[:, :, 0:ntok], op=ALU.mult), reads=[kkn, ic], writes=[tm])
            k.op(POOL, lambda e: e.tensor_tensor(out=Bt[:, :, 0:ntok], in0=tm[:, :, 0:ntok], in1=Em[:, :, 0:ntok], op=ALU.mult), reads=[tm, Em], writes=[Bt])
            k.copy(Vb[:, :, 0:ntok], vT[:, :, 0:ntok], [vT], [Vb], eng=ACT)
            for j, ob in enumerate(outs[ti % 2]):
                k.dma(k.POOL, opnd.t[ti, j].rearrange("p (c t) -> p c t", t=128)[:, :, 0:ntok], ob[:, :, 0:ntok], reads=[ob], writes=[opnd])
            wcb = wcs[ti % 2]
            for n in range(nch):
                k.op(DVE, lambda e: e.tensor_copy(out=wcb[:, n, :], in_=Ep[:, :, (n + 1) * C - 1]), reads=[Ep], writes=[wcb])
            k.dma(k.POOL, wcd.t[ti][:, 0:nch, :], wcb[:, 0:nch, :], reads=[wcb], writes=[wcd])

        front(0)
        for ti in range(len(p1segs)):
            if ti + 1 < len(p1segs):
                front(ti + 1)
            back(ti)
        k.barrier()
    P2E = DVE
    with contextlib.ExitStack() as st:
        Wout = k.sb("Wout", [128, KC, D], BF16, st)
        load_ln_tabs(k, g, 1, 1)
        load_weight(k, g, Wout, g.w_out1.t, D, D)
        gngc = load_cols(k, st, "gngc", g.gn_g.t, 8); gnbc = load_cols(k, st, "gnbc", g.gn_b.t, 8)
        ob32 = k.sb("ob32", [128, 128], F32, st); onesbd64 = k.sb("onesbd64", [128, 128], BF16, st)
        k.dma(k.SP, ob32[:], g.c_onesbd.t[:, :], writes=[ob32])
        k.op(ACT, lambda e: e.activation(out=onesbd64[:], in_=ob32[:], func=AF.Copy, scale=1.0 / 64.0), reads=[ob32], writes=[onesbd64])
        epsg = k.sb("epsg", [128, 1], F32, st)
        k.op(DVE, lambda e: e.memset(epsg[:], 64e-5), writes=[epsg])
        msk = k.sb("msk", [128, 3, 128], F32, st)
        Hx32 = k.sb("Hx32", [128, KC, 128], F32, st); Hb = k.sb("Hb", [128, KC, 128], BF16, st)
        Sx32 = k.sb("Sx32", [128, KC, 128], F32, st)
        X = lambda n: k.sb(n, [128, KC, 128], BF16, st)
        sets = []
        for par in range(2):
            s_ = Ctx()
            s_.Ear = k.sb("Ear", [128, KC, 2, 128], BF16, st)
            s_.Eb, s_.Ek, s_.Ev = X("Eb"), X("Ek"), X("Ev")
            s_.Gb = k.sb("Gb", [128, KC, 64], BF16, st); s_.Bon = k.sb("Bon", [128, KC, 64], BF16, st)
            s_.wc = k.sb("wc", [128, KC], F32, st); s_.x32 = k.sb("x32", [128, 1, D], F32, st)
            s_.VsT, s_.EbT, s_.EkT, s_.ArbT, s_.AakT, s_.ArkT, s_.PTb = [X(n) for n in ("VsT", "EbT", "EkT", "ArbT", "AakT", "ArkT", "PTb")]
            sets.append(s_)
        Mb = [X("Mb0"), X("Mb1")]; MTb = [X("MTb0"), X("MTb1")]
        Xb, Ub = X("Xb"), X("Ub")
        oT = k.sb("oT", [128, KC, 64], F32, st); tm = k.sb("tm", [128, KC, 64], F32, st)
        obf = k.sb("obf", [128, KC, 64], BF16, st); osq = k.sb("osq", [128, KC, 64], BF16, st); yT = k.sb("yT", [128, KC, 64], BF16, st)

        def zero_all():
            for s_ in sets:
                for zb in (s_.Ear, s_.Eb, s_.Ek, s_.Ev, s_.VsT, s_.EbT, s_.EkT, s_.ArbT, s_.AakT, s_.ArkT, s_.PTb):
                    k.op(DVE, lambda e: e.memset(zb[:], 0.0), writes=[zb])
            for zb in (Xb, Ub, Mb[0], Mb[1], MTb[0], MTb[1], obf, osq):
                k.op(DVE, lambda e: e.memset(zb[:], 0.0), writes=[zb])

        def fe2(ch, S, C, row0):
            tl, nn = ch
            R = 2 * C
            vR = lambda ps: ps[0:R, :].rearrange("p (c t) -> p c t", t=128)[:, :, 0:R]
            k.rotate()
            for hp in range(2):
                rw = slice(64 * hp, 64 * hp + 64); cl = slice(hp * C, (hp + 1) * C)
                src3 = lambda j: opnd.t[tl, j, 64 * hp:64 * hp + 64, :].rearrange("p (c t) -> p c t", t=128)[:, :, nn * 64:nn * 64 + C]
                k.dma(k.SP, S.Ear[rw, :, 0, cl], src3(0), reads=[opnd], writes=[S.Ear])
                k.dma(k.SP, S.Ear[rw, :, 1, cl], src3(1), reads=[opnd], writes=[S.Ear])
                k.dma(k.SP, S.Ek[rw, :, cl], src3(2), reads=[opnd], writes=[S.Ek])
                k.dma(k.SP, S.Eb[rw, :, cl], src3(3), reads=[opnd], writes=[S.Eb])
                k.dma(k.SP, S.Ev[rw, :, cl], src3(4), reads=[opnd], writes=[S.Ev])
            k.dma(k.SP, S.Gb[:, :, 0:C], opnd.t[tl, 5].rearrange("p (c t) -> p c t", t=128)[:, :, nn * 64:nn * 64 + C], reads=[opnd], writes=[S.Gb])
            k.dma(k.SP, S.Bon[:, :, 0:C], opnd.t[tl, 6].rearrange("p (c t) -> p c t", t=128)[:, :, nn * 64:nn * 64 + C], reads=[opnd], writes=[S.Bon])
            k.dma(k.SP, S.wc[:], wcd.t[tl][:, nn, :], reads=[wcd], writes=[S.wc])
            k.dma(k.SP, S.x32[0:C, 0, :], src.t[row0:row0 + C, :], reads=[src], writes=[S.x32])
            yield
            for (srcb, dstb) in ((S.Ev, S.VsT), (S.Eb, S.EbT), (S.Ek, S.EkT)):
                ps = k.psn()
                psb = ps.t[:, 0:512].bitcast(BF16)
                for c in range(KC):
                    k.tr(psb[0:R, c * 128:(c + 1) * 128], srcb[:, c, 0:R], g.identb[:, :], reads=[srcb, g.identb], writes=[ps])
                k.copy(dstb[0:R, :, :], psb[0:R, :].rearrange("p (c t) -> p c t", t=128), [ps], [dstb])
                yield
            mb = lambda j: msk[0:R, j, 0:R].unsqueeze(1).broadcast_to([R, KC, R])
            ea = lambda c: S.Ear[:, c, 0, 0:R]; er = lambda c: S.Ear[:, c, 1, 0:R]
            eb = lambda c: S.Eb[:, c, 0:R]; ek = lambda c: S.Ek[:, c, 0:R]
            for (lhs_sel, rhs_sel, dstb, mj) in ((ea, eb, Mb[0], 2), (eb, ea, MTb[0], 0), (eb, er, S.ArbT, 1), (ek, ea, S.AakT, 0), (ek, er, S.ArkT, 1)):
                ps = k.psn()
                for c in range(KC):
                    k.mm(ps[0:R, c * 128:c * 128 + R], lhs_sel(c), rhs_sel(c), True, True, reads=[S.Ear, S.Eb, S.Ek], writes=[ps])
                k.op(DVE, lambda e: e.tensor_tensor(out=dstb[0:R, :, 0:R], in0=vR(ps), in1=mb(mj), op=ALU.mult), reads=[ps, msk], writes=[dstb])
                yield
            k.op(P2E, lambda e: e.tensor_tensor(out=S.PTb[0:R, :, 0:R], in0=MTb[0][0:R, :, 0:R],
                                                 in1=g.identb[0:R, 0:R].unsqueeze(1).broadcast_to([R, KC, R]), op=ALU.add), reads=[MTb[0], g.identb], writes=[S.PTb])
            nlev = int(np.log2(C)) - 1
            cur = 0
            for j in range(1, nlev + 1):
                nx = 1 - cur
                ps = k.psn()
                for c in range(KC):
                    k.mm(ps[0:R, c * 128:c * 128 + R], MTb[cur][:, c, 0:R], Mb[cur][:, c, 0:R], True, True, reads=[MTb[cur], Mb[cur]], writes=[ps])
                k.copy(Mb[nx][0:R, :, 0:R], vR(ps), [ps], [Mb[nx]], eng=ACT)
                yield
                if j < nlev:
                    ps = k.psn()
                    for c in range(KC):
                        k.mm(ps[0:R, c * 128:c * 128 + R], Mb[cur][:, c, 0:R], MTb[cur][:, c, 0:R], True, True, reads=[MTb[cur], Mb[cur]], writes=[ps])
                    k.copy(MTb[nx][0:R, :, 0:R], vR(ps), [ps], [MTb[nx]], eng=ACT)
                    yield
                ps = k.psn()
                for c in range(KC):
                    k.mm(ps[0:R, c * 128:c * 128 + R], g.identb[:, 0:R], S.PTb[:, c, 0:R], True, False, reads=[g.identb, S.PTb], writes=[ps])
                    k.mm(ps[0:R, c * 128:c * 128 + R], Mb[nx][:, c, 0:R], S.PTb[:, c, 0:R], False, True, reads=[Mb[nx], S.PTb], writes=[ps])
                k.copy(S.PTb[0:R, :, 0:R], vR(ps), [ps], [S.PTb], eng=DVE)
                cur = nx
                yield

        def be2(ch, S, C, row0, seq, last):
            R = 2 * C; ntok = C
            ea = lambda c: S.Ear[:, c, 0, 0:R]; er = lambda c: S.Ear[:, c, 1, 0:R]
            v3 = lambda ps, w: ps[:, 0:512].rearrange("p (c t) -> p c t", t=64)[:, :, 0:w]
            v3b = lambda ps, w: ps[:, 512:1024].rearrange("p (c t) -> p c t", t=64)[:, :, 0:w]
            ps = k.psn()
            for c in range(KC):
                k.mm(ps[0:R, c * 128:(c + 1) * 128], ea(c), Hb[:, c, :], True, False, reads=[S.Ear, Hb], writes=[ps])
                k.mm(ps[0:R, c * 128:(c + 1) * 128], S.AakT[:, c, 0:R], S.VsT[:, c, :], False, True, reads=[S.AakT, S.VsT], writes=[ps])
            k.copy(Xb[0:R, :, :], ps[0:R, :].rearrange("p (c t) -> p c t", t=128), [ps], [Xb], eng=ACT)
            yield
            ps = k.psn()
            for c in range(KC):
                k.mm(ps[0:R, c * 128:(c + 1) * 128], S.PTb[:, c, 0:R], Xb[:, c, :], True, True, reads=[S.PTb, Xb], writes=[ps])
            k.copy(Ub[0:R, :, :], ps[0:R, :].rearrange("p (c t) -> p c t", t=128), [ps], [Ub], eng=ACT)
            yield
            ps = k.psn()
            for c in range(KC):
                k.mm(ps[:, c * 128:c * 128 + R], Hb[:, c, :], er(c), True, False, reads=[Hb, S.Ear], writes=[ps])
                k.mm(ps[:, c * 128:c * 128 + R], Ub[:, c, :], S.ArbT[:, c, 0:R], False, False, reads=[Ub, S.ArbT], writes=[ps])
                k.mm(ps[:, c * 128:c * 128 + R], S.VsT[:, c, :], S.ArkT[:, c, 0:R], False, True, reads=[S.VsT, S.ArkT], writes=[ps])
            psO = ps
            ps = k.psn()
            for c in range(KC):
                k.mm(ps[:, c * 128:(c + 1) * 128], S.EbT[:, c, :], Ub[:, c, :], True, False, reads=[S.EbT, Ub], writes=[ps])
                k.mm(ps[:, c * 128:(c + 1) * 128], S.EkT[:, c, :], S.VsT[:, c, :], False, True, reads=[S.EkT, S.VsT], writes=[ps])
            k.op(DVE, lambda e: e.tensor_tensor(out=Hx32[:], in0=ps[:, :].rearrange("p (c t) -> p c t", t=128), in1=Hx32[:], op=ALU.add), reads=[ps, Hx32], writes=[Hx32])
            k.op(P2E, lambda e: e.tensor_tensor(out=Hx32[:], in0=Hx32[:], in1=S.wc[:, :].unsqueeze(2).broadcast_to([128, KC, 128]), op=ALU.mult),
                 reads=[Hx32, S.wc], writes=[Hx32])
            k.copy(Hb[:], Hx32[:], [Hx32], [Hb], eng=ACT)
            yield
            for hp in range(2):
                rw = slice(64 * hp, 64 * hp + 64)
                k.copy(oT[rw, :, 0:ntok], psO[rw, :].rearrange("p (c t) -> p c t", t=128)[:, :, hp * C:(hp + 1) * C], [psO], [oT],
                       eng=(ACT if hp == 0 else DVE))
            yield
            k.op(ACT, lambda e: e.activation(out=obf[:, :, 0:ntok], in_=oT[:, :, 0:ntok], func=AF.Copy), reads=[oT], writes=[obf])
            k.op(ACT, lambda e: e.activation(out=osq[:, :, 0:ntok], in_=oT[:, :, 0:ntok], func=AF.Square), reads=[oT], writes=[osq])
            ps = k.psn()
            k.mm(ps[:, 0:512], onesbd64[:, :], obf[:, :, :].rearrange("p c t -> p (c t)"), True, True, reads=[onesbd64, obf], writes=[ps])
            k.mm(ps[:, 512:1024], onesbd64[:, :], osq[:, :, :].rearrange("p c t -> p (c t)"), True, True, reads=[onesbd64, osq], writes=[ps])
            yield
            k.op(ACT, lambda e: e.activation(out=tm[:, :, 0:ntok], in_=v3(ps, ntok), func=AF.Square), reads=[ps], writes=[tm])
            k.op(DVE, lambda e: e.tensor_tensor(out=tm[:, :, 0:ntok], in0=v3b(ps, ntok), in1=tm[:, :, 0:ntok], op=ALU.subtract), reads=[ps, tm], writes=[tm])
            k.op(ACT, lambda e: e.activation(out=tm[:, :, 0:ntok], in_=tm[:, :, 0:ntok], func=AF.Ln, bias=epsg[:, 0:1]), reads=[tm, epsg], writes=[tm])
            k.op(ACT, lambda e: e.activation(out=tm[:, :, 0:ntok], in_=tm[:, :, 0:ntok], func=AF.Exp, scale=-0.5), reads=[tm], writes=[tm])
            k.op(DVE, lambda e: e.tensor_tensor(out=oT[:, :, 0:ntok], in0=oT[:, :, 0:ntok], in1=v3(ps, ntok), op=ALU.subtract), reads=[oT, ps], writes=[oT])
            yield
            k.op(P2E, lambda e: e.tensor_tensor(out=oT[:, :, 0:ntok], in0=oT[:, :, 0:ntok], in1=tm[:, :, 0:ntok], op=ALU.mult), reads=[oT, tm], writes=[oT])
            k.op(P2E, lambda e: e.tensor_tensor(out=oT[:, :, 0:ntok], in0=oT[:, :, 0:ntok], in1=gngc[:, :].unsqueeze(2).broadcast_to([128, KC, ntok]), op=ALU.mult),
                 reads=[oT, gngc], writes=[oT])
            k.op(P2E, lambda e: e.tensor_tensor(out=oT[:, :, 0:ntok], in0=oT[:, :, 0:ntok], in1=gnbc[:, :].unsqueeze(2).broadcast_to([128, KC, ntok]), op=ALU.add),
                 reads=[oT, gnbc], writes=[oT])
            k.op(P2E, lambda e: e.tensor_tensor(out=oT[:, :, 0:ntok], in0=oT[:, :, 0:ntok], in1=S.Bon[:, :, 0:ntok], op=ALU.add), reads=[oT, S.Bon], writes=[oT])
            k.op(P2E, lambda e: e.tensor_tensor(out=yT[:, :, 0:ntok], in0=oT[:, :, 0:ntok], in1=S.Gb[:, :, 0:ntok], op=ALU.mult), reads=[oT, S.Gb], writes=[yT])
            yield
            ps = k.psn()
            for hf in range(2):
                for c in range(KC):
                    k.mm(ps[0:ntok, hf * 512:(hf + 1) * 512], yT[:, c, 0:ntok], Wout[:, c, hf * 512:(hf + 1) * 512], c == 0, c == KC - 1,
                         reads=[yT, Wout], writes=[ps])
            yield
            ln_epilogue(k, g, ps, S.x32[0:ntok, 0, :], S.x32, ntok, dst, row0, ALPHA)
            if last:
                rl = row0 + ntok - 1
                k.dma(k.POOL, g.o_shift.t[seq:seq + 1, :], src.t[rl:rl + 1, :], reads=[src], writes=[g.o_shift])
                ps = k.psn()
                for c in range(KC):
                    k.tr(ps[:, c * 128:(c + 1) * 128], Hx32[:, c, :], g.ident[:, :], reads=[Hx32, g.ident], writes=[ps])
                k.copy(Sx32[:], ps[:, :].rearrange("p (c t) -> p c t", t=128), [ps], [Sx32], eng=DVE)
                for hp in range(2):
                    k.dma(k.POOL, g.o_wkv.t[seq].rearrange("(c hp) v kk -> hp v c kk", hp=2)[hp],
                          Sx32[64 * hp:64 * hp + 64, :, 64 * hp:64 * hp + 64], reads=[Sx32], writes=[g.o_wkv])
            yield

        for seq in range(3):
            ci = 0 if seq == 0 else 1
            C = 64 if seq == 0 else 32
            chunks = [((n // 2, n % 2), n * 64) for n in range(TP // 64)] if seq == 0 else [((TP // 128 + seq - 1, 0), TP + (seq - 1) * TS)]
            zero_all()
            k.dma(k.SP, msk[:], g.c_msk.t[ci], writes=[msk])
            k.op(DVE, lambda e: e.memset(Hx32[:], 0.0), writes=[Hx32])
            if seq > 0:
                k.op(DVE, lambda e: e.memset(Sx32[:], 0.0), writes=[Sx32])
                for hp in range(2):
                    k.dma(k.SP, Sx32[64 * hp:64 * hp + 64, :, 64 * hp:64 * hp + 64],
                          g.st_wkv.t[seq - 1].rearrange("(c hp) v kk -> hp v c kk", hp=2)[hp], writes=[Sx32])
                ps = k.psn()
                for c in range(KC):
                    k.tr(ps[:, c * 128:(c + 1) * 128], Sx32[:, c, :], g.ident[:, :], reads=[Sx32, g.ident], writes=[ps])
                k.copy(Hx32[:], ps[:, :].rearrange("p (c t) -> p c t", t=128), [ps], [Hx32], eng=DVE)
            k.copy(Hb[:], Hx32[:], [Hx32], [Hb], eng=ACT)
            for _ in fe2(chunks[0][0], sets[0], C, chunks[0][1]):
                pass
            for i, (ch, row0) in enumerate(chunks):
                nxt = fe2(chunks[i + 1][0], sets[(i + 1) % 2], C, chunks[i + 1][1]) if i + 1 < len(chunks) else None
                _interleave(nxt, be2(ch, sets[i % 2], C, row0, seq, i == len(chunks) - 1), ra=1000, rb=1)
        k.barrier()


def build(TP, nstage=8):
    nc = bass.Bass("TRN2", target_bir_lowering=False)
    k = KB(nc)
    g = declare_io(k, TP)
    setup_globals(k, g)
    stages = [
        lambda s, d: stage_ffn(k, g, TP, 0, 0, s, d),
        lambda s, d: stage_mixer_ab(k, g, TP, s, d),
        lambda s, d: stage_xattn(k, g, TP, 0, s, d),
        lambda s, d: stage_ffn(k, g, TP, 0, 1, s, d),
        lambda s, d: stage_ffn(k, g, TP, 1, 0, s, d),
        lambda s, d: stage_rwkv(k, g, TP, s, d),
        lambda s, d: stage_xattn(k, g, TP, 1, s, d),
        lambda s, d: stage_ffn(k, g, TP, 1, 1, s, d),
    ][:nstage]
    src = g.x
    for i, stf in enumerate(stages):
        dst = g.y if i == len(stages) - 1 else g.xs[i % 2]
        stf(src, dst)
        src = dst
    k.finish()
    return nc


def host_consts(TP):
    c = {}
    c["c_ident"] = np.eye(128, dtype=np.float32)
    half = 64
    inv_freq = (10000.0 ** (-np.arange(half, dtype=np.float32) / np.float32(half))).astype(np.float32)
    pos = np.concatenate([np.arange(TP), PAST + np.arange(TS)]).astype(np.float32)
    ang = (pos[:, None] * inv_freq[None, :]).astype(np.float32)
    cos = np.cos(ang.astype(np.float64)).astype(np.float32).T
    sin = np.sin(ang.astype(np.float64)).astype(np.float32).T
    c["c_cos"] = np.ascontiguousarray(np.concatenate([cos, cos], 0))
    c["c_sin"] = np.ascontiguousarray(np.concatenate([-sin, sin], 0))
    M = np.zeros((2, 128, 4, 128), np.float32); XI = np.zeros((2, 128, 4, 128), np.float32); Z = np.zeros((2, 128, 4, 128), np.float32)
    for ci, C in enumerate((128, 32)):
        idx = np.arange(C, dtype=np.float64)
        for h in range(4):
            lg = np.log1p(-(2.0 ** (-5.0 - h)))
            m = np.where(idx[None, :] >= idx[:, None], np.exp(-lg * C), 0.0)
            M[ci, :C, h, :C] = m
            XI[ci, :, h, :C] = np.exp(lg * (idx + 1.0))[None, :]
            Z[ci, :, h, :C] = (np.exp(lg * (C - 1.0 - idx)) * 128 ** -0.5)[None, :]
    c["c_retM"] = M; c["c_retXI"] = XI; c["c_retZ"] = Z
    msk = np.zeros((2, 128, 3, 128), np.float32)
    for ci, C in enumerate((64, 32)):
        for hp in range(2):
            for s in range(C):
                msk[ci, hp * C + s, 0, hp * C + s + 1: hp * C + C] = 1.0
                msk[ci, hp * C + s, 1, hp * C + s: hp * C + C] = 1.0
        msk[ci, :, 2, :] = msk[ci, :, 0, :].T
    c["c_msk"] = msk
    ob = np.zeros((128, 128), np.float32); ob[:64, :64] = 1.0; ob[64:, 64:] = 1.0
    c["c_onesbd"] = ob
    return c


_W_NAMES = ["ln_g", "ln_b", "ffn_up", "ffn_down", "xa_q", "xa_k", "xa_v", "xa_o", "l0_w_in", "l0_conv_w", "l0_conv_b",
            "l0_lru_wa", "l0_lru_ba", "l0_lru_wx", "l0_lru_bx", "l0_lru_lambda", "l0_ret_gn_g", "l0_ret_gn_b", "l0_w_out",
            "l1_mu", "l1_w_rkv", "l1_w0", "l1_w1", "l1_w2", "l1_a0", "l1_a1", "l1_a2", "l1_g1", "l1_g2", "l1_k_k", "l1_k_a",
            "l1_gn_g", "l1_gn_b", "l1_w_out"]


def make_in_maps(inp, TP):
    f = lambda a: np.ascontiguousarray(np.asarray(a, dtype=np.float32))
    shared = {n: f(inp[n]) for n in _W_NAMES}
    shared["l1_r_k"] = f(inp["l1_r_k"]).reshape(-1)
    w_in = f(inp["l0_w_in"])
    rot = []
    for base in (1024, 1536):
        for h in range(4):
            b0 = base + h * 128
            rot.append(w_in[:, b0 + 64: b0 + 128]); rot.append(w_in[:, b0: b0 + 64])
    shared["l0_w_rot"] = np.ascontiguousarray(np.concatenate(rot, axis=1))
    shared.update(host_consts(TP))
    maps = []
    for b in range(NCORES):
        m = dict(shared)
        m["x"] = np.ascontiguousarray(np.concatenate([f(inp["x_prompt"][b]), f(inp["x_sample"][2 * b]), f(inp["x_sample"][2 * b + 1])], 0))
        m["mem"] = f(inp["mem_prompt"][b])
        sl = slice(2 * b, 2 * b + 2)
        m["st_conv"] = f(inp["state_conv0"][sl]); m["st_lru"] = f(inp["state_lru0"][sl]); m["st_ret"] = f(inp["state_ret0"][sl])
        m["st_shift"] = f(inp["state_shift1"][sl]).reshape(2, D); m["st_wkv"] = f(inp["state_wkv1"][sl])
        m["ck"] = f(inp["cache_mem_k"][:, sl]).reshape(2, 2, 256, D); m["cv"] = f(inp["cache_mem_v"][:, sl]).reshape(2, 2, 256, D)
        maps.append(m)
    return maps


_NC_CACHE = {}


def run(inp, TP, nstage=8, ncores=NCORES):
    key = (TP, nstage)
    if key not in _NC_CACHE:
        _NC_CACHE[key] = build(TP, nstage)
    nc = _NC_CACHE[key]
    res = run_bass_kernel_spmd(nc, make_in_maps(inp, TP)[:ncores], core_ids=list(range(ncores)))
    R = list(res.results)
    while len(R) < NCORES:
        R.append(R[0])
    st = lambda n: np.stack([r[n] for r in R], 0)
    y = st("y")
    y_p = y[:, :TP]
    y_s = y[:, TP:].reshape(NCORES * 2, TS, D)
    memk = st("o_memk").transpose(1, 0, 2, 3).reshape(2, NCORES, 256, 4, 256)
    memv = st("o_memv").transpose(1, 0, 2, 3).reshape(2, NCORES, 256, 4, 256)
    oc, ol, orr, osh, ow = st("o_conv"), st("o_lru"), st("o_ret"), st("o_shift"), st("o_wkv")
    pf = lambda a: np.ascontiguousarray(a[:, 0])
    sf = lambda a: np.ascontiguousarray(a[:, 1:3].reshape((NCORES * 2,) + a.shape[2:]))
    return (np.ascontiguousarray(y_p), np.ascontiguousarray(y_s), np.ascontiguousarray(memk), np.ascontiguousarray(memv),
            pf(oc), pf(ol), pf(orr), pf(osh)[:, None, :], pf(ow),
            sf(oc), sf(ol), sf(orr), sf(osh)[:, None, :], sf(ow))


def kernel(**inputs):
    TP = int(np.asarray(inputs["x_prompt"]).shape[1])
    return run(inputs, TP, 8)
```

```python
import contextlib
import numpy as np
import concourse.bass as bass
import concourse.mybir as mybir
from concourse.bass_utils import run_bass_kernel_spmd

F32 = mybir.dt.float32
BF16 = mybir.dt.bfloat16
AF = mybir.ActivationFunctionType
ALU = mybir.AluOpType

D = 1024
KC = 8
NCORES = 8
TS = 32
DFF = 2816
ALPHA = 4.0 ** 0.25
LN_EPS = 1e-5
PAST = 4096
SAME_ENGINE_WAITS = True


class Eng:
    def __init__(self, name, eng, sem):
        self.name = name
        self.eng = eng
        self.sem = sem
        self.count = 0
        self.known = {}


class Buf:
    def __init__(self, t, name):
        self.t = t
        self.name = name
        self.w = {}
        self.r = {}

    def __getitem__(self, key):
        return self.t[key]


class KB:
    def __init__(self, nc, n_dsem=32):
        self.nc = nc
        self.stack = contextlib.ExitStack()
        mk = lambda n: self.stack.enter_context(nc.semaphore(n))
        self.PE = Eng("pe", nc.tensor, mk("s_pe"))
        self.DVE = Eng("dve", nc.vector, mk("s_dve"))
        self.ACT = Eng("act", nc.scalar, mk("s_act"))
        self.POOL = Eng("pool", nc.gpsimd, mk("s_pool"))
        self.SP = Eng("sp", nc.sync, mk("s_sp"))
        self.engs = [self.PE, self.DVE, self.ACT, self.POOL, self.SP]
        self.dpools = {q.name: [[mk(f"s_d{q.name}{i}"), 0] for i in range(n_dsem // 2)] for q in (self.SP, self.POOL)}
        self.dnexts = {q.name: 0 for q in (self.SP, self.POOL)}
        self.dsems = [s for p in self.dpools.values() for s in p]
        self.nid = 0
        self.pspool = []
        self.psnext = 0
        self.flip = 0

    def sb(self, name, shape, dtype, stack=None):
        self.nid += 1
        t = (stack or self.stack).enter_context(self.nc.sbuf_tensor(f"{name}_{self.nid}", list(shape), dtype))
        return Buf(t, name)

    def ps(self, name, shape, dtype):
        self.nid += 1
        t = self.stack.enter_context(self.nc.psum_tensor(f"{name}_{self.nid}", list(shape), dtype))
        return Buf(t, name)

    def dram(self, name, shape, dtype, kind):
        t = self.nc.dram_tensor(name, list(shape), dtype, kind=kind)
        return Buf(t.ap(), name)

    def psn(self):
        b = self.pspool[self.psnext]
        self.psnext = (self.psnext + 1) % len(self.pspool)
        return b

    def _wait(self, E, sem, val):
        if val <= 0:
            return
        key = id(sem)
        if E.known.get(key, 0) >= val:
            return
        E.eng.wait_ge(sem, val)
        E.known[key] = val

    def _deps(self, E, reads, writes, same_engine=True):
        for b in reads:
            for (sem, val) in list(b.w.values()):
                if (not same_engine) and sem is E.sem:
                    continue
                self._wait(E, sem, val)
        for b in writes:
            for (sem, val) in list(b.w.values()) + list(b.r.values()):
                if (not same_engine) and sem is E.sem:
                    continue
                self._wait(E, sem, val)

    def _mark(self, sem, val, reads, writes):
        for b in reads:
            b.r[id(sem)] = (sem, val)
        for b in writes:
            b.r = {}
            b.w[id(sem)] = (sem, val)

    def op(self, E, emit, reads=(), writes=(), same_engine=SAME_ENGINE_WAITS):
        self._deps(E, reads, writes, same_engine)
        ins = emit(E.eng)
        E.count += 1
        ins.then_inc(E.sem, 1)
        self._mark(E.sem, E.count, reads, writes)
        return ins

    def mm(self, out_ap, lhsT, rhs, start, stop, reads=(), writes=()):
        return self.op(self.PE, lambda e: e.matmul(out_ap, lhsT=lhsT, rhs=rhs, start=start, stop=stop),
                       reads=reads, writes=writes, same_engine=False)

    def tr(self, out_ap, in_ap, ident_ap, reads=(), writes=()):
        return self.op(self.PE, lambda e: e.transpose(out_ap, in_ap, ident_ap),
                       reads=reads, writes=writes, same_engine=False)

    def dma(self, Q, out_ap, in_ap, reads=(), writes=(), **kw):
        self._deps(Q, reads, writes)
        pool = self.dpools[Q.name]
        slot = pool[self.dnexts[Q.name]]
        self.dnexts[Q.name] = (self.dnexts[Q.name] + 1) % len(pool)
        sem, val = slot
        self._wait(Q, sem, val)
        ins = Q.eng.dma_start(out=out_ap, in_=in_ap, **kw)
        slot[1] = val + 16
        ins.then_inc(sem, 16)
        self._mark(sem, val + 16, reads, writes)
        return ins

    def barrier(self):
        for E in self.engs:
            for F in self.engs:
                if F is not E:
                    self._wait(E, F.sem, F.count)
            for (sem, val) in self.dsems:
                self._wait(E, sem, val)

    def rotate(self, limit=20000):
        if max(E.count for E in self.engs) < limit:
            return
        self.barrier()
        for E in self.engs:
            self.nid += 1
            E.sem = self.stack.enter_context(self.nc.semaphore(f"s_{E.name}_{self.nid}"))
            E.count = 0

    def finish(self):
        for (sem, val) in self.dsems:
            self._wait(self.SP, sem, val)
        for F in self.engs:
            if F is not self.SP:
                self._wait(self.SP, F.sem, F.count)

    def copy(self, out_ap, in_ap, reads, writes, eng=None):
        if eng is None:
            self.flip ^= 1
            eng = self.ACT if self.flip else self.DVE
        if eng is self.ACT:
            return self.op(self.ACT, lambda e: e.activation(out=out_ap, in_=in_ap, func=AF.Copy), reads=reads, writes=writes)
        return self.op(eng, lambda e: e.tensor_copy(out=out_ap, in_=in_ap), reads=reads, writes=writes)


class Ctx:
    pass


def declare_io(k, TP):
    NT = TP + 2 * TS
    g = Ctx()
    I = lambda n, s: k.dram(n, s, F32, "ExternalInput")
    O = lambda n, s: k.dram(n, s, F32, "ExternalOutput")
    g.x = I("x", [NT, D]); g.mem = I("mem", [256, D])
    g.st_conv = I("st_conv", [2, 3, 512]); g.st_lru = I("st_lru", [2, 512]); g.st_ret = I("st_ret", [2, 4, 128, 128])
    g.st_shift = I("st_shift", [2, D]); g.st_wkv = I("st_wkv", [2, 16, 64, 64])
    g.ck = I("ck", [2, 2, 256, D]); g.cv = I("cv", [2, 2, 256, D])
    g.ln_g = I("ln_g", [2, 4, D]); g.ln_b = I("ln_b", [2, 4, D])
    g.ffn_up = I("ffn_up", [2, 2, D, 2 * DFF]); g.ffn_down = I("ffn_down", [2, 2, DFF, D])
    g.xa_q = I("xa_q", [2, D, D]); g.xa_k = I("xa_k", [2, D, D]); g.xa_v = I("xa_v", [2, D, D]); g.xa_o = I("xa_o", [2, D, D])
    g.w_in = I("l0_w_in", [D, 3072]); g.w_rot = I("l0_w_rot", [D, 1024])
    g.conv_w = I("l0_conv_w", [4, 512]); g.conv_b = I("l0_conv_b", [512])
    g.lru_wa = I("l0_lru_wa", [8, 64, 64]); g.lru_ba = I("l0_lru_ba", [512])
    g.lru_wx = I("l0_lru_wx", [8, 64, 64]); g.lru_bx = I("l0_lru_bx", [512]); g.lru_lam = I("l0_lru_lambda", [512])
    g.ret_g = I("l0_ret_gn_g", [512]); g.ret_b = I("l0_ret_gn_b", [512]); g.w_out0 = I("l0_w_out", [D, D])
    g.mu = I("l1_mu", [6, D]); g.w_rkv = I("l1_w_rkv", [3, D, D]); g.w0 = I("l1_w0", [D]); g.w1 = I("l1_w1", [D, 64]); g.w2 = I("l1_w2", [64, D])
    g.a0 = I("l1_a0", [D]); g.a1 = I("l1_a1", [D, 64]); g.a2 = I("l1_a2", [64, D]); g.g1 = I("l1_g1", [D, 128]); g.g2 = I("l1_g2", [128, D])
    g.k_k = I("l1_k_k", [D]); g.k_a = I("l1_k_a", [D]); g.r_k = I("l1_r_k", [D]); g.gn_g = I("l1_gn_g", [D]); g.gn_b = I("l1_gn_b", [D])
    g.w_out1 = I("l1_w_out", [D, D])
    g.c_ident = I("c_ident", [128, 128]); g.c_cos = I("c_cos", [128, TP + TS]); g.c_sin = I("c_sin", [128, TP + TS])
    g.c_retM = I("c_retM", [2, 128, 4, 128]); g.c_retXI = I("c_retXI", [2, 128, 4, 128]); g.c_retZ = I("c_retZ", [2, 128, 4, 128])
    g.c_msk = I("c_msk", [2, 128, 3, 128]); g.c_onesbd = I("c_onesbd", [128, 128])
    g.y = O("y", [NT, D]); g.o_memk = O("o_memk", [2, 256, D]); g.o_memv = O("o_memv", [2, 256, D])
    g.o_conv = O("o_conv", [3, 3, 512]); g.o_lru = O("o_lru", [3, 512]); g.o_ret = O("o_ret", [3, 4, 128, 128])
    g.o_shift = O("o_shift", [3, D]); g.o_wkv = O("o_wkv", [3, 16, 64, 64])
    g.xs = [k.dram("scr0", [NT, D], F32, "Internal"), k.dram("scr1", [NT, D], F32, "Internal")]
    return g


def load_cols(k, st, name, src_ap, ncol, rows=128):
    b = k.sb(name, [rows, ncol], F32, st)
    k.dma(k.SP, b[:], src_ap.rearrange("(c p) -> p c", p=rows), writes=[b], allow_slow_non_contiguous=True)
    return b


def load_weight(k, g, dst, src2d, K, N, kc0=0):
    nk = max(1, K // 128)
    rows = min(K, 128)
    for kc in range(nk):
        for n0 in range(0, N, 704):
            n1 = min(N, n0 + 704)
            stg = g.stage[g.stage_i % len(g.stage)]
            g.stage_i += 1
            k.dma(k.SP, stg[0:rows, 0:n1 - n0], src2d[kc * 128: kc * 128 + rows, n0:n1], writes=[stg])
            k.copy(dst[0:rows, kc0 + kc, n0:n1], stg[0:rows, 0:n1 - n0], [stg], [dst])


def load_tok(k, g, src, row0, PT, NS, pool):
    b = pool[g.tok_i % len(pool)]
    g.tok_i += 1
    k.dma(k.SP, b[0:PT, 0:NS, :], src.t[row0: row0 + PT * NS, :].rearrange("(s p) d -> p s d", p=PT), writes=[b])
    return b


def to_featmajor(k, g, x32, PT, NS, xT, col0=0):
    for s in range(NS):
        ps = k.psn()
        for c in range(KC):
            k.tr(ps[:, c * PT:(c + 1) * PT], x32[0:PT, s, c * 128:(c + 1) * 128], g.ident[0:PT, 0:PT],
                 reads=[x32, g.ident], writes=[ps])
        k.copy(xT[:, 0:KC, col0 + s * PT: col0 + (s + 1) * PT],
               ps[:, 0:KC * PT].rearrange("p (c t) -> p c t", t=PT), [ps], [xT])


def ln_epilogue(k, g, ps, base_ap, base_buf, PT, dst, row0, alpha, do_ln=True):
    y = g.ybuf[g.y_i % 2]; o = g.obuf[g.y_i % 2]; sm = g.small[g.y_i % 2]
    g.y_i += 1
    DVE, ACT = k.DVE, k.ACT
    k.op(DVE, lambda e: e.scalar_tensor_tensor(out=y[0:PT, :], in0=base_ap, scalar=float(alpha), in1=ps[0:PT, :],
                                               op0=ALU.mult, op1=ALU.add), reads=[base_buf, ps], writes=[y])
    if not do_ln:
        k.dma(k.POOL, dst.t[row0:row0 + PT, :], y[0:PT, :], reads=[y], writes=[dst])
        return
    k.op(DVE, lambda e: e.bn_stats(out=sm[0:PT, 0:6], in_=y[0:PT, 0:512]), reads=[y], writes=[sm])
    k.op(DVE, lambda e: e.bn_stats(out=sm[0:PT, 6:12], in_=y[0:PT, 512:1024]), reads=[y], writes=[sm])
    k.op(DVE, lambda e: e.bn_aggr(out=sm[0:PT, 12:14], in_=sm[0:PT, 0:12]), reads=[sm], writes=[sm])
    k.op(ACT, lambda e: e.activation(out=sm[0:PT, 14:15], in_=sm[0:PT, 13:14], func=AF.Ln, bias=g.eps_ln[0:PT, 0:1]),
         reads=[sm, g.eps_ln], writes=[sm])
    k.op(ACT, lambda e: e.activation(out=sm[0:PT, 14:15], in_=sm[0:PT, 14:15], func=AF.Exp, scale=-0.5), reads=[sm], writes=[sm])
    k.op(DVE, lambda e: e.tensor_scalar(out=sm[0:PT, 15:16], in0=sm[0:PT, 12:13], scalar1=-1.0, scalar2=sm[0:PT, 14:15],
                                        op0=ALU.mult, op1=ALU.mult), reads=[sm], writes=[sm])
    k.op(ACT, lambda e: e.activation(out=o[0:PT, :], in_=y[0:PT, :], func=AF.Identity, scale=sm[0:PT, 14:15],
                                     bias=sm[0:PT, 15:16]), reads=[y, sm], writes=[o])
    k.op(DVE, lambda e: e.tensor_tensor(out=o[0:PT, :], in0=o[0:PT, :], in1=g.gtab[0:PT, :], op=ALU.mult),
         reads=[o, g.gtab], writes=[o])
    k.op(DVE, lambda e: e.tensor_tensor(out=o[0:PT, :], in0=o[0:PT, :], in1=g.btab[0:PT, :], op=ALU.add),
         reads=[o, g.btab], writes=[o])
    k.dma(k.POOL, dst.t[row0:row0 + PT, :], o[0:PT, :], reads=[o], writes=[dst])


def load_ln_tabs(k, g, l, j):
    k.dma(k.SP, g.gtab[:], g.ln_g.t[l, j].partition_broadcast(128), writes=[g.gtab])
    k.dma(k.SP, g.btab[:], g.ln_b.t[l, j].partition_broadcast(128), writes=[g.btab])


def segs(TP, tile):
    out = [(r, tile, 0) for r in range(0, TP, tile)]
    out += [(TP, TS, 1), (TP + TS, TS, 2)]
    return out


def stage_ffn(k, g, TP, l, which, src, dst):
    TT = 256
    with contextlib.ExitStack() as st:
        Wg = k.sb("Wup", [128, KC, 2 * DFF], BF16, st)
        Wd = k.sb("Wd", [128, 22, D], BF16, st)
        load_ln_tabs(k, g, l, 0 if which == 0 else 3)
        load_weight(k, g, Wg, g.ffn_up.t[l, which], D, 2 * DFF)
        load_weight(k, g, Wd, g.ffn_down.t[l, which], DFF, D)
        xTs = [k.sb("xT", [128, KC, TT], BF16, st) for _ in range(2)]
        hT = k.sb("hT", [128, 22, TT], BF16, st)
        sgs = [k.sb("sg", [128, TT], F32, st) for _ in range(2)]
        toks = [k.sb("tok2", [128, 2, D], F32, st) for _ in range(2)]
        ffn_tiles = [(r, TT, 0) for r in range(0, TP, TT)] + [(TP, 2 * TS, 1)]
        for ti, (row0, ntok, seq) in enumerate(ffn_tiles):
            PT = min(128, ntok); NS = ntok // PT
            k.rotate()
            x32 = load_tok(k, g, src, row0, PT, NS, toks)
            xT = xTs[ti % 2]
            to_featmajor(k, g, x32, PT, NS, xT)
            for fc in range(22):
                ps = k.psn()
                for kc in range(KC):
                    k.mm(ps[:, 0:ntok], Wg[:, kc, fc * 128:(fc + 1) * 128], xT[:, kc, 0:ntok], kc == 0, kc == KC - 1,
                         reads=[Wg, xT], writes=[ps])
                for kc in range(KC):
                    k.mm(ps[:, 512:512 + ntok], Wg[:, kc, DFF + fc * 128: DFF + (fc + 1) * 128], xT[:, kc, 0:ntok],
                         kc == 0, kc == KC - 1, reads=[Wg, xT], writes=[ps])
                sg = sgs[fc % 2]
                k.op(k.ACT, lambda e: e.activation(out=sg[:, 0:ntok], in_=ps[:, 0:ntok], func=AF.Silu), reads=[ps], writes=[sg])
                k.op(k.DVE, lambda e: e.scalar_tensor_tensor(out=hT[:, fc, 0:ntok], in0=sg[:, 0:ntok], scalar=0.5,
                                                             in1=ps[:, 512:512 + ntok], op0=ALU.mult, op1=ALU.mult),
                     reads=[sg, ps], writes=[hT])
            for s in range(NS):
                ps = k.psn()
                for hf in range(2):
                    for fc in range(22):
                        k.mm(ps[0:PT, hf * 512:(hf + 1) * 512], hT[:, fc, s * PT:(s + 1) * PT], Wd[:, fc, hf * 512:(hf + 1) * 512],
                             fc == 0, fc == 21, reads=[hT, Wd], writes=[ps])
                ln_epilogue(k, g, ps, x32[0:PT, s, :], x32, PT, dst, row0 + s * PT, ALPHA)
        k.barrier()


def setup_globals(k, g):
    g.stage = [k.sb("stg", [128, 704], F32) for _ in range(4)]
    g.stage_i = 0
    g.tok_i = 0
    g.ybuf = [k.sb("yb", [128, D], F32) for _ in range(2)]
    g.obuf = [k.sb("ob", [128, D], F32) for _ in range(2)]
    g.small = [k.sb("sm", [128, 16], F32) for _ in range(2)]
    g.y_i = 0
    g.gtab = k.sb("gtab", [128, D], F32); g.btab = k.sb("btab", [128, D], F32)
    g.ident = k.sb("ident", [128, 128], F32)
    g.identb = k.sb("identb", [128, 128], BF16)
    g.eps_ln = k.sb("epsln", [128, 1], F32)
    k.dma(k.SP, g.ident[:], g.c_ident.t[:, :], writes=[g.ident])
    k.copy(g.identb[:], g.ident[:], [g.ident], [g.identb], eng=k.DVE)
    k.op(k.DVE, lambda e: e.memset(g.eps_ln[:], LN_EPS), writes=[g.eps_ln])
    g.one_c = k.sb("one_c", [128, 1], F32)
    k.op(k.DVE, lambda e: e.memset(g.one_c[:], 1.0), writes=[g.one_c])
    k.pspool = [k.ps("psp", [128, 1024], F32) for _ in range(4)]


def stage_xattn(k, g, TP, l, src, dst):
    DVE, ACT = k.DVE, k.ACT
    with contextlib.ExitStack() as st:
        Wq = k.sb("Wq", [128, KC, D], BF16, st); Wo = k.sb("Wo", [128, KC, D], BF16, st)
        Wk = k.sb("Wk", [128, KC, D], BF16, st); Wv = k.sb("Wv", [128, KC, D], BF16, st)
        load_ln_tabs(k, g, l, 2)
        load_weight(k, g, Wq, g.xa_q.t[l], D, D); load_weight(k, g, Wo, g.xa_o.t[l], D, D)
        load_weight(k, g, Wk, g.xa_k.t[l], D, D); load_weight(k, g, Wv, g.xa_v.t[l], D, D)
        ones = k.sb("ones", [128, 128], BF16, st)
        k.op(DVE, lambda e: e.memset(ones[:], 1.0), writes=[ones])
        m32 = k.sb("m32", [128, 2, D], F32, st)
        memT = k.sb("memT", [128, KC, 256], BF16, st)
        KTs = [k.sb("KT", [128, KC, 256], BF16, st) for _ in range(3)]
        Vts = [k.sb("Vt", [128, 2, D], BF16, st) for _ in range(3)]
        o32s = [k.sb("mo32", [128, D], F32, st) for _ in range(2)]
        k.dma(k.SP, m32[:], g.mem.t[:, :].rearrange("(s p) d -> p s d", p=128), writes=[m32])
        to_featmajor(k, g, m32, 128, 2, memT)
        for ec in range(KC):
            ps = k.psn()
            for kc in range(KC):
                k.mm(ps[:, 0:256], Wk[:, kc, ec * 128:(ec + 1) * 128], memT[:, kc, :], kc == 0, kc == KC - 1, reads=[Wk, memT], writes=[ps])
            k.copy(KTs[0][:, ec, :], ps[:, 0:256], [ps], [KTs[0]])
        oi = 0
        for (W, outd, isv) in ((Wk, g.o_memk, False), (Wv, g.o_memv, True)):
            for s in range(2):
                ps = k.psn()
                for hf in range(2):
                    for kc in range(KC):
                        k.mm(ps[:, hf * 512:(hf + 1) * 512], memT[:, kc, s * 128:(s + 1) * 128], W[:, kc, hf * 512:(hf + 1) * 512],
                             kc == 0, kc == KC - 1, reads=[W, memT], writes=[ps])
                o32 = o32s[oi % 2]; oi += 1
                k.copy(o32[:], ps[:, :], [ps], [o32])
                k.dma(k.POOL, outd.t[l, s * 128:(s + 1) * 128, :], o32[:], reads=[o32], writes=[outd])
                if isv:
                    k.copy(Vts[0][:, s, :], o32[:], [o32], [Vts[0]])
        for sq in range(2):
            k.dma(k.SP, m32[:], g.ck.t[l, sq].rearrange("(s p) d -> p s d", p=128), writes=[m32])
            to_featmajor(k, g, m32, 128, 2, KTs[1 + sq])
            k.dma(k.SP, m32[:], g.cv.t[l, sq].rearrange("(s p) d -> p s d", p=128), writes=[m32])
            k.copy(Vts[1 + sq][:], m32[:], [m32], [Vts[1 + sq]])
        XT = 512
        xTs = [k.sb("xT", [128, KC, XT], BF16, st) for _ in range(2)]
        qT = k.sb("qT", [128, KC, XT], BF16, st)
        pTs = [k.sb("pT", [128, 2, XT], BF16, st) for _ in range(2)]
        rdens = [k.sb("rden", [128, XT], F32, st) for _ in range(2)]
        oT = k.sb("oT", [128, KC, XT], BF16, st)
        toks = [k.sb("tok4", [128, 4, D], F32, st)]
        for ti, (row0, ntok, seq) in enumerate(segs(TP, XT)):
            PT = min(128, ntok); NS = ntok // PT
            k.rotate()
            x32 = load_tok(k, g, src, row0, PT, NS, toks)
            xT = xTs[ti % 2]
            to_featmajor(k, g, x32, PT, NS, xT)
            KT = KTs[seq]; Vt = Vts[seq]
            for ec in range(KC):
                ps = k.psn()
                for kc in range(KC):
                    k.mm(ps[:, 0:ntok], Wq[:, kc, ec * 128:(ec + 1) * 128], xT[:, kc, 0:ntok], kc == 0, kc == KC - 1, reads=[Wq, xT], writes=[ps])
                k.op(ACT, lambda e: e.activation(out=qT[:, ec, 0:ntok], in_=ps[:, 0:ntok], func=AF.Copy, scale=0.0625), reads=[ps], writes=[qT])
            for h in range(4):
                pT = pTs[h % 2]; rden = rdens[h % 2]
                for mc in range(2):
                    ps = k.psn()
                    for dc in range(2):
                        k.mm(ps[:, 0:ntok], KT[:, 2 * h + dc, mc * 128:(mc + 1) * 128], qT[:, 2 * h + dc, 0:ntok], dc == 0, dc == 1,
                             reads=[KT, qT], writes=[ps])
                    k.op(ACT, lambda e: e.activation(out=pT[:, mc, 0:ntok], in_=ps[:, 0:ntok], func=AF.Exp), reads=[ps], writes=[pT])
                ps = k.psn()
                for mc in range(2):
                    k.mm(ps[:, 0:ntok], ones[:, :], pT[:, mc, 0:ntok], mc == 0, mc == 1, reads=[ones, pT], writes=[ps])
                k.op(ACT, lambda e: e.activation(out=rden[:, 0:ntok], in_=ps[:, 0:ntok], func=AF.Ln), reads=[ps], writes=[rden])
                k.op(ACT, lambda e: e.activation(out=rden[:, 0:ntok], in_=rden[:, 0:ntok], func=AF.Exp, scale=-1.0), reads=[rden], writes=[rden])
                for dc in range(2):
                    ps = k.psn()
                    for mc in range(2):
                        k.mm(ps[:, 0:ntok], Vt[:, mc, (2 * h + dc) * 128:(2 * h + dc + 1) * 128], pT[:, mc, 0:ntok], mc == 0, mc == 1,
                             reads=[Vt, pT], writes=[ps])
                    k.op(DVE, lambda e: e.tensor_tensor(out=oT[:, 2 * h + dc, 0:ntok], in0=ps[:, 0:ntok], in1=rden[:, 0:ntok], op=ALU.mult),
                         reads=[ps, rden], writes=[oT])
            for s in range(NS):
                ps = k.psn()
                for hf in range(2):
                    for ec in range(KC):
                        k.mm(ps[0:PT, hf * 512:(hf + 1) * 512], oT[:, ec, s * PT:(s + 1) * PT], Wo[:, ec, hf * 512:(hf + 1) * 512],
                             ec == 0, ec == KC - 1, reads=[oT, Wo], writes=[ps])
                ln_epilogue(k, g, ps, x32[0:PT, s, :], x32, PT, dst, row0 + s * PT, ALPHA)
        k.barrier()


def stage_mixer_ab(k, g, TP, src, dst):
    DVE, ACT = k.DVE, k.ACT
    TT = 256
    with contextlib.ExitStack() as st:
        Win = k.sb("Win", [128, KC, 3072], BF16, st); Wrot = k.sb("Wrot", [128, KC, 1024], BF16, st)
        Wout = k.sb("Wout", [128, KC, D], BF16, st)
        load_ln_tabs(k, g, 0, 1)
        load_weight(k, g, Win, g.w_in.t, D, 3072); load_weight(k, g, Wrot, g.w_rot.t, D, 1024)
        load_weight(k, g, Wout, g.w_out0.t, D, D)
        bd32 = k.sb("bd32", [128, 2, 4, 128], F32, st); Wbd = k.sb("Wbd", [128, 2, 4, 128], BF16, st)
        k.op(DVE, lambda e: e.memset(bd32[:], 0.0), writes=[bd32])
        for wi, wsrc in enumerate((g.lru_wa, g.lru_wx)):
            for hp in range(2):
                k.dma(k.SP, bd32[64 * hp:64 * hp + 64, wi, :, 64 * hp:64 * hp + 64],
                      wsrc.t.rearrange("(c hp) i j -> hp i c j", hp=2)[hp], writes=[bd32])
        k.copy(Wbd[:], bd32[:], [bd32], [Wbd], eng=DVE)
        cw = k.sb("cw", [128, 4, 4], F32, st)
        for j in range(4):
            k.dma(k.SP, cw[:, :, j], g.conv_w.t[j].rearrange("(c p) -> p c", p=128), writes=[cw], allow_slow_non_contiguous=True)
        cb = load_cols(k, st, "cb", g.conv_b.t, 4); ba = load_cols(k, st, "ba", g.lru_ba.t, 4); bx = load_cols(k, st, "bx", g.lru_bx.t, 4)
        lam = load_cols(k, st, "lam", g.lru_lam.t, 4); gng = load_cols(k, st, "gng", g.ret_g.t, 4); gnb = load_cols(k, st, "gnb", g.ret_b.t, 4)
        cl = k.sb("cl", [128, 4], F32, st); cl2 = k.sb("cl2", [128, 4], F32, st)
        k.op(ACT, lambda e: e.activation(out=cl[:], in_=lam[:], func=AF.Exp, scale=-1.0), reads=[lam], writes=[cl])
        k.op(ACT, lambda e: e.activation(out=cl[:], in_=cl[:], func=AF.Ln, bias=g.one_c[:, 0:1]), reads=[cl, g.one_c], writes=[cl])
        k.op(DVE, lambda e: e.tensor_scalar(out=cl2[:], in0=cl[:], scalar1=-16.0, scalar2=None, op0=ALU.mult), reads=[cl], writes=[cl2])
        k.op(DVE, lambda e: e.tensor_scalar(out=cl[:], in0=cl[:], scalar1=-8.0, scalar2=None, op0=ALU.mult), reads=[cl], writes=[cl])
        ones = k.sb("ones", [128, 128], BF16, st)
        k.op(DVE, lambda e: e.memset(ones[:], 1.0 / 128.0), writes=[ones])
        epsc = k.sb("epsc", [128, 1], F32, st)
        k.op(DVE, lambda e: e.memset(epsc[:], LN_EPS), writes=[epsc])
        cosT = k.sb("cosT", [128, TT], F32, st); sinT = k.sb("sinT", [128, TT], F32, st)
        retM = k.sb("retM", [128, 4, 128], F32, st); retXI = k.sb("retXI", [128, 4, 128], F32, st); retZ = k.sb("retZ", [128, 4, 128], F32, st)
        xaT = k.sb("xaT", [128, 4, 3 + TT], F32, st); hl = k.sb("hl", [128, 4], F32, st)
        S32 = k.sb("S32", [128, 4, 128], F32, st); Sb = k.sb("Sb", [128, 4, 128], BF16, st)
        xTs = [k.sb("xT", [128, KC, TT], BF16, st) for _ in range(2)]
        toks = [k.sb("tok2", [128, 2, D], F32, st) for _ in range(2)]
        gaT = k.sb("gaT", [128, 4, TT], F32, st)
        qr = k.sb("qr", [128, 4, TT], BF16, st); kz = k.sb("kz", [128, 4, TT], BF16, st)
        t1s = [k.sb("t1", [128, TT], F32, st) for _ in range(2)]; t2s = [k.sb("t2", [128, TT], F32, st) for _ in range(2)]
        t3s = [k.sb("t3", [128, TT], F32, st) for _ in range(2)]
        Ktok = k.sb("Ktok", [128, 2, 4, 128], BF16, st); Vtok = k.sb("Vtok", [128, 2, 512], BF16, st)
        PTb = k.sb("PTb", [128, 4, 128], BF16, st)
        oT = k.sb("oT", [128, 4, TT], F32, st); obf = k.sb("obf", [128, 4, TT], BF16, st); osq = k.sb("osq", [128, 4, TT], BF16, st)
        lru = [[k.sb(n, [128, TT], (BF16 if n == "xcb" else F32), st) for n in ("xc", "xcb", "rr", "ii", "aa", "hh")] for _ in range(1)]
        yT = k.sb("yT", [128, KC, TT], BF16, st)
        T1 = k.sb("T1", [128, 4, TT], F32, st); T3 = k.sb("T3", [128, 4, TT], F32, st)
        cur_seq = -1
        allsegs = segs(TP, TT)
        for ti, (row0, ntok, seq) in enumerate(allsegs):
            PT = min(128, ntok); NS = ntok // PT
            k.rotate()
            C = PT; nch = NS
            ci = 0 if seq == 0 else 1
            last = (ti + 1 == len(allsegs)) or (allsegs[ti + 1][2] != seq)
            if seq != cur_seq:
                cur_seq = seq
                k.op(DVE, lambda e: e.memset(PTb[:], 0.0), writes=[PTb])
                k.op(DVE, lambda e: e.memset(Ktok[:], 0.0), writes=[Ktok])
                k.op(DVE, lambda e: e.memset(Vtok[:], 0.0), writes=[Vtok])
                k.dma(k.SP, retM[:], g.c_retM.t[ci], writes=[retM]); k.dma(k.SP, retXI[:], g.c_retXI.t[ci], writes=[retXI])
                k.dma(k.SP, retZ[:], g.c_retZ.t[ci], writes=[retZ])
                if seq == 0:
                    k.op(DVE, lambda e: e.memset(xaT[:], 0.0), writes=[xaT])
                    k.op(DVE, lambda e: e.memset(hl[:], 0.0), writes=[hl])
                    k.op(DVE, lambda e: e.memset(S32[:], 0.0), writes=[S32])
                else:
                    for j in range(3):
                        k.dma(k.SP, xaT[:, :, j], g.st_conv.t[seq - 1, j].rearrange("(c p) -> p c", p=128), writes=[xaT], allow_slow_non_contiguous=True)
                    k.dma(k.SP, hl[:], g.st_lru.t[seq - 1].rearrange("(c p) -> p c", p=128), writes=[hl], allow_slow_non_contiguous=True)
                    k.dma(k.SP, S32[:], g.st_ret.t[seq - 1].rearrange("h d v -> d h v"), writes=[S32])
                k.copy(Sb[:], S32[:], [S32], [Sb], eng=ACT)
            pos0 = row0 if seq == 0 else TP
            k.dma(k.SP, cosT[:, 0:ntok], g.c_cos.t[:, pos0:pos0 + ntok], writes=[cosT])
            k.dma(k.SP, sinT[:, 0:ntok], g.c_sin.t[:, pos0:pos0 + ntok], writes=[sinT])
            x32 = load_tok(k, g, src, row0, PT, NS, toks)
            xT = xTs[ti % 2]
            to_featmajor(k, g, x32, PT, NS, xT)

            def proj(W, col0):
                ps = k.psn()
                for kc in range(KC):
                    k.mm(ps[:, 0:ntok], W[:, kc, col0:col0 + 128], xT[:, kc, 0:ntok], kc == 0, kc == KC - 1, reads=[W, xT], writes=[ps])
                return ps
            for c in range(4):
                ps = proj(Win, c * 128)
                k.copy(xaT[:, c, 3:3 + ntok], ps[:, 0:ntok], [ps], [xaT], eng=ACT)
                ps = proj(Win, 512 + c * 128)
                k.op(ACT, lambda e: e.activation(out=gaT[:, c, 0:ntok], in_=ps[:, 0:ntok], func=AF.Gelu_apprx_tanh), reads=[ps], writes=[gaT])
            for (dst_b, base, rbase, isk) in ((qr, 1024, 0, False), (kz, 1536, 512, True)):
                for h in range(4):
                    t1, t2, t3 = t1s[h % 2], t2s[h % 2], t3s[h % 2]
                    ps = proj(Win, base + h * 128)
                    ps2 = proj(Wrot, rbase + h * 128)
                    k.op(DVE, lambda e: e.tensor_tensor(out=t1[:, 0:ntok], in0=ps[:, 0:ntok], in1=cosT[:, 0:ntok], op=ALU.mult), reads=[ps, cosT], writes=[t1])
                    k.op(DVE, lambda e: e.tensor_tensor(out=t2[:, 0:ntok], in0=ps2[:, 0:ntok], in1=sinT[:, 0:ntok], op=ALU.mult), reads=[ps2, sinT], writes=[t2])
                    if not isk:
                        k.op(DVE, lambda e: e.tensor_tensor(out=qr[:, h, 0:ntok], in0=t1[:, 0:ntok], in1=t2[:, 0:ntok], op=ALU.add), reads=[t1, t2], writes=[qr])
                    else:
                        k.op(DVE, lambda e: e.tensor_tensor(out=t3[:, 0:ntok], in0=t1[:, 0:ntok], in1=t2[:, 0:ntok], op=ALU.add), reads=[t1, t2], writes=[t3])
                        k.op(DVE, lambda e: e.tensor_tensor(out=kz[:, h, 0:ntok].rearrange("p (n c) -> p n c", c=C),
                                                            in0=t3[:, 0:ntok].rearrange("p (n c) -> p n c", c=C),
                                                            in1=retZ[:, h, 0:C].unsqueeze(1).broadcast_to([128, nch, C]), op=ALU.mult),
                             reads=[t3, retZ], writes=[kz])
            for n in range(nch):
                cs = slice(n * C, (n + 1) * C)
                ps = k.psn()
                for kc in range(KC):
                    k.mm(ps[0:C, 0:512], xT[:, kc, cs], Win[:, kc, 2048:2560], kc == 0, kc == KC - 1, reads=[xT, Win], writes=[ps])
                k.copy(Vtok[0:C, n % 2, :], ps[0:C, 0:512], [ps], [Vtok])
                ps = k.psn()
                psb = ps.t[:, 0:256].bitcast(BF16)
                for h in range(4):
                    k.tr(psb[0:C, h * 128:(h + 1) * 128], kz[:, h, cs], g.identb[:, :], reads=[kz, g.identb], writes=[ps])
                k.copy(Ktok[0:C, n % 2, :, :], psb[0:C, 0:512].rearrange("p (h d) -> p h d", d=128), [ps], [Ktok])
                ps = k.psn()
                for h in range(4):
                    k.mm(ps[0:C, h * 128:h * 128 + C], kz[:, h, cs], qr[:, h, cs], True, True, reads=[kz, qr], writes=[ps])
                k.op(DVE, lambda e: e.tensor_tensor(out=PTb[0:C, :, 0:C], in0=ps[0:C, 0:512].rearrange("p (h c) -> p h c", c=128)[:, :, 0:C],
                                                    in1=retM[0:C, :, 0:C], op=ALU.mult), reads=[ps, retM], writes=[PTb])
                ps = k.psn()
                for h in range(4):
                    k.mm(ps[:, h * 128:h * 128 + C], Vtok[:, n % 2, h * 128:(h + 1) * 128], PTb[:, h, 0:C], True, False, reads=[Vtok, PTb], writes=[ps])
                    k.mm(ps[:, h * 128:h * 128 + C], Sb[:, h, :], qr[:, h, cs], False, True, reads=[Sb, qr], writes=[ps])
                k.op(DVE, lambda e: e.tensor_tensor(out=oT[:, :, cs], in0=ps[:, 0:512].rearrange("p (h c) -> p h c", c=128)[:, :, 0:C],
                                                    in1=retXI[:, :, 0:C], op=ALU.mult), reads=[ps, retXI], writes=[oT])
                ps = k.psn()
                for h in range(4):
                    k.mm(ps[:, h * 128:(h + 1) * 128], Ktok[:, n % 2, h, :], Vtok[:, n % 2, h * 128:(h + 1) * 128], True, True, reads=[Ktok, Vtok], writes=[ps])
                for h in range(4):
                    gam = float(np.exp(np.log1p(-(2.0 ** (-5.0 - h))) * C))
                    k.op(DVE, lambda e: e.scalar_tensor_tensor(out=S32[:, h, :], in0=S32[:, h, :], scalar=gam, in1=ps[:, h * 128:(h + 1) * 128],
                                                               op0=ALU.mult, op1=ALU.add), reads=[S32, ps], writes=[S32])
                k.copy(Sb[:], S32[:], [S32], [Sb], eng=ACT)
            k.op(ACT, lambda e: e.activation(out=obf[:, :, 0:ntok], in_=oT[:, :, 0:ntok], func=AF.Copy), reads=[oT], writes=[obf])
            k.op(ACT, lambda e: e.activation(out=osq[:, :, 0:ntok], in_=oT[:, :, 0:ntok], func=AF.Square), reads=[oT], writes=[osq])
            vH = lambda ps_: ps_[:, :].rearrange("p (h t) -> p h t", t=TT)[:, :, 0:ntok]
            psm = k.psn(); psq = k.psn()
            for h in range(4):
                k.mm(psm[:, h * TT:h * TT + ntok], ones[:, :], obf[:, h, 0:ntok], True, True, reads=[ones, obf], writes=[psm])
            for h in range(4):
                k.mm(psq[:, h * TT:h * TT + ntok], ones[:, :], osq[:, h, 0:ntok], True, True, reads=[ones, osq], writes=[psq])
            k.op(ACT, lambda e: e.activation(out=T1[:, :, 0:ntok], in_=vH(psm), func=AF.Square), reads=[psm], writes=[T1])
            k.op(DVE, lambda e: e.tensor_tensor(out=T1[:, :, 0:ntok], in0=vH(psq), in1=T1[:, :, 0:ntok], op=ALU.subtract), reads=[psq, T1], writes=[T1])
            k.op(ACT, lambda e: e.activation(out=T1[:, :, 0:ntok], in_=T1[:, :, 0:ntok], func=AF.Ln, bias=epsc[:, 0:1]), reads=[T1, epsc], writes=[T1])
            k.op(ACT, lambda e: e.activation(out=T1[:, :, 0:ntok], in_=T1[:, :, 0:ntok], func=AF.Exp, scale=-0.5), reads=[T1], writes=[T1])
            k.op(DVE, lambda e: e.tensor_tensor(out=oT[:, :, 0:ntok], in0=oT[:, :, 0:ntok], in1=vH(psm), op=ALU.subtract), reads=[oT, psm], writes=[oT])
            k.op(DVE, lambda e: e.tensor_tensor(out=oT[:, :, 0:ntok], in0=oT[:, :, 0:ntok], in1=T1[:, :, 0:ntok], op=ALU.mult), reads=[oT, T1], writes=[oT])
            for h in range(4):
                k.op(DVE, lambda e: e.tensor_scalar(out=oT[:, h, 0:ntok], in0=oT[:, h, 0:ntok], scalar1=gng[:, h:h + 1], scalar2=gnb[:, h:h + 1],
                                                    op0=ALU.mult, op1=ALU.add), reads=[oT, gng, gnb], writes=[oT])
            for h in range(4):
                ps = proj(Win, 2560 + h * 128)
                k.op(ACT, lambda e: e.activation(out=T3[:, h, 0:ntok], in_=ps[:, 0:ntok], func=AF.Silu), reads=[ps], writes=[T3])
            k.op(DVE, lambda e: e.tensor_tensor(out=yT[:, 4:8, 0:ntok], in0=oT[:, :, 0:ntok], in1=T3[:, :, 0:ntok], op=ALU.mult), reads=[oT, T3], writes=[yT])
            for c in range(4):
                xc, xcb, rr, ii, aa, hh = lru[0]
                k.op(DVE, lambda e: e.tensor_scalar(out=xc[:, 0:ntok], in0=xaT[:, c, 0:ntok], scalar1=cw[:, c, 0:1], scalar2=cb[:, c:c + 1],
                                                    op0=ALU.mult, op1=ALU.add), reads=[xaT, cw, cb], writes=[xc])
                for j in range(1, 4):
                    k.op(DVE, lambda e: e.scalar_tensor_tensor(out=xc[:, 0:ntok], in0=xaT[:, c, j:j + ntok], scalar=cw[:, c, j:j + 1], in1=xc[:, 0:ntok],
                                                               op0=ALU.mult, op1=ALU.add), reads=[xaT, cw, xc], writes=[xc])
                k.copy(xcb[:, 0:ntok], xc[:, 0:ntok], [xc], [xcb], eng=ACT)
                ps = k.psn()
                k.mm(ps[:, 0:ntok], Wbd[:, 0, c, :], xcb[:, 0:ntok], True, True, reads=[Wbd, xcb], writes=[ps])
                k.mm(ps[:, 512:512 + ntok], Wbd[:, 1, c, :], xcb[:, 0:ntok], True, True, reads=[Wbd, xcb], writes=[ps])
                k.op(ACT, lambda e: e.activation(out=rr[:, 0:ntok], in_=ps[:, 0:ntok], func=AF.Sigmoid, bias=ba[:, c:c + 1]), reads=[ps, ba], writes=[rr])
                k.op(ACT, lambda e: e.activation(out=ii[:, 0:ntok], in_=ps[:, 512:512 + ntok], func=AF.Sigmoid, bias=bx[:, c:c + 1]), reads=[ps, bx], writes=[ii])
                k.op(ACT, lambda e: e.activation(out=aa[:, 0:ntok], in_=rr[:, 0:ntok], func=AF.Exp, scale=cl[:, c:c + 1]), reads=[rr, cl], writes=[aa])
                k.op(ACT, lambda e: e.activation(out=rr[:, 0:ntok], in_=rr[:, 0:ntok], func=AF.Exp, scale=cl2[:, c:c + 1]), reads=[rr, cl2], writes=[rr])
                k.op(ACT, lambda e: e.activation(out=rr[:, 0:ntok], in_=rr[:, 0:ntok], func=AF.Ln, scale=-1.0, bias=g.one_c[:, 0:1]), reads=[rr, g.one_c], writes=[rr])
                k.op(ACT, lambda e: e.activation(out=rr[:, 0:ntok], in_=rr[:, 0:ntok], func=AF.Exp, scale=0.5), reads=[rr], writes=[rr])
                k.op(DVE, lambda e: e.tensor_tensor(out=ii[:, 0:ntok], in0=ii[:, 0:ntok], in1=xc[:, 0:ntok], op=ALU.mult), reads=[ii, xc], writes=[ii])
                k.op(DVE, lambda e: e.tensor_tensor(out=ii[:, 0:ntok], in0=ii[:, 0:ntok], in1=rr[:, 0:ntok], op=ALU.mult), reads=[ii, rr], writes=[ii])
                k.op(DVE, lambda e: e.tensor_tensor_scan(out=hh[:, 0:ntok], data0=aa[:, 0:ntok], data1=ii[:, 0:ntok], initial=hl[:, c:c + 1],
                                                         op0=ALU.mult, op1=ALU.add), reads=[aa, ii, hl], writes=[hh])
                k.op(DVE, lambda e: e.tensor_copy(out=hl[:, c:c + 1], in_=hh[:, ntok - 1:ntok]), reads=[hh], writes=[hl])
                k.op(DVE, lambda e: e.tensor_tensor(out=yT[:, c, 0:ntok], in0=hh[:, 0:ntok], in1=gaT[:, c, 0:ntok], op=ALU.mult), reads=[hh, gaT], writes=[yT])
            k.op(DVE, lambda e: e.tensor_copy(out=xaT[:, :, 0:3], in_=xaT[:, :, ntok:ntok + 3]), reads=[xaT], writes=[xaT])
            for s in range(NS):
                ps = k.psn()
                for hf in range(2):
                    for c in range(KC):
                        k.mm(ps[0:PT, hf * 512:(hf + 1) * 512], yT[:, c, s * PT:(s + 1) * PT], Wout[:, c, hf * 512:(hf + 1) * 512],
                             c == 0, c == KC - 1, reads=[yT, Wout], writes=[ps])
                ln_epilogue(k, g, ps, x32[0:PT, s, :], x32, PT, dst, row0 + s * PT, ALPHA)
            if last:
                for j in range(3):
                    k.dma(k.POOL, g.o_conv.t[seq, j].rearrange("(c p) -> p c", p=128), xaT[:, :, j], reads=[xaT], writes=[g.o_conv], allow_slow_non_contiguous=True)
                k.dma(k.POOL, g.o_lru.t[seq].rearrange("(c p) -> p c", p=128), hl[:], reads=[hl], writes=[g.o_lru], allow_slow_non_contiguous=True)
                k.dma(k.POOL, g.o_ret.t[seq].rearrange("h d v -> d h v"), S32[:], reads=[S32], writes=[g.o_ret])
        k.barrier()


def _interleave(a, b, ra=1, rb=1):
    alive_a, alive_b = a is not None, b is not None
    while alive_a or alive_b:
        for _ in range(ra):
            if alive_a:
                try:
                    next(a)
                except StopIteration:
                    alive_a = False
        for _ in range(rb):
            if alive_b:
                try:
                    next(b)
                except StopIteration:
                    alive_b = False


def stage_rwkv(k, g, TP, src, dst):
    DVE, ACT, POOL = k.DVE, k.ACT, k.DVE
    DK = float(np.exp(-0.5))
    NT1 = TP // 128 + 2
    opnd = k.dram("rw_opnd", [NT1, 7, 128, 1024], BF16, "Internal")
    wcd = k.dram("rw_wc", [NT1, 128, 2, KC], F32, "Internal")
    with contextlib.ExitStack() as st:
        Wr = k.sb("Wr", [128, KC, D], BF16, st); Wk = k.sb("Wk", [128, KC, D], BF16, st); Wv = k.sb("Wv", [128, KC, D], BF16, st)
        for i, W in enumerate((Wr, Wk, Wv)):
            load_weight(k, g, W, g.w_rkv.t[i], D, D)
        w1 = k.sb("w1", [128, KC, 64], BF16, st); a1 = k.sb("a1", [128, KC, 64], BF16, st); g1 = k.sb("g1", [128, KC, 128], BF16, st)
        w2 = k.sb("w2", [128, 1, D], BF16, st); a2 = k.sb("a2", [128, 1, D], BF16, st); g2 = k.sb("g2", [128, 1, D], BF16, st)
        load_weight(k, g, w1, g.w1.t, D, 64); load_weight(k, g, a1, g.a1.t, D, 64); load_weight(k, g, g1, g.g1.t, D, 128)
        load_weight(k, g, w2, g.w2.t, 64, D); load_weight(k, g, a2, g.a2.t, 64, D); load_weight(k, g, g2, g.g2.t, 128, D)
        mu = k.sb("mu", [128, 6, KC], F32, st)
        for p in range(6):
            k.dma(k.SP, mu[:, p, :], g.mu.t[p].rearrange("(c p) -> p c", p=128), writes=[mu], allow_slow_non_contiguous=True)
        w0c = load_cols(k, st, "w0c", g.w0.t, 8); a0c = load_cols(k, st, "a0c", g.a0.t, 8); kkc = load_cols(k, st, "kkc", g.k_k.t, 8)
        kac = load_cols(k, st, "kac", g.k_a.t, 8); rkc = load_cols(k, st, "rkc", g.r_k.t, 8)
        ob32 = k.sb("ob32", [128, 128], F32, st); onesbd = k.sb("onesbd", [128, 128], BF16, st)
        k.dma(k.SP, ob32[:], g.c_onesbd.t[:, :], writes=[ob32])
        k.copy(onesbd[:], ob32[:], [ob32], [onesbd], eng=DVE)
        onesf = k.sb("onesf", [128, 64], F32, st)
        k.op(DVE, lambda e: e.memset(onesf[:], 1.0), writes=[onesf])
        xprev = k.sb("xprev", [128, KC], F32, st)
        TT = 128
        toks = [k.sb("tok1", [128, 1, D], F32, st) for _ in range(2)]
        xT32 = k.sb("xT32", [128, KC, 1 + TT], F32, st); dd = k.sb("dd", [128, KC, TT], F32, st)
        xms = [k.sb("xm", [128, KC, TT], BF16, st) for _ in range(2)]
        F = lambda n: k.sb(n, [128, KC, TT], F32, st)
        iface = [[F(n + str(par)) for n in ("rT", "kT", "vT", "sg", "ic")] for par in range(2)]
        Lc, Ep, Em, Ea, kk, kf, tm, Lm = [F(n) for n in ("Lc", "Ep", "Em", "Ea", "kk", "kf", "tm", "Lm")]
        kkn = kk
        B = lambda n: k.sb(n, [128, KC, TT], BF16, st)
        kk2, rkb = B("kk2"), B("rkb")
        _o = [B(f"o{j}") for j in range(7)]
        _g1 = B("o5b")
        outs = [_o, _o[:5] + [_g1] + _o[6:]]
        th = k.sb("th", [128, TT], BF16, st)
        wcs = [k.sb("wcs", [128, 2, KC], F32, st) for _ in range(2)]
        p1segs = segs(TP, TT)

        def front(ti):
            row0, ntok, seq = p1segs[ti]
            PT = ntok
            k.rotate()
            rT, kT, vT, sg, ic = iface[ti % 2]
            At, Rt, Kt, Bt, Vb, Gb, Bon = outs[ti % 2]
            if ti == 0 or p1segs[ti - 1][2] != seq:
                if seq == 0:
                    k.op(DVE, lambda e: e.memset(xprev[:], 0.0), writes=[xprev])
                else:
                    k.dma(k.SP, xprev[:], g.st_shift.t[seq - 1].rearrange("(c p) -> p c", p=128), writes=[xprev], allow_slow_non_contiguous=True)
            x32 = load_tok(k, g, src, row0, PT, 1, toks)
            k.op(DVE, lambda e: e.tensor_copy(out=xT32[:, :, 0], in_=xprev[:, :]), reads=[xprev], writes=[xT32])
            to_featmajor(k, g, x32, PT, 1, xT32, col0=1)
            k.op(DVE, lambda e: e.tensor_copy(out=xprev[:, :], in_=xT32[:, :, ntok]), reads=[xT32], writes=[xprev])
            k.op(DVE, lambda e: e.tensor_tensor(out=dd[:, :, 0:ntok], in0=xT32[:, :, 0:ntok], in1=xT32[:, :, 1:1 + ntok], op=ALU.subtract),
                 reads=[xT32], writes=[dd])

            def mix(p):
                xm = xms[p % 2]
                for c in range(KC):
                    k.op(DVE, lambda e: e.scalar_tensor_tensor(out=xm[:, c, 0:ntok], in0=dd[:, c, 0:ntok], scalar=mu[:, p, c:c + 1],
                                                               in1=xT32[:, c, 1:1 + ntok], op0=ALU.mult, op1=ALU.add), reads=[dd, mu, xT32], writes=[xm])
                return xm

            def proj_full(W, xm, dstb):
                for ec in range(KC):
                    ps = k.psn()
                    for kc in range(KC):
                        k.mm(ps[:, 0:ntok], W[:, kc, ec * 128:(ec + 1) * 128], xm[:, kc, 0:ntok], kc == 0, kc == KC - 1, reads=[W, xm], writes=[ps])
                    k.copy(dstb[:, ec, 0:ntok], ps[:, 0:ntok], [ps], [dstb], eng=ACT)

            def lora(xm, wA, nA, wB, func1, emit2):
                ps = k.psn()
                for kc in range(KC):
                    k.mm(ps[0:nA, 0:ntok], wA[:, kc, :], xm[:, kc, 0:ntok], kc == 0, kc == KC - 1, reads=[wA, xm], writes=[ps])
                k.op(ACT, lambda e: e.activation(out=th[0:nA, 0:ntok], in_=ps[0:nA, 0:ntok], func=func1), reads=[ps], writes=[th])
                for c in range(KC):
                    ps2 = k.psn()
                    k.mm(ps2[:, 0:ntok], wB[0:nA, 0, c * 128:(c + 1) * 128], th[0:nA, 0:ntok], True, True, reads=[wB, th], writes=[ps2])
                    emit2(c, ps2)

            proj_full(Wr, mix(0), rT); proj_full(Wk, mix(1), kT); proj_full(Wv, mix(2), vT)
            lora(mix(3), w1, 64, w2, AF.Tanh, lambda c, ps2: k.op(ACT, lambda e: e.activation(
                out=sg[:, c, 0:ntok], in_=ps2[:, 0:ntok], func=AF.Sigmoid, bias=w0c[:, c:c + 1]), reads=[ps2, w0c], writes=[sg]))
            lora(mix(4), a1, 64, a2, AF.Copy, lambda c, ps2: k.op(ACT, lambda e: e.activation(
                out=ic[:, c, 0:ntok], in_=ps2[:, 0:ntok], func=AF.Sigmoid, bias=a0c[:, c:c + 1]), reads=[ps2, a0c], writes=[ic]))
            lora(mix(5), g1, 128, g2, AF.Sigmoid, lambda c, ps2: k.copy(Gb[:, c, 0:ntok], ps2[:, 0:ntok], [ps2], [Gb], eng=ACT))

        def back(ti):
            row0, ntok, seq = p1segs[ti]
            C = min(64, ntok); nch = ntok // C
            chunk0 = row0 // 64 if seq == 0 else TP // 64 + (seq - 1)
            rT, kT, vT, sg, ic = iface[ti % 2]
            At, Rt, Kt, Bt, Vb, Gb, Bon = outs[ti % 2]
            v3 = lambda ps: ps[:, :].rearrange("p (c t) -> p c t", t=128)[:, :, 0:ntok]
            for c in range(KC):
                for n in range(nch):
                    k.op(DVE, lambda e: e.tensor_tensor_scan(out=Lc[:, c, n * C:(n + 1) * C], data0=onesf[:, 0:C], data1=sg[:, c, n * C:(n + 1) * C],
                                                             initial=0.0, op0=ALU.mult, op1=ALU.add), reads=[onesf, sg], writes=[Lc])
            k.op(POOL, lambda e: e.tensor_tensor(out=Lm[:, :, 0:ntok], in0=Lc[:, :, 0:ntok], in1=sg[:, :, 0:ntok], op=ALU.subtract), reads=[Lc, sg], writes=[Lm])
            k.op(ACT, lambda e: e.activation(out=Ep[:, :, 0:ntok], in_=Lc[:, :, 0:ntok], func=AF.Exp, scale=-DK), reads=[Lc], writes=[Ep])
            k.op(ACT, lambda e: e.activation(out=Em[:, :, 0:ntok], in_=Lc[:, :, 0:ntok], func=AF.Exp, scale=DK), reads=[Lc], writes=[Em])
            k.op(ACT, lambda e: e.activation(out=Ea[:, :, 0:ntok], in_=Lm[:, :, 0:ntok], func=AF.Exp, scale=-DK), reads=[Lm], writes=[Ea])
            for c in range(KC):
                k.op(DVE, lambda e: e.tensor_scalar(out=kk[:, c, 0:ntok], in0=kT[:, c, 0:ntok], scalar1=kkc[:, c:c + 1], scalar2=None, op0=ALU.mult),
                     reads=[kT, kkc], writes=[kk])
            k.op(ACT, lambda e: e.activation(out=kk2[:, :, 0:ntok], in_=kk[:, :, 0:ntok], func=AF.Square), reads=[kk], writes=[kk2])
            ps = k.psn()
            if ntok == 128:
                for hf in range(2):
                    k.mm(ps[:, hf * 512:(hf + 1) * 512], onesbd[:, :], kk2[:, 4 * hf:4 * hf + 4, :].rearrange("p c t -> p (c t)"), True, True,
                         reads=[onesbd, kk2], writes=[ps])
            else:
                for c in range(KC):
                    k.mm(ps[:, c * 128:c * 128 + ntok], onesbd[:, :], kk2[:, c, 0:ntok], True, True, reads=[onesbd, kk2], writes=[ps])
            k.op(DVE, lambda e: e.tensor_scalar(out=tm[:, :, 0:ntok], in0=v3(ps), scalar1=1e-24, scalar2=None, op0=ALU.max), reads=[ps], writes=[tm])
            k.op(ACT, lambda e: e.activation(out=tm[:, :, 0:ntok], in_=tm[:, :, 0:ntok], func=AF.Ln), reads=[tm], writes=[tm])
            k.op(ACT, lambda e: e.activation(out=tm[:, :, 0:ntok], in_=tm[:, :, 0:ntok], func=AF.Exp, scale=-0.5), reads=[tm], writes=[tm])
            k.op(POOL, lambda e: e.tensor_tensor(out=kkn[:, :, 0:ntok], in0=kk[:, :, 0:ntok], in1=tm[:, :, 0:ntok], op=ALU.mult), reads=[kk, tm], writes=[kkn])
            for c in range(KC):
                k.op(DVE, lambda e: e.tensor_scalar(out=tm[:, c, 0:ntok], in0=ic[:, c, 0:ntok], scalar1=-1.0, scalar2=kac[:, c:c + 1],
                                                    op0=ALU.add, op1=ALU.mult), reads=[ic, kac], writes=[tm])
            k.op(DVE, lambda e: e.scalar_tensor_tensor(out=kf[:, :, 0:ntok], in0=tm[:, :, 0:ntok], scalar=1.0, in1=kT[:, :, 0:ntok],
                                                       op0=ALU.add, op1=ALU.mult), reads=[tm, kT], writes=[kf])
            for c in range(KC):
                k.op(DVE, lambda e: e.scalar_tensor_tensor(out=rkb[:, c, 0:ntok], in0=rT[:, c, 0:ntok], scalar=rkc[:, c:c + 1], in1=kf[:, c, 0:ntok],
                                                           op0=ALU.mult, op1=ALU.mult), reads=[rT, rkc, kf], writes=[rkb])
            ps = k.psn()
            if ntok == 128:
                for hf in range(2):
                    k.mm(ps[:, hf * 512:(hf + 1) * 512], onesbd[:, :], rkb[:, 4 * hf:4 * hf + 4, :].rearrange("p c t -> p (c t)"), True, True,
                         reads=[onesbd, rkb], writes=[ps])
            else:
                for c in range(KC):
                    k.mm(ps[:, c * 128:c * 128 + ntok], onesbd[:, :], rkb[:, c, 0:ntok], True, True, reads=[onesbd, rkb], writes=[ps])
            k.op(DVE, lambda e: e.tensor_tensor(out=Bon[:, :, 0:ntok], in0=v3(ps), in1=vT[:, :, 0:ntok], op=ALU.mult), reads=[ps, vT], writes=[Bon])
            k.op(DVE, lambda e: e.scalar_tensor_tensor(out=At[:, :, 0:ntok], in0=kkn[:, :, 0:ntok], scalar=-1.0, in1=Ea[:, :, 0:ntok],
                                                       op0=ALU.mult, op1=ALU.mult), reads=[kkn, Ea], writes=[At])
            k.op(POOL, lambda e: e.tensor_tensor(out=Rt[:, :, 0:ntok], in0=rT[:, :, 0:ntok], in1=Ep[:, :, 0:ntok], op=ALU.mult), reads=[rT, Ep], writes=[Rt])
            k.op(POOL, lambda e: e.tensor_tensor(out=Kt[:, :, 0:ntok], in0=kf[:, :, 0:ntok], in1=Em[:, :, 0:ntok], op=ALU.mult), reads=[kf, Em], writes=[Kt])
            k.op(POOL, lambda e: e.tensor_tensor(out=tm[:, :, 0:ntok], in0=kkn[:, :, 0:ntok], in1=ic[:, :, 0:ntok], op=ALU.mult), reads=[kkn, ic], writes=[tm])
            k.op(POOL, lambda e: e.tensor_tensor(out=Bt[:, :, 0:ntok], in0=tm[:, :, 0:ntok], in1=Em[:, :, 0:ntok], op=ALU.mult), reads=[tm, Em], writes=[Bt])
            k.copy(Vb[:, :, 0:ntok], vT[:, :, 0:ntok], [vT], [Vb], eng=ACT)
            for j, ob in enumerate(outs[ti % 2]):
                k.dma(k.POOL, opnd.t[ti, j].rearrange("p (c t) -> p c t", t=128)[:, :, 0:ntok], ob[:, :, 0:ntok], reads=[ob], writes=[opnd])
            wcb = wcs[ti % 2]
            for n in range(nch):
                k.op(DVE, lambda e: e.tensor_copy(out=wcb[:, n, :], in_=Ep[:, :, (n + 1) * C - 1]), reads=[Ep], writes=[wcb])
            k.dma(k.POOL, wcd.t[ti][:, 0:nch, :], wcb[:, 0:nch, :], reads=[wcb], writes=[wcd])

        front(0)
        for ti in range(len(p1segs)):
            if ti + 1 < len(p1segs):
                front(ti + 1)
            back(ti)
        k.barrier()
    P2E = DVE
    with contextlib.ExitStack() as st:
        Wout = k.sb("Wout", [128, KC, D], BF16, st)
        load_ln_tabs(k, g, 1, 1)
        load_weight(k, g, Wout, g.w_out1.t, D, D)
        gngc = load_cols(k, st, "gngc", g.gn_g.t, 8); gnbc = load_cols(k, st, "gnbc", g.gn_b.t, 8)
        ob32 = k.sb("ob32", [128, 128], F32, st); onesbd64 = k.sb("onesbd64", [128, 128], BF16, st)
        k.dma(k.SP, ob32[:], g.c_onesbd.t[:, :], writes=[ob32])
        k.op(ACT, lambda e: e.activation(out=onesbd64[:], in_=ob32[:], func=AF.Copy, scale=1.0 / 64.0), reads=[ob32], writes=[onesbd64])
        epsg = k.sb("epsg", [128, 1], F32, st)
        k.op(DVE, lambda e: e.memset(epsg[:], 64e-5), writes=[epsg])
        msk = k.sb("msk", [128, 3, 128], F32, st)
        Hx32 = k.sb("Hx32", [128, KC, 128], F32, st); Hb = k.sb("Hb", [128, KC, 128], BF16, st)
        Sx32 = k.sb("Sx32", [128, KC, 128], F32, st)
        X = lambda n: k.sb(n, [128, KC, 128], BF16, st)
        sets = []
        for par in range(2):
            s_ = Ctx()
            s_.Ear = k.sb("Ear", [128, KC, 2, 128], BF16, st)
            s_.Eb, s_.Ek, s_.Ev = X("Eb"), X("Ek"), X("Ev")
            s_.Gb = k.sb("Gb", [128, KC, 64], BF16, st); s_.Bon = k.sb("Bon", [128, KC, 64], BF16, st)
            s_.wc = k.sb("wc", [128, KC], F32, st); s_.x32 = k.sb("x32", [128, 1, D], F32, st)
            s_.VsT, s_.EbT, s_.EkT, s_.ArbT, s_.AakT, s_.ArkT, s_.PTb = [X(n) for n in ("VsT", "EbT", "EkT", "ArbT", "AakT", "ArkT", "PTb")]
            sets.append(s_)
        Mb = [X("Mb0"), X("Mb1")]; MTb = [X("MTb0"), X("MTb1")]
        Xb, Ub = X("Xb"), X("Ub")
        oT = k.sb("oT", [128, KC, 64], F32, st); tm = k.sb("tm", [128, KC, 64], F32, st)
        obf = k.sb("obf", [128, KC, 64], BF16, st); osq = k.sb("osq", [128, KC, 64], BF16, st); yT = k.sb("yT", [128, KC, 64], BF16, st)

        def zero_all():
            for s_ in sets:
                for zb in (s_.Ear, s_.Eb, s_.Ek, s_.Ev, s_.VsT, s_.EbT, s_.EkT, s_.ArbT, s_.AakT, s_.ArkT, s_.PTb):
                    k.op(DVE, lambda e: e.memset(zb[:], 0.0), writes=[zb])
            for zb in (Xb, Ub, Mb[0], Mb[1], MTb[0], MTb[1], obf, osq):
                k.op(DVE, lambda e: e.memset(zb[:], 0.0), writes=[zb])

        def fe2(ch, S, C, row0):
            tl, nn = ch
            R = 2 * C
            vR = lambda ps: ps[0:R, :].rearrange("p (c t) -> p c t", t=128)[:, :, 0:R]
            k.rotate()
            for hp in range(2):
                rw = slice(64 * hp, 64 * hp + 64); cl = slice(hp * C, (hp + 1) * C)
                src3 = lambda j: opnd.t[tl, j, 64 * hp:64 * hp + 64, :].rearrange("p (c t) -> p c t", t=128)[:, :, nn * 64:nn * 64 + C]
                k.dma(k.SP, S.Ear[rw, :, 0, cl], src3(0), reads=[opnd], writes=[S.Ear])
                k.dma(k.SP, S.Ear[rw, :, 1, cl], src3(1), reads=[opnd], writes=[S.Ear])
                k.dma(k.SP, S.Ek[rw, :, cl], src3(2), reads=[opnd], writes=[S.Ek])
                k.dma(k.SP, S.Eb[rw, :, cl], src3(3), reads=[opnd], writes=[S.Eb])
                k.dma(k.SP, S.Ev[rw, :, cl], src3(4), reads=[opnd], writes=[S.Ev])
            k.dma(k.SP, S.Gb[:, :, 0:C], opnd.t[tl, 5].rearrange("p (c t) -> p c t", t=128)[:, :, nn * 64:nn * 64 + C], reads=[opnd], writes=[S.Gb])
            k.dma(k.SP, S.Bon[:, :, 0:C], opnd.t[tl, 6].rearrange("p (c t) -> p c t", t=128)[:, :, nn * 64:nn * 64 + C], reads=[opnd], writes=[S.Bon])
            k.dma(k.SP, S.wc[:], wcd.t[tl][:, nn, :], reads=[wcd], writes=[S.wc])
            k.dma(k.SP, S.x32[0:C, 0, :], src.t[row0:row0 + C, :], reads=[src], writes=[S.x32])
            yield
            for (srcb, dstb) in ((S.Ev, S.VsT), (S.Eb, S.EbT), (S.Ek, S.EkT)):
                ps = k.psn()
                psb = ps.t[:, 0:512].bitcast(BF16)
                for c in range(KC):
                    k.tr(psb[0:R, c * 128:(c + 1) * 128], srcb[:, c, 0:R], g.identb[:, :], reads=[srcb, g.identb], writes=[ps])
                k.copy(dstb[0:R, :, :], psb[0:R, :].rearrange("p (c t) -> p c t", t=128), [ps], [dstb])
                yield
            mb = lambda j: msk[0:R, j, 0:R].unsqueeze(1).broadcast_to([R, KC, R])
            ea = lambda c: S.Ear[:, c, 0, 0:R]; er = lambda c: S.Ear[:, c, 1, 0:R]
            eb = lambda c: S.Eb[:, c, 0:R]; ek = lambda c: S.Ek[:, c, 0:R]
            for (lhs_sel, rhs_sel, dstb, mj) in ((ea, eb, Mb[0], 2), (eb, ea, MTb[0], 0), (eb, er, S.ArbT, 1), (ek, ea, S.AakT, 0), (ek, er, S.ArkT, 1)):
                ps = k.psn()
                for c in range(KC):
                    k.mm(ps[0:R, c * 128:c * 128 + R], lhs_sel(c), rhs_sel(c), True, True, reads=[S.Ear, S.Eb, S.Ek], writes=[ps])
                k.op(DVE, lambda e: e.tensor_tensor(out=dstb[0:R, :, 0:R], in0=vR(ps), in1=mb(mj), op=ALU.mult), reads=[ps, msk], writes=[dstb])
                yield
            k.op(P2E, lambda e: e.tensor_tensor(out=S.PTb[0:R, :, 0:R], in0=MTb[0][0:R, :, 0:R],
                                                 in1=g.identb[0:R, 0:R].unsqueeze(1).broadcast_to([R, KC, R]), op=ALU.add), reads=[MTb[0], g.identb], writes=[S.PTb])
            nlev = int(np.log2(C)) - 1
            cur = 0
            for j in range(1, nlev + 1):
                nx = 1 - cur
                ps = k.psn()
                for c in range(KC):
                    k.mm(ps[0:R, c * 128:c * 128 + R], MTb[cur][:, c, 0:R], Mb[cur][:, c, 0:R], True, True, reads=[MTb[cur], Mb[cur]], writes=[ps])
                k.copy(Mb[nx][0:R, :, 0:R], vR(ps), [ps], [Mb[nx]], eng=ACT)
                yield
                if j < nlev:
                    ps = k.psn()
                    for c in range(KC):
                        k.mm(ps[0:R, c * 128:c * 128 + R], Mb[cur][:, c, 0:R], MTb[cur][:, c, 0:R], True, True, reads=[MTb[cur], Mb[cur]], writes=[ps])
                    k.copy(MTb[nx][0:R, :, 0:R], vR(ps), [ps], [MTb[nx]], eng=ACT)
                    yield
                ps = k.psn()
                for c in range(KC):
                    k.mm(ps[0:R, c * 128:c * 128 + R], g.identb[:, 0:R], S.PTb[:, c, 0:R], True, False, reads=[g.identb, S.PTb], writes=[ps])
                    k.mm(ps[0:R, c * 128:c * 128 + R], Mb[nx][:, c, 0:R], S.PTb[:, c, 0:R], False, True, reads=[Mb[nx], S.PTb], writes=[ps])
                k.copy(S.PTb[0:R, :, 0:R], vR(ps), [ps], [S.PTb], eng=DVE)
                cur = nx
                yield

        def be2(ch, S, C, row0, seq, last):
            R = 2 * C; ntok = C
            ea = lambda c: S.Ear[:, c, 0, 0:R]; er = lambda c: S.Ear[:, c, 1, 0:R]
            v3 = lambda ps, w: ps[:, 0:512].rearrange("p (c t) -> p c t", t=64)[:, :, 0:w]
            v3b = lambda ps, w: ps[:, 512:1024].rearrange("p (c t) -> p c t", t=64)[:, :, 0:w]
            ps = k.psn()
            for c in range(KC):
                k.mm(ps[0:R, c * 128:(c + 1) * 128], ea(c), Hb[:, c, :], True, False, reads=[S.Ear, Hb], writes=[ps])
                k.mm(ps[0:R, c * 128:(c + 1) * 128], S.AakT[:, c, 0:R], S.VsT[:, c, :], False, True, reads=[S.AakT, S.VsT], writes=[ps])
            k.copy(Xb[0:R, :, :], ps[0:R, :].rearrange("p (c t) -> p c t", t=128), [ps], [Xb], eng=ACT)
            yield
            ps = k.psn()
            for c in range(KC):
                k.mm(ps[0:R, c * 128:(c + 1) * 128], S.PTb[:, c, 0:R], Xb[:, c, :], True, True, reads=[S.PTb, Xb], writes=[ps])
            k.copy(Ub[0:R, :, :], ps[0:R, :].rearrange("p (c t) -> p c t", t=128), [ps], [Ub], eng=ACT)
            yield
            ps = k.psn()
            for c in range(KC):
                k.mm(ps[:, c * 128:c * 128 + R], Hb[:, c, :], er(c), True, False, reads=[Hb, S.Ear], writes=[ps])
                k.mm(ps[:, c * 128:c * 128 + R], Ub[:, c, :], S.ArbT[:, c, 0:R], False, False, reads=[Ub, S.ArbT], writes=[ps])
                k.mm(ps[:, c * 128:c * 128 + R], S.VsT[:, c, :], S.ArkT[:, c, 0:R], False, True, reads=[S.VsT, S.ArkT], writes=[ps])
            psO = ps
            ps = k.psn()
            for c in range(KC):
                k.mm(ps[:, c * 128:(c + 1) * 128], S.EbT[:, c, :], Ub[:, c, :], True, False, reads=[S.EbT, Ub], writes=[ps])
                k.mm(ps[:, c * 128:(c + 1) * 128], S.EkT[:, c, :], S.VsT[:, c, :], False, True, reads=[S.EkT, S.VsT], writes=[ps])
            k.op(DVE, lambda e: e.tensor_tensor(out=Hx32[:], in0=ps[:, :].rearrange("p (c t) -> p c t", t=128), in1=Hx32[:], op=ALU.add), reads=[ps, Hx32], writes=[Hx32])
            k.op(P2E, lambda e: e.tensor_tensor(out=Hx32[:], in0=Hx32[:], in1=S.wc[:, :].unsqueeze(2).broadcast_to([128, KC, 128]), op=ALU.mult),
                 reads=[Hx32, S.wc], writes=[Hx32])
            k.copy(Hb[:], Hx32[:], [Hx32], [Hb], eng=ACT)
            yield
            for hp in range(2):
                rw = slice(64 * hp, 64 * hp + 64)
                k.copy(oT[rw, :, 0:ntok], psO[rw, :].rearrange("p (c t) -> p c t", t=128)[:, :, hp * C:(hp + 1) * C], [psO], [oT],
                       eng=(ACT if hp == 0 else DVE))
            yield
            k.op(ACT, lambda e: e.activation(out=obf[:, :, 0:ntok], in_=oT[:, :, 0:ntok], func=AF.Copy), reads=[oT], writes=[obf])
            k.op(ACT, lambda e: e.activation(out=osq[:, :, 0:ntok], in_=oT[:, :, 0:ntok], func=AF.Square), reads=[oT], writes=[osq])
            ps = k.psn()
            k.mm(ps[:, 0:512], onesbd64[:, :], obf[:, :, :].rearrange("p c t -> p (c t)"), True, True, reads=[onesbd64, obf], writes=[ps])
            k.mm(ps[:, 512:1024], onesbd64[:, :], osq[:, :, :].rearrange("p c t -> p (c t)"), True, True, reads=[onesbd64, osq], writes=[ps])
            yield
            k.op(ACT, lambda e: e.activation(out=tm[:, :, 0:ntok], in_=v3(ps, ntok), func=AF.Square), reads=[ps], writes=[tm])
            k.op(DVE, lambda e: e.tensor_tensor(out=tm[:, :, 0:ntok], in0=v3b(ps, ntok), in1=tm[:, :, 0:ntok], op=ALU.subtract), reads=[ps, tm], writes=[tm])
            k.op(ACT, lambda e: e.activation(out=tm[:, :, 0:ntok], in_=tm[:, :, 0:ntok], func=AF.Ln, bias=epsg[:, 0:1]), reads=[tm, epsg], writes=[tm])
            k.op(ACT, lambda e: e.activation(out=tm[:, :, 0:ntok], in_=tm[:, :, 0:ntok], func=AF.Exp, scale=-0.5), reads=[tm], writes=[tm])
            k.op(DVE, lambda e: e.tensor_tensor(out=oT[:, :, 0:ntok], in0=oT[:, :, 0:ntok], in1=v3(ps, ntok), op=ALU.subtract), reads=[oT, ps], writes=[oT])
            yield
            k.op(P2E, lambda e: e.tensor_tensor(out=oT[:, :, 0:ntok], in0=oT[:, :, 0:ntok], in1=tm[:, :, 0:ntok], op=ALU.mult), reads=[oT, tm], writes=[oT])
            k.op(P2E, lambda e: e.tensor_tensor(out=oT[:, :, 0:ntok], in0=oT[:, :, 0:ntok], in1=gngc[:, :].unsqueeze(2).broadcast_to([128, KC, ntok]), op=ALU.mult),
                 reads=[oT, gngc], writes=[oT])
            k.op(P2E, lambda e: e.tensor_tensor(out=oT[:, :, 0:ntok], in0=oT[:, :, 0:ntok], in1=gnbc[:, :].unsqueeze(2).broadcast_to([128, KC, ntok]), op=ALU.add),
                 reads=[oT, gnbc], writes=[oT])
            k.op(P2E, lambda e: e.tensor_tensor(out=oT[:, :, 0:ntok], in0=oT[:, :, 0:ntok], in1=S.Bon[:, :, 0:ntok], op=ALU.add), reads=[oT, S.Bon], writes=[oT])
            k.op(P2E, lambda e: e.tensor_tensor(out=yT[:, :, 0:ntok], in0=oT[:, :, 0:ntok], in1=S.Gb[:, :, 0:ntok], op=ALU.mult), reads=[oT, S.Gb], writes=[yT])
            yield
            ps = k.psn()
            for hf in range(2):
                for c in range(KC):
                    k.mm(ps[0:ntok, hf * 512:(hf + 1) * 512], yT[:, c, 0:ntok], Wout[:, c, hf * 512:(hf + 1) * 512], c == 0, c == KC - 1,
                         reads=[yT, Wout], writes=[ps])
            yield
            ln_epilogue(k, g, ps, S.x32[0:ntok, 0, :], S.x32, ntok, dst, row0, ALPHA)
            if last:
                rl = row0 + ntok - 1
                k.dma(k.POOL, g.o_shift.t[seq:seq + 1, :], src.t[rl:rl + 1, :], reads=[src], writes=[g.o_shift])
                ps = k.psn()
                for c in range(KC):
                    k.tr(ps[:, c * 128:(c + 1) * 128], Hx32[:, c, :], g.ident[:, :], reads=[Hx32, g.ident], writes=[ps])
                k.copy(Sx32[:], ps[:, :].rearrange("p (c t) -> p c t", t=128), [ps], [Sx32], eng=DVE)
                for hp in range(2):
                    k.dma(k.POOL, g.o_wkv.t[seq].rearrange("(c hp) v kk -> hp v c kk", hp=2)[hp],
                          Sx32[64 * hp:64 * hp + 64, :, 64 * hp:64 * hp + 64], reads=[Sx32], writes=[g.o_wkv])
            yield

        for seq in range(3):
            ci = 0 if seq == 0 else 1
            C = 64 if seq == 0 else 32
            chunks = [((n // 2, n % 2), n * 64) for n in range(TP // 64)] if seq == 0 else [((TP // 128 + seq - 1, 0), TP + (seq - 1) * TS)]
            zero_all()
            k.dma(k.SP, msk[:], g.c_msk.t[ci], writes=[msk])
            k.op(DVE, lambda e: e.memset(Hx32[:], 0.0), writes=[Hx32])
            if seq > 0:
                k.op(DVE, lambda e: e.memset(Sx32[:], 0.0), writes=[Sx32])
                for hp in range(2):
                    k.dma(k.SP, Sx32[64 * hp:64 * hp + 64, :, 64 * hp:64 * hp + 64],
                          g.st_wkv.t[seq - 1].rearrange("(c hp) v kk -> hp v c kk", hp=2)[hp], writes=[Sx32])
                ps = k.psn()
                for c in range(KC):
                    k.tr(ps[:, c * 128:(c + 1) * 128], Sx32[:, c, :], g.ident[:, :], reads=[Sx32, g.ident], writes=[ps])
                k.copy(Hx32[:], ps[:, :].rearrange("p (c t) -> p c t", t=128), [ps], [Hx32], eng=DVE)
            k.copy(Hb[:], Hx32[:], [Hx32], [Hb], eng=ACT)
            for _ in fe2(chunks[0][0], sets[0], C, chunks[0][1]):
                pass
            for i, (ch, row0) in enumerate(chunks):
                nxt = fe2(chunks[i + 1][0], sets[(i + 1) % 2], C, chunks[i + 1][1]) if i + 1 < len(chunks) else None
                _interleave(nxt, be2(ch, sets[i % 2], C, row0, seq, i == len(chunks) - 1), ra=1000, rb=1)
        k.barrier()


def build(TP, nstage=8):
    nc = bass.Bass("TRN2", target_bir_lowering=False)
    k = KB(nc)
    g = declare_io(k, TP)
    setup_globals(k, g)
    stages = [
        lambda s, d: stage_ffn(k, g, TP, 0, 0, s, d),
        lambda s, d: stage_mixer_ab(k, g, TP, s, d),
        lambda s, d: stage_xattn(k, g, TP, 0, s, d),
        lambda s, d: stage_ffn(k, g, TP, 0, 1, s, d),
        lambda s, d: stage_ffn(k, g, TP, 1, 0, s, d),
        lambda s, d: stage_rwkv(k, g, TP, s, d),
        lambda s, d: stage_xattn(k, g, TP, 1, s, d),
        lambda s, d: stage_ffn(k, g, TP, 1, 1, s, d),
    ][:nstage]
    src = g.x
    for i, stf in enumerate(stages):
        dst = g.y if i == len(stages) - 1 else g.xs[i % 2]
        stf(src, dst)
        src = dst
    k.finish()
    return nc


def host_consts(TP):
    c = {}
    c["c_ident"] = np.eye(128, dtype=np.float32)
    half = 64
    inv_freq = (10000.0 ** (-np.arange(half, dtype=np.float32) / np.float32(half))).astype(np.float32)
    pos = np.concatenate([np.arange(TP), PAST + np.arange(TS)]).astype(np.float32)
    ang = (pos[:, None] * inv_freq[None, :]).astype(np.float32)
    cos = np.cos(ang.astype(np.float64)).astype(np.float32).T
    sin = np.sin(ang.astype(np.float64)).astype(np.float32).T
    c["c_cos"] = np.ascontiguousarray(np.concatenate([cos, cos], 0))
    c["c_sin"] = np.ascontiguousarray(np.concatenate([-sin, sin], 0))
    M = np.zeros((2, 128, 4, 128), np.float32); XI = np.zeros((2, 128, 4, 128), np.float32); Z = np.zeros((2, 128, 4, 128), np.float32)
    for ci, C in enumerate((128, 32)):
        idx = np.arange(C, dtype=np.float64)
        for h in range(4):
            lg = np.log1p(-(2.0 ** (-5.0 - h)))
            m = np.where(idx[None, :] >= idx[:, None], np.exp(-lg * C), 0.0)
            M[ci, :C, h, :C] = m
            XI[ci, :, h, :C] = np.exp(lg * (idx + 1.0))[None, :]
            Z[ci, :, h, :C] = (np.exp(lg * (C - 1.0 - idx)) * 128 ** -0.5)[None, :]
    c["c_retM"] = M; c["c_retXI"] = XI; c["c_retZ"] = Z
    msk = np.zeros((2, 128, 3, 128), np.float32)
    for ci, C in enumerate((64, 32)):
        for hp in range(2):
            for s in range(C):
                msk[ci, hp * C + s, 0, hp * C + s + 1: hp * C + C] = 1.0
                msk[ci, hp * C + s, 1, hp * C + s: hp * C + C] = 1.0
        msk[ci, :, 2, :] = msk[ci, :, 0, :].T
    c["c_msk"] = msk
    ob = np.zeros((128, 128), np.float32); ob[:64, :64] = 1.0; ob[64:, 64:] = 1.0
    c["c_onesbd"] = ob
    return c


_W_NAMES = ["ln_g", "ln_b", "ffn_up", "ffn_down", "xa_q", "xa_k", "xa_v", "xa_o", "l0_w_in", "l0_conv_w", "l0_conv_b",
            "l0_lru_wa", "l0_lru_ba", "l0_lru_wx", "l0_lru_bx", "l0_lru_lambda", "l0_ret_gn_g", "l0_ret_gn_b", "l0_w_out",
            "l1_mu", "l1_w_rkv", "l1_w0", "l1_w1", "l1_w2", "l1_a0", "l1_a1", "l1_a2", "l1_g1", "l1_g2", "l1_k_k", "l1_k_a",
            "l1_gn_g", "l1_gn_b", "l1_w_out"]


def make_in_maps(inp, TP):
    f = lambda a: np.ascontiguousarray(np.asarray(a, dtype=np.float32))
    shared = {n: f(inp[n]) for n in _W_NAMES}
    shared["l1_r_k"] = f(inp["l1_r_k"]).reshape(-1)
    w_in = f(inp["l0_w_in"])
    rot = []
    for base in (1024, 1536):
        for h in range(4):
            b0 = base + h * 128
            rot.append(w_in[:, b0 + 64: b0 + 128]); rot.append(w_in[:, b0: b0 + 64])
    shared["l0_w_rot"] = np.ascontiguousarray(np.concatenate(rot, axis=1))
    shared.update(host_consts(TP))
    maps = []
    for b in range(NCORES):
        m = dict(shared)
        m["x"] = np.ascontiguousarray(np.concatenate([f(inp["x_prompt"][b]), f(inp["x_sample"][2 * b]), f(inp["x_sample"][2 * b + 1])], 0))
        m["mem"] = f(inp["mem_prompt"][b])
        sl = slice(2 * b, 2 * b + 2)
        m["st_conv"] = f(inp["state_conv0"][sl]); m["st_lru"] = f(inp["state_lru0"][sl]); m["st_ret"] = f(inp["state_ret0"][sl])
        m["st_shift"] = f(inp["state_shift1"][sl]).reshape(2, D); m["st_wkv"] = f(inp["state_wkv1"][sl])
        m["ck"] = f(inp["cache_mem_k"][:, sl]).reshape(2, 2, 256, D); m["cv"] = f(inp["cache_mem_v"][:, sl]).reshape(2, 2, 256, D)
        maps.append(m)
    return maps


_NC_CACHE = {}


def run(inp, TP, nstage=8, ncores=NCORES):
    key = (TP, nstage)
    if key not in _NC_CACHE:
        _NC_CACHE[key] = build(TP, nstage)
    nc = _NC_CACHE[key]
    res = run_bass_kernel_spmd(nc, make_in_maps(inp, TP)[:ncores], core_ids=list(range(ncores)))
    R = list(res.results)
    while len(R) < NCORES:
        R.append(R[0])
    st = lambda n: np.stack([r[n] for r in R], 0)
    y = st("y")
    y_p = y[:, :TP]
    y_s = y[:, TP:].reshape(NCORES * 2, TS, D)
    memk = st("o_memk").transpose(1, 0, 2, 3).reshape(2, NCORES, 256, 4, 256)
    memv = st("o_memv").transpose(1, 0, 2, 3).reshape(2, NCORES, 256, 4, 256)
    oc, ol, orr, osh, ow = st("o_conv"), st("o_lru"), st("o_ret"), st("o_shift"), st("o_wkv")
    pf = lambda a: np.ascontiguousarray(a[:, 0])
    sf = lambda a: np.ascontiguousarray(a[:, 1:3].reshape((NCORES * 2,) + a.shape[2:]))
    return (np.ascontiguousarray(y_p), np.ascontiguousarray(y_s), np.ascontiguousarray(memk), np.ascontiguousarray(memv),
            pf(oc), pf(ol), pf(orr), pf(osh)[:, None, :], pf(ow),
            sf(oc), sf(ol), sf(orr), sf(osh)[:, None, :], sf(ow))


def kernel(**inputs):
    TP = int(np.asarray(inputs["x_prompt"]).shape[1])
    return run(inputs, TP, 8)
```
